# Optimizing a Trainium2 kernel written in Bass

```python
import jax, jax.numpy as jnp
from jax import lax
import numpy as np

D_MODEL = 1024
BATCH = 8
SEQ = 4096
DEPTH = 1

HEAD_DIM = 64
SB_HEADS = 8
SW_HEADS = 8
SW_KV_HEADS = 2
SB_WIDTH = SB_HEADS * HEAD_DIM
SW_WIDTH = SW_HEADS * HEAD_DIM
SW_KV_WIDTH = SW_KV_HEADS * HEAD_DIM
MIX_WIDTH = SB_WIDTH + SW_WIDTH
WINDOW = 128
BLOCK = 128
RMS_EPS = 1e-6
NEG_INF = -1e30
IN_SPLIT_SIZES = (SB_WIDTH, SB_WIDTH, SB_WIDTH, SB_WIDTH, SW_WIDTH, SW_KV_WIDTH, SW_KV_WIDTH, SW_WIDTH)
IN_WIDTH = 4 * SB_WIDTH + 2 * SW_WIDTH + 2 * SW_KV_WIDTH

kernel_name = "hymba_stickbreak_swa_sink_adaln"


def rmsnorm(x, g):
    xf = x.astype(jnp.float32)
    y = xf * lax.rsqrt(jnp.mean(xf * xf, axis=-1, keepdims=True) + RMS_EPS)
    return (y * g.astype(jnp.float32)).astype(x.dtype)


def alibi_slopes(n):
    return jnp.asarray([2.0 ** (-8.0 * (h + 1) / n) for h in range(n)], dtype=jnp.float32)


def stick_breaking_attention(q, k, v):
    S, Dh = q.shape[1], q.shape[3]
    scale = Dh ** -0.5
    outs = []
    for start in range(0, S, BLOCK):
        end = start + BLOCK
        qb = q[:, start:end]
        kb = k[:, :end]
        vb = v[:, :end]
        z = jnp.einsum('bqhd,bkhd->bhqk', qb, kb).astype(jnp.float32) * scale
        qpos = start + jnp.arange(BLOCK)[:, None]
        kpos = jnp.arange(end)[None, :]
        causal = kpos < qpos
        log_beta = jax.nn.log_sigmoid(z)
        log_1mb = jnp.where(causal, jax.nn.log_sigmoid(-z), 0.0)
        later = lax.cumsum(log_1mb, axis=3, reverse=True) - log_1mb
        w = jnp.where(causal, jnp.exp(log_beta + later), 0.0)
        outs.append(jnp.einsum('bhqk,bkhd->bqhd', w.astype(v.dtype), vb))
    return jnp.concatenate(outs, axis=1)


def sliding_window_sink_attention(q, k, v, sinks, slopes):
    B, S, H, Dh = q.shape
    Hkv = k.shape[2]
    G = H // Hkv
    nb = S // BLOCK
    qb = q.reshape(B, nb, BLOCK, Hkv, G, Dh)

    def band(t):
        tb = t.reshape(B, nb, BLOCK, Hkv, Dh)
        prev = jnp.pad(tb[:, :-1], ((0, 0), (1, 0), (0, 0), (0, 0), (0, 0)))
        return jnp.concatenate([prev, tb], axis=2)

    kb, vb = band(k), band(v)
    s = jnp.einsum('bnqhgd,bnshd->bhgnqs', qb, kb).astype(jnp.float32) * (Dh ** -0.5)
    r = jnp.arange(BLOCK)[:, None]
    j = jnp.arange(2 * BLOCK)[None, :]
    rel = (BLOCK + r - j)
    kpos = (jnp.arange(nb)[:, None, None] - 1) * BLOCK + j[None]
    valid = (rel >= 0)[None] & (rel < WINDOW)[None] & (kpos >= 0)
    m = slopes.reshape(Hkv, G)[:, :, None, None, None]
    logits = s - m * rel.astype(jnp.float32)
    logits = jnp.where(valid, logits, NEG_INF)
    sink = jnp.broadcast_to(sinks.astype(jnp.float32).reshape(1, Hkv, G, 1, 1, 1),
                            logits.shape[:-1] + (1,))
    p = jax.nn.softmax(jnp.concatenate([logits, sink], axis=-1), axis=-1)[..., :-1]
    o = jnp.einsum('bhgnqs,bnshd->bnqhgd', p.astype(v.dtype), vb)
    return o.reshape(B, S, H, Dh)


def setup_inputs(seed: int = 0) -> dict:
    key = jax.random.key(seed)
    ks = jax.random.split(key, 11)
    D = D_MODEL
    x = jax.random.normal(ks[0], (BATCH, SEQ, D), jnp.float32)
    c = jax.random.normal(ks[1], (BATCH, D), jnp.float32)
    w_ada = jax.random.normal(ks[2], (DEPTH, D, 3 * D), jnp.float32) * (0.5 * D ** -0.5)
    b_ada = jax.random.normal(ks[3], (DEPTH, 3 * D), jnp.float32) * 0.01
    norm_g = 1.0 + 0.01 * jax.random.normal(ks[4], (DEPTH, D), jnp.float32)
    w_in = jax.random.normal(ks[5], (DEPTH, D, IN_WIDTH), jnp.float32) * (D ** -0.5)
    sinks = jax.random.normal(ks[6], (DEPTH, SW_HEADS), jnp.float32) * 0.5
    w_out = jax.random.normal(ks[7], (DEPTH, MIX_WIDTH, D), jnp.float32) * (MIX_WIDTH ** -0.5)
    final_g = 1.0 + 0.01 * jax.random.normal(ks[8], (D,), jnp.float32)
    return {"x": x, "c": c, "w_ada": w_ada, "b_ada": b_ada, "norm_g": norm_g,
            "w_in": w_in, "sinks": sinks, "w_out": w_out, "final_g": final_g}


def reference(x, c, w_ada, b_ada, norm_g, w_in, sinks, w_out, final_g):
    B, S, _ = x.shape
    slopes = alibi_slopes(SW_HEADS)
    offsets = np.cumsum(IN_SPLIT_SIZES)[:-1].tolist()
    cond = jax.nn.silu(c)
    for l in range(DEPTH):
        mod = cond @ w_ada[l] + b_ada[l]
        shift, scale, gate = jnp.split(mod, 3, axis=-1)
        h = rmsnorm(x, norm_g[l]) * (1.0 + scale[:, None, :]) + shift[:, None, :]
        proj = h @ w_in[l]
        sb_q, sb_k, sb_v, sb_g, sw_q, sw_k, sw_v, sw_g = jnp.split(proj, offsets, axis=-1)
        y_sb = stick_breaking_attention(
            sb_q.reshape(B, S, SB_HEADS, HEAD_DIM),
            sb_k.reshape(B, S, SB_HEADS, HEAD_DIM),
            sb_v.reshape(B, S, SB_HEADS, HEAD_DIM)).reshape(B, S, SB_WIDTH)
        y_sb = y_sb * jax.nn.silu(sb_g)
        y_sw = sliding_window_sink_attention(
            sw_q.reshape(B, S, SW_HEADS, HEAD_DIM),
            sw_k.reshape(B, S, SW_KV_HEADS, HEAD_DIM),
            sw_v.reshape(B, S, SW_KV_HEADS, HEAD_DIM),
            sinks[l], slopes).reshape(B, S, SW_WIDTH)
        y_sw = y_sw * jax.nn.silu(sw_g)
        y = jnp.concatenate([y_sb, y_sw], axis=-1) @ w_out[l]
        x = x + gate[:, None, :] * y
    return rmsnorm(x, final_g)
```

```python
from contextlib import ExitStack

import numpy as np
import concourse.bass as bass
import concourse.mybir as mybir
from concourse.bass_utils import run_bass_kernel_spmd

F32 = mybir.dt.float32
BF16 = mybir.dt.bfloat16
AF = mybir.ActivationFunctionType
ALU = mybir.AluOpType

S = 4096
D = 1024
NCORE = 8
NT = S // 128
QC = 1024
NEG = -30000.0
NCONST = 128 * 6 + 2048
C_ID, C_TRI, C_TRIC, C_ZERO, C_NEGM, C_ONES, C_SWB = 0, 128, 256, 384, 512, 640, 768


LEVEL = 99
SW_DBG = 0
PUMP = 6


class _Stop(Exception):
    pass


def ckpt(level):
    if LEVEL <= level:
        raise _Stop()


class Plan:
    def __init__(self):
        self.ops = {"pe": [], "act": [], "dve": [], "pool": [], "sp": []}
        self.cnt = {}

    def op(self, eng, fn, waits=(), sig=None, inc=1):
        v = None
        if sig is not None:
            self.cnt[sig] = self.cnt.get(sig, 0) + inc
            v = self.cnt[sig]
        ws = tuple(w for w in waits if w is not None and w[1] is not None and w[1] > 0)
        self.ops[eng].append((fn, ws, sig, inc))
        return v


def make_consts():
    c = np.zeros((128, NCONST), np.float32)
    j = np.arange(128)[:, None]
    s = np.arange(128)[None, :]
    c[:, C_ID:C_ID + 128] = (j == s)
    c[:, C_TRI:C_TRI + 128] = -1.0 * (j >= s)
    c[:, C_TRIC:C_TRIC + 128] = -1.0 * (j < s)
    c[:, C_NEGM:C_NEGM + 128] = np.where(j < s, 0.0, NEG)
    c[:, C_ONES:C_ONES + 128] = 1.0
    for h in range(8):
        m = 2.0 ** (-8.0 * (h + 1) / 8)
        rel_cur = (s - j).astype(np.float32)
        cur = np.where(s >= j, -m * rel_cur, NEG)
        rel_prev = (128 + s - j).astype(np.float32)
        prev = np.where(j > s, -m * rel_prev, NEG)
        c[:, C_SWB + h * 256:C_SWB + h * 256 + 128] = cur
        c[:, C_SWB + h * 256 + 128:C_SWB + h * 256 + 256] = prev
    return c


def sb_tiles():
    out = []
    for qc in range(S // QC):
        nkb = (QC // 128) * (qc + 1)
        for kb in range(nkb - 1, -1, -1):
            jd = kb - (QC // 128) * qc
            diag = jd >= 0
            c0 = 128 * jd if diag else 0
            out.append((qc, kb, c0, diag, kb == nkb - 1, kb == 0))
    return out


def col_segs(c0, c1=QC):
    segs = []
    a = c0
    while a < c1:
        b = min(c1, (a // 512 + 1) * 512)
        segs.append((a, b))
        a = b
    return segs


def build_nc():
    nc = bass.Bass("TRN2", target_bir_lowering=False)
    x = nc.dram_tensor("x", [S, D], F32, kind="ExternalInput").ap()
    c_l = nc.dram_tensor("c_l", [128, 8], F32, kind="ExternalInput").ap()
    w_ada = nc.dram_tensor("w_ada", [D, 3 * D], F32, kind="ExternalInput").ap()
    bada_l = nc.dram_tensor("bada_l", [128, 16], F32, kind="ExternalInput").ap()
    bg_bc_d = nc.dram_tensor("bg_bc", [128, D], F32, kind="ExternalInput").ap()
    normg_l = nc.dram_tensor("normg_l", [128, 8], F32, kind="ExternalInput").ap()
    w_in = nc.dram_tensor("w_in", [D, 3328], F32, kind="ExternalInput").ap()
    sinks_l = nc.dram_tensor("sinks_l", [128, 4], F32, kind="ExternalInput").ap()
    w_out = nc.dram_tensor("w_out", [D, D], F32, kind="ExternalInput").ap()
    fg_bc_d = nc.dram_tensor("fg_bc", [128, D], F32, kind="ExternalInput").ap()
    consts_d = nc.dram_tensor("consts", [128, NCONST], F32, kind="ExternalInput").ap()
    out = nc.dram_tensor("out", [S, D], F32, kind="ExternalOutput").ap()

    P = Plan()
    es = ExitStack()
    with es:
        def sb(name, shape, dt):
            return es.enter_context(nc.sbuf_tensor(name, shape, dt))

        ygT = sb("ygT", [128, 8, S], BF16)
        hT = sb("hT", [128, 8, S], BF16)
        ov = sb("ov", [128, 30720], BF16)
        cst = sb("cst", [128, NCONST], BF16)
        gate_bc = sb("gate_bc", [128, D], F32)
        c_sb = sb("c_sb", [128, 8], F32)
        etmp = sb("etmp", [128, 8], F32)
        cond = sb("cond", [128, 8], F32)
        bada = sb("bada", [128, 16], F32)
        normg = sb("normg", [128, 8], F32)
        mod_sb = sb("mod_sb", [128, 16], F32)
        gs = sb("gs", [128, 8], F32)
        ones_f = sb("ones_f", [128, 128], F32)
        ss = sb("ss", [128, 32], F32)
        rstd = sb("rstd", [128, 32], F32)
        ss2 = sb("ss2", [128, 32], F32)
        rstd2 = sb("rstd2", [128, 32], F32)
        snk = sb("snk", [128, 4], F32)
        esk = sb("esk", [128, 4], F32)
        eps_t = sb("eps_t", [128, 1], F32)
        etmpf = [sb("etmpf%d" % i, [128, 512], F32) for i in range(2)]
        ps = es.enter_context(nc.psum_tensor("ps", [128, 4096], F32))

        qT = ov[:, 0:4096]
        kT = ov[:, 4096:8192]
        sg = ov[:, 8192:12288]
        vv = ov[:, 12288:16384].rearrange("p (b c) -> p b c", c=128)
        wsl = ov[:, 16384:20480].rearrange("p (k c) -> p k c", c=512)
        PB = 20480
        Eb = [ov[:, PB + i * 1024:PB + (i + 1) * 1024] for i in range(3)]
        Gb = [ov[:, PB + (3 + i) * 1024:PB + (4 + i) * 1024] for i in range(3)]
        Pb = [ov[:, PB + (6 + i) * 1024:PB + (7 + i) * 1024] for i in range(2)]
        Ab = [ov[:, PB + (8 + i) * 1024:PB + (9 + i) * 1024] for i in range(2)]
        Psw = [ov[:, PB + i * 512:PB + (i + 1) * 512] for i in range(2)]
        rden = ov[:, PB + 1024:PB + 2048].bitcast(F32)
        ytmp = ov[:, PB + 2048:PB + 3072].bitcast(F32)
        wada = [ov[:, i * 6144:(i + 1) * 6144].bitcast(F32) for i in range(2)]
        xt = [ov[:, 12288 + i * 2048:12288 + (i + 1) * 2048].bitcast(F32) for i in range(4)]
        xnb = [[ov[:, 20480 + (g * 4 + t) * 1024:20480 + (g * 4 + t + 1) * 1024] for t in range(4)] for g in range(2)]
        junk = ov[:, 28672:29696]
        hflat = hT[:, :, :].rearrange("p a b -> p (a b)")
        condb = hflat[:, 0:2048].bitcast(F32).rearrange("p (k m) -> p k m", m=128)
        bg_bc = hflat[:, 2048:4096].bitcast(F32)
        wout = hflat[:, 0:8192].rearrange("p (k e) -> p k e", e=1024)
        xf = [hflat[:, 8192 + i * 2048:8192 + (i + 1) * 2048].bitcast(F32) for i in range(3)]
        rf = [hflat[:, 14336 + i * 2048:14336 + (i + 1) * 2048].bitcast(F32) for i in range(2)]
        of = [hflat[:, 18432 + i * 2048:18432 + (i + 1) * 2048].bitcast(F32) for i in range(2)]
        fg_bc = hflat[:, 22528:24576].bitcast(F32)
        junkf = hflat[:, 24576:25600]
        Zp = ps[:, 0:1024]
        Rp = ps[:, 1024:2048]
        Yp = [ps[:, 2048:3072], ps[:, 3072:4096]]
        pp = [ps[:, 0:512], ps[:, 512:1024]]
        tp = [ps[:, i * 512:(i + 1) * 512].bitcast(BF16)[:, 0:512] for i in range(2)]
        modps = ps[:, 1024:1040]
        gateps = ps[:, 2048:3072]
        po = [[ps[:, t * 1024 + e * 512:t * 1024 + (e + 1) * 512] for e in range(2)] for t in range(2)]

        ident = cst[:, C_ID:C_ID + 128]
        triN = cst[:, C_TRI:C_TRI + 128]
        tricN = cst[:, C_TRIC:C_TRIC + 128]
        zer = cst[:, C_ZERO:C_ZERO + 128]
        negm = cst[:, C_NEGM:C_NEGM + 128]
        ones_bf = cst[:, C_ONES:C_ONES + 128]

        def plan_all():
            t_cst = P.op("pool", lambda e: e.dma_start(out=cst[:, :], in_=consts_d[:, :]), sig="ld_cst", inc=16)
            for dst, src in ((c_sb, c_l), (bada, bada_l), (normg, normg_l), (snk, sinks_l)):
                t_small = P.op("sp", lambda e, dst=dst, src=src: e.dma_start(out=dst[:, :], in_=src[:, :]), sig="ld_small", inc=16)
            t_small = P.op("sp", lambda e: e.dma_start(out=bg_bc, in_=bg_bc_d[:, :]), sig="ld_small", inc=16)

            ckpt(0)
            P.op("dve", lambda e: e.memset(ones_f[:, :], 1.0), sig="dve0")
            P.op("dve", lambda e: e.memset(eps_t[:, :], 1e-6), sig="dve0")
            P.op("dve", lambda e: e.memset(ss[:, :], 0.0), sig="dve0")
            t_d = P.op("dve", lambda e: e.memset(ss2[:, :], 0.0), sig="dve0")
            t_a = P.op("act", lambda e: e.activation(out=etmp[:, :], in_=c_sb[:, :], func=AF.Exp, scale=-1.0),
                       waits=[("ld_small", t_small)], sig="act0")
            t_d = P.op("dve", lambda e: e.tensor_scalar_add(out=etmp[:, :], in0=etmp[:, :], scalar1=1.0),
                       waits=[("act0", t_a), ("dve0", t_d)], sig="dve0")
            t_d = P.op("dve", lambda e: e.reciprocal(out=etmp[:, :], in_=etmp[:, :]), waits=[("dve0", t_d)], sig="dve0")
            t_d = P.op("dve", lambda e: e.tensor_mul(out=cond[:, :], in0=c_sb[:, :], in1=etmp[:, :]),
                       waits=[("dve0", t_d)], sig="dve0")
            t_cond = t_d
            for k in range(8):
                t_d = P.op("dve", lambda e, k=k: e.tensor_scalar(out=condb[:, k, :], in0=ones_f[:, :], scalar1=cond[:, k:k + 1],
                                                                 scalar2=None, op0=ALU.mult),
                           waits=[("dve0", t_cond)], sig="dve0")
            t_condb = t_d
            t_pe0 = {}
            t_wada = {}
            for k in range(8):
                t_wada[k] = P.op("sp", lambda e, k=k: e.dma_start(out=wada[k % 2], in_=w_ada[k * 128:(k + 1) * 128, :]),
                                 waits=[("pe0", t_pe0.get(k - 2))], sig="ld_wada%d" % (k % 2), inc=16)
                for j in range(16):
                    P.op("pe", lambda e, k=k, j=j: e.matmul(modps[:, j:j + 1], lhsT=wada[k % 2][:, j * 128:(j + 1) * 128],
                                                            rhs=cond[:, k:k + 1], start=(k == 0 and j == 0), stop=(k == 7),
                                                            skip_group_check=True),
                         waits=[("ld_wada%d" % (k % 2), t_wada[k]), ("dve0", t_condb)])
                for eh in range(2):
                    t_pe0[k] = P.op("pe", lambda e, k=k, eh=eh: e.matmul(gateps[:, eh * 512:(eh + 1) * 512], lhsT=condb[:, k, :],
                                                                         rhs=wada[k % 2][:, 2048 + eh * 512:2048 + (eh + 1) * 512],
                                                                         start=(k == 0), stop=(k == 7)),
                                    waits=[("ld_wada%d" % (k % 2), t_wada[k]), ("dve0", t_condb)], sig=("pe0" if eh == 1 else None))
            t_d = P.op("dve", lambda e: e.tensor_add(out=mod_sb[:, :], in0=modps, in1=bada[:, :]),
                       waits=[("pe0", t_pe0[7]), ("ld_small", t_small)], sig="dve0")
            t_d = P.op("dve", lambda e: e.scalar_tensor_tensor(out=gs[:, :], in0=mod_sb[:, 8:16], scalar=1.0, in1=normg[:, :],
                                                               op0=ALU.add, op1=ALU.mult),
                       waits=[("dve0", t_d)], sig="dve0")
            t_d = P.op("dve", lambda e: e.tensor_add(out=gate_bc[:, :], in0=gateps, in1=bg_bc), sig="dve0")
            t_mod = t_d

            ckpt(1)
            t_xld = {}
            t_sq = {}
            t_xn = {}
            t_tp = {}
            t_ev = {}

            def issue_xload(tt):
                t_xld[tt] = P.op("sp", lambda e, tt=tt: e.dma_start(out=xt[tt % 4], in_=x[tt * 128:(tt + 1) * 128, :]),
                                 waits=[("p1xn", t_xn.get(tt - 4)), ("p1sq", t_sq.get(tt - 4))],
                                 sig="ld_x%d" % (tt % 4), inc=16)

            for tt in range(4):
                issue_xload(tt)
            for g in range(8):
                if True:
                    for t4 in range(4):
                        tt = g * 4 + t4
                        t_sq[tt] = P.op("act", lambda e, tt=tt: e.activation(out=junk, in_=xt[tt % 4], func=AF.Square,
                                                                             accum_out=ss[:, tt:tt + 1]),
                                        waits=[("ld_x%d" % (tt % 4), t_xld[tt]), ("dve0", t_mod)], sig="p1sq")
                        t_r = P.op("act", lambda e, tt=tt: e.activation(out=rstd[:, tt:tt + 1], in_=ss[:, tt:tt + 1], func=AF.Sqrt,
                                                                        scale=1.0 / D, bias=eps_t[:, 0:1]),
                                   waits=[("p1sq", t_sq[tt])], sig="p1sqrt")
                        t_r = P.op("dve", lambda e, tt=tt: e.reciprocal(out=rstd[:, tt:tt + 1], in_=rstd[:, tt:tt + 1]),
                                   waits=[("p1sqrt", t_r)], sig="dve1")
                        t_xn[tt] = P.op("dve", lambda e, tt=tt, g=g, t4=t4: e.tensor_scalar(
                            out=xnb[g % 2][t4], in0=xt[tt % 4], scalar1=rstd[:, tt:tt + 1], scalar2=None, op0=ALU.mult),
                            waits=[("dve1", t_r), ("p1tp", t_tp.get((g - 2, 7)))], sig="p1xn")
                        if tt + 4 < NT:
                            issue_xload(tt + 4)
                    for j in range(8):
                        prev_ev = t_ev[(g, j - 2)] if j >= 2 else (t_ev[(g - 1, 6 + j)] if g >= 1 else None)
                        for t4 in range(4):
                            t_tp[(g, j)] = P.op("pe", lambda e, g=g, j=j, t4=t4: e.transpose(
                                out=tp[j % 2][:, t4 * 128:(t4 + 1) * 128], in_=xnb[g % 2][t4][:, j * 128:(j + 1) * 128], identity=ident),
                                waits=[("p1xn", t_xn[g * 4 + 3]), ("p1ev", prev_ev), ("ld_cst", t_cst)],
                                sig=("p1tp" if t4 == 3 else None))
                        t_ev[(g, j)] = P.op("act", lambda e, g=g, j=j: e.activation(
                            out=hT[:, j, g * 512:(g + 1) * 512], in_=tp[j % 2], func=AF.Identity,
                            scale=gs[:, j:j + 1], bias=mod_sb[:, j:j + 1]),
                            waits=[("p1tp", t_tp[(g, j)]), ("dve0", t_mod)], sig="p1ev")
            t_hT = t_ev[(7, 7)]

            ckpt(2)
            t_wsl = None
            t_projpe = None
            t_attpe = None
            t_attdve = None
            t_pev = {}
            nproj = 0
            t_esk = P.op("act", lambda e: e.activation(out=esk[:, :], in_=snk[:, :], func=AF.Exp),
                         waits=[("ld_small", t_small)], sig="act0")
            sbt = sb_tiles()
            sbi = 0
            tk = {}
            n_chunk = 0
            t_evy = {}
            swn = 0
            swg = 0
            t_sw = {}

            wv_all = w_in.rearrange("(k p) c -> p k c", p=128)
            wslh = [ov[:, 16384 + i * 2048:16384 + (i + 1) * 2048].rearrange("p (k c) -> p k c", c=256) for i in range(2)]
            ppb = [ps[:, 3072:3584], ps[:, 3584:4096]]
            Ysb = ps[:, 2048:3072]
            hst = {"nproj": 0, "pev": {}, "projpe_head": {}, "lastpev": {}, "lastpv": {}}

            def head_proj_gen(h):
                half = h % 2
                hp = half * 64
                wb = wslh[h % 2]
                t_w = None
                for (dc, sc) in ((0, h * 64), (64, 512 + h * 64), (128, 1536 + h * 64), (192, 1024 + h * 64)):
                    t_w = P.op("pool", lambda e, wb=wb, dc=dc, sc=sc: e.dma_start(out=wb[:, :, dc:dc + 64], in_=wv_all[:, :, sc:sc + 64]),
                               waits=[hst["projpe_head"].get(h - 2), ("p1ev", t_hT)], sig="ld_wh%d" % (h % 2), inc=16)
                t_w = ("ld_wh%d" % (h % 2), t_w)
                free_tok = [hst["lastpv"].get(h - 1)]
                free_tok = [hst["lastpv"].get(h - 2)]
                tpe = None
                for kind, wc in (("q", 0), ("k", 64), ("g", 128)):
                    for tc in range(8):
                        n = hst["nproj"]
                        par = n % 2
                        for kc in range(8):
                            tpe = P.op("pe", lambda e, par=par, kc=kc, wc=wc, tc=tc, hp=hp, wb=wb: e.matmul(
                                ppb[par][hp:hp + 64, :], lhsT=wb[:, kc, wc:wc + 64], rhs=hT[:, kc, tc * 512:(tc + 1) * 512],
                                start=(kc == 0), stop=(kc == 7)),
                                waits=[t_w, hst["pev"].get(n - 2), ("p1ev", t_hT)], sig=("hprojpe" if kc == 7 else None))
                            if kc < 7:
                                yield
                        tpe_t = ("hprojpe", tpe)
                        cols = slice(tc * 512, (tc + 1) * 512)
                        if kind == "q":
                            tv = P.op("dve", lambda e, par=par, hp=hp, cols=cols: e.tensor_scalar(
                                out=qT[hp:hp + 64, cols], in0=ppb[par][hp:hp + 64, :], scalar1=0.125, scalar2=None, op0=ALU.mult),
                                waits=[tpe_t] + free_tok, sig="hpev")
                        elif kind == "k":
                            tv = P.op("dve", lambda e, par=par, hp=hp, cols=cols: e.tensor_copy(
                                out=kT[hp:hp + 64, cols], in_=ppb[par][hp:hp + 64, :]),
                                waits=[tpe_t] + free_tok, sig="hpev")
                        else:
                            eb = etmpf[n % 2]
                            ta = P.op("act", lambda e, par=par, hp=hp, eb=eb: e.activation(
                                out=eb[hp:hp + 64, :], in_=ppb[par][hp:hp + 64, :], func=AF.Exp, scale=-1.0),
                                waits=[tpe_t, hst["pev"].get(n - 2)], sig="hpevA")
                            t1 = P.op("dve", lambda e, hp=hp, eb=eb: e.tensor_scalar_add(out=eb[hp:hp + 64, :], in0=eb[hp:hp + 64, :], scalar1=1.0),
                                      waits=[("hpevA", ta)], sig="hsil")
                            t1 = P.op("dve", lambda e, hp=hp, eb=eb: e.reciprocal(out=eb[hp:hp + 64, :], in_=eb[hp:hp + 64, :]),
                                      waits=[("hsil", t1)], sig="hsil")
                            tv = P.op("dve", lambda e, par=par, hp=hp, cols=cols, eb=eb: e.tensor_mul(
                                out=sg[hp:hp + 64, cols], in0=ppb[par][hp:hp + 64, :], in1=eb[hp:hp + 64, :]),
                                waits=[("hsil", t1)] + free_tok, sig="hpev")
                        hst["pev"][n] = ("hpev", tv)
                        hst["nproj"] += 1
                        yield
                for tg in range(4):
                    n = hst["nproj"]
                    par = n % 2
                    for t8 in range(8):
                        tok = (tg * 8 + t8) * 128
                        for kc in range(8):
                            last = (kc == 7 and t8 == 7)
                            tpe = P.op("pe", lambda e, par=par, kc=kc, tok=tok, t8=t8, wb=wb: e.matmul(
                                ppb[par][:, t8 * 64:(t8 + 1) * 64], lhsT=hT[:, kc, tok:tok + 128], rhs=wb[:, kc, 192:256],
                                start=(kc == 0), stop=(kc == 7)),
                                waits=[t_w, hst["pev"].get(n - 2)], sig=("hprojpe" if last else None))
                            if not last:
                                yield
                    tpe_t = ("hprojpe", tpe)
                    tv = P.op("dve", lambda e, par=par, tg=tg, hp=hp: e.tensor_copy(
                        out=vv[:, tg * 8:(tg + 1) * 8, hp:hp + 64], in_=ppb[par].rearrange("p (a b) -> p a b", a=8)),
                        waits=[tpe_t] + free_tok, sig="hpev")
                    hst["pev"][n] = ("hpev", tv)
                    hst["nproj"] += 1
                    yield
                hst["projpe_head"][h] = ("hprojpe", tpe)
                hst["lastpev"][h] = hst["pev"][hst["nproj"] - 1]

            def pump(gen, k):
                if gen is None:
                    return None
                for _ in range(k):
                    try:
                        next(gen)
                    except StopIteration:
                        return None
                return gen

            def drain(gen):
                while gen is not None:
                    gen = pump(gen, 64)

            gen_next = head_proj_gen(0)
            for h in range(8):
                drain(gen_next)
                projw_h = [hst["lastpev"][h]]
                gen_next = head_proj_gen(h + 1) if h + 1 < 8 else None
                hp = (h % 2) * 64
                tiles = [(hp, qc, kb, c0, diag, cs, ce) for (qc, kb, c0, diag, cs, ce) in sbt]
                T = len(tiles)
                base = sbi
                chunk_of = {}
                cc = n_chunk - 1
                for i, tl in enumerate(tiles):
                    if tl[5]:
                        cc += 1
                    chunk_of[i] = cc

                def QK(i):
                    hp, qc, kb, c0, diag, cs, ce = tiles[i]
                    g = base + i
                    segs = col_segs(c0)
                    for si, (a, b) in enumerate(segs):
                        lastseg = (si == len(segs) - 1) and not diag
                        v = P.op("pe", lambda e, hp=hp, kb=kb, qc=qc, a=a, b=b, diag=diag, c0=c0: e.matmul(
                            Zp[:, a:b], lhsT=kT[hp:hp + 64, kb * 128:(kb + 1) * 128],
                            rhs=qT[hp:hp + 64, qc * QC + a:qc * QC + b], start=True, stop=not (diag and a == c0),
                            skip_group_check=True),
                            waits=[("sbE", tk.get(("E", g - 1)))] + (projw_h if i < 2 else []),
                            sig=("sbQK" if lastseg else None))
                        if lastseg:
                            tk[("QK", g)] = v
                    if diag:
                        tk[("QK", g)] = P.op("pe", lambda e, c0=c0: e.matmul(
                            Zp[:, c0:c0 + 128], lhsT=ident, rhs=negm, start=False, stop=True, skip_group_check=True),
                            sig="sbQK")

                def ACT_E(i):
                    hp, qc, kb, c0, diag, cs, ce = tiles[i]
                    g = base + i
                    tk[("E", g)] = P.op("act", lambda e, g=g, c0=c0: e.activation(out=Eb[g % 3][:, c0:QC], in_=Zp[:, c0:QC], func=AF.Exp),
                                        waits=[("sbQK", tk[("QK", g)]), ("sbA", tk.get(("A", g - 3)))], sig="sbE")

                def ACT_G(i):
                    hp, qc, kb, c0, diag, cs, ce = tiles[i]
                    g = base + i
                    tk[("G", g)] = P.op("act", lambda e, g=g, c0=c0: e.activation(out=Gb[g % 3][:, c0:QC], in_=Eb[g % 3][:, c0:QC],
                                                                               func=AF.Ln, bias=1.0),
                                        waits=[("sbE", tk[("E", g)]), ("sbTRIC", tk.get(("TRIC", g - 3)))], sig="sbG")

                def ACT_P(i):
                    hp, qc, kb, c0, diag, cs, ce = tiles[i]
                    g = base + i
                    tk[("P", g)] = P.op("act", lambda e, g=g, c0=c0: e.activation(out=Pb[g % 2][:, c0:QC], in_=Rp[:, c0:QC], func=AF.Exp),
                                        waits=[("sbTRI", tk[("TRI", g)]), ("sbA", tk.get(("A", g - 2)))], sig="sbP")

                def PE_TRI(i):
                    hp, qc, kb, c0, diag, cs, ce = tiles[i]
                    g = base + i
                    if cs:
                        for (a, b) in ((0, 512), (512, 1024)):
                            P.op("pe", lambda e, a=a, b=b: e.matmul(Rp[:, a:b], lhsT=zer, rhs=cst[:, 0:512], start=True, stop=False,
                                                                    skip_group_check=True),
                                 waits=[("sbP", tk.get(("P", g - 1)))])
                            P.op("pe", lambda e, a=a, b=b, hp=hp: e.matmul(
                                Ysb[hp:hp + 64, a:b], lhsT=zer[:, 0:64], rhs=cst[:, 0:512], start=True, stop=False,
                                skip_group_check=True),
                                waits=[("sbEV", t_evy.get(chunk_of[i] - 1))])
                    segs = col_segs(c0)
                    for si, (a, b) in enumerate(segs):
                        v = P.op("pe", lambda e, g=g, a=a, b=b: e.matmul(Rp[:, a:b], lhsT=triN, rhs=Gb[g % 3][:, a:b], start=False, stop=False,
                                                                        skip_group_check=True),
                                 waits=[("sbG", tk[("G", g)]), ("sbP", tk.get(("P", g - 1)))],
                                 sig=("sbTRI" if si == len(segs) - 1 else None))
                    tk[("TRI", g)] = v

                def PE_TRIC(i):
                    hp, qc, kb, c0, diag, cs, ce = tiles[i]
                    g = base + i
                    segs = col_segs(c0)
                    for si, (a, b) in enumerate(segs):
                        v = P.op("pe", lambda e, g=g, a=a, b=b: e.matmul(Rp[:, a:b], lhsT=tricN, rhs=Gb[g % 3][:, a:b], start=False, stop=False,
                                                                        skip_group_check=True),
                                 waits=[("sbP", tk[("P", g)])],
                                 sig=("sbTRIC" if si == len(segs) - 1 else None))
                    tk[("TRIC", g)] = v

                def PE_PV(i):
                    hp, qc, kb, c0, diag, cs, ce = tiles[i]
                    g = base + i
                    segs = col_segs(c0)
                    for si, (a, b) in enumerate(segs):
                        v = P.op("pe", lambda e, g=g, a=a, b=b, hp=hp, kb=kb: e.matmul(
                            Ysb[hp:hp + 64, a:b], lhsT=vv[:, kb, hp:hp + 64], rhs=Ab[g % 2][:, a:b], start=False, stop=False,
                            skip_group_check=True),
                            waits=[("sbA", tk[("A", g)])],
                            sig=("sbPV" if si == len(segs) - 1 else None))
                    tk[("PV", g)] = v

                def DVE_A(i):
                    hp, qc, kb, c0, diag, cs, ce = tiles[i]
                    g = base + i
                    tk[("A", g)] = P.op("dve", lambda e, g=g, c0=c0: e.tensor_mul(out=Ab[g % 2][:, c0:QC], in0=Eb[g % 3][:, c0:QC],
                                                                               in1=Pb[g % 2][:, c0:QC]),
                                        waits=[("sbP", tk[("P", g)]), ("sbPV", tk.get(("PV", g - 2)))], sig="sbA")

                def DVE_EV(i):
                    hp, qc, kb, c0, diag, cs, ce = tiles[i]
                    g = base + i
                    ch = chunk_of[i]
                    t_evy[ch] = P.op("dve", lambda e, hp=hp, qc=qc, h=h: e.tensor_mul(
                        out=ygT[hp:hp + 64, h // 2, qc * QC:(qc + 1) * QC], in0=Ysb[hp:hp + 64, :],
                        in1=sg[hp:hp + 64, qc * QC:(qc + 1) * QC]),
                        waits=[("sbPV", tk[("PV", g)])], sig="sbEV")

                QK(0)
                for i in range(-1, T):
                    if i + 1 < T:
                        ACT_E(i + 1)
                    if i >= 0:
                        PE_TRI(i)
                        ACT_P(i)
                    if i + 1 < T:
                        ACT_G(i + 1)
                    if i + 2 < T:
                        QK(i + 2)
                    if i >= 0:
                        PE_TRIC(i)
                        DVE_A(i)
                        PE_PV(i)
                        if tiles[i][6]:
                            DVE_EV(i)
                        gen_next = pump(gen_next, PUMP)
                sbi += T
                n_chunk = cc + 1
                hst["lastpv"][h] = ("sbEV", t_evy[cc])
                t_attdve = ("sbEV", t_evy[cc])
            t_projpe_tok = hst["projpe_head"][7]
            ckpt(10)

            for step in range(4, 8):
                is_sb = step < 4
                if is_sb:
                    cq, ck, cv, cg = step * 128, 512 + step * 128, 1024 + step * 128, 1536 + step * 128
                    kw = 128
                    vw = 128
                    do_kv = True
                else:
                    j = step - 4
                    cq, ck, cv, cg = 2048 + j * 128, 2560 + (j // 2) * 64, 2688 + (j // 2) * 64, 2816 + j * 128
                    kw = 64
                    vw = 64
                    do_kv = (j % 2 == 0)
                wv = w_in.rearrange("(k p) c -> p k c", p=128)
                dmas = [(0, cq, 128), (256, cg, 128)]
                if do_kv:
                    if kw == 128:
                        dmas.append((128, ck, 128))
                    else:
                        dmas.append((128, ck, 64))
                        dmas.append((192, ck, 64))
                    if vw == 128:
                        dmas.append((384, cv, 128))
                    else:
                        dmas.append((384, cv, 64))
                        dmas.append((448, cv, 64))
                        vw = 128
                for (dc, sc, w) in dmas:
                    t_wsl = P.op("pool", lambda e, dc=dc, sc=sc, w=w: e.dma_start(out=wsl[:, :, dc:dc + w], in_=wv[:, :, sc:sc + w]),
                                 waits=[("projpe", t_projpe), t_projpe_tok, t_attdve, ("p1ev", t_hT)], sig="ld_wsl", inc=16)
                ckpt(2.5 + 2 * step)
                kinds = [("q", 0), ("g", 256)] + ([("k", 128)] if do_kv else [])
                for kind, wc in kinds:
                    for tc in range(8):
                        par = nproj % 2
                        for kc in range(8):
                            last = kc == 7
                            t_projpe_new = P.op("pe", lambda e, par=par, kc=kc, wc=wc, tc=tc: e.matmul(
                                pp[par], lhsT=wsl[:, kc, wc:wc + 128], rhs=hT[:, kc, tc * 512:(tc + 1) * 512],
                                start=(kc == 0), stop=(kc == 7)),
                                waits=[("ld_wsl", t_wsl), t_pev.get(nproj - 2), ("p1ev", t_hT),
                                       t_attdve],
                                sig=("projpe" if last else None))
                        t_projpe = t_projpe_new
                        dst = {"q": qT, "k": kT, "g": sg}[kind][:, tc * 512:(tc + 1) * 512]
                        if kind == "q":
                            t_pev[nproj] = ("pev", P.op("dve", lambda e, dst=dst, par=par: e.tensor_scalar(
                                out=dst, in0=pp[par], scalar1=0.125, scalar2=None, op0=ALU.mult),
                                waits=[("projpe", t_projpe)], sig="pev"))
                            last_dve_pev = t_pev[nproj]
                        elif kind == "k":
                            t_pev[nproj] = ("pev", P.op("dve", lambda e, dst=dst, par=par: e.tensor_copy(out=dst, in_=pp[par]),
                                                waits=[("projpe", t_projpe)], sig="pev"))
                            last_dve_pev = t_pev[nproj]
                        else:
                            t_pev[nproj] = ("pevA", P.op("act", lambda e, dst=dst, par=par: e.activation(out=dst, in_=pp[par], func=AF.Silu),
                                                         waits=[("projpe", t_projpe), t_attdve], sig="pevA"))
                            last_act_pev = t_pev[nproj]
                        nproj += 1
                ckpt(2.75 + 2 * step)
                if do_kv:
                    for tg in range(8):
                        par = nproj % 2
                        for t4 in range(4):
                            tok = (tg * 4 + t4) * 128
                            for kc in range(8):
                                last = (kc == 7 and t4 == 3)
                                t_projpe_new = P.op("pe", lambda e, par=par, kc=kc, tok=tok, t4=t4, vw=vw: e.matmul(
                                    pp[par][:, t4 * 128:t4 * 128 + vw], lhsT=hT[:, kc, tok:tok + 128], rhs=wsl[:, kc, 384:384 + vw],
                                    start=(kc == 0), stop=(kc == 7)),
                                    waits=[("ld_wsl", t_wsl), t_pev.get(nproj - 2), t_attdve],
                                    sig=("projpe" if last else None))
                        t_projpe = t_projpe_new
                        t_pev[nproj] = ("pev", P.op("dve", lambda e, par=par, tg=tg, vw=vw: e.tensor_copy(
                            out=vv[:, tg * 4:(tg + 1) * 4, 0:vw],
                            in_=pp[par].rearrange("p (a b) -> p a b", a=4)[:, :, 0:vw]),
                            waits=[("projpe", t_projpe)], sig="pev"))
                        last_dve_pev = t_pev[nproj]
                        nproj += 1
                projw = [last_dve_pev, last_act_pev]
                ckpt(3 + 2 * step)

                if is_sb:
                    tiles = []
                    for hh in range(2):
                        for (qc, kb, c0, diag, cs, ce) in sbt:
                            tiles.append((hh * 64, qc, kb, c0, diag, cs, ce))
                    T = len(tiles)
                    base = sbi
                    chunk_of = {}
                    cc = n_chunk - 1
                    for i, tl in enumerate(tiles):
                        if tl[5]:
                            cc += 1
                        chunk_of[i] = cc

                    def QK(i):
                        hp, qc, kb, c0, diag, cs, ce = tiles[i]
                        g = base + i
                        segs = col_segs(c0)
                        for si, (a, b) in enumerate(segs):
                            lastseg = (si == len(segs) - 1) and not diag
                            v = P.op("pe", lambda e, hp=hp, kb=kb, qc=qc, a=a, b=b, diag=diag: e.matmul(
                                Zp[:, a:b], lhsT=kT[hp:hp + 64, kb * 128:(kb + 1) * 128],
                                rhs=qT[hp:hp + 64, qc * QC + a:qc * QC + b], start=True, stop=not (diag and a == c0),
                                skip_group_check=True),
                                waits=[("sbE", tk.get(("E", g - 1)))] + (projw if i < 2 else []),
                                sig=("sbQK" if lastseg else None))
                            if lastseg:
                                tk[("QK", g)] = v
                        if diag:
                            tk[("QK", g)] = P.op("pe", lambda e, c0=c0: e.matmul(
                                Zp[:, c0:c0 + 128], lhsT=ident, rhs=negm, start=False, stop=True, skip_group_check=True),
                                sig="sbQK")

                    def ACT_E(i):
                        hp, qc, kb, c0, diag, cs, ce = tiles[i]
                        g = base + i
                        tk[("E", g)] = P.op("act", lambda e, g=g, c0=c0: e.activation(out=Eb[g % 3][:, c0:QC], in_=Zp[:, c0:QC], func=AF.Exp),
                                            waits=[("sbQK", tk[("QK", g)]), ("sbA", tk.get(("A", g - 3)))], sig="sbE")

                    def ACT_G(i):
                        hp, qc, kb, c0, diag, cs, ce = tiles[i]
                        g = base + i
                        tk[("G", g)] = P.op("act", lambda e, g=g, c0=c0: e.activation(out=Gb[g % 3][:, c0:QC], in_=Eb[g % 3][:, c0:QC],
                                                                                   func=AF.Ln, bias=1.0),
                                            waits=[("sbE", tk[("E", g)]), ("sbTRIC", tk.get(("TRIC", g - 3)))], sig="sbG")

                    def ACT_P(i):
                        hp, qc, kb, c0, diag, cs, ce = tiles[i]
                        g = base + i
                        tk[("P", g)] = P.op("act", lambda e, g=g, c0=c0: e.activation(out=Pb[g % 2][:, c0:QC], in_=Rp[:, c0:QC], func=AF.Exp),
                                            waits=[("sbTRI", tk[("TRI", g)]), ("sbA", tk.get(("A", g - 2)))], sig="sbP")

                    def PE_TRI(i):
                        hp, qc, kb, c0, diag, cs, ce = tiles[i]
                        g = base + i
                        if cs:
                            par = chunk_of[i] % 2
                            for (a, b) in ((0, 512), (512, 1024)):
                                P.op("pe", lambda e, a=a, b=b: e.matmul(Rp[:, a:b], lhsT=zer, rhs=cst[:, 0:512], start=True, stop=False,
                                                                        skip_group_check=True),
                                     waits=[("sbP", tk.get(("P", g - 1)))])
                                P.op("pe", lambda e, a=a, b=b, hp=hp, par=par: e.matmul(
                                    Yp[par][hp:hp + 64, a:b], lhsT=zer[:, 0:64], rhs=cst[:, 0:512], start=True, stop=False,
                                    skip_group_check=True),
                                    waits=[("sbEV", t_evy.get(chunk_of[i] - 2))])
                        segs = col_segs(c0)
                        for si, (a, b) in enumerate(segs):
                            v = P.op("pe", lambda e, g=g, a=a, b=b: e.matmul(Rp[:, a:b], lhsT=triN, rhs=Gb[g % 3][:, a:b], start=False, stop=False,
                                                                            skip_group_check=True),
                                     waits=[("sbG", tk[("G", g)]), ("sbP", tk.get(("P", g - 1)))],
                                     sig=("sbTRI" if si == len(segs) - 1 else None))
                        tk[("TRI", g)] = v

                    def PE_TRIC(i):
                        hp, qc, kb, c0, diag, cs, ce = tiles[i]
                        g = base + i
                        segs = col_segs(c0)
                        for si, (a, b) in enumerate(segs):
                            v = P.op("pe", lambda e, g=g, a=a, b=b: e.matmul(Rp[:, a:b], lhsT=tricN, rhs=Gb[g % 3][:, a:b], start=False, stop=False,
                                                                            skip_group_check=True),
                                     waits=[("sbP", tk[("P", g)])],
                                     sig=("sbTRIC" if si == len(segs) - 1 else None))
                        tk[("TRIC", g)] = v

                    def PE_PV(i):
                        hp, qc, kb, c0, diag, cs, ce = tiles[i]
                        g = base + i
                        par = chunk_of[i] % 2
                        segs = col_segs(c0)
                        for si, (a, b) in enumerate(segs):
                            v = P.op("pe", lambda e, g=g, a=a, b=b, hp=hp, kb=kb, par=par: e.matmul(
                                Yp[par][hp:hp + 64, a:b], lhsT=vv[:, kb, hp:hp + 64], rhs=Ab[g % 2][:, a:b], start=False, stop=False,
                                skip_group_check=True),
                                waits=[("sbA", tk[("A", g)])],
                                sig=("sbPV" if si == len(segs) - 1 else None))
                        tk[("PV", g)] = v

                    def DVE_A(i):
                        hp, qc, kb, c0, diag, cs, ce = tiles[i]
                        g = base + i
                        tk[("A", g)] = P.op("dve", lambda e, g=g, c0=c0: e.tensor_mul(out=Ab[g % 2][:, c0:QC], in0=Eb[g % 3][:, c0:QC],
                                                                                   in1=Pb[g % 2][:, c0:QC]),
                                            waits=[("sbP", tk[("P", g)]), ("sbPV", tk.get(("PV", g - 2)))], sig="sbA")

                    def DVE_EV(i):
                        hp, qc, kb, c0, diag, cs, ce = tiles[i]
                        g = base + i
                        ch = chunk_of[i]
                        par = ch % 2
                        t_evy[ch] = P.op("dve", lambda e, hp=hp, qc=qc, par=par, step=step: e.tensor_mul(
                            out=ygT[hp:hp + 64, step, qc * QC:(qc + 1) * QC], in0=Yp[par][hp:hp + 64, :],
                            in1=sg[hp:hp + 64, qc * QC:(qc + 1) * QC]),
                            waits=[("sbPV", tk[("PV", g)])], sig="sbEV")

                    QK(0)
                    for i in range(-1, T):
                        if i + 1 < T:
                            ACT_E(i + 1)
                        if i >= 0:
                            PE_TRI(i)
                            ACT_P(i)
                        if i + 1 < T:
                            ACT_G(i + 1)
                        if i + 2 < T:
                            QK(i + 2)
                        if i >= 0:
                            PE_TRIC(i)
                            DVE_A(i)
                            PE_PV(i)
                            if tiles[i][6]:
                                DVE_EV(i)
                    sbi += T
                    n_chunk = cc + 1
                    t_attdve = ("sbEV", t_evy[cc])
                else:
                    j = step - 4
                    heads = (2 * j, 2 * j + 1)
                    for n in range(NT):
                        gi = swn
                        grp = swg + n // 4
                        zbase = (gi % 2) * 1024
                        wz = 256 if n > 0 else 128
                        for which, kblk in ((0, n), (1, n - 1)):
                            if kblk < 0:
                                continue
                            for hh in range(2):
                                hp = hh * 64
                                zc = zbase + hh * 512 + which * 128
                                P.op("pe", lambda e, zc=zc, hp=hp, kblk=kblk, n=n, which=which: e.matmul(
                                    ps[:, zc:zc + 128], lhsT=kT[hp:hp + 64, kblk * 128:(kblk + 1) * 128],
                                    rhs=qT[hp:hp + 64, n * 128:(n + 1) * 128], start=(which == 0), stop=False, skip_group_check=True),
                                    waits=[("swP", t_sw.get(("P", gi - 2)))] + (projw if n < 2 else []))
                        for hh in range(2):
                            h = heads[hh]
                            bc = C_SWB + h * 256
                            zc = zbase + hh * 512
                            v = P.op("pe", lambda e, zc=zc, bc=bc, wz=wz: e.matmul(
                                ps[:, zc:zc + wz], lhsT=ident, rhs=cst[:, bc:bc + wz], start=False, stop=True, skip_group_check=True),
                                sig=("swQK" if hh == 1 else None))
                        t_sw[("QK", gi)] = v
                        zin = ps[:, zbase:zbase + 1024].rearrange("p (b c) -> p b c", b=2)[:, :, 0:wz]
                        pout = Psw[gi % 2].rearrange("p (b c) -> p b c", b=2)[:, :, 0:wz]
                        t_sw[("P", gi)] = P.op("act", lambda e, zin=zin, pout=pout: e.activation(out=pout, in_=zin, func=AF.Exp),
                                               waits=[("swQK", t_sw[("QK", gi)]), ("swPV", t_sw.get(("PV", gi - 2)))], sig="swP")
                        if SW_DBG == 2:
                            ckpt(1000)
                        par = grp % 2
                        Yb = Yp[par][:, 0:512]
                        Db = Yp[par][:, 512:1024]
                        col = (n % 4) * 128
                        for hh in range(2):
                            hp = hh * 64
                            srcs = [(hh * 256, n)] + ([(hh * 256 + 128, n - 1)] if n > 0 else [])
                            for si, (pc, kblk) in enumerate(srcs):
                                P.op("pe", lambda e, Yb=Yb, hp=hp, col=col, kblk=kblk, gi=gi, pc=pc, si=si: e.matmul(
                                    Yb[hp:hp + 64, col:col + 128], lhsT=vv[:, kblk, 0:64], rhs=Psw[gi % 2][:, pc:pc + 128],
                                    start=(si == 0), stop=(si == len(srcs) - 1), skip_group_check=True),
                                    waits=[("swP", t_sw[("P", gi)]), ("swEV", t_sw.get(("EV", grp - 2)))])
                            for si, (pc, kblk) in enumerate(srcs):
                                v = P.op("pe", lambda e, Db=Db, hp=hp, col=col, gi=gi, pc=pc, si=si: e.matmul(
                                    Db[hp:hp + 64, col:col + 128], lhsT=ones_bf[:, 0:64], rhs=Psw[gi % 2][:, pc:pc + 128],
                                    start=(si == 0), stop=(si == len(srcs) - 1), skip_group_check=True),
                                    sig=("swPV" if (hh == 1 and si == len(srcs) - 1) else None))
                        t_sw[("PV", gi)] = v
                        if SW_DBG == 3:
                            ckpt(1000)
                        if n % 4 == 3:
                            q0 = (n - 3) * 128
                            t1 = P.op("dve", lambda e, Db=Db, j=j: e.tensor_scalar(out=rden, in0=Db, scalar1=esk[:, j:j + 1], scalar2=None,
                                                                                    op0=ALU.add),
                                      waits=[("swPV", t_sw[("PV", gi)]), ("act0", t_esk)], sig="swD")
                            t1 = P.op("dve", lambda e: e.reciprocal(out=rden, in_=rden), waits=[("swD", t1)], sig="swD")
                            t1 = P.op("dve", lambda e, Yb=Yb: e.tensor_mul(out=ytmp, in0=Yb, in1=rden), waits=[("swD", t1)], sig="swD")
                            t_sw[("EV", grp)] = P.op("dve", lambda e, q0=q0, step=step: e.tensor_mul(
                                out=ygT[:, step, q0:q0 + 512], in0=ytmp, in1=sg[:, q0:q0 + 512]),
                                waits=[("swD", t1)], sig="swEV")
                        swn += 1
                    swg += NT // 4
                    t_last_sw = t_sw[("EV", swg - 1)]
                if not is_sb:
                    t_attdve = ("swEV", t_last_sw)
                ckpt(4 + 2 * step)

            ckpt(20)
            wo_v = w_out.rearrange("(k p) e -> p k e", p=128)
            t_wo = None
            for k in range(8):
                t_wo = P.op("pool", lambda e, k=k: e.dma_start(out=wout[:, k, :], in_=wo_v[:, k, :]),
                            waits=[("projpe", t_projpe)], sig="ld_wo", inc=16)
            t_fg = P.op("sp", lambda e: e.dma_start(out=fg_bc, in_=fg_bc_d[:, :]), waits=[("projpe", t_projpe)], sig="ld_fg", inc=16)
            t_xf = {}
            t_o = {}
            t_st = {}
            t_r2 = {}
            t_sq2 = {}

            def issue_xf(tt):
                t_xf[tt] = P.op("sp", lambda e, tt=tt: e.dma_start(out=xf[tt % 3], in_=x[tt * 128:(tt + 1) * 128, :]),
                                waits=[("fr2", t_r2.get(tt - 3)), ("projpe", t_projpe)], sig="ld_xf%d" % (tt % 3), inc=16)

            t_po = {}
            for tt in range(3):
                issue_xf(tt)
            for tt in range(NT):
                for eh in range(2):
                    for kc in range(8):
                        v = P.op("pe", lambda e, tt=tt, eh=eh, kc=kc: e.matmul(
                            po[tt % 2][eh], lhsT=ygT[:, kc, tt * 128:(tt + 1) * 128], rhs=wout[:, kc, eh * 512:(eh + 1) * 512],
                            start=(kc == 0), stop=(kc == 7)),
                            waits=[("ld_wo", t_wo), t_attdve, ("fr2", t_r2.get(tt - 2))],
                            sig=("fpo" if kc == 7 else None))
                    t_po[(tt, eh)] = v
                rb = rf[tt % 2]
                for eh in range(2):
                    t1 = P.op("dve", lambda e, tt=tt, eh=eh, rb=rb: e.tensor_mul(out=rb[:, eh * 512:(eh + 1) * 512], in0=po[tt % 2][eh],
                                                                               in1=gate_bc[:, eh * 512:(eh + 1) * 512]),
                              waits=[("fpo", t_po[(tt, eh)]), ("fo", t_o.get(tt - 2)), ("fsq", t_sq2.get(tt - 2))], sig="fd")
                t_r2[tt] = P.op("dve", lambda e, tt=tt, rb=rb: e.tensor_add(out=rb, in0=rb, in1=xf[tt % 3]),
                                waits=[("fd", t1), ("ld_xf%d" % (tt % 3), t_xf[tt])], sig="fr2")
                if tt + 3 < NT:
                    issue_xf(tt + 3)
                t_sq2[tt] = P.op("act", lambda e, tt=tt, rb=rb: e.activation(out=junkf, in_=rb, func=AF.Square, accum_out=ss2[:, tt:tt + 1]),
                                 waits=[("fr2", t_r2[tt])], sig="fsq")
                t1 = P.op("act", lambda e, tt=tt: e.activation(out=rstd2[:, tt:tt + 1], in_=ss2[:, tt:tt + 1], func=AF.Sqrt,
                                                               scale=1.0 / D, bias=eps_t[:, 0:1]),
                          waits=[("fsq", t_sq2[tt])], sig="fsqrt")
                t1 = P.op("dve", lambda e, tt=tt: e.reciprocal(out=rstd2[:, tt:tt + 1], in_=rstd2[:, tt:tt + 1]),
                          waits=[("fsqrt", t1)], sig="fd")
                t_o[tt] = P.op("dve", lambda e, tt=tt, rb=rb: e.scalar_tensor_tensor(
                    out=of[tt % 2], in0=rb, scalar=rstd2[:, tt:tt + 1], in1=fg_bc, op0=ALU.mult, op1=ALU.mult),
                    waits=[("fd", t1), ("ld_fg", t_fg), ("ld_out%d" % (tt % 2), t_st.get(tt - 2))], sig="fo")
                t_st[tt] = P.op("sp", lambda e, tt=tt: e.dma_start(out=out[tt * 128:(tt + 1) * 128, :], in_=of[tt % 2]),
                                waits=[("fo", t_o[tt])], sig="ld_out%d" % (tt % 2), inc=16)
            P.op("sp", lambda e: e.nop(), waits=[("ld_out0", t_st[NT - 2]), ("ld_out1", t_st[NT - 1])])

        try:
            plan_all()
        except _Stop:
            pass

        names = sorted(P.cnt.keys())
        sems = {n: es.enter_context(nc.semaphore(n)) for n in names}
        block = es.enter_context(nc.Block())

        def emit(eng, oplist):
            seen = {}
            for fn, waits, sig, inc in oplist:
                for (name, val) in waits:
                    if seen.get(name, 0) < val:
                        eng.wait_ge(sems[name], val)
                        seen[name] = val
                ins = fn(eng)
                if sig is not None:
                    ins.then_inc(sems[sig], inc)

        @block.sync
        def _(eng):
            emit(eng, P.ops["sp"])

        @block.gpsimd
        def _(eng):
            emit(eng, P.ops["pool"])

        @block.tensor
        def _(eng):
            emit(eng, P.ops["pe"])

        @block.scalar
        def _(eng):
            emit(eng, P.ops["act"])

        @block.vector
        def _(eng):
            emit(eng, P.ops["dve"])
    return nc


_CACHE = {}


def kernel(x, c, w_ada, b_ada, norm_g, w_in, sinks, w_out, final_g):
    x = np.asarray(x, np.float32)
    c = np.asarray(c, np.float32)
    w_ada = np.ascontiguousarray(np.asarray(w_ada, np.float32)[0])
    b_ada = np.asarray(b_ada, np.float32)[0]
    norm_g = np.asarray(norm_g, np.float32)[0]
    w_in = np.ascontiguousarray(np.asarray(w_in, np.float32)[0])
    sinks = np.asarray(sinks, np.float32)[0]
    w_out = np.ascontiguousarray(np.asarray(w_out, np.float32)[0])
    final_g = np.asarray(final_g, np.float32)

    def lay(v):
        return np.ascontiguousarray(v.reshape(-1, 128).T)

    bada_l = lay(b_ada[:2048])
    bg_bc = np.ascontiguousarray(np.broadcast_to(b_ada[2048:3072][None, :], (128, D)))
    normg_l = lay(norm_g)
    fg_bc = np.ascontiguousarray(np.broadcast_to(final_g[None, :], (128, D)))
    sinks_l = np.ascontiguousarray(np.stack([np.repeat(sinks[2 * j:2 * j + 2], 64) for j in range(4)], axis=1))
    consts = make_consts()
    if "nc" not in _CACHE:
        _CACHE["nc"] = build_nc()
    nc = _CACHE["nc"]
    in_maps = []
    for b in range(NCORE):
        in_maps.append({
            "x": np.ascontiguousarray(x[b]), "c_l": lay(c[b]), "w_ada": w_ada, "bada_l": bada_l, "bg_bc": bg_bc,
            "normg_l": normg_l, "w_in": w_in, "sinks_l": sinks_l, "w_out": w_out, "fg_bc": fg_bc, "consts": consts,
        })
    res = run_bass_kernel_spmd(nc, in_maps, core_ids=list(range(NCORE)))
    return np.stack([np.asarray(r["out"], np.float32) for r in res.results], axis=0)
```

```python
from contextlib import ExitStack

import numpy as np
import concourse.bass as bass
import concourse.mybir as mybir
from concourse.bass_utils import run_bass_kernel_spmd

F32 = mybir.dt.float32
BF16 = mybir.dt.bfloat16
AF = mybir.ActivationFunctionType
ALU = mybir.AluOpType

S = 4096
D = 1024
NCORE = 8
NT = S // 128
QC = 1024
NEG = -30000.0
NCONST = 128 * 6 + 2048
C_ID, C_TRI, C_TRIC, C_ZERO, C_NEGM, C_ONES, C_SWB = 0, 128, 256, 384, 512, 640, 768


LEVEL = 99
SW_DBG = 0


class _Stop(Exception):
    pass


def ckpt(level):
    if LEVEL <= level:
        raise _Stop()


class Plan:
    def __init__(self):
        self.ops = {"pe": [], "act": [], "dve": [], "pool": [], "sp": []}
        self.cnt = {}

    def op(self, eng, fn, waits=(), sig=None, inc=1):
        v = None
        if sig is not None:
            self.cnt[sig] = self.cnt.get(sig, 0) + inc
            v = self.cnt[sig]
        ws = tuple(w for w in waits if w is not None and w[1] is not None and w[1] > 0)
        self.ops[eng].append((fn, ws, sig, inc))
        return v


def make_consts():
    c = np.zeros((128, NCONST), np.float32)
    j = np.arange(128)[:, None]
    s = np.arange(128)[None, :]
    c[:, C_ID:C_ID + 128] = (j == s)
    c[:, C_TRI:C_TRI + 128] = -1.0 * (j >= s)
    c[:, C_TRIC:C_TRIC + 128] = -1.0 * (j < s)
    c[:, C_NEGM:C_NEGM + 128] = np.where(j < s, 0.0, NEG)
    c[:, C_ONES:C_ONES + 128] = 1.0
    for h in range(8):
        m = 2.0 ** (-8.0 * (h + 1) / 8)
        rel_cur = (s - j).astype(np.float32)
        cur = np.where(s >= j, -m * rel_cur, NEG)
        rel_prev = (128 + s - j).astype(np.float32)
        prev = np.where(j > s, -m * rel_prev, NEG)
        c[:, C_SWB + h * 256:C_SWB + h * 256 + 128] = cur
        c[:, C_SWB + h * 256 + 128:C_SWB + h * 256 + 256] = prev
    return c


def sb_tiles():
    out = []
    for qc in range(S // QC):
        nkb = (QC // 128) * (qc + 1)
        for kb in range(nkb - 1, -1, -1):
            jd = kb - (QC // 128) * qc
            diag = jd >= 0
            c0 = 128 * jd if diag else 0
            out.append((qc, kb, c0, diag, kb == nkb - 1, kb == 0))
    return out


def col_segs(c0, c1=QC):
    segs = []
    a = c0
    while a < c1:
        b = min(c1, (a // 512 + 1) * 512)
        segs.append((a, b))
        a = b
    return segs


def build_nc():
    nc = bass.Bass("TRN2", target_bir_lowering=False)
    x = nc.dram_tensor("x", [S, D], F32, kind="ExternalInput").ap()
    c_l = nc.dram_tensor("c_l", [128, 8], F32, kind="ExternalInput").ap()
    w_ada = nc.dram_tensor("w_ada", [D, 3 * D], F32, kind="ExternalInput").ap()
    bada_l = nc.dram_tensor("bada_l", [128, 16], F32, kind="ExternalInput").ap()
    bg_bc_d = nc.dram_tensor("bg_bc", [128, D], F32, kind="ExternalInput").ap()
    normg_l = nc.dram_tensor("normg_l", [128, 8], F32, kind="ExternalInput").ap()
    w_in = nc.dram_tensor("w_in", [D, 3328], F32, kind="ExternalInput").ap()
    sinks_l = nc.dram_tensor("sinks_l", [128, 4], F32, kind="ExternalInput").ap()
    w_out = nc.dram_tensor("w_out", [D, D], F32, kind="ExternalInput").ap()
    fg_bc_d = nc.dram_tensor("fg_bc", [128, D], F32, kind="ExternalInput").ap()
    consts_d = nc.dram_tensor("consts", [128, NCONST], F32, kind="ExternalInput").ap()
    out = nc.dram_tensor("out", [S, D], F32, kind="ExternalOutput").ap()

    P = Plan()
    es = ExitStack()
    with es:
        def sb(name, shape, dt):
            return es.enter_context(nc.sbuf_tensor(name, shape, dt))

        ygT = sb("ygT", [128, 8, S], BF16)
        hT = sb("hT", [128, 8, S], BF16)
        ov = sb("ov", [128, 30720], BF16)
        cst = sb("cst", [128, NCONST], BF16)
        gate_bc = sb("gate_bc", [128, D], F32)
        c_sb = sb("c_sb", [128, 8], F32)
        etmp = sb("etmp", [128, 8], F32)
        cond = sb("cond", [128, 8], F32)
        bada = sb("bada", [128, 16], F32)
        normg = sb("normg", [128, 8], F32)
        mod_sb = sb("mod_sb", [128, 16], F32)
        gs = sb("gs", [128, 8], F32)
        ones_f = sb("ones_f", [128, 128], F32)
        ss = sb("ss", [128, 32], F32)
        rstd = sb("rstd", [128, 32], F32)
        ss2 = sb("ss2", [128, 32], F32)
        rstd2 = sb("rstd2", [128, 32], F32)
        snk = sb("snk", [128, 4], F32)
        esk = sb("esk", [128, 4], F32)
        eps_t = sb("eps_t", [128, 1], F32)
        ps = es.enter_context(nc.psum_tensor("ps", [128, 4096], F32))

        qT = ov[:, 0:4096]
        kT = ov[:, 4096:8192]
        sg = ov[:, 8192:12288]
        vv = ov[:, 12288:16384].rearrange("p (b c) -> p b c", c=128)
        wsl = ov[:, 16384:20480].rearrange("p (k c) -> p k c", c=512)
        PB = 20480
        Eb = [ov[:, PB + i * 1024:PB + (i + 1) * 1024] for i in range(3)]
        Gb = [ov[:, PB + (3 + i) * 1024:PB + (4 + i) * 1024] for i in range(3)]
        Pb = [ov[:, PB + (6 + i) * 1024:PB + (7 + i) * 1024] for i in range(2)]
        Ab = [ov[:, PB + (8 + i) * 1024:PB + (9 + i) * 1024] for i in range(2)]
        Psw = [ov[:, PB + i * 512:PB + (i + 1) * 512] for i in range(2)]
        rden = ov[:, PB + 1024:PB + 2048].bitcast(F32)
        ytmp = ov[:, PB + 2048:PB + 3072].bitcast(F32)
        wada = [ov[:, i * 6144:(i + 1) * 6144].bitcast(F32) for i in range(2)]
        xt = [ov[:, 12288 + i * 2048:12288 + (i + 1) * 2048].bitcast(F32) for i in range(4)]
        xnb = [[ov[:, 20480 + (g * 4 + t) * 1024:20480 + (g * 4 + t + 1) * 1024] for t in range(4)] for g in range(2)]
        junk = ov[:, 28672:29696]
        hflat = hT[:, :, :].rearrange("p a b -> p (a b)")
        condb = hflat[:, 0:2048].bitcast(F32).rearrange("p (k m) -> p k m", m=128)
        bg_bc = hflat[:, 2048:4096].bitcast(F32)
        wout = hflat[:, 0:8192].rearrange("p (k e) -> p k e", e=1024)
        xf = [hflat[:, 8192 + i * 2048:8192 + (i + 1) * 2048].bitcast(F32) for i in range(3)]
        rf = [hflat[:, 14336 + i * 2048:14336 + (i + 1) * 2048].bitcast(F32) for i in range(2)]
        of = [hflat[:, 18432 + i * 2048:18432 + (i + 1) * 2048].bitcast(F32) for i in range(2)]
        fg_bc = hflat[:, 22528:24576].bitcast(F32)
        junkf = hflat[:, 24576:25600]
        Zp = ps[:, 0:1024]
        Rp = ps[:, 1024:2048]
        Yp = [ps[:, 2048:3072], ps[:, 3072:4096]]
        pp = [ps[:, 0:512], ps[:, 512:1024]]
        tp = [ps[:, i * 512:(i + 1) * 512].bitcast(BF16)[:, 0:512] for i in range(2)]
        modps = ps[:, 1024:1040]
        gateps = ps[:, 2048:3072]
        po = [[ps[:, t * 1024 + e * 512:t * 1024 + (e + 1) * 512] for e in range(2)] for t in range(2)]

        ident = cst[:, C_ID:C_ID + 128]
        triN = cst[:, C_TRI:C_TRI + 128]
        tricN = cst[:, C_TRIC:C_TRIC + 128]
        zer = cst[:, C_ZERO:C_ZERO + 128]
        negm = cst[:, C_NEGM:C_NEGM + 128]
        ones_bf = cst[:, C_ONES:C_ONES + 128]

        def plan_all():
            t_cst = P.op("pool", lambda e: e.dma_start(out=cst[:, :], in_=consts_d[:, :]), sig="ld_cst", inc=16)
            for dst, src in ((c_sb, c_l), (bada, bada_l), (normg, normg_l), (snk, sinks_l)):
                t_small = P.op("sp", lambda e, dst=dst, src=src: e.dma_start(out=dst[:, :], in_=src[:, :]), sig="ld_small", inc=16)
            t_small = P.op("sp", lambda e: e.dma_start(out=bg_bc, in_=bg_bc_d[:, :]), sig="ld_small", inc=16)

            ckpt(0)
            P.op("dve", lambda e: e.memset(ones_f[:, :], 1.0), sig="dve0")
            P.op("dve", lambda e: e.memset(eps_t[:, :], 1e-6), sig="dve0")
            P.op("dve", lambda e: e.memset(ss[:, :], 0.0), sig="dve0")
            t_d = P.op("dve", lambda e: e.memset(ss2[:, :], 0.0), sig="dve0")
            t_a = P.op("act", lambda e: e.activation(out=etmp[:, :], in_=c_sb[:, :], func=AF.Exp, scale=-1.0),
                       waits=[("ld_small", t_small)], sig="act0")
            t_d = P.op("dve", lambda e: e.tensor_scalar_add(out=etmp[:, :], in0=etmp[:, :], scalar1=1.0),
                       waits=[("act0", t_a), ("dve0", t_d)], sig="dve0")
            t_d = P.op("dve", lambda e: e.reciprocal(out=etmp[:, :], in_=etmp[:, :]), waits=[("dve0", t_d)], sig="dve0")
            t_d = P.op("dve", lambda e: e.tensor_mul(out=cond[:, :], in0=c_sb[:, :], in1=etmp[:, :]),
                       waits=[("dve0", t_d)], sig="dve0")
            t_cond = t_d
            for k in range(8):
                t_d = P.op("dve", lambda e, k=k: e.tensor_scalar(out=condb[:, k, :], in0=ones_f[:, :], scalar1=cond[:, k:k + 1],
                                                                 scalar2=None, op0=ALU.mult),
                           waits=[("dve0", t_cond)], sig="dve0")
            t_condb = t_d
            t_pe0 = {}
            t_wada = {}
            for k in range(8):
                t_wada[k] = P.op("sp", lambda e, k=k: e.dma_start(out=wada[k % 2], in_=w_ada[k * 128:(k + 1) * 128, :]),
                                 waits=[("pe0", t_pe0.get(k - 2))], sig="ld_wada%d" % (k % 2), inc=16)
                for j in range(16):
                    P.op("pe", lambda e, k=k, j=j: e.matmul(modps[:, j:j + 1], lhsT=wada[k % 2][:, j * 128:(j + 1) * 128],
                                                            rhs=cond[:, k:k + 1], start=(k == 0 and j == 0), stop=(k == 7),
                                                            skip_group_check=True),
                         waits=[("ld_wada%d" % (k % 2), t_wada[k]), ("dve0", t_condb)])
                for eh in range(2):
                    t_pe0[k] = P.op("pe", lambda e, k=k, eh=eh: e.matmul(gateps[:, eh * 512:(eh + 1) * 512], lhsT=condb[:, k, :],
                                                                         rhs=wada[k % 2][:, 2048 + eh * 512:2048 + (eh + 1) * 512],
                                                                         start=(k == 0), stop=(k == 7)),
                                    waits=[("ld_wada%d" % (k % 2), t_wada[k]), ("dve0", t_condb)], sig=("pe0" if eh == 1 else None))
            t_d = P.op("dve", lambda e: e.tensor_add(out=mod_sb[:, :], in0=modps, in1=bada[:, :]),
                       waits=[("pe0", t_pe0[7]), ("ld_small", t_small)], sig="dve0")
            t_d = P.op("dve", lambda e: e.scalar_tensor_tensor(out=gs[:, :], in0=mod_sb[:, 8:16], scalar=1.0, in1=normg[:, :],
                                                               op0=ALU.add, op1=ALU.mult),
                       waits=[("dve0", t_d)], sig="dve0")
            t_d = P.op("dve", lambda e: e.tensor_add(out=gate_bc[:, :], in0=gateps, in1=bg_bc), sig="dve0")
            t_mod = t_d

            ckpt(1)
            t_xld = {}
            t_sq = {}
            t_xn = {}
            t_tp = {}
            t_ev = {}

            def issue_xload(tt):
                t_xld[tt] = P.op("sp", lambda e, tt=tt: e.dma_start(out=xt[tt % 4], in_=x[tt * 128:(tt + 1) * 128, :]),
                                 waits=[("p1xn", t_xn.get(tt - 4)), ("p1sq", t_sq.get(tt - 4))],
                                 sig="ld_x%d" % (tt % 4), inc=16)

            for tt in range(4):
                issue_xload(tt)
            for g in range(8):
                if True:
                    for t4 in range(4):
                        tt = g * 4 + t4
                        t_sq[tt] = P.op("act", lambda e, tt=tt: e.activation(out=junk, in_=xt[tt % 4], func=AF.Square,
                                                                             accum_out=ss[:, tt:tt + 1]),
                                        waits=[("ld_x%d" % (tt % 4), t_xld[tt]), ("dve0", t_mod)], sig="p1sq")
                        t_r = P.op("act", lambda e, tt=tt: e.activation(out=rstd[:, tt:tt + 1], in_=ss[:, tt:tt + 1], func=AF.Sqrt,
                                                                        scale=1.0 / D, bias=eps_t[:, 0:1]),
                                   waits=[("p1sq", t_sq[tt])], sig="p1sqrt")
                        t_r = P.op("dve", lambda e, tt=tt: e.reciprocal(out=rstd[:, tt:tt + 1], in_=rstd[:, tt:tt + 1]),
                                   waits=[("p1sqrt", t_r)], sig="dve1")
                        t_xn[tt] = P.op("dve", lambda e, tt=tt, g=g, t4=t4: e.tensor_scalar(
                            out=xnb[g % 2][t4], in0=xt[tt % 4], scalar1=rstd[:, tt:tt + 1], scalar2=None, op0=ALU.mult),
                            waits=[("dve1", t_r), ("p1tp", t_tp.get((g - 2, 7)))], sig="p1xn")
                        if tt + 4 < NT:
                            issue_xload(tt + 4)
                    for j in range(8):
                        prev_ev = t_ev[(g, j - 2)] if j >= 2 else (t_ev[(g - 1, 6 + j)] if g >= 1 else None)
                        for t4 in range(4):
                            t_tp[(g, j)] = P.op("pe", lambda e, g=g, j=j, t4=t4: e.transpose(
                                out=tp[j % 2][:, t4 * 128:(t4 + 1) * 128], in_=xnb[g % 2][t4][:, j * 128:(j + 1) * 128], identity=ident),
                                waits=[("p1xn", t_xn[g * 4 + 3]), ("p1ev", prev_ev), ("ld_cst", t_cst)],
                                sig=("p1tp" if t4 == 3 else None))
                        t_ev[(g, j)] = P.op("act", lambda e, g=g, j=j: e.activation(
                            out=hT[:, j, g * 512:(g + 1) * 512], in_=tp[j % 2], func=AF.Identity,
                            scale=gs[:, j:j + 1], bias=mod_sb[:, j:j + 1]),
                            waits=[("p1tp", t_tp[(g, j)]), ("dve0", t_mod)], sig="p1ev")
            t_hT = t_ev[(7, 7)]

            ckpt(2)
            t_wsl = None
            t_projpe = None
            t_attpe = None
            t_attdve = None
            t_pev = {}
            nproj = 0
            t_esk = P.op("act", lambda e: e.activation(out=esk[:, :], in_=snk[:, :], func=AF.Exp),
                         waits=[("ld_small", t_small)], sig="act0")
            sbt = sb_tiles()
            sbi = 0
            tk = {}
            n_chunk = 0
            t_evy = {}
            swn = 0
            swg = 0
            t_sw = {}

            for step in range(8):
                is_sb = step < 4
                if is_sb:
                    cq, ck, cv, cg = step * 128, 512 + step * 128, 1024 + step * 128, 1536 + step * 128
                    kw = 128
                    vw = 128
                    do_kv = True
                else:
                    j = step - 4
                    cq, ck, cv, cg = 2048 + j * 128, 2560 + (j // 2) * 64, 2688 + (j // 2) * 64, 2816 + j * 128
                    kw = 64
                    vw = 64
                    do_kv = (j % 2 == 0)
                wv = w_in.rearrange("(k p) c -> p k c", p=128)
                dmas = [(0, cq, 128), (256, cg, 128)]
                if do_kv:
                    if kw == 128:
                        dmas.append((128, ck, 128))
                    else:
                        dmas.append((128, ck, 64))
                        dmas.append((192, ck, 64))
                    if vw == 128:
                        dmas.append((384, cv, 128))
                    else:
                        dmas.append((384, cv, 64))
                        dmas.append((448, cv, 64))
                        vw = 128
                for (dc, sc, w) in dmas:
                    t_wsl = P.op("pool", lambda e, dc=dc, sc=sc, w=w: e.dma_start(out=wsl[:, :, dc:dc + w], in_=wv[:, :, sc:sc + w]),
                                 waits=[("projpe", t_projpe), ("p1ev", t_hT)], sig="ld_wsl", inc=16)
                ckpt(2.5 + 2 * step)
                kinds = [("q", 0), ("g", 256)] + ([("k", 128)] if do_kv else [])
                for kind, wc in kinds:
                    for tc in range(8):
                        par = nproj % 2
                        for kc in range(8):
                            last = kc == 7
                            t_projpe_new = P.op("pe", lambda e, par=par, kc=kc, wc=wc, tc=tc: e.matmul(
                                pp[par], lhsT=wsl[:, kc, wc:wc + 128], rhs=hT[:, kc, tc * 512:(tc + 1) * 512],
                                start=(kc == 0), stop=(kc == 7)),
                                waits=[("ld_wsl", t_wsl), t_pev.get(nproj - 2), ("p1ev", t_hT),
                                       t_attdve],
                                sig=("projpe" if last else None))
                        t_projpe = t_projpe_new
                        dst = {"q": qT, "k": kT, "g": sg}[kind][:, tc * 512:(tc + 1) * 512]
                        if kind == "q":
                            t_pev[nproj] = ("pev", P.op("dve", lambda e, dst=dst, par=par: e.tensor_scalar(
                                out=dst, in0=pp[par], scalar1=0.125, scalar2=None, op0=ALU.mult),
                                waits=[("projpe", t_projpe)], sig="pev"))
                            last_dve_pev = t_pev[nproj]
                        elif kind == "k":
                            t_pev[nproj] = ("pev", P.op("dve", lambda e, dst=dst, par=par: e.tensor_copy(out=dst, in_=pp[par]),
                                                waits=[("projpe", t_projpe)], sig="pev"))
                            last_dve_pev = t_pev[nproj]
                        else:
                            t_pev[nproj] = ("pevA", P.op("act", lambda e, dst=dst, par=par: e.activation(out=dst, in_=pp[par], func=AF.Silu),
                                                         waits=[("projpe", t_projpe), t_attdve], sig="pevA"))
                            last_act_pev = t_pev[nproj]
                        nproj += 1
                ckpt(2.75 + 2 * step)
                if do_kv:
                    for tg in range(8):
                        par = nproj % 2
                        for t4 in range(4):
                            tok = (tg * 4 + t4) * 128
                            for kc in range(8):
                                last = (kc == 7 and t4 == 3)
                                t_projpe_new = P.op("pe", lambda e, par=par, kc=kc, tok=tok, t4=t4, vw=vw: e.matmul(
                                    pp[par][:, t4 * 128:t4 * 128 + vw], lhsT=hT[:, kc, tok:tok + 128], rhs=wsl[:, kc, 384:384 + vw],
                                    start=(kc == 0), stop=(kc == 7)),
                                    waits=[("ld_wsl", t_wsl), t_pev.get(nproj - 2), t_attdve],
                                    sig=("projpe" if last else None))
                        t_projpe = t_projpe_new
                        t_pev[nproj] = ("pev", P.op("dve", lambda e, par=par, tg=tg, vw=vw: e.tensor_copy(
                            out=vv[:, tg * 4:(tg + 1) * 4, 0:vw],
                            in_=pp[par].rearrange("p (a b) -> p a b", a=4)[:, :, 0:vw]),
                            waits=[("projpe", t_projpe)], sig="pev"))
                        last_dve_pev = t_pev[nproj]
                        nproj += 1
                projw = [last_dve_pev, last_act_pev]
                ckpt(3 + 2 * step)

                if is_sb:
                    tiles = []
                    for hh in range(2):
                        for (qc, kb, c0, diag, cs, ce) in sbt:
                            tiles.append((hh * 64, qc, kb, c0, diag, cs, ce))
                    T = len(tiles)
                    base = sbi
                    chunk_of = {}
                    cc = n_chunk - 1
                    for i, tl in enumerate(tiles):
                        if tl[5]:
                            cc += 1
                        chunk_of[i] = cc

                    def QK(i):
                        hp, qc, kb, c0, diag, cs, ce = tiles[i]
                        g = base + i
                        segs = col_segs(c0)
                        for si, (a, b) in enumerate(segs):
                            lastseg = (si == len(segs) - 1) and not diag
                            v = P.op("pe", lambda e, hp=hp, kb=kb, qc=qc, a=a, b=b, diag=diag: e.matmul(
                                Zp[:, a:b], lhsT=kT[hp:hp + 64, kb * 128:(kb + 1) * 128],
                                rhs=qT[hp:hp + 64, qc * QC + a:qc * QC + b], start=True, stop=not (diag and a == c0),
                                skip_group_check=True),
                                waits=[("sbE", tk.get(("E", g - 1)))] + (projw if i < 2 else []),
                                sig=("sbQK" if lastseg else None))
                            if lastseg:
                                tk[("QK", g)] = v
                        if diag:
                            tk[("QK", g)] = P.op("pe", lambda e, c0=c0: e.matmul(
                                Zp[:, c0:c0 + 128], lhsT=ident, rhs=negm, start=False, stop=True, skip_group_check=True),
                                sig="sbQK")

                    def ACT_E(i):
                        hp, qc, kb, c0, diag, cs, ce = tiles[i]
                        g = base + i
                        tk[("E", g)] = P.op("act", lambda e, g=g, c0=c0: e.activation(out=Eb[g % 3][:, c0:QC], in_=Zp[:, c0:QC], func=AF.Exp),
                                            waits=[("sbQK", tk[("QK", g)]), ("sbA", tk.get(("A", g - 3)))], sig="sbE")

                    def ACT_G(i):
                        hp, qc, kb, c0, diag, cs, ce = tiles[i]
                        g = base + i
                        tk[("G", g)] = P.op("act", lambda e, g=g, c0=c0: e.activation(out=Gb[g % 3][:, c0:QC], in_=Eb[g % 3][:, c0:QC],
                                                                                   func=AF.Ln, bias=1.0),
                                            waits=[("sbE", tk[("E", g)]), ("sbTRIC", tk.get(("TRIC", g - 3)))], sig="sbG")

                    def ACT_P(i):
                        hp, qc, kb, c0, diag, cs, ce = tiles[i]
                        g = base + i
                        tk[("P", g)] = P.op("act", lambda e, g=g, c0=c0: e.activation(out=Pb[g % 2][:, c0:QC], in_=Rp[:, c0:QC], func=AF.Exp),
                                            waits=[("sbTRI", tk[("TRI", g)]), ("sbA", tk.get(("A", g - 2)))], sig="sbP")

                    def PE_TRI(i):
                        hp, qc, kb, c0, diag, cs, ce = tiles[i]
                        g = base + i
                        if cs:
                            par = chunk_of[i] % 2
                            for (a, b) in ((0, 512), (512, 1024)):
                                P.op("pe", lambda e, a=a, b=b: e.matmul(Rp[:, a:b], lhsT=zer, rhs=cst[:, 0:512], start=True, stop=False,
                                                                        skip_group_check=True),
                                     waits=[("sbP", tk.get(("P", g - 1)))])
                                P.op("pe", lambda e, a=a, b=b, hp=hp, par=par: e.matmul(
                                    Yp[par][hp:hp + 64, a:b], lhsT=zer[:, 0:64], rhs=cst[:, 0:512], start=True, stop=False,
                                    skip_group_check=True),
                                    waits=[("sbEV", t_evy.get(chunk_of[i] - 2))])
                        segs = col_segs(c0)
                        for si, (a, b) in enumerate(segs):
                            v = P.op("pe", lambda e, g=g, a=a, b=b: e.matmul(Rp[:, a:b], lhsT=triN, rhs=Gb[g % 3][:, a:b], start=False, stop=False,
                                                                            skip_group_check=True),
                                     waits=[("sbG", tk[("G", g)]), ("sbP", tk.get(("P", g - 1)))],
                                     sig=("sbTRI" if si == len(segs) - 1 else None))
                        tk[("TRI", g)] = v

                    def PE_TRIC(i):
                        hp, qc, kb, c0, diag, cs, ce = tiles[i]
                        g = base + i
                        segs = col_segs(c0)
                        for si, (a, b) in enumerate(segs):
                            v = P.op("pe", lambda e, g=g, a=a, b=b: e.matmul(Rp[:, a:b], lhsT=tricN, rhs=Gb[g % 3][:, a:b], start=False, stop=False,
                                                                            skip_group_check=True),
                                     waits=[("sbP", tk[("P", g)])],
                                     sig=("sbTRIC" if si == len(segs) - 1 else None))
                        tk[("TRIC", g)] = v

                    def PE_PV(i):
                        hp, qc, kb, c0, diag, cs, ce = tiles[i]
                        g = base + i
                        par = chunk_of[i] % 2
                        segs = col_segs(c0)
                        for si, (a, b) in enumerate(segs):
                            v = P.op("pe", lambda e, g=g, a=a, b=b, hp=hp, kb=kb, par=par: e.matmul(
                                Yp[par][hp:hp + 64, a:b], lhsT=vv[:, kb, hp:hp + 64], rhs=Ab[g % 2][:, a:b], start=False, stop=False,
                                skip_group_check=True),
                                waits=[("sbA", tk[("A", g)])],
                                sig=("sbPV" if si == len(segs) - 1 else None))
                        tk[("PV", g)] = v

                    def DVE_A(i):
                        hp, qc, kb, c0, diag, cs, ce = tiles[i]
                        g = base + i
                        tk[("A", g)] = P.op("dve", lambda e, g=g, c0=c0: e.tensor_mul(out=Ab[g % 2][:, c0:QC], in0=Eb[g % 3][:, c0:QC],
                                                                                   in1=Pb[g % 2][:, c0:QC]),
                                            waits=[("sbP", tk[("P", g)]), ("sbPV", tk.get(("PV", g - 2)))], sig="sbA")

                    def DVE_EV(i):
                        hp, qc, kb, c0, diag, cs, ce = tiles[i]
                        g = base + i
                        ch = chunk_of[i]
                        par = ch % 2
                        t_evy[ch] = P.op("dve", lambda e, hp=hp, qc=qc, par=par, step=step: e.tensor_mul(
                            out=ygT[hp:hp + 64, step, qc * QC:(qc + 1) * QC], in0=Yp[par][hp:hp + 64, :],
                            in1=sg[hp:hp + 64, qc * QC:(qc + 1) * QC]),
                            waits=[("sbPV", tk[("PV", g)])], sig="sbEV")

                    QK(0)
                    ACT_E(0)
                    if T > 1:
                        QK(1)
                    ACT_G(0)
                    if T > 1:
                        ACT_E(1)
                    if T > 2:
                        QK(2)
                    PE_TRI(0)
                    for i in range(T):
                        ACT_P(i)
                        PE_TRIC(i)
                        if i + 1 < T:
                            ACT_G(i + 1)
                            PE_TRI(i + 1)
                        DVE_A(i)
                        PE_PV(i)
                        if tiles[i][6]:
                            DVE_EV(i)
                        if i + 2 < T:
                            ACT_E(i + 2)
                        if i + 3 < T:
                            QK(i + 3)
                    sbi += T
                    n_chunk = cc + 1
                    t_attdve = ("sbEV", t_evy[cc])
                else:
                    j = step - 4
                    heads = (2 * j, 2 * j + 1)
                    for n in range(NT):
                        gi = swn
                        grp = swg + n // 4
                        zbase = (gi % 2) * 1024
                        wz = 256 if n > 0 else 128
                        for which, kblk in ((0, n), (1, n - 1)):
                            if kblk < 0:
                                continue
                            for hh in range(2):
                                hp = hh * 64
                                zc = zbase + hh * 512 + which * 128
                                P.op("pe", lambda e, zc=zc, hp=hp, kblk=kblk, n=n, which=which: e.matmul(
                                    ps[:, zc:zc + 128], lhsT=kT[hp:hp + 64, kblk * 128:(kblk + 1) * 128],
                                    rhs=qT[hp:hp + 64, n * 128:(n + 1) * 128], start=(which == 0), stop=False, skip_group_check=True),
                                    waits=[("swP", t_sw.get(("P", gi - 2)))] + (projw if n < 2 else []))
                        for hh in range(2):
                            h = heads[hh]
                            bc = C_SWB + h * 256
                            zc = zbase + hh * 512
                            v = P.op("pe", lambda e, zc=zc, bc=bc, wz=wz: e.matmul(
                                ps[:, zc:zc + wz], lhsT=ident, rhs=cst[:, bc:bc + wz], start=False, stop=True, skip_group_check=True),
                                sig=("swQK" if hh == 1 else None))
                        t_sw[("QK", gi)] = v
                        zin = ps[:, zbase:zbase + 1024].rearrange("p (b c) -> p b c", b=2)[:, :, 0:wz]
                        pout = Psw[gi % 2].rearrange("p (b c) -> p b c", b=2)[:, :, 0:wz]
                        t_sw[("P", gi)] = P.op("act", lambda e, zin=zin, pout=pout: e.activation(out=pout, in_=zin, func=AF.Exp),
                                               waits=[("swQK", t_sw[("QK", gi)]), ("swPV", t_sw.get(("PV", gi - 2)))], sig="swP")
                        if SW_DBG == 2:
                            ckpt(1000)
                        par = grp % 2
                        Yb = Yp[par][:, 0:512]
                        Db = Yp[par][:, 512:1024]
                        col = (n % 4) * 128
                        for hh in range(2):
                            hp = hh * 64
                            srcs = [(hh * 256, n)] + ([(hh * 256 + 128, n - 1)] if n > 0 else [])
                            for si, (pc, kblk) in enumerate(srcs):
                                P.op("pe", lambda e, Yb=Yb, hp=hp, col=col, kblk=kblk, gi=gi, pc=pc, si=si: e.matmul(
                                    Yb[hp:hp + 64, col:col + 128], lhsT=vv[:, kblk, 0:64], rhs=Psw[gi % 2][:, pc:pc + 128],
                                    start=(si == 0), stop=(si == len(srcs) - 1), skip_group_check=True),
                                    waits=[("swP", t_sw[("P", gi)]), ("swEV", t_sw.get(("EV", grp - 2)))])
                            for si, (pc, kblk) in enumerate(srcs):
                                v = P.op("pe", lambda e, Db=Db, hp=hp, col=col, gi=gi, pc=pc, si=si: e.matmul(
                                    Db[hp:hp + 64, col:col + 128], lhsT=ones_bf[:, 0:64], rhs=Psw[gi % 2][:, pc:pc + 128],
                                    start=(si == 0), stop=(si == len(srcs) - 1), skip_group_check=True),
                                    sig=("swPV" if (hh == 1 and si == len(srcs) - 1) else None))
                        t_sw[("PV", gi)] = v
                        if SW_DBG == 3:
                            ckpt(1000)
                        if n % 4 == 3:
                            q0 = (n - 3) * 128
                            t1 = P.op("dve", lambda e, Db=Db, j=j: e.tensor_scalar(out=rden, in0=Db, scalar1=esk[:, j:j + 1], scalar2=None,
                                                                                    op0=ALU.add),
                                      waits=[("swPV", t_sw[("PV", gi)]), ("act0", t_esk)], sig="swD")
                            t1 = P.op("dve", lambda e: e.reciprocal(out=rden, in_=rden), waits=[("swD", t1)], sig="swD")
                            t1 = P.op("dve", lambda e, Yb=Yb: e.tensor_mul(out=ytmp, in0=Yb, in1=rden), waits=[("swD", t1)], sig="swD")
                            t_sw[("EV", grp)] = P.op("dve", lambda e, q0=q0, step=step: e.tensor_mul(
                                out=ygT[:, step, q0:q0 + 512], in0=ytmp, in1=sg[:, q0:q0 + 512]),
                                waits=[("swD", t1)], sig="swEV")
                        swn += 1
                    swg += NT // 4
                    t_last_sw = t_sw[("EV", swg - 1)]
                if not is_sb:
                    t_attdve = ("swEV", t_last_sw)
                ckpt(4 + 2 * step)

            ckpt(20)
            wo_v = w_out.rearrange("(k p) e -> p k e", p=128)
            t_wo = None
            for k in range(8):
                t_wo = P.op("pool", lambda e, k=k: e.dma_start(out=wout[:, k, :], in_=wo_v[:, k, :]),
                            waits=[("projpe", t_projpe)], sig="ld_wo", inc=16)
            t_fg = P.op("sp", lambda e: e.dma_start(out=fg_bc, in_=fg_bc_d[:, :]), waits=[("projpe", t_projpe)], sig="ld_fg", inc=16)
            t_xf = {}
            t_o = {}
            t_st = {}
            t_r2 = {}
            t_sq2 = {}

            def issue_xf(tt):
                t_xf[tt] = P.op("sp", lambda e, tt=tt: e.dma_start(out=xf[tt % 3], in_=x[tt * 128:(tt + 1) * 128, :]),
                                waits=[("fr2", t_r2.get(tt - 3)), ("projpe", t_projpe)], sig="ld_xf%d" % (tt % 3), inc=16)

            t_po = {}
            for tt in range(3):
                issue_xf(tt)
            for tt in range(NT):
                for eh in range(2):
                    for kc in range(8):
                        v = P.op("pe", lambda e, tt=tt, eh=eh, kc=kc: e.matmul(
                            po[tt % 2][eh], lhsT=ygT[:, kc, tt * 128:(tt + 1) * 128], rhs=wout[:, kc, eh * 512:(eh + 1) * 512],
                            start=(kc == 0), stop=(kc == 7)),
                            waits=[("ld_wo", t_wo), t_attdve, ("fr2", t_r2.get(tt - 2))],
                            sig=("fpo" if kc == 7 else None))
                    t_po[(tt, eh)] = v
                rb = rf[tt % 2]
                for eh in range(2):
                    t1 = P.op("dve", lambda e, tt=tt, eh=eh, rb=rb: e.tensor_mul(out=rb[:, eh * 512:(eh + 1) * 512], in0=po[tt % 2][eh],
                                                                               in1=gate_bc[:, eh * 512:(eh + 1) * 512]),
                              waits=[("fpo", t_po[(tt, eh)]), ("fo", t_o.get(tt - 2)), ("fsq", t_sq2.get(tt - 2))], sig="fd")
                t_r2[tt] = P.op("dve", lambda e, tt=tt, rb=rb: e.tensor_add(out=rb, in0=rb, in1=xf[tt % 3]),
                                waits=[("fd", t1), ("ld_xf%d" % (tt % 3), t_xf[tt])], sig="fr2")
                if tt + 3 < NT:
                    issue_xf(tt + 3)
                t_sq2[tt] = P.op("act", lambda e, tt=tt, rb=rb: e.activation(out=junkf, in_=rb, func=AF.Square, accum_out=ss2[:, tt:tt + 1]),
                                 waits=[("fr2", t_r2[tt])], sig="fsq")
                t1 = P.op("act", lambda e, tt=tt: e.activation(out=rstd2[:, tt:tt + 1], in_=ss2[:, tt:tt + 1], func=AF.Sqrt,
                                                               scale=1.0 / D, bias=eps_t[:, 0:1]),
                          waits=[("fsq", t_sq2[tt])], sig="fsqrt")
                t1 = P.op("dve", lambda e, tt=tt: e.reciprocal(out=rstd2[:, tt:tt + 1], in_=rstd2[:, tt:tt + 1]),
                          waits=[("fsqrt", t1)], sig="fd")
                t_o[tt] = P.op("dve", lambda e, tt=tt, rb=rb: e.scalar_tensor_tensor(
                    out=of[tt % 2], in0=rb, scalar=rstd2[:, tt:tt + 1], in1=fg_bc, op0=ALU.mult, op1=ALU.mult),
                    waits=[("fd", t1), ("ld_fg", t_fg), ("ld_out%d" % (tt % 2), t_st.get(tt - 2))], sig="fo")
                t_st[tt] = P.op("sp", lambda e, tt=tt: e.dma_start(out=out[tt * 128:(tt + 1) * 128, :], in_=of[tt % 2]),
                                waits=[("fo", t_o[tt])], sig="ld_out%d" % (tt % 2), inc=16)
            P.op("sp", lambda e: e.nop(), waits=[("ld_out0", t_st[NT - 2]), ("ld_out1", t_st[NT - 1])])

        try:
            plan_all()
        except _Stop:
            pass

        names = sorted(P.cnt.keys())
        sems = {n: es.enter_context(nc.semaphore(n)) for n in names}
        block = es.enter_context(nc.Block())

        def emit(eng, oplist):
            seen = {}
            for fn, waits, sig, inc in oplist:
                for (name, val) in waits:
                    if seen.get(name, 0) < val:
                        eng.wait_ge(sems[name], val)
                        seen[name] = val
                ins = fn(eng)
                if sig is not None:
                    ins.then_inc(sems[sig], inc)

        @block.sync
        def _(eng):
            emit(eng, P.ops["sp"])

        @block.gpsimd
        def _(eng):
            emit(eng, P.ops["pool"])

        @block.tensor
        def _(eng):
            emit(eng, P.ops["pe"])

        @block.scalar
        def _(eng):
            emit(eng, P.ops["act"])

        @block.vector
        def _(eng):
            emit(eng, P.ops["dve"])
    return nc


_CACHE = {}


def kernel(x, c, w_ada, b_ada, norm_g, w_in, sinks, w_out, final_g):
    x = np.asarray(x, np.float32)
    c = np.asarray(c, np.float32)
    w_ada = np.ascontiguousarray(np.asarray(w_ada, np.float32)[0])
    b_ada = np.asarray(b_ada, np.float32)[0]
    norm_g = np.asarray(norm_g, np.float32)[0]
    w_in = np.ascontiguousarray(np.asarray(w_in, np.float32)[0])
    sinks = np.asarray(sinks, np.float32)[0]
    w_out = np.ascontiguousarray(np.asarray(w_out, np.float32)[0])
    final_g = np.asarray(final_g, np.float32)

    def lay(v):
        return np.ascontiguousarray(v.reshape(-1, 128).T)

    bada_l = lay(b_ada[:2048])
    bg_bc = np.ascontiguousarray(np.broadcast_to(b_ada[2048:3072][None, :], (128, D)))
    normg_l = lay(norm_g)
    fg_bc = np.ascontiguousarray(np.broadcast_to(final_g[None, :], (128, D)))
    sinks_l = np.ascontiguousarray(np.stack([np.repeat(sinks[2 * j:2 * j + 2], 64) for j in range(4)], axis=1))
    consts = make_consts()
    if "nc" not in _CACHE:
        _CACHE["nc"] = build_nc()
    nc = _CACHE["nc"]
    in_maps = []
    for b in range(NCORE):
        in_maps.append({
            "x": np.ascontiguousarray(x[b]), "c_l": lay(c[b]), "w_ada": w_ada, "bada_l": bada_l, "bg_bc": bg_bc,
            "normg_l": normg_l, "w_in": w_in, "sinks_l": sinks_l, "w_out": w_out, "fg_bc": fg_bc, "consts": consts,
        })
    res = run_bass_kernel_spmd(nc, in_maps, core_ids=list(range(NCORE)))
    return np.stack([np.asarray(r["out"], np.float32) for r in res.results], axis=0)
```

```python
from contextlib import ExitStack

import numpy as np
import concourse.bass as bass
import concourse.mybir as mybir
from concourse.bass_utils import run_bass_kernel_spmd

F32 = mybir.dt.float32
BF16 = mybir.dt.bfloat16
AF = mybir.ActivationFunctionType
ALU = mybir.AluOpType

S = 4096
D = 1024
NCORE = 8
NT = S // 128
QC = 1024
NEG = -30000.0
NCONST = 128 * 6 + 2048
C_ID, C_TRI, C_TRIC, C_ZERO, C_NEGM, C_ONES, C_SWB = 0, 128, 256, 384, 512, 640, 768


LEVEL = 99
SW_DBG = 0


class _Stop(Exception):
    pass


def ckpt(level):
    if LEVEL <= level:
        raise _Stop()


class Plan:
    def __init__(self):
        self.ops = {"pe": [], "act": [], "dve": [], "pool": [], "sp": []}
        self.cnt = {}

    def op(self, eng, fn, waits=(), sig=None, inc=1):
        v = None
        if sig is not None:
            self.cnt[sig] = self.cnt.get(sig, 0) + inc
            v = self.cnt[sig]
        ws = tuple(w for w in waits if w is not None and w[1] is not None and w[1] > 0)
        self.ops[eng].append((fn, ws, sig, inc))
        return v


def make_consts():
    c = np.zeros((128, NCONST), np.float32)
    j = np.arange(128)[:, None]
    s = np.arange(128)[None, :]
    c[:, C_ID:C_ID + 128] = (j == s)
    c[:, C_TRI:C_TRI + 128] = -1.0 * (j >= s)
    c[:, C_TRIC:C_TRIC + 128] = -1.0 * (j < s)
    c[:, C_NEGM:C_NEGM + 128] = np.where(j < s, 0.0, NEG)
    c[:, C_ONES:C_ONES + 128] = 1.0
    for h in range(8):
        m = 2.0 ** (-8.0 * (h + 1) / 8)
        rel_cur = (s - j).astype(np.float32)
        cur = np.where(s >= j, -m * rel_cur, NEG)
        rel_prev = (128 + s - j).astype(np.float32)
        prev = np.where(j > s, -m * rel_prev, NEG)
        c[:, C_SWB + h * 256:C_SWB + h * 256 + 128] = cur
        c[:, C_SWB + h * 256 + 128:C_SWB + h * 256 + 256] = prev
    return c


def sb_tiles():
    out = []
    for qc in range(S // QC):
        nkb = (QC // 128) * (qc + 1)
        for kb in range(nkb - 1, -1, -1):
            jd = kb - (QC // 128) * qc
            diag = jd >= 0
            c0 = 128 * jd if diag else 0
            out.append((qc, kb, c0, diag, kb == nkb - 1, kb == 0))
    return out


def col_segs(c0, c1=QC):
    segs = []
    a = c0
    while a < c1:
        b = min(c1, (a // 512 + 1) * 512)
        segs.append((a, b))
        a = b
    return segs


def build_nc():
    nc = bass.Bass("TRN2", target_bir_lowering=False)
    x = nc.dram_tensor("x", [S, D], F32, kind="ExternalInput").ap()
    c_l = nc.dram_tensor("c_l", [128, 8], F32, kind="ExternalInput").ap()
    w_ada = nc.dram_tensor("w_ada", [D, 3 * D], F32, kind="ExternalInput").ap()
    bada_l = nc.dram_tensor("bada_l", [128, 16], F32, kind="ExternalInput").ap()
    bg_bc_d = nc.dram_tensor("bg_bc", [128, D], F32, kind="ExternalInput").ap()
    normg_l = nc.dram_tensor("normg_l", [128, 8], F32, kind="ExternalInput").ap()
    w_in = nc.dram_tensor("w_in", [D, 3328], F32, kind="ExternalInput").ap()
    sinks_l = nc.dram_tensor("sinks_l", [128, 4], F32, kind="ExternalInput").ap()
    w_out = nc.dram_tensor("w_out", [D, D], F32, kind="ExternalInput").ap()
    fg_bc_d = nc.dram_tensor("fg_bc", [128, D], F32, kind="ExternalInput").ap()
    consts_d = nc.dram_tensor("consts", [128, NCONST], F32, kind="ExternalInput").ap()
    out = nc.dram_tensor("out", [S, D], F32, kind="ExternalOutput").ap()

    P = Plan()
    es = ExitStack()
    with es:
        def sb(name, shape, dt):
            return es.enter_context(nc.sbuf_tensor(name, shape, dt))

        ygT = sb("ygT", [128, 8, S], BF16)
        hT = sb("hT", [128, 8, S], BF16)
        ov = sb("ov", [128, 30720], BF16)
        cst = sb("cst", [128, NCONST], BF16)
        gate_bc = sb("gate_bc", [128, D], F32)
        c_sb = sb("c_sb", [128, 8], F32)
        etmp = sb("etmp", [128, 8], F32)
        cond = sb("cond", [128, 8], F32)
        bada = sb("bada", [128, 16], F32)
        normg = sb("normg", [128, 8], F32)
        mod_sb = sb("mod_sb", [128, 16], F32)
        gs = sb("gs", [128, 8], F32)
        ones_f = sb("ones_f", [128, 128], F32)
        ss = sb("ss", [128, 32], F32)
        rstd = sb("rstd", [128, 32], F32)
        ss2 = sb("ss2", [128, 32], F32)
        rstd2 = sb("rstd2", [128, 32], F32)
        snk = sb("snk", [128, 4], F32)
        esk = sb("esk", [128, 4], F32)
        eps_t = sb("eps_t", [128, 1], F32)
        ps = es.enter_context(nc.psum_tensor("ps", [128, 4096], F32))

        qT = ov[:, 0:4096]
        kT = ov[:, 4096:8192]
        sg = ov[:, 8192:12288]
        vv = ov[:, 12288:16384].rearrange("p (b c) -> p b c", c=128)
        wsl = ov[:, 16384:20480].rearrange("p (k c) -> p k c", c=512)
        PB = 20480
        Eb = [ov[:, PB + i * 1024:PB + (i + 1) * 1024] for i in range(3)]
        Gb = [ov[:, PB + (3 + i) * 1024:PB + (4 + i) * 1024] for i in range(3)]
        Pb = [ov[:, PB + (6 + i) * 1024:PB + (7 + i) * 1024] for i in range(2)]
        Ab = [ov[:, PB + (8 + i) * 1024:PB + (9 + i) * 1024] for i in range(2)]
        Psw = [ov[:, PB + i * 512:PB + (i + 1) * 512] for i in range(2)]
        rden = ov[:, PB + 1024:PB + 2048].bitcast(F32)
        ytmp = ov[:, PB + 2048:PB + 3072].bitcast(F32)
        yflat = ygT[:, :, :].rearrange("p a b -> p (a b)")
        wada = [yflat[:, i * 6144:(i + 1) * 6144].bitcast(F32) for i in range(4)]
        xt = [ov[:, 12288 + i * 2048:12288 + (i + 1) * 2048].bitcast(F32) for i in range(4)]
        xnb = [[ov[:, 20480 + (g * 4 + t) * 1024:20480 + (g * 4 + t + 1) * 1024] for t in range(4)] for g in range(2)]
        junk = ov[:, 28672:29696]
        hflat = hT[:, :, :].rearrange("p a b -> p (a b)")
        condb = yflat[:, 24576:26624].bitcast(F32).rearrange("p (k m) -> p k m", m=128)
        bg_bc = yflat[:, 26624:28672].bitcast(F32)
        wout = hflat[:, 0:8192].rearrange("p (k e) -> p k e", e=1024)
        xf = [hflat[:, 8192 + i * 2048:8192 + (i + 1) * 2048].bitcast(F32) for i in range(3)]
        rf = [hflat[:, 14336 + i * 2048:14336 + (i + 1) * 2048].bitcast(F32) for i in range(2)]
        of = [hflat[:, 18432 + i * 2048:18432 + (i + 1) * 2048].bitcast(F32) for i in range(2)]
        fg_bc = hflat[:, 22528:24576].bitcast(F32)
        junkf = hflat[:, 24576:25600]
        Zp = ps[:, 0:1024]
        Rp = ps[:, 1024:2048]
        Yp = [ps[:, 2048:3072], ps[:, 3072:4096]]
        pp = [ps[:, 0:512], ps[:, 512:1024]]
        tp = [ps[:, i * 512:(i + 1) * 512].bitcast(BF16)[:, 0:512] for i in range(2)]
        modps = ps[:, 1024:1040]
        gateps = ps[:, 2048:3072]
        po = [[ps[:, t * 1024 + e * 512:t * 1024 + (e + 1) * 512] for e in range(2)] for t in range(2)]

        ident = cst[:, C_ID:C_ID + 128]
        triN = cst[:, C_TRI:C_TRI + 128]
        tricN = cst[:, C_TRIC:C_TRIC + 128]
        zer = cst[:, C_ZERO:C_ZERO + 128]
        negm = cst[:, C_NEGM:C_NEGM + 128]
        ones_bf = cst[:, C_ONES:C_ONES + 128]

        def plan_all():
            t_cst = P.op("pool", lambda e: e.dma_start(out=cst[:, :], in_=consts_d[:, :]), sig="ld_cst", inc=16)
            for dst, src in ((c_sb, c_l), (bada, bada_l), (normg, normg_l), (snk, sinks_l)):
                t_small = P.op("sp", lambda e, dst=dst, src=src: e.dma_start(out=dst[:, :], in_=src[:, :]), sig="ld_small", inc=16)
            t_small = P.op("sp", lambda e: e.dma_start(out=bg_bc, in_=bg_bc_d[:, :]), sig="ld_small", inc=16)

            ckpt(0)
            P.op("dve", lambda e: e.memset(ones_f[:, :], 1.0), sig="dve0")
            P.op("dve", lambda e: e.memset(eps_t[:, :], 1e-6), sig="dve0")
            P.op("dve", lambda e: e.memset(ss[:, :], 0.0), sig="dve0")
            t_d = P.op("dve", lambda e: e.memset(ss2[:, :], 0.0), sig="dve0")
            t_ms = t_d
            t_a = P.op("act", lambda e: e.activation(out=etmp[:, :], in_=c_sb[:, :], func=AF.Exp, scale=-1.0),
                       waits=[("ld_small", t_small)], sig="act0")
            t_d = P.op("dve", lambda e: e.tensor_scalar_add(out=etmp[:, :], in0=etmp[:, :], scalar1=1.0),
                       waits=[("act0", t_a), ("dve0", t_d)], sig="dve0")
            t_d = P.op("dve", lambda e: e.reciprocal(out=etmp[:, :], in_=etmp[:, :]), waits=[("dve0", t_d)], sig="dve0")
            t_d = P.op("dve", lambda e: e.tensor_mul(out=cond[:, :], in0=c_sb[:, :], in1=etmp[:, :]),
                       waits=[("dve0", t_d)], sig="dve0")
            t_cond = t_d
            for k in range(8):
                t_d = P.op("dve", lambda e, k=k: e.tensor_scalar(out=condb[:, k, :], in0=ones_f[:, :], scalar1=cond[:, k:k + 1],
                                                                 scalar2=None, op0=ALU.mult),
                           waits=[("dve0", t_cond)], sig="dve0")
            t_condb = t_d
            t_pe0 = {}
            t_wada = {}
            t_xld = {}
            t_sq = {}
            t_xn = {}
            t_tp = {}
            t_ev = {}
            NWB = 4

            def issue_wada(k):
                t_wada[k] = P.op("sp", lambda e, k=k: e.dma_start(out=wada[k % NWB], in_=w_ada[k * 128:(k + 1) * 128, :]),
                                 waits=[("pe0", t_pe0.get(k - NWB))], sig="ld_wada%d" % (k % NWB), inc=16)

            def issue_xload(tt):
                t_xld[tt] = P.op("sp", lambda e, tt=tt: e.dma_start(out=xt[tt % 4], in_=x[tt * 128:(tt + 1) * 128, :]),
                                 waits=[("p1xn", t_xn.get(tt - 4)), ("p1sq", t_sq.get(tt - 4))],
                                 sig="ld_x%d" % (tt % 4), inc=16)

            for k in range(NWB):
                issue_wada(k)
            for tt in range(4):
                issue_xload(tt)
            for k in range(8):
                for j in range(16):
                    P.op("pe", lambda e, k=k, j=j: e.matmul(modps[:, j:j + 1], lhsT=wada[k % NWB][:, j * 128:(j + 1) * 128],
                                                            rhs=cond[:, k:k + 1], start=(k == 0 and j == 0), stop=(k == 7),
                                                            skip_group_check=True),
                         waits=[("ld_wada%d" % (k % NWB), t_wada[k]), ("dve0", t_condb)])
                for eh in range(2):
                    t_pe0[k] = P.op("pe", lambda e, k=k, eh=eh: e.matmul(gateps[:, eh * 512:(eh + 1) * 512], lhsT=condb[:, k, :],
                                                                         rhs=wada[k % NWB][:, 2048 + eh * 512:2048 + (eh + 1) * 512],
                                                                         start=(k == 0), stop=(k == 7)),
                                    waits=[("ld_wada%d" % (k % NWB), t_wada[k]), ("dve0", t_condb)], sig=("pe0" if eh == 1 else None))
                if k + NWB < 8:
                    issue_wada(k + NWB)
                g = k
                for t4 in range(4):
                    tt = g * 4 + t4
                    t_sq[tt] = P.op("act", lambda e, tt=tt: e.activation(out=junk, in_=xt[tt % 4], func=AF.Square,
                                                                         accum_out=ss[:, tt:tt + 1]),
                                    waits=[("ld_x%d" % (tt % 4), t_xld[tt]), ("dve0", t_ms)], sig="p1sq")
                    t_r = P.op("act", lambda e, tt=tt: e.activation(out=rstd[:, tt:tt + 1], in_=ss[:, tt:tt + 1], func=AF.Sqrt,
                                                                    scale=1.0 / D, bias=eps_t[:, 0:1]),
                               waits=[("p1sq", t_sq[tt])], sig="p1sqrt")
                    t_r = P.op("dve", lambda e, tt=tt: e.reciprocal(out=rstd[:, tt:tt + 1], in_=rstd[:, tt:tt + 1]),
                               waits=[("p1sqrt", t_r)], sig="dve1")
                    t_xn[tt] = P.op("dve", lambda e, tt=tt, g=g, t4=t4: e.tensor_scalar(
                        out=xnb[g % 2][t4], in0=xt[tt % 4], scalar1=rstd[:, tt:tt + 1], scalar2=None, op0=ALU.mult),
                        waits=[("dve1", t_r), ("p1tp", t_tp.get((g - 2, 7)))], sig="p1xn")
                    if tt + 4 < NT:
                        issue_xload(tt + 4)
                for j in range(8):
                    prev_ev = t_ev[(g, j - 2)] if j >= 2 else (t_ev[(g - 1, 6 + j)] if g >= 1 else None)
                    for t4 in range(4):
                        t_tp[(g, j)] = P.op("pe", lambda e, g=g, j=j, t4=t4: e.transpose(
                            out=tp[j % 2][:, t4 * 128:(t4 + 1) * 128], in_=xnb[g % 2][t4][:, j * 128:(j + 1) * 128], identity=ident),
                            waits=[("p1xn", t_xn[g * 4 + 3]), ("p1ev", prev_ev), ("ld_cst", t_cst)],
                            sig=("p1tp" if t4 == 3 else None))
                    t_ev[(g, j)] = P.op("act", lambda e, g=g, j=j: e.activation(
                        out=hT[:, j, g * 512:(g + 1) * 512], in_=tp[j % 2], func=AF.Identity),
                        waits=[("p1tp", t_tp[(g, j)])], sig="p1ev")
            t_d = P.op("dve", lambda e: e.tensor_add(out=mod_sb[:, :], in0=modps, in1=bada[:, :]),
                       waits=[("pe0", t_pe0[7]), ("ld_small", t_small)], sig="dve0")
            t_d = P.op("dve", lambda e: e.scalar_tensor_tensor(out=gs[:, :], in0=mod_sb[:, 8:16], scalar=1.0, in1=normg[:, :],
                                                               op0=ALU.add, op1=ALU.mult),
                       waits=[("dve0", t_d)], sig="dve0")
            t_d = P.op("dve", lambda e: e.tensor_add(out=gate_bc[:, :], in0=gateps, in1=bg_bc), sig="dve0")
            t_mod = t_d
            for j in range(8):
                for hh in range(2):
                    t_fix = P.op("dve", lambda e, j=j, hh=hh: e.tensor_scalar(
                        out=hT[:, j, hh * 2048:(hh + 1) * 2048], in0=hT[:, j, hh * 2048:(hh + 1) * 2048],
                        scalar1=gs[:, j:j + 1], scalar2=mod_sb[:, j:j + 1], op0=ALU.mult, op1=ALU.add),
                        waits=[("dve0", t_mod), ("p1ev", t_ev[(7, 7)])], sig="hfix")
            t_hT = t_fix
            ckpt(2)
            t_wsl = None
            t_projpe = None
            t_attpe = None
            t_attdve = None
            t_pev = {}
            nproj = 0
            t_esk = P.op("act", lambda e: e.activation(out=esk[:, :], in_=snk[:, :], func=AF.Exp),
                         waits=[("ld_small", t_small)], sig="act0")
            sbt = sb_tiles()
            sbi = 0
            tk = {}
            n_chunk = 0
            t_evy = {}
            swn = 0
            swg = 0
            t_sw = {}

            for step in range(8):
                is_sb = step < 4
                if is_sb:
                    cq, ck, cv, cg = step * 128, 512 + step * 128, 1024 + step * 128, 1536 + step * 128
                    kw = 128
                    vw = 128
                    do_kv = True
                else:
                    j = step - 4
                    cq, ck, cv, cg = 2048 + j * 128, 2560 + (j // 2) * 64, 2688 + (j // 2) * 64, 2816 + j * 128
                    kw = 64
                    vw = 64
                    do_kv = (j % 2 == 0)
                wv = w_in.rearrange("(k p) c -> p k c", p=128)
                dmas = [(0, cq, 128), (256, cg, 128)]
                if do_kv:
                    if kw == 128:
                        dmas.append((128, ck, 128))
                    else:
                        dmas.append((128, ck, 64))
                        dmas.append((192, ck, 64))
                    if vw == 128:
                        dmas.append((384, cv, 128))
                    else:
                        dmas.append((384, cv, 64))
                        dmas.append((448, cv, 64))
                        vw = 128
                for (dc, sc, w) in dmas:
                    t_wsl = P.op("pool", lambda e, dc=dc, sc=sc, w=w: e.dma_start(out=wsl[:, :, dc:dc + w], in_=wv[:, :, sc:sc + w]),
                                 waits=[("projpe", t_projpe), ("hfix", t_hT)], sig="ld_wsl", inc=16)
                ckpt(2.5 + 2 * step)
                kinds = [("q", 0), ("g", 256)] + ([("k", 128)] if do_kv else [])
                for kind, wc in kinds:
                    for tc in range(8):
                        par = nproj % 2
                        for kc in range(8):
                            last = kc == 7
                            t_projpe_new = P.op("pe", lambda e, par=par, kc=kc, wc=wc, tc=tc: e.matmul(
                                pp[par], lhsT=wsl[:, kc, wc:wc + 128], rhs=hT[:, kc, tc * 512:(tc + 1) * 512],
                                start=(kc == 0), stop=(kc == 7)),
                                waits=[("ld_wsl", t_wsl), t_pev.get(nproj - 2), ("hfix", t_hT),
                                       t_attdve],
                                sig=("projpe" if last else None))
                        t_projpe = t_projpe_new
                        dst = {"q": qT, "k": kT, "g": sg}[kind][:, tc * 512:(tc + 1) * 512]
                        if kind == "q":
                            t_pev[nproj] = ("pev", P.op("dve", lambda e, dst=dst, par=par: e.tensor_scalar(
                                out=dst, in0=pp[par], scalar1=0.125, scalar2=None, op0=ALU.mult),
                                waits=[("projpe", t_projpe)], sig="pev"))
                            last_dve_pev = t_pev[nproj]
                        elif kind == "k":
                            t_pev[nproj] = ("pev", P.op("dve", lambda e, dst=dst, par=par: e.tensor_copy(out=dst, in_=pp[par]),
                                                waits=[("projpe", t_projpe)], sig="pev"))
                            last_dve_pev = t_pev[nproj]
                        else:
                            t_pev[nproj] = ("pevA", P.op("act", lambda e, dst=dst, par=par: e.activation(out=dst, in_=pp[par], func=AF.Silu),
                                                         waits=[("projpe", t_projpe), t_attdve], sig="pevA"))
                            last_act_pev = t_pev[nproj]
                        nproj += 1
                ckpt(2.75 + 2 * step)
                if do_kv:
                    for tg in range(8):
                        par = nproj % 2
                        for t4 in range(4):
                            tok = (tg * 4 + t4) * 128
                            for kc in range(8):
                                last = (kc == 7 and t4 == 3)
                                t_projpe_new = P.op("pe", lambda e, par=par, kc=kc, tok=tok, t4=t4, vw=vw: e.matmul(
                                    pp[par][:, t4 * 128:t4 * 128 + vw], lhsT=hT[:, kc, tok:tok + 128], rhs=wsl[:, kc, 384:384 + vw],
                                    start=(kc == 0), stop=(kc == 7)),
                                    waits=[("ld_wsl", t_wsl), t_pev.get(nproj - 2), t_attdve],
                                    sig=("projpe" if last else None))
                        t_projpe = t_projpe_new
                        t_pev[nproj] = ("pev", P.op("dve", lambda e, par=par, tg=tg, vw=vw: e.tensor_copy(
                            out=vv[:, tg * 4:(tg + 1) * 4, 0:vw],
                            in_=pp[par].rearrange("p (a b) -> p a b", a=4)[:, :, 0:vw]),
                            waits=[("projpe", t_projpe)], sig="pev"))
                        last_dve_pev = t_pev[nproj]
                        nproj += 1
                projw = [last_dve_pev, last_act_pev]
                ckpt(3 + 2 * step)

                if is_sb:
                    tiles = []
                    for hh in range(2):
                        for (qc, kb, c0, diag, cs, ce) in sbt:
                            tiles.append((hh * 64, qc, kb, c0, diag, cs, ce))
                    T = len(tiles)
                    base = sbi
                    chunk_of = {}
                    cc = n_chunk - 1
                    for i, tl in enumerate(tiles):
                        if tl[5]:
                            cc += 1
                        chunk_of[i] = cc

                    def QK(i):
                        hp, qc, kb, c0, diag, cs, ce = tiles[i]
                        g = base + i
                        segs = col_segs(c0)
                        for si, (a, b) in enumerate(segs):
                            lastseg = (si == len(segs) - 1) and not diag
                            v = P.op("pe", lambda e, hp=hp, kb=kb, qc=qc, a=a, b=b, diag=diag: e.matmul(
                                Zp[:, a:b], lhsT=kT[hp:hp + 64, kb * 128:(kb + 1) * 128],
                                rhs=qT[hp:hp + 64, qc * QC + a:qc * QC + b], start=True, stop=not (diag and a == c0),
                                skip_group_check=True),
                                waits=[("sbE", tk.get(("E", g - 1)))] + (projw if i < 2 else []),
                                sig=("sbQK" if lastseg else None))
                            if lastseg:
                                tk[("QK", g)] = v
                        if diag:
                            tk[("QK", g)] = P.op("pe", lambda e, c0=c0: e.matmul(
                                Zp[:, c0:c0 + 128], lhsT=ident, rhs=negm, start=False, stop=True, skip_group_check=True),
                                sig="sbQK")

                    def ACT_E(i):
                        hp, qc, kb, c0, diag, cs, ce = tiles[i]
                        g = base + i
                        tk[("E", g)] = P.op("act", lambda e, g=g, c0=c0: e.activation(out=Eb[g % 3][:, c0:QC], in_=Zp[:, c0:QC], func=AF.Exp),
                                            waits=[("sbQK", tk[("QK", g)]), ("sbA", tk.get(("A", g - 3)))], sig="sbE")

                    def ACT_G(i):
                        hp, qc, kb, c0, diag, cs, ce = tiles[i]
                        g = base + i
                        tk[("G", g)] = P.op("act", lambda e, g=g, c0=c0: e.activation(out=Gb[g % 3][:, c0:QC], in_=Eb[g % 3][:, c0:QC],
                                                                                   func=AF.Ln, bias=1.0),
                                            waits=[("sbE", tk[("E", g)]), ("sbTRIC", tk.get(("TRIC", g - 3)))], sig="sbG")

                    def ACT_P(i):
                        hp, qc, kb, c0, diag, cs, ce = tiles[i]
                        g = base + i
                        tk[("P", g)] = P.op("act", lambda e, g=g, c0=c0: e.activation(out=Pb[g % 2][:, c0:QC], in_=Rp[:, c0:QC], func=AF.Exp),
                                            waits=[("sbTRI", tk[("TRI", g)]), ("sbA", tk.get(("A", g - 2)))], sig="sbP")

                    def PE_TRI(i):
                        hp, qc, kb, c0, diag, cs, ce = tiles[i]
                        g = base + i
                        if cs:
                            par = chunk_of[i] % 2
                            for (a, b) in ((0, 512), (512, 1024)):
                                P.op("pe", lambda e, a=a, b=b: e.matmul(Rp[:, a:b], lhsT=zer, rhs=cst[:, 0:512], start=True, stop=False,
                                                                        skip_group_check=True),
                                     waits=[("sbP", tk.get(("P", g - 1)))])
                                P.op("pe", lambda e, a=a, b=b, hp=hp, par=par: e.matmul(
                                    Yp[par][hp:hp + 64, a:b], lhsT=zer[:, 0:64], rhs=cst[:, 0:512], start=True, stop=False,
                                    skip_group_check=True),
                                    waits=[("sbEV", t_evy.get(chunk_of[i] - 2))])
                        segs = col_segs(c0)
                        for si, (a, b) in enumerate(segs):
                            v = P.op("pe", lambda e, g=g, a=a, b=b: e.matmul(Rp[:, a:b], lhsT=triN, rhs=Gb[g % 3][:, a:b], start=False, stop=False,
                                                                            skip_group_check=True),
                                     waits=[("sbG", tk[("G", g)]), ("sbP", tk.get(("P", g - 1)))],
                                     sig=("sbTRI" if si == len(segs) - 1 else None))
                        tk[("TRI", g)] = v

                    def PE_TRIC(i):
                        hp, qc, kb, c0, diag, cs, ce = tiles[i]
                        g = base + i
                        segs = col_segs(c0)
                        for si, (a, b) in enumerate(segs):
                            v = P.op("pe", lambda e, g=g, a=a, b=b: e.matmul(Rp[:, a:b], lhsT=tricN, rhs=Gb[g % 3][:, a:b], start=False, stop=False,
                                                                            skip_group_check=True),
                                     waits=[("sbP", tk[("P", g)])],
                                     sig=("sbTRIC" if si == len(segs) - 1 else None))
                        tk[("TRIC", g)] = v

                    def PE_PV(i):
                        hp, qc, kb, c0, diag, cs, ce = tiles[i]
                        g = base + i
                        par = chunk_of[i] % 2
                        segs = col_segs(c0)
                        for si, (a, b) in enumerate(segs):
                            v = P.op("pe", lambda e, g=g, a=a, b=b, hp=hp, kb=kb, par=par: e.matmul(
                                Yp[par][hp:hp + 64, a:b], lhsT=vv[:, kb, hp:hp + 64], rhs=Ab[g % 2][:, a:b], start=False, stop=False,
                                skip_group_check=True),
                                waits=[("sbA", tk[("A", g)])],
                                sig=("sbPV" if si == len(segs) - 1 else None))
                        tk[("PV", g)] = v

                    def DVE_A(i):
                        hp, qc, kb, c0, diag, cs, ce = tiles[i]
                        g = base + i
                        tk[("A", g)] = P.op("dve", lambda e, g=g, c0=c0: e.tensor_mul(out=Ab[g % 2][:, c0:QC], in0=Eb[g % 3][:, c0:QC],
                                                                                   in1=Pb[g % 2][:, c0:QC]),
                                            waits=[("sbP", tk[("P", g)]), ("sbPV", tk.get(("PV", g - 2)))], sig="sbA")

                    def DVE_EV(i):
                        hp, qc, kb, c0, diag, cs, ce = tiles[i]
                        g = base + i
                        ch = chunk_of[i]
                        par = ch % 2
                        t_evy[ch] = P.op("dve", lambda e, hp=hp, qc=qc, par=par, step=step: e.tensor_mul(
                            out=ygT[hp:hp + 64, step, qc * QC:(qc + 1) * QC], in0=Yp[par][hp:hp + 64, :],
                            in1=sg[hp:hp + 64, qc * QC:(qc + 1) * QC]),
                            waits=[("sbPV", tk[("PV", g)])], sig="sbEV")

                    QK(0)
                    ACT_E(0)
                    if T > 1:
                        QK(1)
                    ACT_G(0)
                    if T > 1:
                        ACT_E(1)
                    if T > 2:
                        QK(2)
                    PE_TRI(0)
                    for i in range(T):
                        ACT_P(i)
                        PE_TRIC(i)
                        if i + 1 < T:
                            ACT_G(i + 1)
                            PE_TRI(i + 1)
                        DVE_A(i)
                        PE_PV(i)
                        if tiles[i][6]:
                            DVE_EV(i)
                        if i + 2 < T:
                            ACT_E(i + 2)
                        if i + 3 < T:
                            QK(i + 3)
                    sbi += T
                    n_chunk = cc + 1
                    t_attdve = ("sbEV", t_evy[cc])
                else:
                    j = step - 4
                    heads = (2 * j, 2 * j + 1)
                    for n in range(NT):
                        gi = swn
                        grp = swg + n // 4
                        zbase = (gi % 2) * 1024
                        wz = 256 if n > 0 else 128
                        for which, kblk in ((0, n), (1, n - 1)):
                            if kblk < 0:
                                continue
                            for hh in range(2):
                                hp = hh * 64
                                zc = zbase + hh * 512 + which * 128
                                P.op("pe", lambda e, zc=zc, hp=hp, kblk=kblk, n=n, which=which: e.matmul(
                                    ps[:, zc:zc + 128], lhsT=kT[hp:hp + 64, kblk * 128:(kblk + 1) * 128],
                                    rhs=qT[hp:hp + 64, n * 128:(n + 1) * 128], start=(which == 0), stop=False, skip_group_check=True),
                                    waits=[("swP", t_sw.get(("P", gi - 2)))] + (projw if n < 2 else []))
                        for hh in range(2):
                            h = heads[hh]
                            bc = C_SWB + h * 256
                            zc = zbase + hh * 512
                            v = P.op("pe", lambda e, zc=zc, bc=bc, wz=wz: e.matmul(
                                ps[:, zc:zc + wz], lhsT=ident, rhs=cst[:, bc:bc + wz], start=False, stop=True, skip_group_check=True),
                                sig=("swQK" if hh == 1 else None))
                        t_sw[("QK", gi)] = v
                        zin = ps[:, zbase:zbase + 1024].rearrange("p (b c) -> p b c", b=2)[:, :, 0:wz]
                        pout = Psw[gi % 2].rearrange("p (b c) -> p b c", b=2)[:, :, 0:wz]
                        t_sw[("P", gi)] = P.op("act", lambda e, zin=zin, pout=pout: e.activation(out=pout, in_=zin, func=AF.Exp),
                                               waits=[("swQK", t_sw[("QK", gi)]), ("swPV", t_sw.get(("PV", gi - 2)))], sig="swP")
                        if SW_DBG == 2:
                            ckpt(1000)
                        par = grp % 2
                        Yb = Yp[par][:, 0:512]
                        Db = Yp[par][:, 512:1024]
                        col = (n % 4) * 128
                        for hh in range(2):
                            hp = hh * 64
                            srcs = [(hh * 256, n)] + ([(hh * 256 + 128, n - 1)] if n > 0 else [])
                            for si, (pc, kblk) in enumerate(srcs):
                                P.op("pe", lambda e, Yb=Yb, hp=hp, col=col, kblk=kblk, gi=gi, pc=pc, si=si: e.matmul(
                                    Yb[hp:hp + 64, col:col + 128], lhsT=vv[:, kblk, 0:64], rhs=Psw[gi % 2][:, pc:pc + 128],
                                    start=(si == 0), stop=(si == len(srcs) - 1), skip_group_check=True),
                                    waits=[("swP", t_sw[("P", gi)]), ("swEV", t_sw.get(("EV", grp - 2)))])
                            for si, (pc, kblk) in enumerate(srcs):
                                v = P.op("pe", lambda e, Db=Db, hp=hp, col=col, gi=gi, pc=pc, si=si: e.matmul(
                                    Db[hp:hp + 64, col:col + 128], lhsT=ones_bf[:, 0:64], rhs=Psw[gi % 2][:, pc:pc + 128],
                                    start=(si == 0), stop=(si == len(srcs) - 1), skip_group_check=True),
                                    sig=("swPV" if (hh == 1 and si == len(srcs) - 1) else None))
                        t_sw[("PV", gi)] = v
                        if SW_DBG == 3:
                            ckpt(1000)
                        if n % 4 == 3:
                            q0 = (n - 3) * 128
                            t1 = P.op("dve", lambda e, Db=Db, j=j: e.tensor_scalar(out=rden, in0=Db, scalar1=esk[:, j:j + 1], scalar2=None,
                                                                                    op0=ALU.add),
                                      waits=[("swPV", t_sw[("PV", gi)]), ("act0", t_esk)], sig="swD")
                            t1 = P.op("dve", lambda e: e.reciprocal(out=rden, in_=rden), waits=[("swD", t1)], sig="swD")
                            t1 = P.op("dve", lambda e, Yb=Yb: e.tensor_mul(out=ytmp, in0=Yb, in1=rden), waits=[("swD", t1)], sig="swD")
                            t_sw[("EV", grp)] = P.op("dve", lambda e, q0=q0, step=step: e.tensor_mul(
                                out=ygT[:, step, q0:q0 + 512], in0=ytmp, in1=sg[:, q0:q0 + 512]),
                                waits=[("swD", t1)], sig="swEV")
                        swn += 1
                    swg += NT // 4
                    t_last_sw = t_sw[("EV", swg - 1)]
                if not is_sb:
                    t_attdve = ("swEV", t_last_sw)
                ckpt(4 + 2 * step)

            ckpt(20)
            wo_v = w_out.rearrange("(k p) e -> p k e", p=128)
            t_wo = None
            for k in range(8):
                t_wo = P.op("pool", lambda e, k=k: e.dma_start(out=wout[:, k, :], in_=wo_v[:, k, :]),
                            waits=[("projpe", t_projpe)], sig="ld_wo", inc=16)
            t_fg = P.op("sp", lambda e: e.dma_start(out=fg_bc, in_=fg_bc_d[:, :]), waits=[("projpe", t_projpe)], sig="ld_fg", inc=16)
            t_xf = {}
            t_o = {}
            t_st = {}
            t_r2 = {}
            t_sq2 = {}

            def issue_xf(tt):
                t_xf[tt] = P.op("sp", lambda e, tt=tt: e.dma_start(out=xf[tt % 3], in_=x[tt * 128:(tt + 1) * 128, :]),
                                waits=[("fr2", t_r2.get(tt - 3)), ("projpe", t_projpe)], sig="ld_xf%d" % (tt % 3), inc=16)

            t_po = {}
            for tt in range(3):
                issue_xf(tt)
            for tt in range(NT):
                for eh in range(2):
                    for kc in range(8):
                        v = P.op("pe", lambda e, tt=tt, eh=eh, kc=kc: e.matmul(
                            po[tt % 2][eh], lhsT=ygT[:, kc, tt * 128:(tt + 1) * 128], rhs=wout[:, kc, eh * 512:(eh + 1) * 512],
                            start=(kc == 0), stop=(kc == 7)),
                            waits=[("ld_wo", t_wo), t_attdve, ("fr2", t_r2.get(tt - 2))],
                            sig=("fpo" if kc == 7 else None))
                    t_po[(tt, eh)] = v
                rb = rf[tt % 2]
                for eh in range(2):
                    t1 = P.op("dve", lambda e, tt=tt, eh=eh, rb=rb: e.tensor_mul(out=rb[:, eh * 512:(eh + 1) * 512], in0=po[tt % 2][eh],
                                                                               in1=gate_bc[:, eh * 512:(eh + 1) * 512]),
                              waits=[("fpo", t_po[(tt, eh)]), ("fo", t_o.get(tt - 2)), ("fsq", t_sq2.get(tt - 2))], sig="fd")
                t_r2[tt] = P.op("dve", lambda e, tt=tt, rb=rb: e.tensor_add(out=rb, in0=rb, in1=xf[tt % 3]),
                                waits=[("fd", t1), ("ld_xf%d" % (tt % 3), t_xf[tt])], sig="fr2")
                if tt + 3 < NT:
                    issue_xf(tt + 3)
                t_sq2[tt] = P.op("act", lambda e, tt=tt, rb=rb: e.activation(out=junkf, in_=rb, func=AF.Square, accum_out=ss2[:, tt:tt + 1]),
                                 waits=[("fr2", t_r2[tt])], sig="fsq")
                t1 = P.op("act", lambda e, tt=tt: e.activation(out=rstd2[:, tt:tt + 1], in_=ss2[:, tt:tt + 1], func=AF.Sqrt,
                                                               scale=1.0 / D, bias=eps_t[:, 0:1]),
                          waits=[("fsq", t_sq2[tt])], sig="fsqrt")
                t1 = P.op("dve", lambda e, tt=tt: e.reciprocal(out=rstd2[:, tt:tt + 1], in_=rstd2[:, tt:tt + 1]),
                          waits=[("fsqrt", t1)], sig="fd")
                t_o[tt] = P.op("dve", lambda e, tt=tt, rb=rb: e.scalar_tensor_tensor(
                    out=of[tt % 2], in0=rb, scalar=rstd2[:, tt:tt + 1], in1=fg_bc, op0=ALU.mult, op1=ALU.mult),
                    waits=[("fd", t1), ("ld_fg", t_fg), ("ld_out%d" % (tt % 2), t_st.get(tt - 2))], sig="fo")
                t_st[tt] = P.op("sp", lambda e, tt=tt: e.dma_start(out=out[tt * 128:(tt + 1) * 128, :], in_=of[tt % 2]),
                                waits=[("fo", t_o[tt])], sig="ld_out%d" % (tt % 2), inc=16)
            P.op("sp", lambda e: e.nop(), waits=[("ld_out0", t_st[NT - 2]), ("ld_out1", t_st[NT - 1])])

        try:
            plan_all()
        except _Stop:
            pass

        names = sorted(P.cnt.keys())
        sems = {n: es.enter_context(nc.semaphore(n)) for n in names}
        block = es.enter_context(nc.Block())

        def emit(eng, oplist):
            seen = {}
            for fn, waits, sig, inc in oplist:
                for (name, val) in waits:
                    if seen.get(name, 0) < val:
                        eng.wait_ge(sems[name], val)
                        seen[name] = val
                ins = fn(eng)
                if sig is not None:
                    ins.then_inc(sems[sig], inc)

        @block.sync
        def _(eng):
            emit(eng, P.ops["sp"])

        @block.gpsimd
        def _(eng):
            emit(eng, P.ops["pool"])

        @block.tensor
        def _(eng):
            emit(eng, P.ops["pe"])

        @block.scalar
        def _(eng):
            emit(eng, P.ops["act"])

        @block.vector
        def _(eng):
            emit(eng, P.ops["dve"])
    return nc


_CACHE = {}


def kernel(x, c, w_ada, b_ada, norm_g, w_in, sinks, w_out, final_g):
    x = np.asarray(x, np.float32)
    c = np.asarray(c, np.float32)
    w_ada = np.ascontiguousarray(np.asarray(w_ada, np.float32)[0])
    b_ada = np.asarray(b_ada, np.float32)[0]
    norm_g = np.asarray(norm_g, np.float32)[0]
    w_in = np.ascontiguousarray(np.asarray(w_in, np.float32)[0])
    sinks = np.asarray(sinks, np.float32)[0]
    w_out = np.ascontiguousarray(np.asarray(w_out, np.float32)[0])
    final_g = np.asarray(final_g, np.float32)

    def lay(v):
        return np.ascontiguousarray(v.reshape(-1, 128).T)

    bada_l = lay(b_ada[:2048])
    bg_bc = np.ascontiguousarray(np.broadcast_to(b_ada[2048:3072][None, :], (128, D)))
    normg_l = lay(norm_g)
    fg_bc = np.ascontiguousarray(np.broadcast_to(final_g[None, :], (128, D)))
    sinks_l = np.ascontiguousarray(np.stack([np.repeat(sinks[2 * j:2 * j + 2], 64) for j in range(4)], axis=1))
    consts = make_consts()
    if "nc" not in _CACHE:
        _CACHE["nc"] = build_nc()
    nc = _CACHE["nc"]
    in_maps = []
    for b in range(NCORE):
        in_maps.append({
            "x": np.ascontiguousarray(x[b]), "c_l": lay(c[b]), "w_ada": w_ada, "bada_l": bada_l, "bg_bc": bg_bc,
            "normg_l": normg_l, "w_in": w_in, "sinks_l": sinks_l, "w_out": w_out, "fg_bc": fg_bc, "consts": consts,
        })
    res = run_bass_kernel_spmd(nc, in_maps, core_ids=list(range(NCORE)))
    return np.stack([np.asarray(r["out"], np.float32) for r in res.results], axis=0)
```

```python
from contextlib import ExitStack

import numpy as np
import concourse.bass as bass
import concourse.mybir as mybir
from concourse.bass_utils import run_bass_kernel_spmd

F32 = mybir.dt.float32
BF16 = mybir.dt.bfloat16
AF = mybir.ActivationFunctionType
ALU = mybir.AluOpType

S = 4096
D = 1024
NCORE = 8
NT = S // 128
QC = 1024
NEG = -30000.0
NCONST = 128 * 6 + 2048
C_ID, C_TRI, C_TRIC, C_ZERO, C_NEGM, C_ONES, C_SWB = 0, 128, 256, 384, 512, 640, 768


LEVEL = 99
SW_DBG = 0


class _Stop(Exception):
    pass


def ckpt(level):
    if LEVEL <= level:
        raise _Stop()


class Plan:
    def __init__(self):
        self.ops = {"pe": [], "act": [], "dve": [], "pool": [], "sp": []}
        self.cnt = {}

    def op(self, eng, fn, waits=(), sig=None, inc=1):
        v = None
        if sig is not None:
            self.cnt[sig] = self.cnt.get(sig, 0) + inc
            v = self.cnt[sig]
        ws = tuple(w for w in waits if w is not None and w[1] is not None and w[1] > 0)
        self.ops[eng].append((fn, ws, sig, inc))
        return v


def make_consts():
    c = np.zeros((128, NCONST), np.float32)
    j = np.arange(128)[:, None]
    s = np.arange(128)[None, :]
    c[:, C_ID:C_ID + 128] = (j == s)
    c[:, C_TRI:C_TRI + 128] = -1.0 * (j >= s)
    c[:, C_TRIC:C_TRIC + 128] = -1.0 * (j < s)
    c[:, C_NEGM:C_NEGM + 128] = np.where(j < s, 0.0, NEG)
    c[:, C_ONES:C_ONES + 128] = 1.0
    for h in range(8):
        m = 2.0 ** (-8.0 * (h + 1) / 8)
        rel_cur = (s - j).astype(np.float32)
        cur = np.where(s >= j, -m * rel_cur, NEG)
        rel_prev = (128 + s - j).astype(np.float32)
        prev = np.where(j > s, -m * rel_prev, NEG)
        c[:, C_SWB + h * 256:C_SWB + h * 256 + 128] = cur
        c[:, C_SWB + h * 256 + 128:C_SWB + h * 256 + 256] = prev
    return c


def sb_tiles():
    out = []
    for qc in range(S // QC):
        nkb = (QC // 128) * (qc + 1)
        for kb in range(nkb - 1, -1, -1):
            jd = kb - (QC // 128) * qc
            diag = jd >= 0
            c0 = 128 * jd if diag else 0
            out.append((qc, kb, c0, diag, kb == nkb - 1, kb == 0))
    return out


def col_segs(c0, c1=QC):
    segs = []
    a = c0
    while a < c1:
        b = min(c1, (a // 512 + 1) * 512)
        segs.append((a, b))
        a = b
    return segs


def build_nc():
    nc = bass.Bass("TRN2", target_bir_lowering=False)
    x = nc.dram_tensor("x", [S, D], F32, kind="ExternalInput").ap()
    c_l = nc.dram_tensor("c_l", [128, 8], F32, kind="ExternalInput").ap()
    w_ada = nc.dram_tensor("w_ada", [D, 3 * D], F32, kind="ExternalInput").ap()
    bada_l = nc.dram_tensor("bada_l", [128, 16], F32, kind="ExternalInput").ap()
    bg_bc_d = nc.dram_tensor("bg_bc", [128, D], F32, kind="ExternalInput").ap()
    normg_l = nc.dram_tensor("normg_l", [128, 8], F32, kind="ExternalInput").ap()
    w_in = nc.dram_tensor("w_in", [D, 3328], F32, kind="ExternalInput").ap()
    sinks_l = nc.dram_tensor("sinks_l", [128, 4], F32, kind="ExternalInput").ap()
    w_out = nc.dram_tensor("w_out", [D, D], F32, kind="ExternalInput").ap()
    fg_bc_d = nc.dram_tensor("fg_bc", [128, D], F32, kind="ExternalInput").ap()
    consts_d = nc.dram_tensor("consts", [128, NCONST], F32, kind="ExternalInput").ap()
    out = nc.dram_tensor("out", [S, D], F32, kind="ExternalOutput").ap()

    P = Plan()
    es = ExitStack()
    with es:
        def sb(name, shape, dt):
            return es.enter_context(nc.sbuf_tensor(name, shape, dt))

        ygT = sb("ygT", [128, 8, S], BF16)
        hT = sb("hT", [128, 8, S], BF16)
        ov = sb("ov", [128, 30720], BF16)
        cst = sb("cst", [128, NCONST], BF16)
        gate_bc = sb("gate_bc", [128, D], F32)
        c_sb = sb("c_sb", [128, 8], F32)
        etmp = sb("etmp", [128, 8], F32)
        cond = sb("cond", [128, 8], F32)
        bada = sb("bada", [128, 16], F32)
        normg = sb("normg", [128, 8], F32)
        mod_sb = sb("mod_sb", [128, 16], F32)
        gs = sb("gs", [128, 8], F32)
        ones_f = sb("ones_f", [128, 128], F32)
        ss = sb("ss", [128, 32], F32)
        rstd = sb("rstd", [128, 32], F32)
        ss2 = sb("ss2", [128, 32], F32)
        rstd2 = sb("rstd2", [128, 32], F32)
        snk = sb("snk", [128, 4], F32)
        esk = sb("esk", [128, 4], F32)
        eps_t = sb("eps_t", [128, 1], F32)
        ps = es.enter_context(nc.psum_tensor("ps", [128, 4096], F32))

        qT = ov[:, 0:4096]
        kT = ov[:, 4096:8192]
        sg = ov[:, 8192:12288]
        vv = ov[:, 12288:16384].rearrange("p (b c) -> p b c", c=128)
        wsl = ov[:, 16384:20480].rearrange("p (k c) -> p k c", c=512)
        PB = 20480
        Eb = [ov[:, PB + i * 1024:PB + (i + 1) * 1024] for i in range(3)]
        Gb = [ov[:, PB + (3 + i) * 1024:PB + (4 + i) * 1024] for i in range(3)]
        Pb = [ov[:, PB + (6 + i) * 1024:PB + (7 + i) * 1024] for i in range(2)]
        Ab = [ov[:, PB + (8 + i) * 1024:PB + (9 + i) * 1024] for i in range(2)]
        Psw = [ov[:, PB + i * 512:PB + (i + 1) * 512] for i in range(2)]
        rden = ov[:, PB + 1024:PB + 2048].bitcast(F32)
        ytmp = ov[:, PB + 2048:PB + 3072].bitcast(F32)
        yflat = ygT[:, :, :].rearrange("p a b -> p (a b)")
        wada = [yflat[:, i * 6144:(i + 1) * 6144].bitcast(F32) for i in range(4)]
        xt = [ov[:, 12288 + i * 2048:12288 + (i + 1) * 2048].bitcast(F32) for i in range(4)]
        xnb = [[ov[:, 20480 + (g * 4 + t) * 1024:20480 + (g * 4 + t + 1) * 1024] for t in range(4)] for g in range(2)]
        junk = ov[:, 28672:29696]
        hflat = hT[:, :, :].rearrange("p a b -> p (a b)")
        condb = yflat[:, 24576:26624].bitcast(F32).rearrange("p (k m) -> p k m", m=128)
        bg_bc = yflat[:, 26624:28672].bitcast(F32)
        wout = hflat[:, 0:8192].rearrange("p (k e) -> p k e", e=1024)
        xf = [hflat[:, 8192 + i * 2048:8192 + (i + 1) * 2048].bitcast(F32) for i in range(3)]
        rf = [hflat[:, 14336 + i * 2048:14336 + (i + 1) * 2048].bitcast(F32) for i in range(2)]
        of = [hflat[:, 18432 + i * 2048:18432 + (i + 1) * 2048].bitcast(F32) for i in range(2)]
        fg_bc = hflat[:, 22528:24576].bitcast(F32)
        junkf = hflat[:, 24576:25600]
        Zp = ps[:, 0:1024]
        Rp = ps[:, 1024:2048]
        Yp = [ps[:, 2048:3072], ps[:, 3072:4096]]
        pp = [ps[:, 0:512], ps[:, 512:1024]]
        tp = [ps[:, i * 512:(i + 1) * 512].bitcast(BF16)[:, 0:512] for i in range(2)]
        modps = ps[:, 1024:1040]
        gateps = ps[:, 2048:3072]
        po = [[ps[:, t * 1024 + e * 512:t * 1024 + (e + 1) * 512] for e in range(2)] for t in range(2)]

        ident = cst[:, C_ID:C_ID + 128]
        triN = cst[:, C_TRI:C_TRI + 128]
        tricN = cst[:, C_TRIC:C_TRIC + 128]
        zer = cst[:, C_ZERO:C_ZERO + 128]
        negm = cst[:, C_NEGM:C_NEGM + 128]
        ones_bf = cst[:, C_ONES:C_ONES + 128]

        def plan_all():
            t_cst = P.op("pool", lambda e: e.dma_start(out=cst[:, :], in_=consts_d[:, :]), sig="ld_cst", inc=16)
            for dst, src in ((c_sb, c_l), (bada, bada_l), (normg, normg_l), (snk, sinks_l)):
                t_small = P.op("sp", lambda e, dst=dst, src=src: e.dma_start(out=dst[:, :], in_=src[:, :]), sig="ld_small", inc=16)
            t_small = P.op("sp", lambda e: e.dma_start(out=bg_bc, in_=bg_bc_d[:, :]), sig="ld_small", inc=16)

            ckpt(0)
            P.op("dve", lambda e: e.memset(ones_f[:, :], 1.0), sig="dve0")
            P.op("dve", lambda e: e.memset(eps_t[:, :], 1e-6), sig="dve0")
            P.op("dve", lambda e: e.memset(ss[:, :], 0.0), sig="dve0")
            t_d = P.op("dve", lambda e: e.memset(ss2[:, :], 0.0), sig="dve0")
            t_ms = t_d
            t_a = P.op("act", lambda e: e.activation(out=etmp[:, :], in_=c_sb[:, :], func=AF.Exp, scale=-1.0),
                       waits=[("ld_small", t_small)], sig="act0")
            t_d = P.op("dve", lambda e: e.tensor_scalar_add(out=etmp[:, :], in0=etmp[:, :], scalar1=1.0),
                       waits=[("act0", t_a), ("dve0", t_d)], sig="dve0")
            t_d = P.op("dve", lambda e: e.reciprocal(out=etmp[:, :], in_=etmp[:, :]), waits=[("dve0", t_d)], sig="dve0")
            t_d = P.op("dve", lambda e: e.tensor_mul(out=cond[:, :], in0=c_sb[:, :], in1=etmp[:, :]),
                       waits=[("dve0", t_d)], sig="dve0")
            t_cond = t_d
            for k in range(8):
                t_d = P.op("dve", lambda e, k=k: e.tensor_scalar(out=condb[:, k, :], in0=ones_f[:, :], scalar1=cond[:, k:k + 1],
                                                                 scalar2=None, op0=ALU.mult),
                           waits=[("dve0", t_cond)], sig="dve0")
            t_condb = t_d
            t_pe0 = {}
            t_wada = {}
            t_xld = {}
            t_sq = {}
            t_xn = {}
            t_tp = {}
            t_ev = {}
            NWB = 4

            def issue_wada(k):
                t_wada[k] = P.op("sp", lambda e, k=k: e.dma_start(out=wada[k % NWB], in_=w_ada[k * 128:(k + 1) * 128, :]),
                                 waits=[("pe0", t_pe0.get(k - NWB))], sig="ld_wada%d" % (k % NWB), inc=16)

            def issue_xload(tt):
                t_xld[tt] = P.op("sp", lambda e, tt=tt: e.dma_start(out=xt[tt % 4], in_=x[tt * 128:(tt + 1) * 128, :]),
                                 waits=[("p1xn", t_xn.get(tt - 4)), ("p1sq", t_sq.get(tt - 4))],
                                 sig="ld_x%d" % (tt % 4), inc=16)

            for k in range(NWB):
                issue_wada(k)
            for tt in range(4):
                issue_xload(tt)
            for k in range(8):
                for j in range(16):
                    P.op("pe", lambda e, k=k, j=j: e.matmul(modps[:, j:j + 1], lhsT=wada[k % NWB][:, j * 128:(j + 1) * 128],
                                                            rhs=cond[:, k:k + 1], start=(k == 0 and j == 0), stop=(k == 7),
                                                            skip_group_check=True),
                         waits=[("ld_wada%d" % (k % NWB), t_wada[k]), ("dve0", t_condb)])
                for eh in range(2):
                    t_pe0[k] = P.op("pe", lambda e, k=k, eh=eh: e.matmul(gateps[:, eh * 512:(eh + 1) * 512], lhsT=condb[:, k, :],
                                                                         rhs=wada[k % NWB][:, 2048 + eh * 512:2048 + (eh + 1) * 512],
                                                                         start=(k == 0), stop=(k == 7)),
                                    waits=[("ld_wada%d" % (k % NWB), t_wada[k]), ("dve0", t_condb)], sig=("pe0" if eh == 1 else None))
                if k + NWB < 8:
                    issue_wada(k + NWB)
                g = k
                for t4 in range(4):
                    tt = g * 4 + t4
                    t_sq[tt] = P.op("act", lambda e, tt=tt: e.activation(out=junk, in_=xt[tt % 4], func=AF.Square,
                                                                         accum_out=ss[:, tt:tt + 1]),
                                    waits=[("ld_x%d" % (tt % 4), t_xld[tt]), ("dve0", t_ms)], sig="p1sq")
                    t_r = P.op("act", lambda e, tt=tt: e.activation(out=rstd[:, tt:tt + 1], in_=ss[:, tt:tt + 1], func=AF.Sqrt,
                                                                    scale=1.0 / D, bias=eps_t[:, 0:1]),
                               waits=[("p1sq", t_sq[tt])], sig="p1sqrt")
                    t_r = P.op("dve", lambda e, tt=tt: e.reciprocal(out=rstd[:, tt:tt + 1], in_=rstd[:, tt:tt + 1]),
                               waits=[("p1sqrt", t_r)], sig="dve1")
                    t_xn[tt] = P.op("dve", lambda e, tt=tt, g=g, t4=t4: e.tensor_scalar(
                        out=xnb[g % 2][t4], in0=xt[tt % 4], scalar1=rstd[:, tt:tt + 1], scalar2=None, op0=ALU.mult),
                        waits=[("dve1", t_r), ("p1tp", t_tp.get((g - 2, 7)))], sig="p1xn")
                    if tt + 4 < NT:
                        issue_xload(tt + 4)
                for j in range(8):
                    prev_ev = t_ev[(g, j - 2)] if j >= 2 else (t_ev[(g - 1, 6 + j)] if g >= 1 else None)
                    for t4 in range(4):
                        t_tp[(g, j)] = P.op("pe", lambda e, g=g, j=j, t4=t4: e.transpose(
                            out=tp[j % 2][:, t4 * 128:(t4 + 1) * 128], in_=xnb[g % 2][t4][:, j * 128:(j + 1) * 128], identity=ident),
                            waits=[("p1xn", t_xn[g * 4 + 3]), ("p1ev", prev_ev), ("ld_cst", t_cst)],
                            sig=("p1tp" if t4 == 3 else None))
                    t_ev[(g, j)] = P.op("act", lambda e, g=g, j=j: e.activation(
                        out=hT[:, j, g * 512:(g + 1) * 512], in_=tp[j % 2], func=AF.Identity),
                        waits=[("p1tp", t_tp[(g, j)])], sig="p1ev")
            t_d = P.op("dve", lambda e: e.tensor_add(out=mod_sb[:, :], in0=modps, in1=bada[:, :]),
                       waits=[("pe0", t_pe0[7]), ("ld_small", t_small)], sig="dve0")
            t_d = P.op("dve", lambda e: e.scalar_tensor_tensor(out=gs[:, :], in0=mod_sb[:, 8:16], scalar=1.0, in1=normg[:, :],
                                                               op0=ALU.add, op1=ALU.mult),
                       waits=[("dve0", t_d)], sig="dve0")
            t_d = P.op("dve", lambda e: e.tensor_add(out=gate_bc[:, :], in0=gateps, in1=bg_bc), sig="dve0")
            t_mod = t_d
            for j in range(8):
                for hh in range(2):
                    t_fix = P.op("dve", lambda e, j=j, hh=hh: e.tensor_scalar(
                        out=hT[:, j, hh * 2048:(hh + 1) * 2048], in0=hT[:, j, hh * 2048:(hh + 1) * 2048],
                        scalar1=gs[:, j:j + 1], scalar2=mod_sb[:, j:j + 1], op0=ALU.mult, op1=ALU.add),
                        waits=[("dve0", t_mod), ("p1ev", t_ev[(7, 7)])], sig="hfix")
            t_hT = t_fix
            ckpt(2)
            t_wsl = None
            t_projpe = None
            t_attpe = None
            t_attdve = None
            t_pev = {}
            nproj = 0
            t_esk = P.op("act", lambda e: e.activation(out=esk[:, :], in_=snk[:, :], func=AF.Exp),
                         waits=[("ld_small", t_small)], sig="act0")
            sbt = sb_tiles()
            sbi = 0
            tk = {}
            n_chunk = 0
            t_evy = {}
            swn = 0
            swg = 0
            t_sw = {}

            for step in range(8):
                is_sb = step < 4
                if is_sb:
                    cq, ck, cv, cg = step * 128, 512 + step * 128, 1024 + step * 128, 1536 + step * 128
                    kw = 128
                    vw = 128
                    do_kv = True
                else:
                    j = step - 4
                    cq, ck, cv, cg = 2048 + j * 128, 2560 + (j // 2) * 64, 2688 + (j // 2) * 64, 2816 + j * 128
                    kw = 64
                    vw = 64
                    do_kv = (j % 2 == 0)
                wv = w_in.rearrange("(k p) c -> p k c", p=128)
                dmas = [(0, cq, 128), (256, cg, 128)]
                if do_kv:
                    if kw == 128:
                        dmas.append((128, ck, 128))
                    else:
                        dmas.append((128, ck, 64))
                        dmas.append((192, ck, 64))
                    if vw == 128:
                        dmas.append((384, cv, 128))
                    else:
                        dmas.append((384, cv, 64))
                        dmas.append((448, cv, 64))
                        vw = 128
                for (dc, sc, w) in dmas:
                    t_wsl = P.op("pool", lambda e, dc=dc, sc=sc, w=w: e.dma_start(out=wsl[:, :, dc:dc + w], in_=wv[:, :, sc:sc + w]),
                                 waits=[("projpe", t_projpe), ("hfix", t_hT)], sig="ld_wsl", inc=16)
                ckpt(2.5 + 2 * step)
                kinds = [("q", 0), ("g", 256)] + ([("k", 128)] if do_kv else [])
                for kind, wc in kinds:
                    for tc in range(8):
                        par = nproj % 2
                        for kc in range(8):
                            last = kc == 7
                            t_projpe_new = P.op("pe", lambda e, par=par, kc=kc, wc=wc, tc=tc: e.matmul(
                                pp[par], lhsT=wsl[:, kc, wc:wc + 128], rhs=hT[:, kc, tc * 512:(tc + 1) * 512],
                                start=(kc == 0), stop=(kc == 7)),
                                waits=[("ld_wsl", t_wsl), t_pev.get(nproj - 2), ("hfix", t_hT),
                                       t_attdve],
                                sig=("projpe" if last else None))
                        t_projpe = t_projpe_new
                        dst = {"q": qT, "k": kT, "g": sg}[kind][:, tc * 512:(tc + 1) * 512]
                        if kind == "q":
                            t_pev[nproj] = ("pev", P.op("dve", lambda e, dst=dst, par=par: e.tensor_scalar(
                                out=dst, in0=pp[par], scalar1=0.125, scalar2=None, op0=ALU.mult),
                                waits=[("projpe", t_projpe)], sig="pev"))
                            last_dve_pev = t_pev[nproj]
                        elif kind == "k":
                            t_pev[nproj] = ("pev", P.op("dve", lambda e, dst=dst, par=par: e.tensor_copy(out=dst, in_=pp[par]),
                                                waits=[("projpe", t_projpe)], sig="pev"))
                            last_dve_pev = t_pev[nproj]
                        else:
                            t_pev[nproj] = ("pevA", P.op("act", lambda e, dst=dst, par=par: e.activation(out=dst, in_=pp[par], func=AF.Silu),
                                                         waits=[("projpe", t_projpe), t_attdve], sig="pevA"))
                            last_act_pev = t_pev[nproj]
                        nproj += 1
                ckpt(2.75 + 2 * step)
                if do_kv:
                    for tg in range(8):
                        par = nproj % 2
                        for t4 in range(4):
                            tok = (tg * 4 + t4) * 128
                            for kc in range(8):
                                last = (kc == 7 and t4 == 3)
                                t_projpe_new = P.op("pe", lambda e, par=par, kc=kc, tok=tok, t4=t4, vw=vw: e.matmul(
                                    pp[par][:, t4 * 128:t4 * 128 + vw], lhsT=hT[:, kc, tok:tok + 128], rhs=wsl[:, kc, 384:384 + vw],
                                    start=(kc == 0), stop=(kc == 7)),
                                    waits=[("ld_wsl", t_wsl), t_pev.get(nproj - 2), t_attdve],
                                    sig=("projpe" if last else None))
                        t_projpe = t_projpe_new
                        t_pev[nproj] = ("pev", P.op("dve", lambda e, par=par, tg=tg, vw=vw: e.tensor_copy(
                            out=vv[:, tg * 4:(tg + 1) * 4, 0:vw],
                            in_=pp[par].rearrange("p (a b) -> p a b", a=4)[:, :, 0:vw]),
                            waits=[("projpe", t_projpe)], sig="pev"))
                        last_dve_pev = t_pev[nproj]
                        nproj += 1
                projw = [last_dve_pev, last_act_pev]
                ckpt(3 + 2 * step)

                if is_sb:
                    tiles = []
                    for hh in range(2):
                        for (qc, kb, c0, diag, cs, ce) in sbt:
                            tiles.append((hh * 64, qc, kb, c0, diag, cs, ce))
                    T = len(tiles)
                    base = sbi
                    chunk_of = {}
                    cc = n_chunk - 1
                    for i, tl in enumerate(tiles):
                        if tl[5]:
                            cc += 1
                        chunk_of[i] = cc

                    def QK(i):
                        hp, qc, kb, c0, diag, cs, ce = tiles[i]
                        g = base + i
                        segs = col_segs(c0)
                        for si, (a, b) in enumerate(segs):
                            lastseg = (si == len(segs) - 1) and not diag
                            v = P.op("pe", lambda e, hp=hp, kb=kb, qc=qc, a=a, b=b, diag=diag: e.matmul(
                                Zp[:, a:b], lhsT=kT[hp:hp + 64, kb * 128:(kb + 1) * 128],
                                rhs=qT[hp:hp + 64, qc * QC + a:qc * QC + b], start=True, stop=not (diag and a == c0),
                                skip_group_check=True),
                                waits=[("sbE", tk.get(("E", g - 1)))] + (projw if i < 2 else []),
                                sig=("sbQK" if lastseg else None))
                            if lastseg:
                                tk[("QK", g)] = v
                        if diag:
                            tk[("QK", g)] = P.op("pe", lambda e, c0=c0: e.matmul(
                                Zp[:, c0:c0 + 128], lhsT=ident, rhs=negm, start=False, stop=True, skip_group_check=True),
                                sig="sbQK")

                    def ACT_E(i):
                        hp, qc, kb, c0, diag, cs, ce = tiles[i]
                        g = base + i
                        tk[("E", g)] = P.op("act", lambda e, g=g, c0=c0: e.activation(out=Eb[g % 3][:, c0:QC], in_=Zp[:, c0:QC], func=AF.Exp),
                                            waits=[("sbQK", tk[("QK", g)]), ("sbA", tk.get(("A", g - 3)))], sig="sbE")

                    def ACT_G(i):
                        hp, qc, kb, c0, diag, cs, ce = tiles[i]
                        g = base + i
                        tk[("G", g)] = P.op("act", lambda e, g=g, c0=c0: e.activation(out=Gb[g % 3][:, c0:QC], in_=Eb[g % 3][:, c0:QC],
                                                                                   func=AF.Ln, bias=1.0),
                                            waits=[("sbE", tk[("E", g)]), ("sbTRIC", tk.get(("TRIC", g - 3)))], sig="sbG")

                    def ACT_P(i):
                        hp, qc, kb, c0, diag, cs, ce = tiles[i]
                        g = base + i
                        tk[("P", g)] = P.op("act", lambda e, g=g, c0=c0: e.activation(out=Pb[g % 2][:, c0:QC], in_=Rp[:, c0:QC], func=AF.Exp),
                                            waits=[("sbTRI", tk[("TRI", g)]), ("sbA", tk.get(("A", g - 2)))], sig="sbP")

                    def PE_TRI(i):
                        hp, qc, kb, c0, diag, cs, ce = tiles[i]
                        g = base + i
                        if cs:
                            par = chunk_of[i] % 2
                            for (a, b) in ((0, 512), (512, 1024)):
                                P.op("pe", lambda e, a=a, b=b: e.matmul(Rp[:, a:b], lhsT=zer, rhs=cst[:, 0:512], start=True, stop=False,
                                                                        skip_group_check=True),
                                     waits=[("sbP", tk.get(("P", g - 1)))])
                                P.op("pe", lambda e, a=a, b=b, hp=hp, par=par: e.matmul(
                                    Yp[par][hp:hp + 64, a:b], lhsT=zer[:, 0:64], rhs=cst[:, 0:512], start=True, stop=False,
                                    skip_group_check=True),
                                    waits=[("sbEV", t_evy.get(chunk_of[i] - 2))])
                        segs = col_segs(c0)
                        for si, (a, b) in enumerate(segs):
                            v = P.op("pe", lambda e, g=g, a=a, b=b: e.matmul(Rp[:, a:b], lhsT=triN, rhs=Gb[g % 3][:, a:b], start=False, stop=False,
                                                                            skip_group_check=True),
                                     waits=[("sbG", tk[("G", g)]), ("sbP", tk.get(("P", g - 1)))],
                                     sig=("sbTRI" if si == len(segs) - 1 else None))
                        tk[("TRI", g)] = v

                    def PE_TRIC(i):
                        hp, qc, kb, c0, diag, cs, ce = tiles[i]
                        g = base + i
                        segs = col_segs(c0)
                        for si, (a, b) in enumerate(segs):
                            v = P.op("pe", lambda e, g=g, a=a, b=b: e.matmul(Rp[:, a:b], lhsT=tricN, rhs=Gb[g % 3][:, a:b], start=False, stop=False,
                                                                            skip_group_check=True),
                                     waits=[("sbP", tk[("P", g)])],
                                     sig=("sbTRIC" if si == len(segs) - 1 else None))
                        tk[("TRIC", g)] = v

                    def PE_PV(i):
                        hp, qc, kb, c0, diag, cs, ce = tiles[i]
                        g = base + i
                        par = chunk_of[i] % 2
                        segs = col_segs(c0)
                        for si, (a, b) in enumerate(segs):
                            v = P.op("pe", lambda e, g=g, a=a, b=b, hp=hp, kb=kb, par=par: e.matmul(
                                Yp[par][hp:hp + 64, a:b], lhsT=vv[:, kb, hp:hp + 64], rhs=Ab[g % 2][:, a:b], start=False, stop=False,
                                skip_group_check=True),
                                waits=[("sbA", tk[("A", g)])],
                                sig=("sbPV" if si == len(segs) - 1 else None))
                        tk[("PV", g)] = v

                    def DVE_A(i):
                        hp, qc, kb, c0, diag, cs, ce = tiles[i]
                        g = base + i
                        tk[("A", g)] = P.op("dve", lambda e, g=g, c0=c0: e.tensor_mul(out=Ab[g % 2][:, c0:QC], in0=Eb[g % 3][:, c0:QC],
                                                                                   in1=Pb[g % 2][:, c0:QC]),
                                            waits=[("sbP", tk[("P", g)]), ("sbPV", tk.get(("PV", g - 2)))], sig="sbA")

                    def DVE_EV(i):
                        hp, qc, kb, c0, diag, cs, ce = tiles[i]
                        g = base + i
                        ch = chunk_of[i]
                        par = ch % 2
                        t_evy[ch] = P.op("dve", lambda e, hp=hp, qc=qc, par=par, step=step: e.tensor_mul(
                            out=ygT[hp:hp + 64, step, qc * QC:(qc + 1) * QC], in0=Yp[par][hp:hp + 64, :],
                            in1=sg[hp:hp + 64, qc * QC:(qc + 1) * QC]),
                            waits=[("sbPV", tk[("PV", g)])], sig="sbEV")

                    QK(0)
                    ACT_E(0)
                    if T > 1:
                        QK(1)
                    ACT_G(0)
                    if T > 1:
                        ACT_E(1)
                    if T > 2:
                        QK(2)
                    PE_TRI(0)
                    for i in range(T):
                        ACT_P(i)
                        PE_TRIC(i)
                        if i + 1 < T:
                            ACT_G(i + 1)
                            PE_TRI(i + 1)
                        DVE_A(i)
                        PE_PV(i)
                        if tiles[i][6]:
                            DVE_EV(i)
                        if i + 2 < T:
                            ACT_E(i + 2)
                        if i + 3 < T:
                            QK(i + 3)
                    sbi += T
                    n_chunk = cc + 1
                    t_attdve = ("sbEV", t_evy[cc])
                else:
                    j = step - 4
                    heads = (2 * j, 2 * j + 1)
                    swn0 = swn

                    def SW_QK(n):
                        gi = swn0 + n
                        zbase = (gi % 2) * 1024
                        wz = 256 if n > 0 else 128
                        for which, kblk in ((0, n), (1, n - 1)):
                            if kblk < 0:
                                continue
                            for hh in range(2):
                                hp = hh * 64
                                zc = zbase + hh * 512 + which * 128
                                P.op("pe", lambda e, zc=zc, hp=hp, kblk=kblk, n=n, which=which: e.matmul(
                                    ps[:, zc:zc + 128], lhsT=kT[hp:hp + 64, kblk * 128:(kblk + 1) * 128],
                                    rhs=qT[hp:hp + 64, n * 128:(n + 1) * 128], start=(which == 0), stop=False, skip_group_check=True),
                                    waits=[("swP", t_sw.get(("P", gi - 2)))] + (projw if n < 2 else []))
                        for hh in range(2):
                            h = heads[hh]
                            bc = C_SWB + h * 256
                            zc = zbase + hh * 512
                            v = P.op("pe", lambda e, zc=zc, bc=bc, wz=wz: e.matmul(
                                ps[:, zc:zc + wz], lhsT=ident, rhs=cst[:, bc:bc + wz], start=False, stop=True, skip_group_check=True),
                                sig=("swQK" if hh == 1 else None))
                        t_sw[("QK", gi)] = v

                    def SW_ACT(n):
                        gi = swn0 + n
                        zbase = (gi % 2) * 1024
                        wz = 256 if n > 0 else 128
                        zin = ps[:, zbase:zbase + 1024].rearrange("p (b c) -> p b c", b=2)[:, :, 0:wz]
                        pout = Psw[gi % 2].rearrange("p (b c) -> p b c", b=2)[:, :, 0:wz]
                        t_sw[("P", gi)] = P.op("act", lambda e, zin=zin, pout=pout: e.activation(out=pout, in_=zin, func=AF.Exp),
                                               waits=[("swQK", t_sw[("QK", gi)]), ("swPV", t_sw.get(("PV", gi - 2)))], sig="swP")

                    def SW_PVD(n):
                        gi = swn0 + n
                        grp = swg + n // 4
                        par = grp % 2
                        Yb = Yp[par][:, 0:512]
                        Db = Yp[par][:, 512:1024]
                        col = (n % 4) * 128
                        for hh in range(2):
                            hp = hh * 64
                            srcs = [(hh * 256, n)] + ([(hh * 256 + 128, n - 1)] if n > 0 else [])
                            ns = len(srcs)
                            for si, (pc, kblk) in enumerate(srcs):
                                P.op("pe", lambda e, Yb=Yb, hp=hp, col=col, kblk=kblk, gi=gi, pc=pc, si=si, ns=ns: e.matmul(
                                    Yb[hp:hp + 64, col:col + 128], lhsT=vv[:, kblk, 0:64], rhs=Psw[gi % 2][:, pc:pc + 128],
                                    start=(si == 0), stop=(si == ns - 1), skip_group_check=True),
                                    waits=[("swP", t_sw[("P", gi)]), ("swEV", t_sw.get(("EV", grp - 2)))])
                            for si, (pc, kblk) in enumerate(srcs):
                                v = P.op("pe", lambda e, Db=Db, hp=hp, col=col, gi=gi, pc=pc, si=si, ns=ns: e.matmul(
                                    Db[hp:hp + 64, col:col + 128], lhsT=ones_bf[:, 0:64], rhs=Psw[gi % 2][:, pc:pc + 128],
                                    start=(si == 0), stop=(si == ns - 1), skip_group_check=True),
                                    sig=("swPV" if (hh == 1 and si == ns - 1) else None))
                        t_sw[("PV", gi)] = v
                        if n % 4 == 3:
                            q0 = (n - 3) * 128
                            t1 = P.op("dve", lambda e, Db=Db, j=j: e.tensor_scalar(out=rden, in0=Db, scalar1=esk[:, j:j + 1], scalar2=None,
                                                                                    op0=ALU.add),
                                      waits=[("swPV", t_sw[("PV", gi)]), ("act0", t_esk)], sig="swD")
                            t1 = P.op("dve", lambda e: e.reciprocal(out=rden, in_=rden), waits=[("swD", t1)], sig="swD")
                            t1 = P.op("dve", lambda e, Yb=Yb: e.tensor_mul(out=ytmp, in0=Yb, in1=rden), waits=[("swD", t1)], sig="swD")
                            t_sw[("EV", grp)] = P.op("dve", lambda e, q0=q0, step=step: e.tensor_mul(
                                out=ygT[:, step, q0:q0 + 512], in0=ytmp, in1=sg[:, q0:q0 + 512]),
                                waits=[("swD", t1)], sig="swEV")

                    SW_QK(0)
                    for n in range(NT):
                        SW_ACT(n)
                        if n + 1 < NT:
                            SW_QK(n + 1)
                        SW_PVD(n)
                    swn += NT
                    swg += NT // 4
                    t_last_sw = t_sw[("EV", swg - 1)]
                if not is_sb:
                    t_attdve = ("swEV", t_last_sw)
                ckpt(4 + 2 * step)

            ckpt(20)
            wo_v = w_out.rearrange("(k p) e -> p k e", p=128)
            t_wo = None
            for k in range(8):
                t_wo = P.op("pool", lambda e, k=k: e.dma_start(out=wout[:, k, :], in_=wo_v[:, k, :]),
                            waits=[("projpe", t_projpe)], sig="ld_wo", inc=16)
            t_fg = P.op("sp", lambda e: e.dma_start(out=fg_bc, in_=fg_bc_d[:, :]), waits=[("projpe", t_projpe)], sig="ld_fg", inc=16)
            t_xf = {}
            t_o = {}
            t_st = {}
            t_r2 = {}
            t_sq2 = {}

            def issue_xf(tt):
                t_xf[tt] = P.op("sp", lambda e, tt=tt: e.dma_start(out=xf[tt % 3], in_=x[tt * 128:(tt + 1) * 128, :]),
                                waits=[("fr2", t_r2.get(tt - 3)), ("projpe", t_projpe)], sig="ld_xf%d" % (tt % 3), inc=16)

            t_po = {}
            for tt in range(3):
                issue_xf(tt)
            for tt in range(NT):
                for eh in range(2):
                    for kc in range(8):
                        v = P.op("pe", lambda e, tt=tt, eh=eh, kc=kc: e.matmul(
                            po[tt % 2][eh], lhsT=ygT[:, kc, tt * 128:(tt + 1) * 128], rhs=wout[:, kc, eh * 512:(eh + 1) * 512],
                            start=(kc == 0), stop=(kc == 7)),
                            waits=[("ld_wo", t_wo), t_attdve, ("fr2", t_r2.get(tt - 2))],
                            sig=("fpo" if kc == 7 else None))
                    t_po[(tt, eh)] = v
                rb = rf[tt % 2]
                for eh in range(2):
                    t1 = P.op("dve", lambda e, tt=tt, eh=eh, rb=rb: e.tensor_mul(out=rb[:, eh * 512:(eh + 1) * 512], in0=po[tt % 2][eh],
                                                                               in1=gate_bc[:, eh * 512:(eh + 1) * 512]),
                              waits=[("fpo", t_po[(tt, eh)]), ("fo", t_o.get(tt - 2)), ("fsq", t_sq2.get(tt - 2))], sig="fd")
                t_r2[tt] = P.op("dve", lambda e, tt=tt, rb=rb: e.tensor_add(out=rb, in0=rb, in1=xf[tt % 3]),
                                waits=[("fd", t1), ("ld_xf%d" % (tt % 3), t_xf[tt])], sig="fr2")
                if tt + 3 < NT:
                    issue_xf(tt + 3)
                t_sq2[tt] = P.op("act", lambda e, tt=tt, rb=rb: e.activation(out=junkf, in_=rb, func=AF.Square, accum_out=ss2[:, tt:tt + 1]),
                                 waits=[("fr2", t_r2[tt])], sig="fsq")
                t1 = P.op("act", lambda e, tt=tt: e.activation(out=rstd2[:, tt:tt + 1], in_=ss2[:, tt:tt + 1], func=AF.Sqrt,
                                                               scale=1.0 / D, bias=eps_t[:, 0:1]),
                          waits=[("fsq", t_sq2[tt])], sig="fsqrt")
                t1 = P.op("dve", lambda e, tt=tt: e.reciprocal(out=rstd2[:, tt:tt + 1], in_=rstd2[:, tt:tt + 1]),
                          waits=[("fsqrt", t1)], sig="fd")
                t_o[tt] = P.op("dve", lambda e, tt=tt, rb=rb: e.scalar_tensor_tensor(
                    out=of[tt % 2], in0=rb, scalar=rstd2[:, tt:tt + 1], in1=fg_bc, op0=ALU.mult, op1=ALU.mult),
                    waits=[("fd", t1), ("ld_fg", t_fg), ("ld_out%d" % (tt % 2), t_st.get(tt - 2))], sig="fo")
                t_st[tt] = P.op("sp", lambda e, tt=tt: e.dma_start(out=out[tt * 128:(tt + 1) * 128, :], in_=of[tt % 2]),
                                waits=[("fo", t_o[tt])], sig="ld_out%d" % (tt % 2), inc=16)
            P.op("sp", lambda e: e.nop(), waits=[("ld_out0", t_st[NT - 2]), ("ld_out1", t_st[NT - 1])])

        try:
            plan_all()
        except _Stop:
            pass

        names = sorted(P.cnt.keys())
        sems = {n: es.enter_context(nc.semaphore(n)) for n in names}
        block = es.enter_context(nc.Block())

        def emit(eng, oplist):
            seen = {}
            for fn, waits, sig, inc in oplist:
                for (name, val) in waits:
                    if seen.get(name, 0) < val:
                        eng.wait_ge(sems[name], val)
                        seen[name] = val
                ins = fn(eng)
                if sig is not None:
                    ins.then_inc(sems[sig], inc)

        @block.sync
        def _(eng):
            emit(eng, P.ops["sp"])

        @block.gpsimd
        def _(eng):
            emit(eng, P.ops["pool"])

        @block.tensor
        def _(eng):
            emit(eng, P.ops["pe"])

        @block.scalar
        def _(eng):
            emit(eng, P.ops["act"])

        @block.vector
        def _(eng):
            emit(eng, P.ops["dve"])
    return nc


_CACHE = {}


def kernel(x, c, w_ada, b_ada, norm_g, w_in, sinks, w_out, final_g):
    x = np.asarray(x, np.float32)
    c = np.asarray(c, np.float32)
    w_ada = np.ascontiguousarray(np.asarray(w_ada, np.float32)[0])
    b_ada = np.asarray(b_ada, np.float32)[0]
    norm_g = np.asarray(norm_g, np.float32)[0]
    w_in = np.ascontiguousarray(np.asarray(w_in, np.float32)[0])
    sinks = np.asarray(sinks, np.float32)[0]
    w_out = np.ascontiguousarray(np.asarray(w_out, np.float32)[0])
    final_g = np.asarray(final_g, np.float32)

    def lay(v):
        return np.ascontiguousarray(v.reshape(-1, 128).T)

    bada_l = lay(b_ada[:2048])
    bg_bc = np.ascontiguousarray(np.broadcast_to(b_ada[2048:3072][None, :], (128, D)))
    normg_l = lay(norm_g)
    fg_bc = np.ascontiguousarray(np.broadcast_to(final_g[None, :], (128, D)))
    sinks_l = np.ascontiguousarray(np.stack([np.repeat(sinks[2 * j:2 * j + 2], 64) for j in range(4)], axis=1))
    consts = make_consts()
    if "nc" not in _CACHE:
        _CACHE["nc"] = build_nc()
    nc = _CACHE["nc"]
    in_maps = []
    for b in range(NCORE):
        in_maps.append({
            "x": np.ascontiguousarray(x[b]), "c_l": lay(c[b]), "w_ada": w_ada, "bada_l": bada_l, "bg_bc": bg_bc,
            "normg_l": normg_l, "w_in": w_in, "sinks_l": sinks_l, "w_out": w_out, "fg_bc": fg_bc, "consts": consts,
        })
    res = run_bass_kernel_spmd(nc, in_maps, core_ids=list(range(NCORE)))
    return np.stack([np.asarray(r["out"], np.float32) for r in res.results], axis=0)
```

```python
from contextlib import ExitStack

import numpy as np
import concourse.bass as bass
import concourse.mybir as mybir
from concourse.bass_utils import run_bass_kernel_spmd

F32 = mybir.dt.float32
BF16 = mybir.dt.bfloat16
AF = mybir.ActivationFunctionType
ALU = mybir.AluOpType

S = 4096
D = 1024
NCORE = 8
NT = S // 128
QC = 1024
NEG = -30000.0
NCONST = 128 * 6 + 2048
C_ID, C_TRI, C_TRIC, C_ZERO, C_NEGM, C_ONES, C_SWB = 0, 128, 256, 384, 512, 640, 768


LEVEL = 99
SW_DBG = 0
ATTACH = True


class _Stop(Exception):
    pass


def ckpt(level):
    if LEVEL <= level:
        raise _Stop()


class Plan:
    def __init__(self):
        self.ops = {"pe": [], "act": [], "dve": [], "pool": [], "sp": []}
        self.cnt = {}

    def op(self, eng, fn, waits=(), sig=None, inc=1, attach=False):
        v = None
        if sig is not None:
            self.cnt[sig] = self.cnt.get(sig, 0) + inc
            v = self.cnt[sig]
        ws = tuple(w for w in waits if w is not None and w[1] is not None and w[1] > 0)
        self.ops[eng].append((fn, ws, sig, inc, attach and ATTACH))
        return v


def make_consts():
    c = np.zeros((128, NCONST), np.float32)
    j = np.arange(128)[:, None]
    s = np.arange(128)[None, :]
    c[:, C_ID:C_ID + 128] = (j == s)
    c[:, C_TRI:C_TRI + 128] = -1.0 * (j >= s)
    c[:, C_TRIC:C_TRIC + 128] = -1.0 * (j < s)
    c[:, C_NEGM:C_NEGM + 128] = np.where(j < s, 0.0, NEG)
    c[:, C_ONES:C_ONES + 128] = 1.0
    for h in range(8):
        m = 2.0 ** (-8.0 * (h + 1) / 8)
        rel_cur = (s - j).astype(np.float32)
        cur = np.where(s >= j, -m * rel_cur, NEG)
        rel_prev = (128 + s - j).astype(np.float32)
        prev = np.where(j > s, -m * rel_prev, NEG)
        c[:, C_SWB + h * 256:C_SWB + h * 256 + 128] = cur
        c[:, C_SWB + h * 256 + 128:C_SWB + h * 256 + 256] = prev
    return c


def sb_tiles():
    out = []
    for qc in range(S // QC):
        nkb = (QC // 128) * (qc + 1)
        for kb in range(nkb - 1, -1, -1):
            jd = kb - (QC // 128) * qc
            diag = jd >= 0
            c0 = 128 * jd if diag else 0
            out.append((qc, kb, c0, diag, kb == nkb - 1, kb == 0))
    return out


def col_segs(c0, c1=QC):
    segs = []
    a = c0
    while a < c1:
        b = min(c1, (a // 512 + 1) * 512)
        segs.append((a, b))
        a = b
    return segs


def build_nc():
    nc = bass.Bass("TRN2", target_bir_lowering=False)
    x = nc.dram_tensor("x", [S, D], F32, kind="ExternalInput").ap()
    c_l = nc.dram_tensor("c_l", [128, 8], F32, kind="ExternalInput").ap()
    w_ada = nc.dram_tensor("w_ada", [D, 3 * D], F32, kind="ExternalInput").ap()
    bada_l = nc.dram_tensor("bada_l", [128, 16], F32, kind="ExternalInput").ap()
    bg_bc_d = nc.dram_tensor("bg_bc", [128, D], F32, kind="ExternalInput").ap()
    normg_l = nc.dram_tensor("normg_l", [128, 8], F32, kind="ExternalInput").ap()
    w_in = nc.dram_tensor("w_in", [D, 3328], F32, kind="ExternalInput").ap()
    sinks_l = nc.dram_tensor("sinks_l", [128, 4], F32, kind="ExternalInput").ap()
    w_out = nc.dram_tensor("w_out", [D, D], F32, kind="ExternalInput").ap()
    fg_bc_d = nc.dram_tensor("fg_bc", [128, D], F32, kind="ExternalInput").ap()
    consts_d = nc.dram_tensor("consts", [128, NCONST], F32, kind="ExternalInput").ap()
    out = nc.dram_tensor("out", [S, D], F32, kind="ExternalOutput").ap()

    P = Plan()
    es = ExitStack()
    with es:
        def sb(name, shape, dt):
            return es.enter_context(nc.sbuf_tensor(name, shape, dt))

        ygT = sb("ygT", [128, 8, S], BF16)
        hT = sb("hT", [128, 8, S], BF16)
        ov = sb("ov", [128, 30720], BF16)
        cst = sb("cst", [128, NCONST], BF16)
        gate_bc = sb("gate_bc", [128, D], F32)
        c_sb = sb("c_sb", [128, 8], F32)
        etmp = sb("etmp", [128, 8], F32)
        cond = sb("cond", [128, 8], F32)
        bada = sb("bada", [128, 16], F32)
        normg = sb("normg", [128, 8], F32)
        mod_sb = sb("mod_sb", [128, 16], F32)
        gs = sb("gs", [128, 8], F32)
        ones_f = sb("ones_f", [128, 128], F32)
        ss = sb("ss", [128, 32], F32)
        rstd = sb("rstd", [128, 32], F32)
        ss2 = sb("ss2", [128, 32], F32)
        rstd2 = sb("rstd2", [128, 32], F32)
        snk = sb("snk", [128, 4], F32)
        esk = sb("esk", [128, 4], F32)
        eps_t = sb("eps_t", [128, 1], F32)
        ps = es.enter_context(nc.psum_tensor("ps", [128, 4096], F32))

        qT = ov[:, 0:4096]
        kT = ov[:, 4096:8192]
        sg = ov[:, 8192:12288]
        vv = ov[:, 12288:16384].rearrange("p (b c) -> p b c", c=128)
        wsl = ov[:, 16384:20480].rearrange("p (k c) -> p k c", c=512)
        PB = 20480
        Eb = [ov[:, PB + i * 1024:PB + (i + 1) * 1024] for i in range(3)]
        Gb = [ov[:, PB + (3 + i) * 1024:PB + (4 + i) * 1024] for i in range(3)]
        Pb = [ov[:, PB + (6 + i) * 1024:PB + (7 + i) * 1024] for i in range(2)]
        Ab = [ov[:, PB + (8 + i) * 1024:PB + (9 + i) * 1024] for i in range(2)]
        Psw = [ov[:, PB + i * 512:PB + (i + 1) * 512] for i in range(2)]
        rden = ov[:, PB + 1024:PB + 2048].bitcast(F32)
        ytmp = ov[:, PB + 2048:PB + 3072].bitcast(F32)
        yflat = ygT[:, :, :].rearrange("p a b -> p (a b)")
        wada = [yflat[:, i * 6144:(i + 1) * 6144].bitcast(F32) for i in range(4)]
        xt = [ov[:, 12288 + i * 2048:12288 + (i + 1) * 2048].bitcast(F32) for i in range(4)]
        xnb = [[ov[:, 20480 + (g * 4 + t) * 1024:20480 + (g * 4 + t + 1) * 1024] for t in range(4)] for g in range(2)]
        junk = ov[:, 28672:29696]
        hflat = hT[:, :, :].rearrange("p a b -> p (a b)")
        condb = yflat[:, 24576:26624].bitcast(F32).rearrange("p (k m) -> p k m", m=128)
        bg_bc = yflat[:, 26624:28672].bitcast(F32)
        wout = hflat[:, 0:8192].rearrange("p (k e) -> p k e", e=1024)
        xf = [hflat[:, 8192 + i * 2048:8192 + (i + 1) * 2048].bitcast(F32) for i in range(3)]
        rf = [hflat[:, 14336 + i * 2048:14336 + (i + 1) * 2048].bitcast(F32) for i in range(2)]
        of = [hflat[:, 18432 + i * 2048:18432 + (i + 1) * 2048].bitcast(F32) for i in range(2)]
        fg_bc = hflat[:, 22528:24576].bitcast(F32)
        junkf = hflat[:, 24576:25600]
        Zp = ps[:, 0:1024]
        Rp = ps[:, 1024:2048]
        Yp = [ps[:, 2048:3072], ps[:, 3072:4096]]
        pp = [ps[:, 0:512], ps[:, 512:1024]]
        tp = [ps[:, i * 512:(i + 1) * 512].bitcast(BF16)[:, 0:512] for i in range(2)]
        modps = ps[:, 1024:1040]
        gateps = ps[:, 2048:3072]
        po = [[ps[:, t * 1024 + e * 512:t * 1024 + (e + 1) * 512] for e in range(2)] for t in range(2)]

        ident = cst[:, C_ID:C_ID + 128]
        triN = cst[:, C_TRI:C_TRI + 128]
        tricN = cst[:, C_TRIC:C_TRIC + 128]
        zer = cst[:, C_ZERO:C_ZERO + 128]
        negm = cst[:, C_NEGM:C_NEGM + 128]
        ones_bf = cst[:, C_ONES:C_ONES + 128]

        def plan_all():
            t_cst = P.op("pool", lambda e: e.dma_start(out=cst[:, :], in_=consts_d[:, :]), sig="ld_cst", inc=16)
            for dst, src in ((c_sb, c_l), (bada, bada_l), (normg, normg_l), (snk, sinks_l)):
                t_small = P.op("sp", lambda e, dst=dst, src=src: e.dma_start(out=dst[:, :], in_=src[:, :]), sig="ld_small", inc=16)
            t_small = P.op("sp", lambda e: e.dma_start(out=bg_bc, in_=bg_bc_d[:, :]), sig="ld_small", inc=16)

            ckpt(0)
            P.op("dve", lambda e: e.memset(ones_f[:, :], 1.0), sig="dve0")
            P.op("dve", lambda e: e.memset(eps_t[:, :], 1e-6), sig="dve0")
            P.op("dve", lambda e: e.memset(ss[:, :], 0.0), sig="dve0")
            t_d = P.op("dve", lambda e: e.memset(ss2[:, :], 0.0), sig="dve0")
            t_ms = t_d
            t_a = P.op("act", lambda e: e.activation(out=etmp[:, :], in_=c_sb[:, :], func=AF.Exp, scale=-1.0),
                       waits=[("ld_small", t_small)], sig="act0")
            t_d = P.op("dve", lambda e: e.tensor_scalar_add(out=etmp[:, :], in0=etmp[:, :], scalar1=1.0),
                       waits=[("act0", t_a), ("dve0", t_d)], sig="dve0")
            t_d = P.op("dve", lambda e: e.reciprocal(out=etmp[:, :], in_=etmp[:, :]), waits=[("dve0", t_d)], sig="dve0")
            t_d = P.op("dve", lambda e: e.tensor_mul(out=cond[:, :], in0=c_sb[:, :], in1=etmp[:, :]),
                       waits=[("dve0", t_d)], sig="dve0")
            t_cond = t_d
            for k in range(8):
                t_d = P.op("dve", lambda e, k=k: e.tensor_scalar(out=condb[:, k, :], in0=ones_f[:, :], scalar1=cond[:, k:k + 1],
                                                                 scalar2=None, op0=ALU.mult),
                           waits=[("dve0", t_cond)], sig="dve0")
            t_condb = t_d
            t_pe0 = {}
            t_wada = {}
            t_xld = {}
            t_sq = {}
            t_xn = {}
            t_tp = {}
            t_ev = {}
            NWB = 4

            def issue_wada(k):
                t_wada[k] = P.op("sp", lambda e, k=k: e.dma_start(out=wada[k % NWB], in_=w_ada[k * 128:(k + 1) * 128, :]),
                                 waits=[("pe0", t_pe0.get(k - NWB))], sig="ld_wada%d" % (k % NWB), inc=16)

            def issue_xload(tt):
                t_xld[tt] = P.op("sp", lambda e, tt=tt: e.dma_start(out=xt[tt % 4], in_=x[tt * 128:(tt + 1) * 128, :]),
                                 waits=[("p1xn", t_xn.get(tt - 4)), ("p1sq", t_sq.get(tt - 4))],
                                 sig="ld_x%d" % (tt % 4), inc=16)

            for k in range(NWB):
                issue_wada(k)
            for tt in range(4):
                issue_xload(tt)
            for k in range(8):
                for j in range(16):
                    P.op("pe", lambda e, k=k, j=j: e.matmul(modps[:, j:j + 1], lhsT=wada[k % NWB][:, j * 128:(j + 1) * 128],
                                                            rhs=cond[:, k:k + 1], start=(k == 0 and j == 0), stop=(k == 7),
                                                            skip_group_check=True),
                         waits=[("ld_wada%d" % (k % NWB), t_wada[k]), ("dve0", t_condb)])
                for eh in range(2):
                    t_pe0[k] = P.op("pe", lambda e, k=k, eh=eh: e.matmul(gateps[:, eh * 512:(eh + 1) * 512], lhsT=condb[:, k, :],
                                                                         rhs=wada[k % NWB][:, 2048 + eh * 512:2048 + (eh + 1) * 512],
                                                                         start=(k == 0), stop=(k == 7)),
                                    waits=[("ld_wada%d" % (k % NWB), t_wada[k]), ("dve0", t_condb)], sig=("pe0" if eh == 1 else None))
                if k + NWB < 8:
                    issue_wada(k + NWB)
                g = k
                for t4 in range(4):
                    tt = g * 4 + t4
                    t_sq[tt] = P.op("act", lambda e, tt=tt: e.activation(out=junk, in_=xt[tt % 4], func=AF.Square,
                                                                         accum_out=ss[:, tt:tt + 1]),
                                    waits=[("ld_x%d" % (tt % 4), t_xld[tt]), ("dve0", t_ms)], sig="p1sq")
                    t_r = P.op("act", lambda e, tt=tt: e.activation(out=rstd[:, tt:tt + 1], in_=ss[:, tt:tt + 1], func=AF.Sqrt,
                                                                    scale=1.0 / D, bias=eps_t[:, 0:1]),
                               waits=[("p1sq", t_sq[tt])], sig="p1sqrt")
                    t_r = P.op("dve", lambda e, tt=tt: e.reciprocal(out=rstd[:, tt:tt + 1], in_=rstd[:, tt:tt + 1]),
                               waits=[("p1sqrt", t_r)], sig="dve1")
                    t_xn[tt] = P.op("dve", lambda e, tt=tt, g=g, t4=t4: e.tensor_scalar(
                        out=xnb[g % 2][t4], in0=xt[tt % 4], scalar1=rstd[:, tt:tt + 1], scalar2=None, op0=ALU.mult),
                        waits=[("dve1", t_r), ("p1tp", t_tp.get((g - 2, 7)))], sig="p1xn")
                    if tt + 4 < NT:
                        issue_xload(tt + 4)
                for j in range(8):
                    prev_ev = t_ev[(g, j - 2)] if j >= 2 else (t_ev[(g - 1, 6 + j)] if g >= 1 else None)
                    for t4 in range(4):
                        t_tp[(g, j)] = P.op("pe", lambda e, g=g, j=j, t4=t4: e.transpose(
                            out=tp[j % 2][:, t4 * 128:(t4 + 1) * 128], in_=xnb[g % 2][t4][:, j * 128:(j + 1) * 128], identity=ident),
                            waits=[("p1xn", t_xn[g * 4 + 3]), ("p1ev", prev_ev), ("ld_cst", t_cst)],
                            sig=("p1tp" if t4 == 3 else None))
                    t_ev[(g, j)] = P.op("act", lambda e, g=g, j=j: e.activation(
                        out=hT[:, j, g * 512:(g + 1) * 512], in_=tp[j % 2], func=AF.Identity),
                        waits=[("p1tp", t_tp[(g, j)])], sig="p1ev")
            t_d = P.op("dve", lambda e: e.tensor_add(out=mod_sb[:, :], in0=modps, in1=bada[:, :]),
                       waits=[("pe0", t_pe0[7]), ("ld_small", t_small)], sig="dve0")
            t_d = P.op("dve", lambda e: e.scalar_tensor_tensor(out=gs[:, :], in0=mod_sb[:, 8:16], scalar=1.0, in1=normg[:, :],
                                                               op0=ALU.add, op1=ALU.mult),
                       waits=[("dve0", t_d)], sig="dve0")
            t_d = P.op("dve", lambda e: e.tensor_add(out=gate_bc[:, :], in0=gateps, in1=bg_bc), sig="dve0")
            t_mod = t_d
            for j in range(8):
                for hh in range(2):
                    t_fix = P.op("dve", lambda e, j=j, hh=hh: e.tensor_scalar(
                        out=hT[:, j, hh * 2048:(hh + 1) * 2048], in0=hT[:, j, hh * 2048:(hh + 1) * 2048],
                        scalar1=gs[:, j:j + 1], scalar2=mod_sb[:, j:j + 1], op0=ALU.mult, op1=ALU.add),
                        waits=[("dve0", t_mod), ("p1ev", t_ev[(7, 7)])], sig="hfix")
            t_hT = t_fix
            ckpt(2)
            t_wsl = None
            t_projpe = None
            t_attpe = None
            t_attdve = None
            t_pev = {}
            nproj = 0
            t_esk = P.op("act", lambda e: e.activation(out=esk[:, :], in_=snk[:, :], func=AF.Exp),
                         waits=[("ld_small", t_small)], sig="act0")
            sbt = sb_tiles()
            sbi = 0
            tk = {}
            n_chunk = 0
            t_evy = {}
            swn = 0
            swg = 0
            t_sw = {}

            for step in range(8):
                is_sb = step < 4
                if is_sb:
                    cq, ck, cv, cg = step * 128, 512 + step * 128, 1024 + step * 128, 1536 + step * 128
                    kw = 128
                    vw = 128
                    do_kv = True
                else:
                    j = step - 4
                    cq, ck, cv, cg = 2048 + j * 128, 2560 + (j // 2) * 64, 2688 + (j // 2) * 64, 2816 + j * 128
                    kw = 64
                    vw = 64
                    do_kv = (j % 2 == 0)
                wv = w_in.rearrange("(k p) c -> p k c", p=128)
                dmas = [(0, cq, 128), (256, cg, 128)]
                if do_kv:
                    if kw == 128:
                        dmas.append((128, ck, 128))
                    else:
                        dmas.append((128, ck, 64))
                        dmas.append((192, ck, 64))
                    if vw == 128:
                        dmas.append((384, cv, 128))
                    else:
                        dmas.append((384, cv, 64))
                        dmas.append((448, cv, 64))
                        vw = 128
                for (dc, sc, w) in dmas:
                    t_wsl = P.op("pool", lambda e, dc=dc, sc=sc, w=w: e.dma_start(out=wsl[:, :, dc:dc + w], in_=wv[:, :, sc:sc + w]),
                                 waits=[("projpe", t_projpe), ("hfix", t_hT)], sig="ld_wsl", inc=16)
                ckpt(2.5 + 2 * step)
                kinds = [("q", 0), ("g", 256)] + ([("k", 128)] if do_kv else [])
                for kind, wc in kinds:
                    for tc in range(8):
                        par = nproj % 2
                        for kc in range(8):
                            last = kc == 7
                            t_projpe_new = P.op("pe", lambda e, par=par, kc=kc, wc=wc, tc=tc: e.matmul(
                                pp[par], lhsT=wsl[:, kc, wc:wc + 128], rhs=hT[:, kc, tc * 512:(tc + 1) * 512],
                                start=(kc == 0), stop=(kc == 7)),
                                waits=[("ld_wsl", t_wsl), t_pev.get(nproj - 2), ("hfix", t_hT),
                                       t_attdve],
                                sig=("projpe" if last else None))
                        t_projpe = t_projpe_new
                        dst = {"q": qT, "k": kT, "g": sg}[kind][:, tc * 512:(tc + 1) * 512]
                        if kind == "q":
                            t_pev[nproj] = ("pev", P.op("dve", lambda e, dst=dst, par=par: e.tensor_scalar(
                                out=dst, in0=pp[par], scalar1=0.125, scalar2=None, op0=ALU.mult),
                                waits=[("projpe", t_projpe)], sig="pev"))
                            last_dve_pev = t_pev[nproj]
                        elif kind == "k":
                            t_pev[nproj] = ("pev", P.op("dve", lambda e, dst=dst, par=par: e.tensor_copy(out=dst, in_=pp[par]),
                                                waits=[("projpe", t_projpe)], sig="pev"))
                            last_dve_pev = t_pev[nproj]
                        else:
                            t_pev[nproj] = ("pevA", P.op("act", lambda e, dst=dst, par=par: e.activation(out=dst, in_=pp[par], func=AF.Silu),
                                                         waits=[("projpe", t_projpe), t_attdve], sig="pevA"))
                            last_act_pev = t_pev[nproj]
                        nproj += 1
                ckpt(2.75 + 2 * step)
                if do_kv:
                    for tg in range(8):
                        par = nproj % 2
                        for t4 in range(4):
                            tok = (tg * 4 + t4) * 128
                            for kc in range(8):
                                last = (kc == 7 and t4 == 3)
                                t_projpe_new = P.op("pe", lambda e, par=par, kc=kc, tok=tok, t4=t4, vw=vw: e.matmul(
                                    pp[par][:, t4 * 128:t4 * 128 + vw], lhsT=hT[:, kc, tok:tok + 128], rhs=wsl[:, kc, 384:384 + vw],
                                    start=(kc == 0), stop=(kc == 7)),
                                    waits=[("ld_wsl", t_wsl), t_pev.get(nproj - 2), t_attdve],
                                    sig=("projpe" if last else None))
                        t_projpe = t_projpe_new
                        t_pev[nproj] = ("pev", P.op("dve", lambda e, par=par, tg=tg, vw=vw: e.tensor_copy(
                            out=vv[:, tg * 4:(tg + 1) * 4, 0:vw],
                            in_=pp[par].rearrange("p (a b) -> p a b", a=4)[:, :, 0:vw]),
                            waits=[("projpe", t_projpe)], sig="pev"))
                        last_dve_pev = t_pev[nproj]
                        nproj += 1
                projw = [last_dve_pev, last_act_pev]
                ckpt(3 + 2 * step)

                if is_sb:
                    tiles = []
                    for hh in range(2):
                        for (qc, kb, c0, diag, cs, ce) in sbt:
                            tiles.append((hh * 64, qc, kb, c0, diag, cs, ce))
                    T = len(tiles)
                    base = sbi
                    chunk_of = {}
                    cc = n_chunk - 1
                    for i, tl in enumerate(tiles):
                        if tl[5]:
                            cc += 1
                        chunk_of[i] = cc

                    def QK(i):
                        hp, qc, kb, c0, diag, cs, ce = tiles[i]
                        g = base + i
                        segs = col_segs(c0)
                        for si, (a, b) in enumerate(segs):
                            lastseg = (si == len(segs) - 1) and not diag
                            v = P.op("pe", lambda e, hp=hp, kb=kb, qc=qc, a=a, b=b, diag=diag: e.matmul(
                                Zp[:, a:b], lhsT=kT[hp:hp + 64, kb * 128:(kb + 1) * 128],
                                rhs=qT[hp:hp + 64, qc * QC + a:qc * QC + b], start=True, stop=not (diag and a == c0),
                                skip_group_check=True),
                                waits=[("sbE", tk.get(("E", g - 1)))] + (projw if i < 2 else []),
                                sig=("sbQK" if lastseg else None), attach=True)
                            if lastseg:
                                tk[("QK", g)] = v
                        if diag:
                            tk[("QK", g)] = P.op("pe", lambda e, c0=c0: e.matmul(
                                Zp[:, c0:c0 + 128], lhsT=ident, rhs=negm, start=False, stop=True, skip_group_check=True),
                                sig="sbQK")

                    def ACT_E(i):
                        hp, qc, kb, c0, diag, cs, ce = tiles[i]
                        g = base + i
                        tk[("E", g)] = P.op("act", lambda e, g=g, c0=c0: e.activation(out=Eb[g % 3][:, c0:QC], in_=Zp[:, c0:QC], func=AF.Exp),
                                            waits=[("sbQK", tk[("QK", g)]), ("sbA", tk.get(("A", g - 3)))], sig="sbE")

                    def ACT_G(i):
                        hp, qc, kb, c0, diag, cs, ce = tiles[i]
                        g = base + i
                        tk[("G", g)] = P.op("act", lambda e, g=g, c0=c0: e.activation(out=Gb[g % 3][:, c0:QC], in_=Eb[g % 3][:, c0:QC],
                                                                                   func=AF.Ln, bias=1.0),
                                            waits=[("sbE", tk[("E", g)]), ("sbTRIC", tk.get(("TRIC", g - 3)))], sig="sbG")

                    def ACT_P(i):
                        hp, qc, kb, c0, diag, cs, ce = tiles[i]
                        g = base + i
                        tk[("P", g)] = P.op("act", lambda e, g=g, c0=c0: e.activation(out=Pb[g % 2][:, c0:QC], in_=Rp[:, c0:QC], func=AF.Exp),
                                            waits=[("sbTRI", tk[("TRI", g)]), ("sbA", tk.get(("A", g - 2)))], sig="sbP")

                    def PE_TRI(i):
                        hp, qc, kb, c0, diag, cs, ce = tiles[i]
                        g = base + i
                        if cs:
                            par = chunk_of[i] % 2
                            for (a, b) in ((0, 512), (512, 1024)):
                                P.op("pe", lambda e, a=a, b=b: e.matmul(Rp[:, a:b], lhsT=zer, rhs=cst[:, 0:512], start=True, stop=False,
                                                                        skip_group_check=True),
                                     waits=[("sbP", tk.get(("P", g - 1)))])
                                P.op("pe", lambda e, a=a, b=b, hp=hp, par=par: e.matmul(
                                    Yp[par][hp:hp + 64, a:b], lhsT=zer[:, 0:64], rhs=cst[:, 0:512], start=True, stop=False,
                                    skip_group_check=True),
                                    waits=[("sbEV", t_evy.get(chunk_of[i] - 2))])
                        segs = col_segs(c0)
                        for si, (a, b) in enumerate(segs):
                            v = P.op("pe", lambda e, g=g, a=a, b=b: e.matmul(Rp[:, a:b], lhsT=triN, rhs=Gb[g % 3][:, a:b], start=False, stop=False,
                                                                            skip_group_check=True),
                                     waits=[("sbG", tk[("G", g)]), ("sbP", tk.get(("P", g - 1)))],
                                     sig=("sbTRI" if si == len(segs) - 1 else None), attach=True)
                        tk[("TRI", g)] = v

                    def PE_TRIC(i):
                        hp, qc, kb, c0, diag, cs, ce = tiles[i]
                        g = base + i
                        segs = col_segs(c0)
                        for si, (a, b) in enumerate(segs):
                            v = P.op("pe", lambda e, g=g, a=a, b=b: e.matmul(Rp[:, a:b], lhsT=tricN, rhs=Gb[g % 3][:, a:b], start=False, stop=False,
                                                                            skip_group_check=True),
                                     waits=[("sbP", tk[("P", g)])],
                                     sig=("sbTRIC" if si == len(segs) - 1 else None), attach=True)
                        tk[("TRIC", g)] = v

                    def PE_PV(i):
                        hp, qc, kb, c0, diag, cs, ce = tiles[i]
                        g = base + i
                        par = chunk_of[i] % 2
                        segs = col_segs(c0)
                        for si, (a, b) in enumerate(segs):
                            v = P.op("pe", lambda e, g=g, a=a, b=b, hp=hp, kb=kb, par=par: e.matmul(
                                Yp[par][hp:hp + 64, a:b], lhsT=vv[:, kb, hp:hp + 64], rhs=Ab[g % 2][:, a:b], start=False, stop=False,
                                skip_group_check=True),
                                waits=[("sbA", tk[("A", g)])],
                                sig=("sbPV" if si == len(segs) - 1 else None), attach=True)
                        tk[("PV", g)] = v

                    def DVE_A(i):
                        hp, qc, kb, c0, diag, cs, ce = tiles[i]
                        g = base + i
                        tk[("A", g)] = P.op("dve", lambda e, g=g, c0=c0: e.tensor_mul(out=Ab[g % 2][:, c0:QC], in0=Eb[g % 3][:, c0:QC],
                                                                                   in1=Pb[g % 2][:, c0:QC]),
                                            waits=[("sbP", tk[("P", g)]), ("sbPV", tk.get(("PV", g - 2)))], sig="sbA")

                    def DVE_EV(i):
                        hp, qc, kb, c0, diag, cs, ce = tiles[i]
                        g = base + i
                        ch = chunk_of[i]
                        par = ch % 2
                        t_evy[ch] = P.op("dve", lambda e, hp=hp, qc=qc, par=par, step=step: e.tensor_mul(
                            out=ygT[hp:hp + 64, step, qc * QC:(qc + 1) * QC], in0=Yp[par][hp:hp + 64, :],
                            in1=sg[hp:hp + 64, qc * QC:(qc + 1) * QC]),
                            waits=[("sbPV", tk[("PV", g)])], sig="sbEV")

                    QK(0)
                    ACT_E(0)
                    if T > 1:
                        QK(1)
                    ACT_G(0)
                    if T > 1:
                        ACT_E(1)
                    if T > 2:
                        QK(2)
                    PE_TRI(0)
                    for i in range(T):
                        ACT_P(i)
                        PE_TRIC(i)
                        if i + 1 < T:
                            ACT_G(i + 1)
                            PE_TRI(i + 1)
                        DVE_A(i)
                        PE_PV(i)
                        if tiles[i][6]:
                            DVE_EV(i)
                        if i + 2 < T:
                            ACT_E(i + 2)
                        if i + 3 < T:
                            QK(i + 3)
                    sbi += T
                    n_chunk = cc + 1
                    t_attdve = ("sbEV", t_evy[cc])
                else:
                    j = step - 4
                    heads = (2 * j, 2 * j + 1)
                    swn0 = swn

                    def SW_QK(n):
                        gi = swn0 + n
                        zbase = (gi % 2) * 1024
                        wz = 256 if n > 0 else 128
                        for which, kblk in ((0, n), (1, n - 1)):
                            if kblk < 0:
                                continue
                            for hh in range(2):
                                hp = hh * 64
                                zc = zbase + hh * 512 + which * 128
                                P.op("pe", lambda e, zc=zc, hp=hp, kblk=kblk, n=n, which=which: e.matmul(
                                    ps[:, zc:zc + 128], lhsT=kT[hp:hp + 64, kblk * 128:(kblk + 1) * 128],
                                    rhs=qT[hp:hp + 64, n * 128:(n + 1) * 128], start=(which == 0), stop=False, skip_group_check=True),
                                    waits=[("swP", t_sw.get(("P", gi - 2)))] + (projw if n < 2 else []))
                        for hh in range(2):
                            h = heads[hh]
                            bc = C_SWB + h * 256
                            zc = zbase + hh * 512
                            v = P.op("pe", lambda e, zc=zc, bc=bc, wz=wz: e.matmul(
                                ps[:, zc:zc + wz], lhsT=ident, rhs=cst[:, bc:bc + wz], start=False, stop=True, skip_group_check=True),
                                sig=("swQK" if hh == 1 else None))
                        t_sw[("QK", gi)] = v

                    def SW_ACT(n):
                        gi = swn0 + n
                        zbase = (gi % 2) * 1024
                        wz = 256 if n > 0 else 128
                        zin = ps[:, zbase:zbase + 1024].rearrange("p (b c) -> p b c", b=2)[:, :, 0:wz]
                        pout = Psw[gi % 2].rearrange("p (b c) -> p b c", b=2)[:, :, 0:wz]
                        t_sw[("P", gi)] = P.op("act", lambda e, zin=zin, pout=pout: e.activation(out=pout, in_=zin, func=AF.Exp),
                                               waits=[("swQK", t_sw[("QK", gi)]), ("swPV", t_sw.get(("PV", gi - 2)))], sig="swP")

                    def SW_PVD(n):
                        gi = swn0 + n
                        grp = swg + n // 4
                        par = grp % 2
                        Yb = Yp[par][:, 0:512]
                        Db = Yp[par][:, 512:1024]
                        col = (n % 4) * 128
                        for hh in range(2):
                            hp = hh * 64
                            srcs = [(hh * 256, n)] + ([(hh * 256 + 128, n - 1)] if n > 0 else [])
                            ns = len(srcs)
                            for si, (pc, kblk) in enumerate(srcs):
                                P.op("pe", lambda e, Yb=Yb, hp=hp, col=col, kblk=kblk, gi=gi, pc=pc, si=si, ns=ns: e.matmul(
                                    Yb[hp:hp + 64, col:col + 128], lhsT=vv[:, kblk, 0:64], rhs=Psw[gi % 2][:, pc:pc + 128],
                                    start=(si == 0), stop=(si == ns - 1), skip_group_check=True),
                                    waits=[("swP", t_sw[("P", gi)]), ("swEV", t_sw.get(("EV", grp - 2)))])
                            for si, (pc, kblk) in enumerate(srcs):
                                v = P.op("pe", lambda e, Db=Db, hp=hp, col=col, gi=gi, pc=pc, si=si, ns=ns: e.matmul(
                                    Db[hp:hp + 64, col:col + 128], lhsT=ones_bf[:, 0:64], rhs=Psw[gi % 2][:, pc:pc + 128],
                                    start=(si == 0), stop=(si == ns - 1), skip_group_check=True),
                                    sig=("swPV" if (hh == 1 and si == ns - 1) else None))
                        t_sw[("PV", gi)] = v
                        if n % 4 == 3:
                            q0 = (n - 3) * 128
                            t1 = P.op("dve", lambda e, Db=Db, j=j: e.tensor_scalar(out=rden, in0=Db, scalar1=esk[:, j:j + 1], scalar2=None,
                                                                                    op0=ALU.add),
                                      waits=[("swPV", t_sw[("PV", gi)]), ("act0", t_esk)], sig="swD")
                            t1 = P.op("dve", lambda e: e.reciprocal(out=rden, in_=rden), waits=[("swD", t1)], sig="swD")
                            t1 = P.op("dve", lambda e, Yb=Yb: e.tensor_mul(out=ytmp, in0=Yb, in1=rden), waits=[("swD", t1)], sig="swD")
                            t_sw[("EV", grp)] = P.op("dve", lambda e, q0=q0, step=step: e.tensor_mul(
                                out=ygT[:, step, q0:q0 + 512], in0=ytmp, in1=sg[:, q0:q0 + 512]),
                                waits=[("swD", t1)], sig="swEV")

                    SW_QK(0)
                    for n in range(NT):
                        SW_ACT(n)
                        if n + 1 < NT:
                            SW_QK(n + 1)
                        SW_PVD(n)
                    swn += NT
                    swg += NT // 4
                    t_last_sw = t_sw[("EV", swg - 1)]
                if not is_sb:
                    t_attdve = ("swEV", t_last_sw)
                ckpt(4 + 2 * step)

            ckpt(20)
            wo_v = w_out.rearrange("(k p) e -> p k e", p=128)
            t_wo = None
            for k in range(8):
                t_wo = P.op("pool", lambda e, k=k: e.dma_start(out=wout[:, k, :], in_=wo_v[:, k, :]),
                            waits=[("projpe", t_projpe)], sig="ld_wo", inc=16)
            t_fg = P.op("sp", lambda e: e.dma_start(out=fg_bc, in_=fg_bc_d[:, :]), waits=[("projpe", t_projpe)], sig="ld_fg", inc=16)
            t_xf = {}
            t_o = {}
            t_st = {}
            t_r2 = {}
            t_sq2 = {}

            def issue_xf(tt):
                t_xf[tt] = P.op("sp", lambda e, tt=tt: e.dma_start(out=xf[tt % 3], in_=x[tt * 128:(tt + 1) * 128, :]),
                                waits=[("fr2", t_r2.get(tt - 3)), ("projpe", t_projpe)], sig="ld_xf%d" % (tt % 3), inc=16)

            t_po = {}
            for tt in range(3):
                issue_xf(tt)
            def F_PE(tt):
                for eh in range(2):
                    for kc in range(8):
                        v = P.op("pe", lambda e, tt=tt, eh=eh, kc=kc: e.matmul(
                            po[tt % 2][eh], lhsT=ygT[:, kc, tt * 128:(tt + 1) * 128], rhs=wout[:, kc, eh * 512:(eh + 1) * 512],
                            start=(kc == 0), stop=(kc == 7)),
                            waits=[("ld_wo", t_wo), t_attdve, ("fr2", t_r2.get(tt - 2))],
                            sig=("fpo" if kc == 7 else None))
                    t_po[(tt, eh)] = v

            def F_A(tt):
                rb = rf[tt % 2]
                for eh in range(2):
                    t1 = P.op("dve", lambda e, tt=tt, eh=eh, rb=rb: e.tensor_mul(out=rb[:, eh * 512:(eh + 1) * 512], in0=po[tt % 2][eh],
                                                                               in1=gate_bc[:, eh * 512:(eh + 1) * 512]),
                              waits=[("fpo", t_po[(tt, eh)]), ("fo", t_o.get(tt - 2)), ("fsq", t_sq2.get(tt - 2))], sig="fd")
                t_r2[tt] = P.op("dve", lambda e, tt=tt, rb=rb: e.tensor_add(out=rb, in0=rb, in1=xf[tt % 3]),
                                waits=[("fd", t1), ("ld_xf%d" % (tt % 3), t_xf[tt])], sig="fr2")
                if tt + 3 < NT:
                    issue_xf(tt + 3)
                t_sq2[tt] = P.op("act", lambda e, tt=tt, rb=rb: e.activation(out=junkf, in_=rb, func=AF.Square, accum_out=ss2[:, tt:tt + 1]),
                                 waits=[("fr2", t_r2[tt])], sig="fsq")
                t_sqrt2[tt] = P.op("act", lambda e, tt=tt: e.activation(out=rstd2[:, tt:tt + 1], in_=ss2[:, tt:tt + 1], func=AF.Sqrt,
                                                                        scale=1.0 / D, bias=eps_t[:, 0:1]),
                                   waits=[("fsq", t_sq2[tt])], sig="fsqrt")

            def F_B(tt):
                rb = rf[tt % 2]
                t1 = P.op("dve", lambda e, tt=tt: e.reciprocal(out=rstd2[:, tt:tt + 1], in_=rstd2[:, tt:tt + 1]),
                          waits=[("fsqrt", t_sqrt2[tt])], sig="fd")
                t_o[tt] = P.op("dve", lambda e, tt=tt, rb=rb: e.scalar_tensor_tensor(
                    out=of[tt % 2], in0=rb, scalar=rstd2[:, tt:tt + 1], in1=fg_bc, op0=ALU.mult, op1=ALU.mult),
                    waits=[("fd", t1), ("ld_fg", t_fg), ("ld_out%d" % (tt % 2), t_st.get(tt - 2))], sig="fo")
                t_st[tt] = P.op("sp", lambda e, tt=tt: e.dma_start(out=out[tt * 128:(tt + 1) * 128, :], in_=of[tt % 2]),
                                waits=[("fo", t_o[tt])], sig="ld_out%d" % (tt % 2), inc=16)

            t_sqrt2 = {}
            F_PE(0)
            F_PE(1)
            F_A(0)
            for tt in range(NT):
                if tt + 2 < NT:
                    F_PE(tt + 2)
                if tt + 1 < NT:
                    F_A(tt + 1)
                F_B(tt)
            P.op("sp", lambda e: e.nop(), waits=[("ld_out0", t_st[NT - 2]), ("ld_out1", t_st[NT - 1])])

        try:
            plan_all()
        except _Stop:
            pass

        names = sorted(P.cnt.keys())
        sems = {n: es.enter_context(nc.semaphore(n)) for n in names}
        block = es.enter_context(nc.Block())

        def emit(eng, oplist):
            seen = {}
            for fn, waits, sig, inc, attach in oplist:
                pend = [(name, val) for (name, val) in waits if seen.get(name, 0) < val]
                if attach and len(pend) == 1:
                    (name, val) = pend[0]
                    ins = fn(eng)
                    ins._wait_ge(sems[name], val)
                    seen[name] = val
                else:
                    for (name, val) in pend:
                        eng.wait_ge(sems[name], val)
                        seen[name] = val
                    ins = fn(eng)
                if sig is not None:
                    ins.then_inc(sems[sig], inc)

        @block.sync
        def _(eng):
            emit(eng, P.ops["sp"])

        @block.gpsimd
        def _(eng):
            emit(eng, P.ops["pool"])

        @block.tensor
        def _(eng):
            emit(eng, P.ops["pe"])

        @block.scalar
        def _(eng):
            emit(eng, P.ops["act"])

        @block.vector
        def _(eng):
            emit(eng, P.ops["dve"])
    return nc


_CACHE = {}


def kernel(x, c, w_ada, b_ada, norm_g, w_in, sinks, w_out, final_g):
    x = np.asarray(x, np.float32)
    c = np.asarray(c, np.float32)
    w_ada = np.ascontiguousarray(np.asarray(w_ada, np.float32)[0])
    b_ada = np.asarray(b_ada, np.float32)[0]
    norm_g = np.asarray(norm_g, np.float32)[0]
    w_in = np.ascontiguousarray(np.asarray(w_in, np.float32)[0])
    sinks = np.asarray(sinks, np.float32)[0]
    w_out = np.ascontiguousarray(np.asarray(w_out, np.float32)[0])
    final_g = np.asarray(final_g, np.float32)

    def lay(v):
        return np.ascontiguousarray(v.reshape(-1, 128).T)

    bada_l = lay(b_ada[:2048])
    bg_bc = np.ascontiguousarray(np.broadcast_to(b_ada[2048:3072][None, :], (128, D)))
    normg_l = lay(norm_g)
    fg_bc = np.ascontiguousarray(np.broadcast_to(final_g[None, :], (128, D)))
    sinks_l = np.ascontiguousarray(np.stack([np.repeat(sinks[2 * j:2 * j + 2], 64) for j in range(4)], axis=1))
    consts = make_consts()
    if "nc" not in _CACHE:
        _CACHE["nc"] = build_nc()
    nc = _CACHE["nc"]
    in_maps = []
    for b in range(NCORE):
        in_maps.append({
            "x": np.ascontiguousarray(x[b]), "c_l": lay(c[b]), "w_ada": w_ada, "bada_l": bada_l, "bg_bc": bg_bc,
            "normg_l": normg_l, "w_in": w_in, "sinks_l": sinks_l, "w_out": w_out, "fg_bc": fg_bc, "consts": consts,
        })
    res = run_bass_kernel_spmd(nc, in_maps, core_ids=list(range(NCORE)))
    return np.stack([np.asarray(r["out"], np.float32) for r in res.results], axis=0)
```

```python
from contextlib import ExitStack

import numpy as np
import concourse.bass as bass
import concourse.mybir as mybir
from concourse.bass_utils import run_bass_kernel_spmd

F32 = mybir.dt.float32
BF16 = mybir.dt.bfloat16
AF = mybir.ActivationFunctionType
ALU = mybir.AluOpType

S = 4096
D = 1024
NCORE = 8
NT = S // 128
QC = 1024
NEG = -30000.0
NCONST = 128 * 6 + 2048
C_ID, C_TRI, C_TRIC, C_ZERO, C_NEGM, C_ONES, C_SWB = 0, 128, 256, 384, 512, 640, 768


LEVEL = 99
SW_DBG = 0
ATTACH = True


class _Stop(Exception):
    pass


def ckpt(level):
    if LEVEL <= level:
        raise _Stop()


class Plan:
    def __init__(self):
        self.ops = {"pe": [], "act": [], "dve": [], "pool": [], "sp": []}
        self.cnt = {}

    def op(self, eng, fn, waits=(), sig=None, inc=1, attach=False):
        v = None
        if sig is not None:
            self.cnt[sig] = self.cnt.get(sig, 0) + inc
            v = self.cnt[sig]
        ws = tuple(w for w in waits if w is not None and w[1] is not None and w[1] > 0)
        self.ops[eng].append((fn, ws, sig, inc, attach and ATTACH))
        return v


def make_consts():
    c = np.zeros((128, NCONST), np.float32)
    j = np.arange(128)[:, None]
    s = np.arange(128)[None, :]
    c[:, C_ID:C_ID + 128] = (j == s)
    c[:, C_TRI:C_TRI + 128] = -1.0 * (j >= s)
    c[:, C_TRIC:C_TRIC + 128] = -1.0 * (j < s)
    c[:, C_NEGM:C_NEGM + 128] = np.where(j < s, 0.0, NEG)
    c[:, C_ONES:C_ONES + 128] = 1.0
    for h in range(8):
        m = 2.0 ** (-8.0 * (h + 1) / 8)
        rel_cur = (s - j).astype(np.float32)
        cur = np.where(s >= j, -m * rel_cur, NEG)
        rel_prev = (128 + s - j).astype(np.float32)
        prev = np.where(j > s, -m * rel_prev, NEG)
        c[:, C_SWB + h * 256:C_SWB + h * 256 + 128] = cur
        c[:, C_SWB + h * 256 + 128:C_SWB + h * 256 + 256] = prev
    return c


def sb_tiles():
    out = []
    for qc in range(S // QC):
        nkb = (QC // 128) * (qc + 1)
        for kb in range(nkb - 1, -1, -1):
            jd = kb - (QC // 128) * qc
            diag = jd >= 0
            c0 = 128 * jd if diag else 0
            out.append((qc, kb, c0, diag, kb == nkb - 1, kb == 0))
    return out


def col_segs(c0, c1=QC):
    segs = []
    a = c0
    while a < c1:
        b = min(c1, (a // 512 + 1) * 512)
        segs.append((a, b))
        a = b
    return segs


def build_nc():
    nc = bass.Bass("TRN2", target_bir_lowering=False)
    x = nc.dram_tensor("x", [S, D], F32, kind="ExternalInput").ap()
    c_l = nc.dram_tensor("c_l", [128, 8], F32, kind="ExternalInput").ap()
    w_ada = nc.dram_tensor("w_ada", [D, 3 * D], F32, kind="ExternalInput").ap()
    bada_l = nc.dram_tensor("bada_l", [128, 16], F32, kind="ExternalInput").ap()
    bg_bc_d = nc.dram_tensor("bg_bc", [128, D], F32, kind="ExternalInput").ap()
    normg_l = nc.dram_tensor("normg_l", [128, 8], F32, kind="ExternalInput").ap()
    w_in = nc.dram_tensor("w_in", [D, 3328], F32, kind="ExternalInput").ap()
    sinks_l = nc.dram_tensor("sinks_l", [128, 4], F32, kind="ExternalInput").ap()
    w_out = nc.dram_tensor("w_out", [D, D], F32, kind="ExternalInput").ap()
    fg_bc_d = nc.dram_tensor("fg_bc", [128, D], F32, kind="ExternalInput").ap()
    consts_d = nc.dram_tensor("consts", [128, NCONST], F32, kind="ExternalInput").ap()
    out = nc.dram_tensor("out", [S, D], F32, kind="ExternalOutput").ap()

    P = Plan()
    es = ExitStack()
    with es:
        def sb(name, shape, dt):
            return es.enter_context(nc.sbuf_tensor(name, shape, dt))

        ygT = sb("ygT", [128, 8, S], BF16)
        hT = sb("hT", [128, 8, S], BF16)
        ov = sb("ov", [128, 30720], BF16)
        cst = sb("cst", [128, NCONST], BF16)
        gate_bc = sb("gate_bc", [128, D], F32)
        c_sb = sb("c_sb", [128, 8], F32)
        etmp = sb("etmp", [128, 8], F32)
        cond = sb("cond", [128, 8], F32)
        bada = sb("bada", [128, 16], F32)
        normg = sb("normg", [128, 8], F32)
        mod_sb = sb("mod_sb", [128, 16], F32)
        gs = sb("gs", [128, 8], F32)
        ones_f = sb("ones_f", [128, 128], F32)
        ss = sb("ss", [128, 32], F32)
        rstd = sb("rstd", [128, 32], F32)
        ss2 = sb("ss2", [128, 32], F32)
        rstd2 = sb("rstd2", [128, 32], F32)
        snk = sb("snk", [128, 4], F32)
        esk = sb("esk", [128, 4], F32)
        eps_t = sb("eps_t", [128, 1], F32)
        ps = es.enter_context(nc.psum_tensor("ps", [128, 4096], F32))

        qT = ov[:, 0:4096]
        kT = ov[:, 4096:8192]
        sg = ov[:, 8192:12288]
        vv = ov[:, 12288:16384].rearrange("p (b c) -> p b c", c=128)
        wsl = ov[:, 16384:20480].rearrange("p (k c) -> p k c", c=512)
        PB = 20480
        Eb = [ov[:, PB + i * 1024:PB + (i + 1) * 1024] for i in range(3)]
        Gb = [ov[:, PB + (3 + i) * 1024:PB + (4 + i) * 1024] for i in range(3)]
        Pb = [ov[:, PB + (6 + i) * 1024:PB + (7 + i) * 1024] for i in range(2)]
        Ab = [ov[:, PB + (8 + i) * 1024:PB + (9 + i) * 1024] for i in range(2)]
        Psw = [ov[:, PB + i * 512:PB + (i + 1) * 512] for i in range(2)]
        rden = ov[:, PB + 1024:PB + 2048].bitcast(F32)
        ytmp = ov[:, PB + 2048:PB + 3072].bitcast(F32)
        yflat = ygT[:, :, :].rearrange("p a b -> p (a b)")
        wada = [yflat[:, i * 6144:(i + 1) * 6144].bitcast(F32) for i in range(4)]
        xt = [ov[:, 12288 + i * 2048:12288 + (i + 1) * 2048].bitcast(F32) for i in range(4)]
        xnb = [[ov[:, 20480 + (g * 4 + t) * 1024:20480 + (g * 4 + t + 1) * 1024] for t in range(4)] for g in range(2)]
        junk = ov[:, 28672:29696]
        hflat = hT[:, :, :].rearrange("p a b -> p (a b)")
        condb = yflat[:, 24576:26624].bitcast(F32).rearrange("p (k m) -> p k m", m=128)
        bg_bc = yflat[:, 26624:28672].bitcast(F32)
        wout = hflat[:, 0:8192].rearrange("p (k e) -> p k e", e=1024)
        xf = [hflat[:, 8192 + i * 2048:8192 + (i + 1) * 2048].bitcast(F32) for i in range(3)]
        rf = [hflat[:, 14336 + i * 2048:14336 + (i + 1) * 2048].bitcast(F32) for i in range(2)]
        of = [hflat[:, 18432 + i * 2048:18432 + (i + 1) * 2048].bitcast(F32) for i in range(2)]
        fg_bc = hflat[:, 22528:24576].bitcast(F32)
        junkf = hflat[:, 24576:25600]
        Zp = ps[:, 0:1024]
        Rp = ps[:, 1024:2048]
        Yp = [ps[:, 2048:3072], ps[:, 3072:4096]]
        pp = [ps[:, 0:512], ps[:, 512:1024]]
        tp = [ps[:, i * 512:(i + 1) * 512].bitcast(BF16)[:, 0:512] for i in range(2)]
        modps = ps[:, 1024:1040]
        gateps = ps[:, 2048:3072]
        po = [[ps[:, t * 1024 + e * 512:t * 1024 + (e + 1) * 512] for e in range(2)] for t in range(2)]

        ident = cst[:, C_ID:C_ID + 128]
        triN = cst[:, C_TRI:C_TRI + 128]
        tricN = cst[:, C_TRIC:C_TRIC + 128]
        zer = cst[:, C_ZERO:C_ZERO + 128]
        negm = cst[:, C_NEGM:C_NEGM + 128]
        ones_bf = cst[:, C_ONES:C_ONES + 128]

        def plan_all():
            t_cst = P.op("pool", lambda e: e.dma_start(out=cst[:, :], in_=consts_d[:, :]), sig="ld_cst", inc=16)
            for dst, src in ((c_sb, c_l), (bada, bada_l), (normg, normg_l), (snk, sinks_l)):
                t_small = P.op("sp", lambda e, dst=dst, src=src: e.dma_start(out=dst[:, :], in_=src[:, :]), sig="ld_small", inc=16)
            t_small = P.op("sp", lambda e: e.dma_start(out=bg_bc, in_=bg_bc_d[:, :]), sig="ld_small", inc=16)

            ckpt(0)
            P.op("dve", lambda e: e.memset(ones_f[:, :], 1.0), sig="dve0")
            P.op("dve", lambda e: e.memset(eps_t[:, :], 1e-6), sig="dve0")
            P.op("dve", lambda e: e.memset(ss[:, :], 0.0), sig="dve0")
            t_d = P.op("dve", lambda e: e.memset(ss2[:, :], 0.0), sig="dve0")
            t_ms = t_d
            t_a = P.op("act", lambda e: e.activation(out=etmp[:, :], in_=c_sb[:, :], func=AF.Exp, scale=-1.0),
                       waits=[("ld_small", t_small)], sig="act0")
            t_d = P.op("dve", lambda e: e.tensor_scalar_add(out=etmp[:, :], in0=etmp[:, :], scalar1=1.0),
                       waits=[("act0", t_a), ("dve0", t_d)], sig="dve0")
            t_d = P.op("dve", lambda e: e.reciprocal(out=etmp[:, :], in_=etmp[:, :]), waits=[("dve0", t_d)], sig="dve0")
            t_d = P.op("dve", lambda e: e.tensor_mul(out=cond[:, :], in0=c_sb[:, :], in1=etmp[:, :]),
                       waits=[("dve0", t_d)], sig="dve0")
            t_cond = t_d
            for k in range(8):
                t_d = P.op("dve", lambda e, k=k: e.tensor_scalar(out=condb[:, k, :], in0=ones_f[:, :], scalar1=cond[:, k:k + 1],
                                                                 scalar2=None, op0=ALU.mult),
                           waits=[("dve0", t_cond)], sig="dve0")
            t_condb = t_d
            t_pe0 = {}
            t_wada = {}
            t_xld = {}
            t_sq = {}
            t_xn = {}
            t_tp = {}
            t_ev = {}
            NWB = 4

            def issue_wada(k):
                t_wada[k] = P.op("sp", lambda e, k=k: e.dma_start(out=wada[k % NWB], in_=w_ada[k * 128:(k + 1) * 128, :]),
                                 waits=[("pe0", t_pe0.get(k - NWB))], sig="ld_wada%d" % (k % NWB), inc=16)

            def issue_xload(tt):
                t_xld[tt] = P.op("sp", lambda e, tt=tt: e.dma_start(out=xt[tt % 4], in_=x[tt * 128:(tt + 1) * 128, :]),
                                 waits=[("p1xn", t_xn.get(tt - 4)), ("p1sq", t_sq.get(tt - 4))],
                                 sig="ld_x%d" % (tt % 4), inc=16)

            for k in range(NWB):
                issue_wada(k)
            for tt in range(4):
                issue_xload(tt)
            for k in range(8):
                for j in range(16):
                    P.op("pe", lambda e, k=k, j=j: e.matmul(modps[:, j:j + 1], lhsT=wada[k % NWB][:, j * 128:(j + 1) * 128],
                                                            rhs=cond[:, k:k + 1], start=(k == 0 and j == 0), stop=(k == 7),
                                                            skip_group_check=True),
                         waits=[("ld_wada%d" % (k % NWB), t_wada[k]), ("dve0", t_condb)])
                for eh in range(2):
                    t_pe0[k] = P.op("pe", lambda e, k=k, eh=eh: e.matmul(gateps[:, eh * 512:(eh + 1) * 512], lhsT=condb[:, k, :],
                                                                         rhs=wada[k % NWB][:, 2048 + eh * 512:2048 + (eh + 1) * 512],
                                                                         start=(k == 0), stop=(k == 7)),
                                    waits=[("ld_wada%d" % (k % NWB), t_wada[k]), ("dve0", t_condb)], sig=("pe0" if eh == 1 else None))
                if k + NWB < 8:
                    issue_wada(k + NWB)
                g = k
                for t4 in range(4):
                    tt = g * 4 + t4
                    t_sq[tt] = P.op("act", lambda e, tt=tt: e.activation(out=junk, in_=xt[tt % 4], func=AF.Square,
                                                                         accum_out=ss[:, tt:tt + 1]),
                                    waits=[("ld_x%d" % (tt % 4), t_xld[tt]), ("dve0", t_ms)], sig="p1sq")
                    t_r = P.op("act", lambda e, tt=tt: e.activation(out=rstd[:, tt:tt + 1], in_=ss[:, tt:tt + 1], func=AF.Sqrt,
                                                                    scale=1.0 / D, bias=eps_t[:, 0:1]),
                               waits=[("p1sq", t_sq[tt])], sig="p1sqrt")
                    t_r = P.op("dve", lambda e, tt=tt: e.reciprocal(out=rstd[:, tt:tt + 1], in_=rstd[:, tt:tt + 1]),
                               waits=[("p1sqrt", t_r)], sig="dve1")
                    t_xn[tt] = P.op("dve", lambda e, tt=tt, g=g, t4=t4: e.tensor_scalar(
                        out=xnb[g % 2][t4], in0=xt[tt % 4], scalar1=rstd[:, tt:tt + 1], scalar2=None, op0=ALU.mult),
                        waits=[("dve1", t_r), ("p1tp", t_tp.get((g - 2, 7)))], sig="p1xn")
                    if tt + 4 < NT:
                        issue_xload(tt + 4)
                for j in range(8):
                    prev_ev = t_ev[(g, j - 2)] if j >= 2 else (t_ev[(g - 1, 6 + j)] if g >= 1 else None)
                    for t4 in range(4):
                        t_tp[(g, j)] = P.op("pe", lambda e, g=g, j=j, t4=t4: e.transpose(
                            out=tp[j % 2][:, t4 * 128:(t4 + 1) * 128], in_=xnb[g % 2][t4][:, j * 128:(j + 1) * 128], identity=ident),
                            waits=[("p1xn", t_xn[g * 4 + 3]), ("p1ev", prev_ev), ("ld_cst", t_cst)],
                            sig=("p1tp" if t4 == 3 else None))
                    t_ev[(g, j)] = P.op("act", lambda e, g=g, j=j: e.activation(
                        out=hT[:, j, g * 512:(g + 1) * 512], in_=tp[j % 2], func=AF.Identity),
                        waits=[("p1tp", t_tp[(g, j)])], sig="p1ev")
            t_d = P.op("dve", lambda e: e.tensor_add(out=mod_sb[:, :], in0=modps, in1=bada[:, :]),
                       waits=[("pe0", t_pe0[7]), ("ld_small", t_small)], sig="dve0")
            t_d = P.op("dve", lambda e: e.scalar_tensor_tensor(out=gs[:, :], in0=mod_sb[:, 8:16], scalar=1.0, in1=normg[:, :],
                                                               op0=ALU.add, op1=ALU.mult),
                       waits=[("dve0", t_d)], sig="dve0")
            t_d = P.op("dve", lambda e: e.tensor_add(out=gate_bc[:, :], in0=gateps, in1=bg_bc), sig="dve0")
            t_mod = t_d
            for j in range(8):
                for hh in range(2):
                    t_fix = P.op("dve", lambda e, j=j, hh=hh: e.tensor_scalar(
                        out=hT[:, j, hh * 2048:(hh + 1) * 2048], in0=hT[:, j, hh * 2048:(hh + 1) * 2048],
                        scalar1=gs[:, j:j + 1], scalar2=mod_sb[:, j:j + 1], op0=ALU.mult, op1=ALU.add),
                        waits=[("dve0", t_mod), ("p1ev", t_ev[(7, 7)])], sig="hfix")
            t_hT = t_fix
            ckpt(2)
            t_wsl = None
            t_projpe = None
            t_attpe = None
            t_attdve = None
            t_pev = {}
            nproj = 0
            t_esk = P.op("act", lambda e: e.activation(out=esk[:, :], in_=snk[:, :], func=AF.Exp),
                         waits=[("ld_small", t_small)], sig="act0")
            sbt = sb_tiles()
            sbi = 0
            tk = {}
            n_chunk = 0
            t_evy = {}
            swn = 0
            swg = 0
            t_sw = {}

            for step in range(8):
                is_sb = step < 4
                if is_sb:
                    cq, ck, cv, cg = step * 128, 512 + step * 128, 1024 + step * 128, 1536 + step * 128
                    kw = 128
                    vw = 128
                    do_kv = True
                else:
                    j = step - 4
                    cq, ck, cv, cg = 2048 + j * 128, 2560 + (j // 2) * 64, 2688 + (j // 2) * 64, 2816 + j * 128
                    kw = 64
                    vw = 64
                    do_kv = (j % 2 == 0)
                wv = w_in.rearrange("(k p) c -> p k c", p=128)
                dmas = [(0, cq, 128), (256, cg, 128)]
                if do_kv:
                    if kw == 128:
                        dmas.append((128, ck, 128))
                    else:
                        dmas.append((128, ck, 64))
                        dmas.append((192, ck, 64))
                    if vw == 128:
                        dmas.append((384, cv, 128))
                    else:
                        dmas.append((384, cv, 64))
                        dmas.append((448, cv, 64))
                        vw = 128
                for (dc, sc, w) in dmas:
                    t_wsl = P.op("pool", lambda e, dc=dc, sc=sc, w=w: e.dma_start(out=wsl[:, :, dc:dc + w], in_=wv[:, :, sc:sc + w]),
                                 waits=[("projpe", t_projpe), ("hfix", t_hT)], sig="ld_wsl", inc=16)
                ckpt(2.5 + 2 * step)
                kinds = [("q", 0), ("g", 256)] + ([("k", 128)] if do_kv else [])
                for kind, wc in kinds:
                    for tc in range(8):
                        par = nproj % 2
                        for kc in range(8):
                            last = kc == 7
                            t_projpe_new = P.op("pe", lambda e, par=par, kc=kc, wc=wc, tc=tc: e.matmul(
                                pp[par], lhsT=wsl[:, kc, wc:wc + 128], rhs=hT[:, kc, tc * 512:(tc + 1) * 512],
                                start=(kc == 0), stop=(kc == 7)),
                                waits=[("ld_wsl", t_wsl), t_pev.get(nproj - 2), ("hfix", t_hT),
                                       t_attdve],
                                sig=("projpe" if last else None))
                        t_projpe = t_projpe_new
                        dst = {"q": qT, "k": kT, "g": sg}[kind][:, tc * 512:(tc + 1) * 512]
                        if kind == "q":
                            t_pev[nproj] = ("pev", P.op("dve", lambda e, dst=dst, par=par: e.tensor_scalar(
                                out=dst, in0=pp[par], scalar1=0.125, scalar2=None, op0=ALU.mult),
                                waits=[("projpe", t_projpe)], sig="pev"))
                            last_dve_pev = t_pev[nproj]
                        elif kind == "k":
                            t_pev[nproj] = ("pev", P.op("dve", lambda e, dst=dst, par=par: e.tensor_copy(out=dst, in_=pp[par]),
                                                waits=[("projpe", t_projpe)], sig="pev"))
                            last_dve_pev = t_pev[nproj]
                        else:
                            t_pev[nproj] = ("pevA", P.op("act", lambda e, dst=dst, par=par: e.activation(out=dst, in_=pp[par], func=AF.Silu),
                                                         waits=[("projpe", t_projpe), t_attdve], sig="pevA"))
                            last_act_pev = t_pev[nproj]
                        nproj += 1
                ckpt(2.75 + 2 * step)
                if do_kv:
                    for tg in range(8):
                        par = nproj % 2
                        for t4 in range(4):
                            tok = (tg * 4 + t4) * 128
                            for kc in range(8):
                                last = (kc == 7 and t4 == 3)
                                t_projpe_new = P.op("pe", lambda e, par=par, kc=kc, tok=tok, t4=t4, vw=vw: e.matmul(
                                    pp[par][:, t4 * 128:t4 * 128 + vw], lhsT=hT[:, kc, tok:tok + 128], rhs=wsl[:, kc, 384:384 + vw],
                                    start=(kc == 0), stop=(kc == 7)),
                                    waits=[("ld_wsl", t_wsl), t_pev.get(nproj - 2), t_attdve],
                                    sig=("projpe" if last else None))
                        t_projpe = t_projpe_new
                        t_pev[nproj] = ("pev", P.op("dve", lambda e, par=par, tg=tg, vw=vw: e.tensor_copy(
                            out=vv[:, tg * 4:(tg + 1) * 4, 0:vw],
                            in_=pp[par].rearrange("p (a b) -> p a b", a=4)[:, :, 0:vw]),
                            waits=[("projpe", t_projpe)], sig="pev"))
                        last_dve_pev = t_pev[nproj]
                        nproj += 1
                projw = [last_dve_pev, last_act_pev]
                ckpt(3 + 2 * step)

                if is_sb:
                    tiles = []
                    for hh in range(2):
                        for (qc, kb, c0, diag, cs, ce) in sbt:
                            tiles.append((hh * 64, qc, kb, c0, diag, cs, ce))
                    T = len(tiles)
                    base = sbi
                    chunk_of = {}
                    cc = n_chunk - 1
                    for i, tl in enumerate(tiles):
                        if tl[5]:
                            cc += 1
                        chunk_of[i] = cc

                    def QK(i):
                        hp, qc, kb, c0, diag, cs, ce = tiles[i]
                        g = base + i
                        segs = col_segs(c0)
                        for si, (a, b) in enumerate(segs):
                            lastseg = (si == len(segs) - 1) and not diag
                            v = P.op("pe", lambda e, hp=hp, kb=kb, qc=qc, a=a, b=b, diag=diag: e.matmul(
                                Zp[:, a:b], lhsT=kT[hp:hp + 64, kb * 128:(kb + 1) * 128],
                                rhs=qT[hp:hp + 64, qc * QC + a:qc * QC + b], start=True, stop=not (diag and a == c0),
                                skip_group_check=True),
                                waits=[("sbE", tk.get(("E", g - 1)))] + (projw if i < 2 else []),
                                sig=("sbQK" if lastseg else None), attach=True)
                            if lastseg:
                                tk[("QK", g)] = v
                        if diag:
                            tk[("QK", g)] = P.op("pe", lambda e, c0=c0: e.matmul(
                                Zp[:, c0:c0 + 128], lhsT=ident, rhs=negm, start=False, stop=True, skip_group_check=True),
                                sig="sbQK")

                    def ACT_E(i):
                        hp, qc, kb, c0, diag, cs, ce = tiles[i]
                        g = base + i
                        tk[("E", g)] = P.op("act", lambda e, g=g, c0=c0: e.activation(out=Eb[g % 3][:, c0:QC], in_=Zp[:, c0:QC], func=AF.Exp),
                                            waits=[("sbQK", tk[("QK", g)]), ("sbA", tk.get(("A", g - 3)))], sig="sbE")

                    def ACT_G(i):
                        hp, qc, kb, c0, diag, cs, ce = tiles[i]
                        g = base + i
                        tk[("G", g)] = P.op("act", lambda e, g=g, c0=c0: e.activation(out=Gb[g % 3][:, c0:QC], in_=Eb[g % 3][:, c0:QC],
                                                                                   func=AF.Ln, bias=1.0),
                                            waits=[("sbE", tk[("E", g)]), ("sbTRIC", tk.get(("TRIC", g - 3)))], sig="sbG")

                    def ACT_P(i):
                        hp, qc, kb, c0, diag, cs, ce = tiles[i]
                        g = base + i
                        tk[("P", g)] = P.op("act", lambda e, g=g, c0=c0: e.activation(out=Pb[g % 2][:, c0:QC], in_=Rp[:, c0:QC], func=AF.Exp),
                                            waits=[("sbTRI", tk[("TRI", g)]), ("sbA", tk.get(("A", g - 2)))], sig="sbP")

                    def PE_TRI(i):
                        hp, qc, kb, c0, diag, cs, ce = tiles[i]
                        g = base + i
                        if cs:
                            par = chunk_of[i] % 2
                            for (a, b) in ((0, 512), (512, 1024)):
                                P.op("pe", lambda e, a=a, b=b: e.matmul(Rp[:, a:b], lhsT=zer, rhs=cst[:, 0:512], start=True, stop=False,
                                                                        skip_group_check=True),
                                     waits=[("sbP", tk.get(("P", g - 1)))])
                                P.op("pe", lambda e, a=a, b=b, hp=hp, par=par: e.matmul(
                                    Yp[par][hp:hp + 64, a:b], lhsT=zer[:, 0:64], rhs=cst[:, 0:512], start=True, stop=False,
                                    skip_group_check=True),
                                    waits=[("sbEV", t_evy.get(chunk_of[i] - 2))])
                        segs = col_segs(c0)
                        for si, (a, b) in enumerate(segs):
                            v = P.op("pe", lambda e, g=g, a=a, b=b: e.matmul(Rp[:, a:b], lhsT=triN, rhs=Gb[g % 3][:, a:b], start=False, stop=False,
                                                                            skip_group_check=True),
                                     waits=[("sbG", tk[("G", g)]), ("sbP", tk.get(("P", g - 1)))],
                                     sig=("sbTRI" if si == len(segs) - 1 else None), attach=True)
                        tk[("TRI", g)] = v

                    def PE_TRIC(i):
                        hp, qc, kb, c0, diag, cs, ce = tiles[i]
                        g = base + i
                        segs = col_segs(c0)
                        for si, (a, b) in enumerate(segs):
                            v = P.op("pe", lambda e, g=g, a=a, b=b: e.matmul(Rp[:, a:b], lhsT=tricN, rhs=Gb[g % 3][:, a:b], start=False, stop=False,
                                                                            skip_group_check=True),
                                     waits=[("sbP", tk[("P", g)])],
                                     sig=("sbTRIC" if si == len(segs) - 1 else None), attach=True)
                        tk[("TRIC", g)] = v

                    def PE_PV(i):
                        hp, qc, kb, c0, diag, cs, ce = tiles[i]
                        g = base + i
                        par = chunk_of[i] % 2
                        segs = col_segs(c0)
                        for si, (a, b) in enumerate(segs):
                            v = P.op("pe", lambda e, g=g, a=a, b=b, hp=hp, kb=kb, par=par: e.matmul(
                                Yp[par][hp:hp + 64, a:b], lhsT=vv[:, kb, hp:hp + 64], rhs=Ab[g % 2][:, a:b], start=False, stop=False,
                                skip_group_check=True),
                                waits=[("sbA", tk[("A", g)])],
                                sig=("sbPV" if si == len(segs) - 1 else None), attach=True)
                        tk[("PV", g)] = v

                    def DVE_A(i):
                        hp, qc, kb, c0, diag, cs, ce = tiles[i]
                        g = base + i
                        tk[("A", g)] = P.op("dve", lambda e, g=g, c0=c0: e.tensor_mul(out=Ab[g % 2][:, c0:QC], in0=Eb[g % 3][:, c0:QC],
                                                                                   in1=Pb[g % 2][:, c0:QC]),
                                            waits=[("sbP", tk[("P", g)]), ("sbPV", tk.get(("PV", g - 2)))], sig="sbA")

                    def DVE_EV(i):
                        hp, qc, kb, c0, diag, cs, ce = tiles[i]
                        g = base + i
                        ch = chunk_of[i]
                        par = ch % 2
                        t_evy[ch] = P.op("dve", lambda e, hp=hp, qc=qc, par=par, step=step: e.tensor_mul(
                            out=ygT[hp:hp + 64, step, qc * QC:(qc + 1) * QC], in0=Yp[par][hp:hp + 64, :],
                            in1=sg[hp:hp + 64, qc * QC:(qc + 1) * QC]),
                            waits=[("sbPV", tk[("PV", g)])], sig="sbEV")

                    QK(0)
                    ACT_E(0)
                    if T > 1:
                        QK(1)
                    ACT_G(0)
                    if T > 1:
                        ACT_E(1)
                    if T > 2:
                        QK(2)
                    PE_TRI(0)
                    for i in range(T):
                        ACT_P(i)
                        PE_TRIC(i)
                        if i + 1 < T:
                            ACT_G(i + 1)
                            PE_TRI(i + 1)
                        DVE_A(i)
                        PE_PV(i)
                        if tiles[i][6]:
                            DVE_EV(i)
                        if i + 2 < T:
                            ACT_E(i + 2)
                        if i + 3 < T:
                            QK(i + 3)
                    sbi += T
                    n_chunk = cc + 1
                    t_attdve = ("sbEV", t_evy[cc])
                else:
                    j = step - 4
                    heads = (2 * j, 2 * j + 1)
                    swn0 = swn

                    def SW_QK(n):
                        gi = swn0 + n
                        zbase = (gi % 2) * 1024
                        wz = 256 if n > 0 else 128
                        for which, kblk in ((0, n), (1, n - 1)):
                            if kblk < 0:
                                continue
                            for hh in range(2):
                                hp = hh * 64
                                zc = zbase + hh * 512 + which * 128
                                P.op("pe", lambda e, zc=zc, hp=hp, kblk=kblk, n=n, which=which: e.matmul(
                                    ps[:, zc:zc + 128], lhsT=kT[hp:hp + 64, kblk * 128:(kblk + 1) * 128],
                                    rhs=qT[hp:hp + 64, n * 128:(n + 1) * 128], start=(which == 0), stop=False, skip_group_check=True),
                                    waits=[("swP", t_sw.get(("P", gi - 2)))] + (projw if n < 2 else []))
                        for hh in range(2):
                            h = heads[hh]
                            bc = C_SWB + h * 256
                            zc = zbase + hh * 512
                            v = P.op("pe", lambda e, zc=zc, bc=bc, wz=wz: e.matmul(
                                ps[:, zc:zc + wz], lhsT=ident, rhs=cst[:, bc:bc + wz], start=False, stop=True, skip_group_check=True),
                                sig=("swQK" if hh == 1 else None))
                        t_sw[("QK", gi)] = v

                    def SW_ACT(n):
                        gi = swn0 + n
                        zbase = (gi % 2) * 1024
                        wz = 256 if n > 0 else 128
                        zin = ps[:, zbase:zbase + 1024].rearrange("p (b c) -> p b c", b=2)[:, :, 0:wz]
                        pout = Psw[gi % 2].rearrange("p (b c) -> p b c", b=2)[:, :, 0:wz]
                        t_sw[("P", gi)] = P.op("act", lambda e, zin=zin, pout=pout: e.activation(out=pout, in_=zin, func=AF.Exp),
                                               waits=[("swQK", t_sw[("QK", gi)]), ("swPV", t_sw.get(("PV", gi - 2)))], sig="swP")

                    def SW_PVD(n):
                        gi = swn0 + n
                        grp = swg + n // 4
                        par = grp % 2
                        Yb = Yp[par][:, 0:512]
                        Db = Yp[par][:, 512:1024]
                        col = (n % 4) * 128
                        for hh in range(2):
                            hp = hh * 64
                            srcs = [(hh * 256, n)] + ([(hh * 256 + 128, n - 1)] if n > 0 else [])
                            ns = len(srcs)
                            for si, (pc, kblk) in enumerate(srcs):
                                P.op("pe", lambda e, Yb=Yb, hp=hp, col=col, kblk=kblk, gi=gi, pc=pc, si=si, ns=ns: e.matmul(
                                    Yb[hp:hp + 64, col:col + 128], lhsT=vv[:, kblk, 0:64], rhs=Psw[gi % 2][:, pc:pc + 128],
                                    start=(si == 0), stop=(si == ns - 1), skip_group_check=True),
                                    waits=[("swP", t_sw[("P", gi)]), ("swEV", t_sw.get(("EV", grp - 2)))])
                            for si, (pc, kblk) in enumerate(srcs):
                                v = P.op("pe", lambda e, Db=Db, hp=hp, col=col, gi=gi, pc=pc, si=si, ns=ns: e.matmul(
                                    Db[hp:hp + 64, col:col + 128], lhsT=ones_bf[:, 0:64], rhs=Psw[gi % 2][:, pc:pc + 128],
                                    start=(si == 0), stop=(si == ns - 1), skip_group_check=True),
                                    sig=("swPV" if (hh == 1 and si == ns - 1) else None))
                        t_sw[("PV", gi)] = v
                    def SW_EV(n):
                        gi = swn0 + n
                        grp = swg + n // 4
                        par = grp % 2
                        Yb = Yp[par][:, 0:512]
                        Db = Yp[par][:, 512:1024]
                        if n % 4 == 3:
                            q0 = (n - 3) * 128
                            t1 = P.op("act", lambda e, Db=Db, j=j: e.activation(out=rden, in_=Db, func=AF.Ln, bias=esk[:, j:j + 1]),
                                      waits=[("swPV", t_sw[("PV", gi)]), ("act0", t_esk), ("swEV", t_sw.get(("EV", grp - 1)))], sig="swDa")
                            t1 = P.op("act", lambda e: e.activation(out=rden, in_=rden, func=AF.Exp, scale=-1.0),
                                      waits=[("swDa", t1)], sig="swDa")
                            t1 = P.op("dve", lambda e, Yb=Yb: e.tensor_mul(out=ytmp, in0=Yb, in1=rden), waits=[("swDa", t1)], sig="swD")
                            t_sw[("EV", grp)] = P.op("dve", lambda e, q0=q0, step=step: e.tensor_mul(
                                out=ygT[:, step, q0:q0 + 512], in0=ytmp, in1=sg[:, q0:q0 + 512]),
                                waits=[("swD", t1)], sig="swEV")

                    SW_QK(0)
                    for n in range(NT):
                        SW_ACT(n)
                        if n >= 1 and (n - 1) % 4 == 3:
                            SW_EV(n - 1)
                        if n + 1 < NT:
                            SW_QK(n + 1)
                        SW_PVD(n)
                    SW_EV(NT - 1)
                    swn += NT
                    swg += NT // 4
                    t_last_sw = t_sw[("EV", swg - 1)]
                if not is_sb:
                    t_attdve = ("swEV", t_last_sw)
                ckpt(4 + 2 * step)

            ckpt(20)
            wo_v = w_out.rearrange("(k p) e -> p k e", p=128)
            t_wo = None
            for k in range(8):
                t_wo = P.op("pool", lambda e, k=k: e.dma_start(out=wout[:, k, :], in_=wo_v[:, k, :]),
                            waits=[("projpe", t_projpe)], sig="ld_wo", inc=16)
            t_fg = P.op("sp", lambda e: e.dma_start(out=fg_bc, in_=fg_bc_d[:, :]), waits=[("projpe", t_projpe)], sig="ld_fg", inc=16)
            t_xf = {}
            t_o = {}
            t_st = {}
            t_r2 = {}
            t_sq2 = {}

            def issue_xf(tt):
                t_xf[tt] = P.op("sp", lambda e, tt=tt: e.dma_start(out=xf[tt % 3], in_=x[tt * 128:(tt + 1) * 128, :]),
                                waits=[("fr2", t_r2.get(tt - 3)), ("projpe", t_projpe)], sig="ld_xf%d" % (tt % 3), inc=16)

            t_po = {}
            for tt in range(3):
                issue_xf(tt)
            def F_PE(tt):
                for eh in range(2):
                    for kc in range(8):
                        v = P.op("pe", lambda e, tt=tt, eh=eh, kc=kc: e.matmul(
                            po[tt % 2][eh], lhsT=ygT[:, kc, tt * 128:(tt + 1) * 128], rhs=wout[:, kc, eh * 512:(eh + 1) * 512],
                            start=(kc == 0), stop=(kc == 7)),
                            waits=[("ld_wo", t_wo), t_attdve, ("fr2", t_r2.get(tt - 2))],
                            sig=("fpo" if kc == 7 else None))
                    t_po[(tt, eh)] = v

            def F_A(tt):
                rb = rf[tt % 2]
                for eh in range(2):
                    t1 = P.op("dve", lambda e, tt=tt, eh=eh, rb=rb: e.tensor_mul(out=rb[:, eh * 512:(eh + 1) * 512], in0=po[tt % 2][eh],
                                                                               in1=gate_bc[:, eh * 512:(eh + 1) * 512]),
                              waits=[("fpo", t_po[(tt, eh)]), ("fo", t_o.get(tt - 2)), ("fsq", t_sq2.get(tt - 2))], sig="fd")
                t_r2[tt] = P.op("dve", lambda e, tt=tt, rb=rb: e.tensor_add(out=rb, in0=rb, in1=xf[tt % 3]),
                                waits=[("fd", t1), ("ld_xf%d" % (tt % 3), t_xf[tt])], sig="fr2")
                if tt + 3 < NT:
                    issue_xf(tt + 3)
                t_sq2[tt] = P.op("act", lambda e, tt=tt, rb=rb: e.activation(out=junkf, in_=rb, func=AF.Square, accum_out=ss2[:, tt:tt + 1]),
                                 waits=[("fr2", t_r2[tt])], sig="fsq")
                t_sqrt2[tt] = P.op("act", lambda e, tt=tt: e.activation(out=rstd2[:, tt:tt + 1], in_=ss2[:, tt:tt + 1], func=AF.Sqrt,
                                                                        scale=1.0 / D, bias=eps_t[:, 0:1]),
                                   waits=[("fsq", t_sq2[tt])], sig="fsqrt")

            def F_B(tt):
                rb = rf[tt % 2]
                t1 = P.op("dve", lambda e, tt=tt: e.reciprocal(out=rstd2[:, tt:tt + 1], in_=rstd2[:, tt:tt + 1]),
                          waits=[("fsqrt", t_sqrt2[tt])], sig="fd")
                t_o[tt] = P.op("dve", lambda e, tt=tt, rb=rb: e.scalar_tensor_tensor(
                    out=of[tt % 2], in0=rb, scalar=rstd2[:, tt:tt + 1], in1=fg_bc, op0=ALU.mult, op1=ALU.mult),
                    waits=[("fd", t1), ("ld_fg", t_fg), ("ld_out%d" % (tt % 2), t_st.get(tt - 2))], sig="fo")
                t_st[tt] = P.op("sp", lambda e, tt=tt: e.dma_start(out=out[tt * 128:(tt + 1) * 128, :], in_=of[tt % 2]),
                                waits=[("fo", t_o[tt])], sig="ld_out%d" % (tt % 2), inc=16)

            t_sqrt2 = {}
            F_PE(0)
            F_PE(1)
            F_A(0)
            for tt in range(NT):
                if tt + 2 < NT:
                    F_PE(tt + 2)
                if tt + 1 < NT:
                    F_A(tt + 1)
                F_B(tt)
            P.op("sp", lambda e: e.nop(), waits=[("ld_out0", t_st[NT - 2]), ("ld_out1", t_st[NT - 1])])

        try:
            plan_all()
        except _Stop:
            pass

        names = sorted(P.cnt.keys())
        sems = {n: es.enter_context(nc.semaphore(n)) for n in names}
        block = es.enter_context(nc.Block())

        def emit(eng, oplist):
            seen = {}
            for fn, waits, sig, inc, attach in oplist:
                pend = [(name, val) for (name, val) in waits if seen.get(name, 0) < val]
                if attach and len(pend) == 1:
                    (name, val) = pend[0]
                    ins = fn(eng)
                    ins._wait_ge(sems[name], val)
                    seen[name] = val
                else:
                    for (name, val) in pend:
                        eng.wait_ge(sems[name], val)
                        seen[name] = val
                    ins = fn(eng)
                if sig is not None:
                    ins.then_inc(sems[sig], inc)

        @block.sync
        def _(eng):
            emit(eng, P.ops["sp"])

        @block.gpsimd
        def _(eng):
            emit(eng, P.ops["pool"])

        @block.tensor
        def _(eng):
            emit(eng, P.ops["pe"])

        @block.scalar
        def _(eng):
            emit(eng, P.ops["act"])

        @block.vector
        def _(eng):
            emit(eng, P.ops["dve"])
    return nc


_CACHE = {}


def kernel(x, c, w_ada, b_ada, norm_g, w_in, sinks, w_out, final_g):
    x = np.asarray(x, np.float32)
    c = np.asarray(c, np.float32)
    w_ada = np.ascontiguousarray(np.asarray(w_ada, np.float32)[0])
    b_ada = np.asarray(b_ada, np.float32)[0]
    norm_g = np.asarray(norm_g, np.float32)[0]
    w_in = np.ascontiguousarray(np.asarray(w_in, np.float32)[0])
    sinks = np.asarray(sinks, np.float32)[0]
    w_out = np.ascontiguousarray(np.asarray(w_out, np.float32)[0])
    final_g = np.asarray(final_g, np.float32)

    def lay(v):
        return np.ascontiguousarray(v.reshape(-1, 128).T)

    bada_l = lay(b_ada[:2048])
    bg_bc = np.ascontiguousarray(np.broadcast_to(b_ada[2048:3072][None, :], (128, D)))
    normg_l = lay(norm_g)
    fg_bc = np.ascontiguousarray(np.broadcast_to(final_g[None, :], (128, D)))
    sinks_l = np.ascontiguousarray(np.stack([np.repeat(sinks[2 * j:2 * j + 2], 64) for j in range(4)], axis=1))
    consts = make_consts()
    if "nc" not in _CACHE:
        _CACHE["nc"] = build_nc()
    nc = _CACHE["nc"]
    in_maps = []
    for b in range(NCORE):
        in_maps.append({
            "x": np.ascontiguousarray(x[b]), "c_l": lay(c[b]), "w_ada": w_ada, "bada_l": bada_l, "bg_bc": bg_bc,
            "normg_l": normg_l, "w_in": w_in, "sinks_l": sinks_l, "w_out": w_out, "fg_bc": fg_bc, "consts": consts,
        })
    res = run_bass_kernel_spmd(nc, in_maps, core_ids=list(range(NCORE)))
    return np.stack([np.asarray(r["out"], np.float32) for r in res.results], axis=0)
```

```python
from contextlib import ExitStack

import numpy as np
import concourse.bass as bass
import concourse.mybir as mybir
from concourse.bass_utils import run_bass_kernel_spmd

F32 = mybir.dt.float32
BF16 = mybir.dt.bfloat16
AF = mybir.ActivationFunctionType
ALU = mybir.AluOpType

S = 4096
D = 1024
NCORE = 8
NT = S // 128
QC = 1024
NEG = -30000.0
NCONST = 128 * 6 + 2048
C_ID, C_TRI, C_TRIC, C_ZERO, C_NEGM, C_ONES, C_SWB = 0, 128, 256, 384, 512, 640, 768


LEVEL = 99
SW_DBG = 0
ATTACH = True


class _Stop(Exception):
    pass


def ckpt(level):
    if LEVEL <= level:
        raise _Stop()


class Plan:
    def __init__(self):
        self.ops = {"pe": [], "act": [], "dve": [], "pool": [], "sp": []}
        self.cnt = {}

    def op(self, eng, fn, waits=(), sig=None, inc=1, attach=False):
        v = None
        if sig is not None:
            self.cnt[sig] = self.cnt.get(sig, 0) + inc
            v = self.cnt[sig]
        ws = tuple(w for w in waits if w is not None and w[1] is not None and w[1] > 0)
        self.ops[eng].append((fn, ws, sig, inc, attach and ATTACH))
        return v


def make_consts():
    c = np.zeros((128, NCONST), np.float32)
    j = np.arange(128)[:, None]
    s = np.arange(128)[None, :]
    c[:, C_ID:C_ID + 128] = (j == s)
    c[:, C_TRI:C_TRI + 128] = -1.0 * (j >= s)
    c[:, C_TRIC:C_TRIC + 128] = -1.0 * (j < s)
    c[:, C_NEGM:C_NEGM + 128] = np.where(j < s, 0.0, NEG)
    c[:, C_ONES:C_ONES + 128] = 1.0
    for h in range(8):
        m = 2.0 ** (-8.0 * (h + 1) / 8)
        rel_cur = (s - j).astype(np.float32)
        cur = np.where(s >= j, -m * rel_cur, NEG)
        rel_prev = (128 + s - j).astype(np.float32)
        prev = np.where(j > s, -m * rel_prev, NEG)
        c[:, C_SWB + h * 256:C_SWB + h * 256 + 128] = cur
        c[:, C_SWB + h * 256 + 128:C_SWB + h * 256 + 256] = prev
    return c


def sb_tiles():
    out = []
    for qc in range(S // QC):
        nkb = (QC // 128) * (qc + 1)
        for kb in range(nkb - 1, -1, -1):
            jd = kb - (QC // 128) * qc
            diag = jd >= 0
            c0 = 128 * jd if diag else 0
            out.append((qc, kb, c0, diag, kb == nkb - 1, kb == 0))
    return out


def col_segs(c0, c1=QC):
    segs = []
    a = c0
    while a < c1:
        b = min(c1, (a // 512 + 1) * 512)
        segs.append((a, b))
        a = b
    return segs


def build_nc():
    nc = bass.Bass("TRN2", target_bir_lowering=False)
    x = nc.dram_tensor("x", [S, D], F32, kind="ExternalInput").ap()
    c_l = nc.dram_tensor("c_l", [128, 8], F32, kind="ExternalInput").ap()
    w_ada = nc.dram_tensor("w_ada", [D, 3 * D], F32, kind="ExternalInput").ap()
    bada_l = nc.dram_tensor("bada_l", [128, 16], F32, kind="ExternalInput").ap()
    bg_bc_d = nc.dram_tensor("bg_bc", [128, D], F32, kind="ExternalInput").ap()
    normg_l = nc.dram_tensor("normg_l", [128, 8], F32, kind="ExternalInput").ap()
    w_in = nc.dram_tensor("w_in", [D, 3328], F32, kind="ExternalInput").ap()
    sinks_l = nc.dram_tensor("sinks_l", [128, 4], F32, kind="ExternalInput").ap()
    w_out = nc.dram_tensor("w_out", [D, D], F32, kind="ExternalInput").ap()
    fg_bc_d = nc.dram_tensor("fg_bc", [128, D], F32, kind="ExternalInput").ap()
    consts_d = nc.dram_tensor("consts", [128, NCONST], F32, kind="ExternalInput").ap()
    out = nc.dram_tensor("out", [S, D], F32, kind="ExternalOutput").ap()

    P = Plan()
    es = ExitStack()
    with es:
        def sb(name, shape, dt):
            return es.enter_context(nc.sbuf_tensor(name, shape, dt))

        ygT = sb("ygT", [128, 8, S], BF16)
        hT = sb("hT", [128, 8, S], BF16)
        ov = sb("ov", [128, 30720], BF16)
        cst = sb("cst", [128, NCONST], BF16)
        gate_bc = sb("gate_bc", [128, D], F32)
        c_sb = sb("c_sb", [128, 8], F32)
        etmp = sb("etmp", [128, 8], F32)
        cond = sb("cond", [128, 8], F32)
        bada = sb("bada", [128, 16], F32)
        normg = sb("normg", [128, 8], F32)
        mod_sb = sb("mod_sb", [128, 16], F32)
        gs = sb("gs", [128, 8], F32)
        ones_f = sb("ones_f", [128, 128], F32)
        ss = sb("ss", [128, 32], F32)
        rstd = sb("rstd", [128, 32], F32)
        ss2 = sb("ss2", [128, 32], F32)
        rstd2 = sb("rstd2", [128, 32], F32)
        snk = sb("snk", [128, 4], F32)
        esk = sb("esk", [128, 4], F32)
        eps_t = sb("eps_t", [128, 1], F32)
        ps = es.enter_context(nc.psum_tensor("ps", [128, 4096], F32))

        qT = ov[:, 0:4096]
        kT = ov[:, 4096:8192]
        sg = ov[:, 8192:12288]
        vv = ov[:, 12288:16384].rearrange("p (b c) -> p b c", c=128)
        wsl = ov[:, 16384:20480].rearrange("p (k c) -> p k c", c=512)
        PB = 20480
        Eb = [ov[:, PB + i * 1024:PB + (i + 1) * 1024] for i in range(3)]
        Gb = [ov[:, PB + (3 + i) * 1024:PB + (4 + i) * 1024] for i in range(3)]
        Pb = [ov[:, PB + (6 + i) * 1024:PB + (7 + i) * 1024] for i in range(2)]
        Ab = [ov[:, PB + (8 + i) * 1024:PB + (9 + i) * 1024] for i in range(2)]
        Psw = [ov[:, PB + i * 512:PB + (i + 1) * 512] for i in range(2)]
        rden = ov[:, PB + 1024:PB + 2048].bitcast(F32)
        ytmp = ov[:, PB + 2048:PB + 3072].bitcast(F32)
        yflat = ygT[:, :, :].rearrange("p a b -> p (a b)")
        wada = [yflat[:, i * 6144:(i + 1) * 6144].bitcast(F32) for i in range(4)]
        xt = [ov[:, 12288 + i * 2048:12288 + (i + 1) * 2048].bitcast(F32) for i in range(4)]
        xnb = [[ov[:, 20480 + (g * 4 + t) * 1024:20480 + (g * 4 + t + 1) * 1024] for t in range(4)] for g in range(2)]
        junk = ov[:, 28672:29696]
        hflat = hT[:, :, :].rearrange("p a b -> p (a b)")
        condb = yflat[:, 24576:26624].bitcast(F32).rearrange("p (k m) -> p k m", m=128)
        bg_bc = yflat[:, 26624:28672].bitcast(F32)
        wout = hflat[:, 0:8192].rearrange("p (k e) -> p k e", e=1024)
        xf = [hflat[:, 8192 + i * 2048:8192 + (i + 1) * 2048].bitcast(F32) for i in range(3)]
        rf = [hflat[:, 14336 + i * 2048:14336 + (i + 1) * 2048].bitcast(F32) for i in range(2)]
        of = [hflat[:, 18432 + i * 2048:18432 + (i + 1) * 2048].bitcast(F32) for i in range(2)]
        fg_bc = hflat[:, 22528:24576].bitcast(F32)
        junkf = hflat[:, 24576:25600]
        Zp = ps[:, 0:1024]
        Rp = ps[:, 1024:2048]
        Yp = [ps[:, 2048:3072], ps[:, 3072:4096]]
        pp = [ps[:, 0:512], ps[:, 512:1024]]
        tp = [ps[:, i * 512:(i + 1) * 512].bitcast(BF16)[:, 0:512] for i in range(2)]
        modps = ps[:, 1024:1040]
        gateps = ps[:, 2048:3072]
        po = [[ps[:, t * 1024 + e * 512:t * 1024 + (e + 1) * 512] for e in range(2)] for t in range(2)]

        ident = cst[:, C_ID:C_ID + 128]
        triN = cst[:, C_TRI:C_TRI + 128]
        tricN = cst[:, C_TRIC:C_TRIC + 128]
        zer = cst[:, C_ZERO:C_ZERO + 128]
        negm = cst[:, C_NEGM:C_NEGM + 128]
        ones_bf = cst[:, C_ONES:C_ONES + 128]

        def plan_all():
            t_cst = P.op("pool", lambda e: e.dma_start(out=cst[:, :], in_=consts_d[:, :]), sig="ld_cst", inc=16)
            for dst, src in ((c_sb, c_l), (bada, bada_l), (normg, normg_l), (snk, sinks_l)):
                t_small = P.op("sp", lambda e, dst=dst, src=src: e.dma_start(out=dst[:, :], in_=src[:, :]), sig="ld_small", inc=16)
            t_small = P.op("sp", lambda e: e.dma_start(out=bg_bc, in_=bg_bc_d[:, :]), sig="ld_small", inc=16)

            ckpt(0)
            P.op("dve", lambda e: e.memset(ones_f[:, :], 1.0), sig="dve0")
            P.op("dve", lambda e: e.memset(eps_t[:, :], 1e-6), sig="dve0")
            P.op("dve", lambda e: e.memset(ss[:, :], 0.0), sig="dve0")
            t_d = P.op("dve", lambda e: e.memset(ss2[:, :], 0.0), sig="dve0")
            t_ms = t_d
            t_a = P.op("act", lambda e: e.activation(out=etmp[:, :], in_=c_sb[:, :], func=AF.Exp, scale=-1.0),
                       waits=[("ld_small", t_small)], sig="act0")
            t_d = P.op("dve", lambda e: e.tensor_scalar_add(out=etmp[:, :], in0=etmp[:, :], scalar1=1.0),
                       waits=[("act0", t_a), ("dve0", t_d)], sig="dve0")
            t_d = P.op("dve", lambda e: e.reciprocal(out=etmp[:, :], in_=etmp[:, :]), waits=[("dve0", t_d)], sig="dve0")
            t_d = P.op("dve", lambda e: e.tensor_mul(out=cond[:, :], in0=c_sb[:, :], in1=etmp[:, :]),
                       waits=[("dve0", t_d)], sig="dve0")
            t_cond = t_d
            for k in range(8):
                t_d = P.op("dve", lambda e, k=k: e.tensor_scalar(out=condb[:, k, :], in0=ones_f[:, :], scalar1=cond[:, k:k + 1],
                                                                 scalar2=None, op0=ALU.mult),
                           waits=[("dve0", t_cond)], sig="dve0")
            t_condb = t_d
            t_pe0 = {}
            t_wada = {}
            t_xld = {}
            t_sq = {}
            t_xn = {}
            t_tp = {}
            t_ev = {}
            NWB = 4

            def issue_wada(k):
                t_wada[k] = P.op("sp", lambda e, k=k: e.dma_start(out=wada[k % NWB], in_=w_ada[k * 128:(k + 1) * 128, :]),
                                 waits=[("pe0", t_pe0.get(k - NWB))], sig="ld_wada%d" % (k % NWB), inc=16)

            def issue_xload(tt):
                t_xld[tt] = P.op("sp", lambda e, tt=tt: e.dma_start(out=xt[tt % 4], in_=x[tt * 128:(tt + 1) * 128, :]),
                                 waits=[("p1xn", t_xn.get(tt - 4)), ("p1sq", t_sq.get(tt - 4))],
                                 sig="ld_x%d" % (tt % 4), inc=16)

            for k in range(NWB):
                issue_wada(k)
            for tt in range(4):
                issue_xload(tt)
            for k in range(8):
                for j in range(16):
                    P.op("pe", lambda e, k=k, j=j: e.matmul(modps[:, j:j + 1], lhsT=wada[k % NWB][:, j * 128:(j + 1) * 128],
                                                            rhs=cond[:, k:k + 1], start=(k == 0 and j == 0), stop=(k == 7),
                                                            skip_group_check=True),
                         waits=[("ld_wada%d" % (k % NWB), t_wada[k]), ("dve0", t_condb)])
                for eh in range(2):
                    t_pe0[k] = P.op("pe", lambda e, k=k, eh=eh: e.matmul(gateps[:, eh * 512:(eh + 1) * 512], lhsT=condb[:, k, :],
                                                                         rhs=wada[k % NWB][:, 2048 + eh * 512:2048 + (eh + 1) * 512],
                                                                         start=(k == 0), stop=(k == 7)),
                                    waits=[("ld_wada%d" % (k % NWB), t_wada[k]), ("dve0", t_condb)], sig=("pe0" if eh == 1 else None))
                if k + NWB < 8:
                    issue_wada(k + NWB)
                g = k
                for t4 in range(4):
                    tt = g * 4 + t4
                    t_sq[tt] = P.op("act", lambda e, tt=tt: e.activation(out=junk, in_=xt[tt % 4], func=AF.Square,
                                                                         accum_out=ss[:, tt:tt + 1]),
                                    waits=[("ld_x%d" % (tt % 4), t_xld[tt]), ("dve0", t_ms)], sig="p1sq")
                    t_r = P.op("act", lambda e, tt=tt: e.activation(out=rstd[:, tt:tt + 1], in_=ss[:, tt:tt + 1], func=AF.Sqrt,
                                                                    scale=1.0 / D, bias=eps_t[:, 0:1]),
                               waits=[("p1sq", t_sq[tt])], sig="p1sqrt")
                    t_r = P.op("dve", lambda e, tt=tt: e.reciprocal(out=rstd[:, tt:tt + 1], in_=rstd[:, tt:tt + 1]),
                               waits=[("p1sqrt", t_r)], sig="dve1")
                    t_xn[tt] = P.op("dve", lambda e, tt=tt, g=g, t4=t4: e.tensor_scalar(
                        out=xnb[g % 2][t4], in0=xt[tt % 4], scalar1=rstd[:, tt:tt + 1], scalar2=None, op0=ALU.mult),
                        waits=[("dve1", t_r), ("p1tp", t_tp.get((g - 2, 7)))], sig="p1xn")
                    if tt + 4 < NT:
                        issue_xload(tt + 4)
                for j in range(8):
                    prev_ev = t_ev[(g, j - 2)] if j >= 2 else (t_ev[(g - 1, 6 + j)] if g >= 1 else None)
                    for t4 in range(4):
                        t_tp[(g, j)] = P.op("pe", lambda e, g=g, j=j, t4=t4: e.transpose(
                            out=tp[j % 2][:, t4 * 128:(t4 + 1) * 128], in_=xnb[g % 2][t4][:, j * 128:(j + 1) * 128], identity=ident),
                            waits=[("p1xn", t_xn[g * 4 + 3]), ("p1ev", prev_ev), ("ld_cst", t_cst)],
                            sig=("p1tp" if t4 == 3 else None))
                    t_ev[(g, j)] = P.op("act", lambda e, g=g, j=j: e.activation(
                        out=hT[:, j, g * 512:(g + 1) * 512], in_=tp[j % 2], func=AF.Identity),
                        waits=[("p1tp", t_tp[(g, j)])], sig="p1ev")
            t_d = P.op("dve", lambda e: e.tensor_add(out=mod_sb[:, :], in0=modps, in1=bada[:, :]),
                       waits=[("pe0", t_pe0[7]), ("ld_small", t_small)], sig="dve0")
            t_d = P.op("dve", lambda e: e.scalar_tensor_tensor(out=gs[:, :], in0=mod_sb[:, 8:16], scalar=1.0, in1=normg[:, :],
                                                               op0=ALU.add, op1=ALU.mult),
                       waits=[("dve0", t_d)], sig="dve0")
            t_d = P.op("dve", lambda e: e.tensor_add(out=gate_bc[:, :], in0=gateps, in1=bg_bc), sig="dve0")
            t_mod = t_d
            for j in range(8):
                for hh in range(2):
                    t_fix = P.op("dve", lambda e, j=j, hh=hh: e.tensor_scalar(
                        out=hT[:, j, hh * 2048:(hh + 1) * 2048], in0=hT[:, j, hh * 2048:(hh + 1) * 2048],
                        scalar1=gs[:, j:j + 1], scalar2=mod_sb[:, j:j + 1], op0=ALU.mult, op1=ALU.add),
                        waits=[("dve0", t_mod), ("p1ev", t_ev[(7, 7)])], sig="hfix")
            t_hT = t_fix
            ckpt(2)
            t_wsl = None
            t_projpe = None
            t_attpe = None
            t_attdve = None
            t_pev = {}
            nproj = 0
            t_esk = P.op("act", lambda e: e.activation(out=esk[:, :], in_=snk[:, :], func=AF.Exp),
                         waits=[("ld_small", t_small)], sig="act0")
            sbt = sb_tiles()
            sbi = 0
            tk = {}
            n_chunk = 0
            t_evy = {}
            swn = 0
            swg = 0
            t_sw = {}

            for step in range(8):
                is_sb = step < 4
                if is_sb:
                    cq, ck, cv, cg = step * 128, 512 + step * 128, 1024 + step * 128, 1536 + step * 128
                    kw = 128
                    vw = 128
                    do_kv = True
                else:
                    j = step - 4
                    cq, ck, cv, cg = 2048 + j * 128, 2560 + (j // 2) * 64, 2688 + (j // 2) * 64, 2816 + j * 128
                    kw = 64
                    vw = 64
                    do_kv = (j % 2 == 0)
                wv = w_in.rearrange("(k p) c -> p k c", p=128)
                dmas = [(0, cq, 128), (256, cg, 128)]
                if do_kv:
                    if kw == 128:
                        dmas.append((128, ck, 128))
                    else:
                        dmas.append((128, ck, 64))
                        dmas.append((192, ck, 64))
                    if vw == 128:
                        dmas.append((384, cv, 128))
                    else:
                        dmas.append((384, cv, 64))
                        dmas.append((448, cv, 64))
                        vw = 128
                for (dc, sc, w) in dmas:
                    t_wsl = P.op("pool", lambda e, dc=dc, sc=sc, w=w: e.dma_start(out=wsl[:, :, dc:dc + w], in_=wv[:, :, sc:sc + w]),
                                 waits=[("projpe", t_projpe), ("hfix", t_hT)], sig="ld_wsl", inc=16)
                ckpt(2.5 + 2 * step)
                kinds = [("q", 0), ("g", 256)] + ([("k", 128)] if do_kv else [])
                for kind, wc in kinds:
                    for tc in range(8):
                        par = nproj % 2
                        for kc in range(8):
                            last = kc == 7
                            t_projpe_new = P.op("pe", lambda e, par=par, kc=kc, wc=wc, tc=tc: e.matmul(
                                pp[par], lhsT=wsl[:, kc, wc:wc + 128], rhs=hT[:, kc, tc * 512:(tc + 1) * 512],
                                start=(kc == 0), stop=(kc == 7)),
                                waits=[("ld_wsl", t_wsl), t_pev.get(nproj - 2), ("hfix", t_hT),
                                       t_attdve],
                                sig=("projpe" if last else None))
                        t_projpe = t_projpe_new
                        dst = {"q": qT, "k": kT, "g": sg}[kind][:, tc * 512:(tc + 1) * 512]
                        if kind == "q":
                            t_pev[nproj] = ("pev", P.op("dve", lambda e, dst=dst, par=par: e.tensor_scalar(
                                out=dst, in0=pp[par], scalar1=0.125, scalar2=None, op0=ALU.mult),
                                waits=[("projpe", t_projpe)], sig="pev"))
                            last_dve_pev = t_pev[nproj]
                        elif kind == "k":
                            t_pev[nproj] = ("pev", P.op("dve", lambda e, dst=dst, par=par: e.tensor_copy(out=dst, in_=pp[par]),
                                                waits=[("projpe", t_projpe)], sig="pev"))
                            last_dve_pev = t_pev[nproj]
                        else:
                            t_pev[nproj] = ("pevA", P.op("act", lambda e, dst=dst, par=par: e.activation(out=dst, in_=pp[par], func=AF.Silu),
                                                         waits=[("projpe", t_projpe), t_attdve], sig="pevA"))
                            last_act_pev = t_pev[nproj]
                        nproj += 1
                ckpt(2.75 + 2 * step)
                if do_kv:
                    for tg in range(8):
                        par = nproj % 2
                        for t4 in range(4):
                            tok = (tg * 4 + t4) * 128
                            for kc in range(8):
                                last = (kc == 7 and t4 == 3)
                                t_projpe_new = P.op("pe", lambda e, par=par, kc=kc, tok=tok, t4=t4, vw=vw: e.matmul(
                                    pp[par][:, t4 * 128:t4 * 128 + vw], lhsT=hT[:, kc, tok:tok + 128], rhs=wsl[:, kc, 384:384 + vw],
                                    start=(kc == 0), stop=(kc == 7)),
                                    waits=[("ld_wsl", t_wsl), t_pev.get(nproj - 2), t_attdve],
                                    sig=("projpe" if last else None))
                        t_projpe = t_projpe_new
                        t_pev[nproj] = ("pev", P.op("dve", lambda e, par=par, tg=tg, vw=vw: e.tensor_copy(
                            out=vv[:, tg * 4:(tg + 1) * 4, 0:vw],
                            in_=pp[par].rearrange("p (a b) -> p a b", a=4)[:, :, 0:vw]),
                            waits=[("projpe", t_projpe)], sig="pev"))
                        last_dve_pev = t_pev[nproj]
                        nproj += 1
                projw = [last_dve_pev, last_act_pev]
                ckpt(3 + 2 * step)

                if is_sb:
                    QCB = 512
                    tiles = []
                    for qc in range(S // QCB):
                        nkb = (QCB // 128) * (qc + 1)
                        for kb in range(nkb - 1, -1, -1):
                            jd = kb - (QCB // 128) * qc
                            diag = jd >= 0
                            c0 = 128 * jd if diag else 0
                            tiles.append((0, qc, kb, c0, diag, kb == nkb - 1, kb == 0))
                    T = len(tiles)
                    base = sbi
                    chunk_of = {}
                    cc = n_chunk - 1
                    for i, tl in enumerate(tiles):
                        if tl[5]:
                            cc += 1
                        chunk_of[i] = cc
                    Zh = [ps[:, 0:512], ps[:, 512:1024]]
                    Rh = [ps[:, 1024:1536], ps[:, 1536:2048]]
                    Yc = [ps[:, 2048:2560], ps[:, 2560:3072]]

                    def two(ap, c0):
                        if c0 == 0:
                            return ap
                        return ap.rearrange("p (h c) -> p h c", h=2)[:, :, c0:QCB]

                    def QK(i):
                        hp0, qc, kb, c0, diag, cs, ce = tiles[i]
                        g = base + i
                        for hh in range(2):
                            hp = hh * 64
                            last = (hh == 1) and not diag
                            v = P.op("pe", lambda e, hp=hp, hh=hh, kb=kb, qc=qc, c0=c0, diag=diag: e.matmul(
                                Zh[hh][:, c0:QCB], lhsT=kT[hp:hp + 64, kb * 128:(kb + 1) * 128],
                                rhs=qT[hp:hp + 64, qc * QCB + c0:(qc + 1) * QCB], start=True, stop=not diag,
                                skip_group_check=True),
                                waits=[("sbE", tk.get(("E", g - 1)))] + (projw if i < 2 else []),
                                sig=("sbQK" if last else None), attach=True)
                        if diag:
                            for hh in range(2):
                                v = P.op("pe", lambda e, hh=hh, c0=c0: e.matmul(
                                    Zh[hh][:, c0:c0 + 128], lhsT=ident, rhs=negm, start=False, stop=True, skip_group_check=True),
                                    sig=("sbQK" if hh == 1 else None))
                        tk[("QK", g)] = v

                    def ACT_E(i):
                        hp0, qc, kb, c0, diag, cs, ce = tiles[i]
                        g = base + i
                        tk[("E", g)] = P.op("act", lambda e, g=g, c0=c0: e.activation(out=two(Eb[g % 3], c0), in_=two(ps[:, 0:1024], c0), func=AF.Exp),
                                            waits=[("sbQK", tk[("QK", g)]), ("sbA", tk.get(("A", g - 3)))], sig="sbE")

                    def ACT_G(i):
                        hp0, qc, kb, c0, diag, cs, ce = tiles[i]
                        g = base + i
                        tk[("G", g)] = P.op("act", lambda e, g=g, c0=c0: e.activation(out=two(Gb[g % 3], c0), in_=two(Eb[g % 3], c0),
                                                                                   func=AF.Ln, bias=1.0),
                                            waits=[("sbE", tk[("E", g)]), ("sbTRIC", tk.get(("TRIC", g - 3)))], sig="sbG")

                    def ACT_P(i):
                        hp0, qc, kb, c0, diag, cs, ce = tiles[i]
                        g = base + i
                        tk[("P", g)] = P.op("act", lambda e, g=g, c0=c0: e.activation(out=two(Pb[g % 2], c0), in_=two(ps[:, 1024:2048], c0), func=AF.Exp),
                                            waits=[("sbTRI", tk[("TRI", g)]), ("sbA", tk.get(("A", g - 2)))], sig="sbP")

                    def PE_TRI(i):
                        hp0, qc, kb, c0, diag, cs, ce = tiles[i]
                        g = base + i
                        if cs:
                            par = chunk_of[i] % 2
                            for hh in range(2):
                                P.op("pe", lambda e, hh=hh: e.matmul(Rh[hh], lhsT=zer, rhs=cst[:, 0:512], start=True, stop=False,
                                                                     skip_group_check=True),
                                     waits=[("sbP", tk.get(("P", g - 1)))])
                            P.op("pe", lambda e, par=par: e.matmul(Yc[par], lhsT=zer, rhs=cst[:, 0:512], start=True, stop=False,
                                                                   skip_group_check=True),
                                 waits=[("sbEV", t_evy.get(chunk_of[i] - 2))])
                        for hh in range(2):
                            v = P.op("pe", lambda e, g=g, hh=hh, c0=c0: e.matmul(
                                Rh[hh][:, c0:QCB], lhsT=triN, rhs=Gb[g % 3][:, hh * QCB + c0:(hh + 1) * QCB], start=False, stop=False,
                                skip_group_check=True),
                                waits=[("sbG", tk[("G", g)]), ("sbP", tk.get(("P", g - 1)))],
                                sig=("sbTRI" if hh == 1 else None), attach=True)
                        tk[("TRI", g)] = v

                    def PE_TRIC(i):
                        hp0, qc, kb, c0, diag, cs, ce = tiles[i]
                        g = base + i
                        for hh in range(2):
                            v = P.op("pe", lambda e, g=g, hh=hh, c0=c0: e.matmul(
                                Rh[hh][:, c0:QCB], lhsT=tricN, rhs=Gb[g % 3][:, hh * QCB + c0:(hh + 1) * QCB], start=False, stop=False,
                                skip_group_check=True),
                                waits=[("sbP", tk[("P", g)])],
                                sig=("sbTRIC" if hh == 1 else None), attach=True)
                        tk[("TRIC", g)] = v

                    def PE_PV(i):
                        hp0, qc, kb, c0, diag, cs, ce = tiles[i]
                        g = base + i
                        par = chunk_of[i] % 2
                        for hh in range(2):
                            hp = hh * 64
                            v = P.op("pe", lambda e, g=g, hh=hh, hp=hp, kb=kb, par=par, c0=c0: e.matmul(
                                Yc[par][hp:hp + 64, c0:QCB], lhsT=vv[:, kb, hp:hp + 64], rhs=Ab[g % 2][:, hh * QCB + c0:(hh + 1) * QCB],
                                start=False, stop=False, skip_group_check=True),
                                waits=[("sbA", tk[("A", g)])],
                                sig=("sbPV" if hh == 1 else None), attach=True)
                        tk[("PV", g)] = v

                    def DVE_A(i):
                        hp0, qc, kb, c0, diag, cs, ce = tiles[i]
                        g = base + i
                        tk[("A", g)] = P.op("dve", lambda e, g=g, c0=c0: e.tensor_mul(out=two(Ab[g % 2], c0), in0=two(Eb[g % 3], c0),
                                                                                   in1=two(Pb[g % 2], c0)),
                                            waits=[("sbP", tk[("P", g)]), ("sbPV", tk.get(("PV", g - 2)))], sig="sbA")

                    def DVE_EV(i):
                        hp0, qc, kb, c0, diag, cs, ce = tiles[i]
                        g = base + i
                        ch = chunk_of[i]
                        par = ch % 2
                        t_evy[ch] = P.op("dve", lambda e, qc=qc, par=par, step=step: e.tensor_mul(
                            out=ygT[:, step, qc * QCB:(qc + 1) * QCB], in0=Yc[par], in1=sg[:, qc * QCB:(qc + 1) * QCB]),
                            waits=[("sbPV", tk[("PV", g)])], sig="sbEV")

                    QK(0)
                    ACT_E(0)
                    if T > 1:
                        QK(1)
                    ACT_G(0)
                    if T > 1:
                        ACT_E(1)
                    if T > 2:
                        QK(2)
                    PE_TRI(0)
                    for i in range(T):
                        ACT_P(i)
                        PE_TRIC(i)
                        if i + 1 < T:
                            ACT_G(i + 1)
                            PE_TRI(i + 1)
                        DVE_A(i)
                        PE_PV(i)
                        if tiles[i][6]:
                            DVE_EV(i)
                        if i + 2 < T:
                            ACT_E(i + 2)
                        if i + 3 < T:
                            QK(i + 3)
                    sbi += T
                    n_chunk = cc + 1
                    t_attdve = ("sbEV", t_evy[cc])
                else:
                    j = step - 4
                    heads = (2 * j, 2 * j + 1)
                    swn0 = swn

                    def SW_QK(n):
                        gi = swn0 + n
                        zbase = (gi % 2) * 1024
                        wz = 256 if n > 0 else 128
                        for which, kblk in ((0, n), (1, n - 1)):
                            if kblk < 0:
                                continue
                            for hh in range(2):
                                hp = hh * 64
                                zc = zbase + hh * 512 + which * 128
                                P.op("pe", lambda e, zc=zc, hp=hp, kblk=kblk, n=n, which=which: e.matmul(
                                    ps[:, zc:zc + 128], lhsT=kT[hp:hp + 64, kblk * 128:(kblk + 1) * 128],
                                    rhs=qT[hp:hp + 64, n * 128:(n + 1) * 128], start=(which == 0), stop=False, skip_group_check=True),
                                    waits=[("swP", t_sw.get(("P", gi - 2)))] + (projw if n < 2 else []))
                        for hh in range(2):
                            h = heads[hh]
                            bc = C_SWB + h * 256
                            zc = zbase + hh * 512
                            v = P.op("pe", lambda e, zc=zc, bc=bc, wz=wz: e.matmul(
                                ps[:, zc:zc + wz], lhsT=ident, rhs=cst[:, bc:bc + wz], start=False, stop=True, skip_group_check=True),
                                sig=("swQK" if hh == 1 else None))
                        t_sw[("QK", gi)] = v

                    def SW_ACT(n):
                        gi = swn0 + n
                        zbase = (gi % 2) * 1024
                        wz = 256 if n > 0 else 128
                        zin = ps[:, zbase:zbase + 1024].rearrange("p (b c) -> p b c", b=2)[:, :, 0:wz]
                        pout = Psw[gi % 2].rearrange("p (b c) -> p b c", b=2)[:, :, 0:wz]
                        t_sw[("P", gi)] = P.op("act", lambda e, zin=zin, pout=pout: e.activation(out=pout, in_=zin, func=AF.Exp),
                                               waits=[("swQK", t_sw[("QK", gi)]), ("swPV", t_sw.get(("PV", gi - 2)))], sig="swP")

                    def SW_PVD(n):
                        gi = swn0 + n
                        grp = swg + n // 4
                        par = grp % 2
                        Yb = Yp[par][:, 0:512]
                        Db = Yp[par][:, 512:1024]
                        col = (n % 4) * 128
                        for hh in range(2):
                            hp = hh * 64
                            srcs = [(hh * 256, n)] + ([(hh * 256 + 128, n - 1)] if n > 0 else [])
                            ns = len(srcs)
                            for si, (pc, kblk) in enumerate(srcs):
                                P.op("pe", lambda e, Yb=Yb, hp=hp, col=col, kblk=kblk, gi=gi, pc=pc, si=si, ns=ns: e.matmul(
                                    Yb[hp:hp + 64, col:col + 128], lhsT=vv[:, kblk, 0:64], rhs=Psw[gi % 2][:, pc:pc + 128],
                                    start=(si == 0), stop=(si == ns - 1), skip_group_check=True),
                                    waits=[("swP", t_sw[("P", gi)]), ("swEV", t_sw.get(("EV", grp - 2)))])
                            for si, (pc, kblk) in enumerate(srcs):
                                v = P.op("pe", lambda e, Db=Db, hp=hp, col=col, gi=gi, pc=pc, si=si, ns=ns: e.matmul(
                                    Db[hp:hp + 64, col:col + 128], lhsT=ones_bf[:, 0:64], rhs=Psw[gi % 2][:, pc:pc + 128],
                                    start=(si == 0), stop=(si == ns - 1), skip_group_check=True),
                                    sig=("swPV" if (hh == 1 and si == ns - 1) else None))
                        t_sw[("PV", gi)] = v
                    def SW_EV(n):
                        gi = swn0 + n
                        grp = swg + n // 4
                        par = grp % 2
                        Yb = Yp[par][:, 0:512]
                        Db = Yp[par][:, 512:1024]
                        if n % 4 == 3:
                            q0 = (n - 3) * 128
                            t1 = P.op("act", lambda e, Db=Db, j=j: e.activation(out=rden, in_=Db, func=AF.Ln, bias=esk[:, j:j + 1]),
                                      waits=[("swPV", t_sw[("PV", gi)]), ("act0", t_esk), ("swEV", t_sw.get(("EV", grp - 1)))], sig="swDa")
                            t1 = P.op("act", lambda e: e.activation(out=rden, in_=rden, func=AF.Exp, scale=-1.0),
                                      waits=[("swDa", t1)], sig="swDa")
                            t1 = P.op("dve", lambda e, Yb=Yb: e.tensor_mul(out=ytmp, in0=Yb, in1=rden), waits=[("swDa", t1)], sig="swD")
                            t_sw[("EV", grp)] = P.op("dve", lambda e, q0=q0, step=step: e.tensor_mul(
                                out=ygT[:, step, q0:q0 + 512], in0=ytmp, in1=sg[:, q0:q0 + 512]),
                                waits=[("swD", t1)], sig="swEV")

                    SW_QK(0)
                    for n in range(NT):
                        SW_ACT(n)
                        if n >= 1 and (n - 1) % 4 == 3:
                            SW_EV(n - 1)
                        if n + 1 < NT:
                            SW_QK(n + 1)
                        SW_PVD(n)
                    SW_EV(NT - 1)
                    swn += NT
                    swg += NT // 4
                    t_last_sw = t_sw[("EV", swg - 1)]
                if not is_sb:
                    t_attdve = ("swEV", t_last_sw)
                ckpt(4 + 2 * step)

            ckpt(20)
            wo_v = w_out.rearrange("(k p) e -> p k e", p=128)
            t_wo = None
            for k in range(8):
                t_wo = P.op("pool", lambda e, k=k: e.dma_start(out=wout[:, k, :], in_=wo_v[:, k, :]),
                            waits=[("projpe", t_projpe)], sig="ld_wo", inc=16)
            t_fg = P.op("sp", lambda e: e.dma_start(out=fg_bc, in_=fg_bc_d[:, :]), waits=[("projpe", t_projpe)], sig="ld_fg", inc=16)
            t_xf = {}
            t_o = {}
            t_st = {}
            t_r2 = {}
            t_sq2 = {}

            def issue_xf(tt):
                t_xf[tt] = P.op("sp", lambda e, tt=tt: e.dma_start(out=xf[tt % 3], in_=x[tt * 128:(tt + 1) * 128, :]),
                                waits=[("fr2", t_r2.get(tt - 3)), ("projpe", t_projpe)], sig="ld_xf%d" % (tt % 3), inc=16)

            t_po = {}
            for tt in range(3):
                issue_xf(tt)
            def F_PE(tt):
                for eh in range(2):
                    for kc in range(8):
                        v = P.op("pe", lambda e, tt=tt, eh=eh, kc=kc: e.matmul(
                            po[tt % 2][eh], lhsT=ygT[:, kc, tt * 128:(tt + 1) * 128], rhs=wout[:, kc, eh * 512:(eh + 1) * 512],
                            start=(kc == 0), stop=(kc == 7)),
                            waits=[("ld_wo", t_wo), t_attdve, ("fr2", t_r2.get(tt - 2))],
                            sig=("fpo" if kc == 7 else None))
                    t_po[(tt, eh)] = v

            def F_A(tt):
                rb = rf[tt % 2]
                for eh in range(2):
                    t1 = P.op("dve", lambda e, tt=tt, eh=eh, rb=rb: e.tensor_mul(out=rb[:, eh * 512:(eh + 1) * 512], in0=po[tt % 2][eh],
                                                                               in1=gate_bc[:, eh * 512:(eh + 1) * 512]),
                              waits=[("fpo", t_po[(tt, eh)]), ("fo", t_o.get(tt - 2)), ("fsq", t_sq2.get(tt - 2))], sig="fd")
                t_r2[tt] = P.op("dve", lambda e, tt=tt, rb=rb: e.tensor_add(out=rb, in0=rb, in1=xf[tt % 3]),
                                waits=[("fd", t1), ("ld_xf%d" % (tt % 3), t_xf[tt])], sig="fr2")
                if tt + 3 < NT:
                    issue_xf(tt + 3)
                t_sq2[tt] = P.op("act", lambda e, tt=tt, rb=rb: e.activation(out=junkf, in_=rb, func=AF.Square, accum_out=ss2[:, tt:tt + 1]),
                                 waits=[("fr2", t_r2[tt])], sig="fsq")
                t_sqrt2[tt] = P.op("act", lambda e, tt=tt: e.activation(out=rstd2[:, tt:tt + 1], in_=ss2[:, tt:tt + 1], func=AF.Sqrt,
                                                                        scale=1.0 / D, bias=eps_t[:, 0:1]),
                                   waits=[("fsq", t_sq2[tt])], sig="fsqrt")

            def F_B(tt):
                rb = rf[tt % 2]
                t1 = P.op("dve", lambda e, tt=tt: e.reciprocal(out=rstd2[:, tt:tt + 1], in_=rstd2[:, tt:tt + 1]),
                          waits=[("fsqrt", t_sqrt2[tt])], sig="fd")
                t_o[tt] = P.op("dve", lambda e, tt=tt, rb=rb: e.scalar_tensor_tensor(
                    out=of[tt % 2], in0=rb, scalar=rstd2[:, tt:tt + 1], in1=fg_bc, op0=ALU.mult, op1=ALU.mult),
                    waits=[("fd", t1), ("ld_fg", t_fg), ("ld_out%d" % (tt % 2), t_st.get(tt - 2))], sig="fo")
                t_st[tt] = P.op("sp", lambda e, tt=tt: e.dma_start(out=out[tt * 128:(tt + 1) * 128, :], in_=of[tt % 2]),
                                waits=[("fo", t_o[tt])], sig="ld_out%d" % (tt % 2), inc=16)

            t_sqrt2 = {}
            F_PE(0)
            F_PE(1)
            F_A(0)
            for tt in range(NT):
                if tt + 2 < NT:
                    F_PE(tt + 2)
                if tt + 1 < NT:
                    F_A(tt + 1)
                F_B(tt)
            P.op("sp", lambda e: e.nop(), waits=[("ld_out0", t_st[NT - 2]), ("ld_out1", t_st[NT - 1])])

        try:
            plan_all()
        except _Stop:
            pass

        names = sorted(P.cnt.keys())
        sems = {n: es.enter_context(nc.semaphore(n)) for n in names}
        block = es.enter_context(nc.Block())

        def emit(eng, oplist):
            seen = {}
            for fn, waits, sig, inc, attach in oplist:
                pend = [(name, val) for (name, val) in waits if seen.get(name, 0) < val]
                if attach and len(pend) == 1:
                    (name, val) = pend[0]
                    ins = fn(eng)
                    ins._wait_ge(sems[name], val)
                    seen[name] = val
                else:
                    for (name, val) in pend:
                        eng.wait_ge(sems[name], val)
                        seen[name] = val
                    ins = fn(eng)
                if sig is not None:
                    ins.then_inc(sems[sig], inc)

        @block.sync
        def _(eng):
            emit(eng, P.ops["sp"])

        @block.gpsimd
        def _(eng):
            emit(eng, P.ops["pool"])

        @block.tensor
        def _(eng):
            emit(eng, P.ops["pe"])

        @block.scalar
        def _(eng):
            emit(eng, P.ops["act"])

        @block.vector
        def _(eng):
            emit(eng, P.ops["dve"])
    return nc


_CACHE = {}


def kernel(x, c, w_ada, b_ada, norm_g, w_in, sinks, w_out, final_g):
    x = np.asarray(x, np.float32)
    c = np.asarray(c, np.float32)
    w_ada = np.ascontiguousarray(np.asarray(w_ada, np.float32)[0])
    b_ada = np.asarray(b_ada, np.float32)[0]
    norm_g = np.asarray(norm_g, np.float32)[0]
    w_in = np.ascontiguousarray(np.asarray(w_in, np.float32)[0])
    sinks = np.asarray(sinks, np.float32)[0]
    w_out = np.ascontiguousarray(np.asarray(w_out, np.float32)[0])
    final_g = np.asarray(final_g, np.float32)

    def lay(v):
        return np.ascontiguousarray(v.reshape(-1, 128).T)

    bada_l = lay(b_ada[:2048])
    bg_bc = np.ascontiguousarray(np.broadcast_to(b_ada[2048:3072][None, :], (128, D)))
    normg_l = lay(norm_g)
    fg_bc = np.ascontiguousarray(np.broadcast_to(final_g[None, :], (128, D)))
    sinks_l = np.ascontiguousarray(np.stack([np.repeat(sinks[2 * j:2 * j + 2], 64) for j in range(4)], axis=1))
    consts = make_consts()
    if "nc" not in _CACHE:
        _CACHE["nc"] = build_nc()
    nc = _CACHE["nc"]
    in_maps = []
    for b in range(NCORE):
        in_maps.append({
            "x": np.ascontiguousarray(x[b]), "c_l": lay(c[b]), "w_ada": w_ada, "bada_l": bada_l, "bg_bc": bg_bc,
            "normg_l": normg_l, "w_in": w_in, "sinks_l": sinks_l, "w_out": w_out, "fg_bc": fg_bc, "consts": consts,
        })
    res = run_bass_kernel_spmd(nc, in_maps, core_ids=list(range(NCORE)))
    return np.stack([np.asarray(r["out"], np.float32) for r in res.results], axis=0)
```

```python
from contextlib import ExitStack

import numpy as np
import concourse.bass as bass
import concourse.mybir as mybir
from concourse.bass_utils import run_bass_kernel_spmd

F32 = mybir.dt.float32
BF16 = mybir.dt.bfloat16
AF = mybir.ActivationFunctionType
ALU = mybir.AluOpType

S = 4096
D = 1024
NCORE = 8
NT = S // 128
QC = 1024
NEG = -30000.0
NCONST = 128 * 6 + 2048
C_ID, C_TRI, C_TRIC, C_ZERO, C_NEGM, C_ONES, C_SWB = 0, 128, 256, 384, 512, 640, 768


LEVEL = 99
SW_DBG = 0
PUMP = 2
PUMP_A = 1
PUMP_B = 1
ATTACH = True


class _Stop(Exception):
    pass


def ckpt(level):
    if LEVEL <= level:
        raise _Stop()


class Plan:
    def __init__(self):
        self.ops = {"pe": [], "act": [], "dve": [], "pool": [], "sp": []}
        self.cnt = {}

    def op(self, eng, fn, waits=(), sig=None, inc=1, attach=False):
        v = None
        if sig is not None:
            self.cnt[sig] = self.cnt.get(sig, 0) + inc
            v = self.cnt[sig]
        ws = tuple(w for w in waits if w is not None and w[1] is not None and w[1] > 0)
        self.ops[eng].append((fn, ws, sig, inc, attach and ATTACH))
        return v


def make_consts():
    c = np.zeros((128, NCONST), np.float32)
    j = np.arange(128)[:, None]
    s = np.arange(128)[None, :]
    c[:, C_ID:C_ID + 128] = (j == s)
    c[:, C_TRI:C_TRI + 128] = -1.0 * (j >= s)
    c[:, C_TRIC:C_TRIC + 128] = -1.0 * (j < s)
    c[:, C_NEGM:C_NEGM + 128] = np.where(j < s, 0.0, NEG)
    c[:, C_ONES:C_ONES + 128] = 1.0
    for h in range(8):
        m = 2.0 ** (-8.0 * (h + 1) / 8)
        rel_cur = (s - j).astype(np.float32)
        cur = np.where(s >= j, -m * rel_cur, NEG)
        rel_prev = (128 + s - j).astype(np.float32)
        prev = np.where(j > s, -m * rel_prev, NEG)
        c[:, C_SWB + h * 256:C_SWB + h * 256 + 128] = cur
        c[:, C_SWB + h * 256 + 128:C_SWB + h * 256 + 256] = prev
    return c


def sb_tiles():
    out = []
    for qc in range(S // QC):
        nkb = (QC // 128) * (qc + 1)
        for kb in range(nkb - 1, -1, -1):
            jd = kb - (QC // 128) * qc
            diag = jd >= 0
            c0 = 128 * jd if diag else 0
            out.append((qc, kb, c0, diag, kb == nkb - 1, kb == 0))
    return out


def col_segs(c0, c1=QC):
    segs = []
    a = c0
    while a < c1:
        b = min(c1, (a // 512 + 1) * 512)
        segs.append((a, b))
        a = b
    return segs


def build_nc():
    nc = bass.Bass("TRN2", target_bir_lowering=False)
    x = nc.dram_tensor("x", [S, D], F32, kind="ExternalInput").ap()
    c_l = nc.dram_tensor("c_l", [128, 8], F32, kind="ExternalInput").ap()
    w_ada = nc.dram_tensor("w_ada", [D, 3 * D], F32, kind="ExternalInput").ap()
    bada_l = nc.dram_tensor("bada_l", [128, 16], F32, kind="ExternalInput").ap()
    bg_bc_d = nc.dram_tensor("bg_bc", [128, D], F32, kind="ExternalInput").ap()
    normg_l = nc.dram_tensor("normg_l", [128, 8], F32, kind="ExternalInput").ap()
    w_in = nc.dram_tensor("w_in", [D, 3328], F32, kind="ExternalInput").ap()
    sinks_l = nc.dram_tensor("sinks_l", [128, 4], F32, kind="ExternalInput").ap()
    w_out = nc.dram_tensor("w_out", [D, D], F32, kind="ExternalInput").ap()
    fg_bc_d = nc.dram_tensor("fg_bc", [128, D], F32, kind="ExternalInput").ap()
    consts_d = nc.dram_tensor("consts", [128, NCONST], F32, kind="ExternalInput").ap()
    out = nc.dram_tensor("out", [S, D], F32, kind="ExternalOutput").ap()

    P = Plan()
    es = ExitStack()
    with es:
        def sb(name, shape, dt):
            return es.enter_context(nc.sbuf_tensor(name, shape, dt))

        ygT = sb("ygT", [128, 8, S], BF16)
        hT = sb("hT", [128, 8, S], BF16)
        ov = sb("ov", [128, 30720], BF16)
        cst = sb("cst", [128, NCONST], BF16)
        gate_bc = sb("gate_bc", [128, D], F32)
        c_sb = sb("c_sb", [128, 8], F32)
        etmp = sb("etmp", [128, 8], F32)
        cond = sb("cond", [128, 8], F32)
        bada = sb("bada", [128, 16], F32)
        normg = sb("normg", [128, 8], F32)
        mod_sb = sb("mod_sb", [128, 16], F32)
        gs = sb("gs", [128, 8], F32)
        ones_f = sb("ones_f", [128, 128], F32)
        ss = sb("ss", [128, 32], F32)
        rstd = sb("rstd", [128, 32], F32)
        ss2 = sb("ss2", [128, 32], F32)
        rstd2 = sb("rstd2", [128, 32], F32)
        snk = sb("snk", [128, 4], F32)
        esk = sb("esk", [128, 4], F32)
        eps_t = sb("eps_t", [128, 1], F32)
        wsl1 = sb("wsl1", [128, 4096], BF16)
        ps = es.enter_context(nc.psum_tensor("ps", [128, 4096], F32))

        qT = ov[:, 0:4096]
        kT = ov[:, 4096:8192]
        sg = ov[:, 8192:12288]
        vv = ov[:, 12288:16384].rearrange("p (b c) -> p b c", c=128)
        wsl = ov[:, 16384:20480].rearrange("p (k c) -> p k c", c=512)
        PB = 20480
        Eb = [ov[:, PB + i * 1024:PB + (i + 1) * 1024] for i in range(3)]
        Gb = [ov[:, PB + (3 + i) * 1024:PB + (4 + i) * 1024] for i in range(2)]
        stmp = ov[:, PB + 5 * 1024:PB + 6 * 1024].bitcast(F32)
        Pb = [ov[:, PB + (6 + i) * 1024:PB + (7 + i) * 1024] for i in range(2)]
        Ab = [ov[:, PB + (8 + i) * 1024:PB + (9 + i) * 1024] for i in range(2)]
        Psw = [ov[:, PB + i * 512:PB + (i + 1) * 512] for i in range(2)]
        rden = ov[:, PB + 1024:PB + 2048].bitcast(F32)
        ytmp = ov[:, PB + 2048:PB + 3072].bitcast(F32)
        yflat = ygT[:, :, :].rearrange("p a b -> p (a b)")
        wada = [yflat[:, i * 6144:(i + 1) * 6144].bitcast(F32) for i in range(4)]
        xt = [ov[:, 12288 + i * 2048:12288 + (i + 1) * 2048].bitcast(F32) for i in range(4)]
        xnb = [[ov[:, 20480 + (g * 4 + t) * 1024:20480 + (g * 4 + t + 1) * 1024] for t in range(4)] for g in range(2)]
        junk = ov[:, 28672:29696]
        hflat = hT[:, :, :].rearrange("p a b -> p (a b)")
        condb = yflat[:, 24576:26624].bitcast(F32).rearrange("p (k m) -> p k m", m=128)
        bg_bc = yflat[:, 26624:28672].bitcast(F32)
        wout = hflat[:, 0:8192].rearrange("p (k e) -> p k e", e=1024)
        xf = [hflat[:, 8192 + i * 2048:8192 + (i + 1) * 2048].bitcast(F32) for i in range(3)]
        rf = [hflat[:, 14336 + i * 2048:14336 + (i + 1) * 2048].bitcast(F32) for i in range(2)]
        of = [hflat[:, 18432 + i * 2048:18432 + (i + 1) * 2048].bitcast(F32) for i in range(2)]
        fg_bc = hflat[:, 22528:24576].bitcast(F32)
        junkf = hflat[:, 24576:25600]
        Zp = ps[:, 0:1024]
        Rp = ps[:, 1024:2048]
        Yp = [ps[:, 2048:3072], ps[:, 3072:4096]]
        pp = [ps[:, 0:512], ps[:, 512:1024]]
        tp = [ps[:, i * 512:(i + 1) * 512].bitcast(BF16)[:, 0:512] for i in range(2)]
        modps = ps[:, 1024:1040]
        gateps = ps[:, 2048:3072]
        po = [[ps[:, t * 1024 + e * 512:t * 1024 + (e + 1) * 512] for e in range(2)] for t in range(2)]

        ident = cst[:, C_ID:C_ID + 128]
        triN = cst[:, C_TRI:C_TRI + 128]
        tricN = cst[:, C_TRIC:C_TRIC + 128]
        zer = cst[:, C_ZERO:C_ZERO + 128]
        negm = cst[:, C_NEGM:C_NEGM + 128]
        ones_bf = cst[:, C_ONES:C_ONES + 128]

        def plan_all():
            nonlocal qT, kT, sg, vv
            t_cst = P.op("pool", lambda e: e.dma_start(out=cst[:, :], in_=consts_d[:, :]), sig="ld_cst", inc=16)
            for dst, src in ((c_sb, c_l), (bada, bada_l), (normg, normg_l), (snk, sinks_l)):
                t_small = P.op("sp", lambda e, dst=dst, src=src: e.dma_start(out=dst[:, :], in_=src[:, :]), sig="ld_small", inc=16)
            t_small = P.op("sp", lambda e: e.dma_start(out=bg_bc, in_=bg_bc_d[:, :]), sig="ld_small", inc=16)

            ckpt(0)
            P.op("dve", lambda e: e.memset(ones_f[:, :], 1.0), sig="dve0")
            P.op("dve", lambda e: e.memset(eps_t[:, :], 1e-6), sig="dve0")
            P.op("dve", lambda e: e.memset(ss[:, :], 0.0), sig="dve0")
            t_d = P.op("dve", lambda e: e.memset(ss2[:, :], 0.0), sig="dve0")
            t_ms = t_d
            t_a = P.op("act", lambda e: e.activation(out=etmp[:, :], in_=c_sb[:, :], func=AF.Exp, scale=-1.0),
                       waits=[("ld_small", t_small)], sig="act0")
            t_d = P.op("dve", lambda e: e.tensor_scalar_add(out=etmp[:, :], in0=etmp[:, :], scalar1=1.0),
                       waits=[("act0", t_a), ("dve0", t_d)], sig="dve0")
            t_d = P.op("dve", lambda e: e.reciprocal(out=etmp[:, :], in_=etmp[:, :]), waits=[("dve0", t_d)], sig="dve0")
            t_d = P.op("dve", lambda e: e.tensor_mul(out=cond[:, :], in0=c_sb[:, :], in1=etmp[:, :]),
                       waits=[("dve0", t_d)], sig="dve0")
            t_cond = t_d
            for k in range(8):
                t_d = P.op("dve", lambda e, k=k: e.tensor_scalar(out=condb[:, k, :], in0=ones_f[:, :], scalar1=cond[:, k:k + 1],
                                                                 scalar2=None, op0=ALU.mult),
                           waits=[("dve0", t_cond)], sig="dve0")
            t_condb = t_d
            t_pe0 = {}
            t_wada = {}
            t_xld = {}
            t_sq = {}
            t_xn = {}
            t_tp = {}
            t_ev = {}
            NWB = 4

            def issue_wada(k):
                t_wada[k] = P.op("sp", lambda e, k=k: e.dma_start(out=wada[k % NWB], in_=w_ada[k * 128:(k + 1) * 128, :]),
                                 waits=[("pe0", t_pe0.get(k - NWB))], sig="ld_wada%d" % (k % NWB), inc=16)

            def issue_xload(tt):
                t_xld[tt] = P.op("sp", lambda e, tt=tt: e.dma_start(out=xt[tt % 4], in_=x[tt * 128:(tt + 1) * 128, :]),
                                 waits=[("p1xn", t_xn.get(tt - 4)), ("p1sq", t_sq.get(tt - 4))],
                                 sig="ld_x%d" % (tt % 4), inc=16)

            for k in range(NWB):
                issue_wada(k)
            for tt in range(4):
                issue_xload(tt)
            for k in range(8):
                for j in range(16):
                    P.op("pe", lambda e, k=k, j=j: e.matmul(modps[:, j:j + 1], lhsT=wada[k % NWB][:, j * 128:(j + 1) * 128],
                                                            rhs=cond[:, k:k + 1], start=(k == 0 and j == 0), stop=(k == 7),
                                                            skip_group_check=True),
                         waits=[("ld_wada%d" % (k % NWB), t_wada[k]), ("dve0", t_condb)])
                for eh in range(2):
                    t_pe0[k] = P.op("pe", lambda e, k=k, eh=eh: e.matmul(gateps[:, eh * 512:(eh + 1) * 512], lhsT=condb[:, k, :],
                                                                         rhs=wada[k % NWB][:, 2048 + eh * 512:2048 + (eh + 1) * 512],
                                                                         start=(k == 0), stop=(k == 7)),
                                    waits=[("ld_wada%d" % (k % NWB), t_wada[k]), ("dve0", t_condb)], sig=("pe0" if eh == 1 else None))
                if k + NWB < 8:
                    issue_wada(k + NWB)
                g = k
                for t4 in range(4):
                    tt = g * 4 + t4
                    t_sq[tt] = P.op("act", lambda e, tt=tt: e.activation(out=junk, in_=xt[tt % 4], func=AF.Square,
                                                                         accum_out=ss[:, tt:tt + 1]),
                                    waits=[("ld_x%d" % (tt % 4), t_xld[tt]), ("dve0", t_ms)], sig="p1sq")
                    t_r = P.op("act", lambda e, tt=tt: e.activation(out=rstd[:, tt:tt + 1], in_=ss[:, tt:tt + 1], func=AF.Sqrt,
                                                                    scale=1.0 / D, bias=eps_t[:, 0:1]),
                               waits=[("p1sq", t_sq[tt])], sig="p1sqrt")
                    t_r = P.op("dve", lambda e, tt=tt: e.reciprocal(out=rstd[:, tt:tt + 1], in_=rstd[:, tt:tt + 1]),
                               waits=[("p1sqrt", t_r)], sig="dve1")
                    t_xn[tt] = P.op("dve", lambda e, tt=tt, g=g, t4=t4: e.tensor_scalar(
                        out=xnb[g % 2][t4], in0=xt[tt % 4], scalar1=rstd[:, tt:tt + 1], scalar2=None, op0=ALU.mult),
                        waits=[("dve1", t_r), ("p1tp", t_tp.get((g - 2, 7)))], sig="p1xn")
                    if tt + 4 < NT:
                        issue_xload(tt + 4)
                for j in range(8):
                    prev_ev = t_ev[(g, j - 2)] if j >= 2 else (t_ev[(g - 1, 6 + j)] if g >= 1 else None)
                    for t4 in range(4):
                        t_tp[(g, j)] = P.op("pe", lambda e, g=g, j=j, t4=t4: e.transpose(
                            out=tp[j % 2][:, t4 * 128:(t4 + 1) * 128], in_=xnb[g % 2][t4][:, j * 128:(j + 1) * 128], identity=ident),
                            waits=[("p1xn", t_xn[g * 4 + 3]), ("p1ev", prev_ev), ("ld_cst", t_cst)],
                            sig=("p1tp" if t4 == 3 else None))
                    t_ev[(g, j)] = P.op("act", lambda e, g=g, j=j: e.activation(
                        out=hT[:, j, g * 512:(g + 1) * 512], in_=tp[j % 2], func=AF.Identity),
                        waits=[("p1tp", t_tp[(g, j)])], sig="p1ev")
            t_d = P.op("dve", lambda e: e.tensor_add(out=mod_sb[:, :], in0=modps, in1=bada[:, :]),
                       waits=[("pe0", t_pe0[7]), ("ld_small", t_small)], sig="dve0")
            t_d = P.op("dve", lambda e: e.scalar_tensor_tensor(out=gs[:, :], in0=mod_sb[:, 8:16], scalar=1.0, in1=normg[:, :],
                                                               op0=ALU.add, op1=ALU.mult),
                       waits=[("dve0", t_d)], sig="dve0")
            t_d = P.op("dve", lambda e: e.tensor_add(out=gate_bc[:, :], in0=gateps, in1=bg_bc), sig="dve0")
            t_mod = t_d
            for j in range(8):
                for hh in range(2):
                    t_fix = P.op("dve", lambda e, j=j, hh=hh: e.tensor_scalar(
                        out=hT[:, j, hh * 2048:(hh + 1) * 2048], in0=hT[:, j, hh * 2048:(hh + 1) * 2048],
                        scalar1=gs[:, j:j + 1], scalar2=mod_sb[:, j:j + 1], op0=ALU.mult, op1=ALU.add),
                        waits=[("dve0", t_mod), ("p1ev", t_ev[(7, 7)])], sig="hfix")
            t_hT = t_fix
            ckpt(2)
            t_wsl = None
            t_projpe = None
            t_attpe = None
            t_attdve = None
            t_pev = {}
            nproj = 0
            t_esk = P.op("act", lambda e: e.activation(out=esk[:, :], in_=snk[:, :], func=AF.Exp),
                         waits=[("ld_small", t_small)], sig="act0")
            sbt = sb_tiles()
            sbi = 0
            tk = {}
            n_chunk = 0
            t_evy = {}
            swn = 0
            swg = 0
            t_sw = {}

            wv_all = w_in.rearrange("(k p) c -> p k c", p=128)
            GSv = [(qT, kT, sg, vv),
                   (yflat[:, 16384:20480], yflat[:, 20480:24576], yflat[:, 24576:28672],
                    yflat[:, 28672:32768].rearrange("p (b c) -> p b c", c=128))]
            wslb = [wsl, wsl1[:, :].rearrange("p (k c) -> p k c", c=512)]
            ppb = [ps[:, 3072:3584], ps[:, 3584:4096]]
            pst = {"n": 0, "pev": {}, "projpe": {}, "lastpev": {}, "attdone": {}, "stmp": None}

            def sb_proj_gen(st):
                gq, gk, gsg, gv = GSv[st % 2]
                wb = wslb[st % 2]
                cols = {"q": st * 128, "k": 512 + st * 128, "v": 1024 + st * 128, "g": 1536 + st * 128}
                wcol = {"q": 0, "k": 128, "g": 256, "v": 384}
                t_w = None
                for kind in ("k", "v", "q", "g"):
                    dc, sc = wcol[kind], cols[kind]
                    t_w = P.op("pool", lambda e, wb=wb, dc=dc, sc=sc: e.dma_start(out=wb[:, :, dc:dc + 128], in_=wv_all[:, :, sc:sc + 128]),
                               waits=[pst["projpe"].get(st - 2), ("hfix", t_hT)], sig="ld_wb%d" % (st % 2), inc=16)
                t_w = ("ld_wb%d" % (st % 2), t_w)
                free_tok = pst["attdone"].get(st - 2)
                tpe = None
                for kind in ("k", "v", "q", "g"):
                    wc = wcol[kind]
                    if kind == "v":
                        for tg in range(8):
                            n = pst["n"]
                            par = n % 2
                            for t4 in range(4):
                                tok = (tg * 4 + t4) * 128
                                for kc in range(8):
                                    last = (kc == 7 and t4 == 3)
                                    tpe = P.op("pe", lambda e, par=par, kc=kc, tok=tok, t4=t4, wb=wb: e.matmul(
                                        ppb[par][:, t4 * 128:(t4 + 1) * 128], lhsT=hT[:, kc, tok:tok + 128], rhs=wb[:, kc, 384:512],
                                        start=(kc == 0), stop=(kc == 7)),
                                        waits=[t_w, pst["pev"].get(n - 2)], sig=("pprojpe" if last else None))
                                    if (not last) and (kc % 3 == 2 or kc == 7):
                                        yield
                            tv = P.op("dve", lambda e, par=par, tg=tg, gv=gv: e.tensor_copy(
                                out=gv[:, tg * 4:(tg + 1) * 4, :], in_=ppb[par].rearrange("p (a b) -> p a b", a=4)),
                                waits=[("pprojpe", tpe), free_tok], sig="ppev")
                            pst["pev"][n] = ("ppev", tv)
                            pst["n"] += 1
                            yield
                        continue
                    for tc in range(8):
                        n = pst["n"]
                        par = n % 2
                        for kc in range(8):
                            tpe = P.op("pe", lambda e, par=par, kc=kc, wc=wc, tc=tc, wb=wb: e.matmul(
                                ppb[par], lhsT=wb[:, kc, wc:wc + 128], rhs=hT[:, kc, tc * 512:(tc + 1) * 512],
                                start=(kc == 0), stop=(kc == 7)),
                                waits=[t_w, pst["pev"].get(n - 2)], sig=("pprojpe" if kc == 7 else None))
                            if kc < 7:
                                yield
                        csl = slice(tc * 512, (tc + 1) * 512)
                        if kind == "q":
                            tv = P.op("dve", lambda e, par=par, gq=gq, csl=csl: e.tensor_scalar(
                                out=gq[:, csl], in0=ppb[par], scalar1=0.125, scalar2=None, op0=ALU.mult),
                                waits=[("pprojpe", tpe), free_tok], sig="ppev")
                        elif kind == "k":
                            tv = P.op("dve", lambda e, par=par, gk=gk, csl=csl: e.tensor_copy(out=gk[:, csl], in_=ppb[par]),
                                      waits=[("pprojpe", tpe), free_tok], sig="ppev")
                        else:
                            yield
                            ta = P.op("act", lambda e, par=par: e.activation(out=stmp, in_=ppb[par], func=AF.Exp, scale=-1.0),
                                      waits=[("pprojpe", tpe), pst["stmp"]], sig="ppevA")
                            yield
                            for qq in range(4):
                                qs = slice(qq * 128, (qq + 1) * 128)
                                t1 = P.op("dve", lambda e, qs=qs: e.tensor_scalar_add(out=stmp[:, qs], in0=stmp[:, qs], scalar1=1.0),
                                          waits=[("ppevA", ta)], sig="psil")
                                t1 = P.op("dve", lambda e, qs=qs: e.reciprocal(out=stmp[:, qs], in_=stmp[:, qs]), waits=[("psil", t1)], sig="psil")
                                osl = slice(tc * 512 + qq * 128, tc * 512 + (qq + 1) * 128)
                                tv = P.op("dve", lambda e, par=par, gsg=gsg, osl=osl, qs=qs: e.tensor_mul(
                                    out=gsg[:, osl], in0=ppb[par][:, qs], in1=stmp[:, qs]),
                                    waits=[("psil", t1), free_tok], sig="ppev")
                                if qq < 3:
                                    yield
                            pst["stmp"] = ("ppev", tv)
                        pst["pev"][n] = ("ppev", tv)
                        pst["n"] += 1
                        yield
                pst["projpe"][st] = ("pprojpe", tpe)
                pst["lastpev"][st] = pst["pev"][pst["n"] - 1]

            def pump(gen, k):
                if gen is None:
                    return None
                for _ in range(k):
                    try:
                        next(gen)
                    except StopIteration:
                        return None
                return gen

            def drain(gen):
                while gen is not None:
                    gen = pump(gen, 64)

            gen_cur = sb_proj_gen(0)

            for step in range(8):
                is_sb = step < 4
                if is_sb:
                    cq, ck, cv, cg = step * 128, 512 + step * 128, 1024 + step * 128, 1536 + step * 128
                    kw = 128
                    vw = 128
                    do_kv = True
                else:
                    j = step - 4
                    cq, ck, cv, cg = 2048 + j * 128, 2560 + (j // 2) * 64, 2688 + (j // 2) * 64, 2816 + j * 128
                    kw = 64
                    vw = 64
                    do_kv = (j % 2 == 0)
                if not is_sb:
                    wv = w_in.rearrange("(k p) c -> p k c", p=128)
                    dmas = [(0, cq, 128), (256, cg, 128)]
                    if do_kv:
                        if kw == 128:
                            dmas.append((128, ck, 128))
                        else:
                            dmas.append((128, ck, 64))
                            dmas.append((192, ck, 64))
                        if vw == 128:
                            dmas.append((384, cv, 128))
                        else:
                            dmas.append((384, cv, 64))
                            dmas.append((448, cv, 64))
                            vw = 128
                    for (dc, sc, w) in dmas:
                        t_wsl = P.op("pool", lambda e, dc=dc, sc=sc, w=w: e.dma_start(out=wsl[:, :, dc:dc + w], in_=wv[:, :, sc:sc + w]),
                                     waits=[("projpe", t_projpe), ("hfix", t_hT), pst["projpe"].get(2)],
                                     sig="ld_wsl", inc=16)
                    ckpt(2.5 + 2 * step)
                    kinds = [("q", 0), ("g", 256)] + ([("k", 128)] if do_kv else [])
                    for kind, wc in kinds:
                        for tc in range(8):
                            par = nproj % 2
                            for kc in range(8):
                                last = kc == 7
                                t_projpe_new = P.op("pe", lambda e, par=par, kc=kc, wc=wc, tc=tc: e.matmul(
                                    pp[par], lhsT=wsl[:, kc, wc:wc + 128], rhs=hT[:, kc, tc * 512:(tc + 1) * 512],
                                    start=(kc == 0), stop=(kc == 7)),
                                    waits=[("ld_wsl", t_wsl), t_pev.get(nproj - 2), ("hfix", t_hT),
                                           t_attdve],
                                    sig=("projpe" if last else None))
                            t_projpe = t_projpe_new
                            dst = {"q": qT, "k": kT, "g": sg}[kind][:, tc * 512:(tc + 1) * 512]
                            if kind == "q":
                                t_pev[nproj] = ("pev", P.op("dve", lambda e, dst=dst, par=par: e.tensor_scalar(
                                    out=dst, in0=pp[par], scalar1=0.125, scalar2=None, op0=ALU.mult),
                                    waits=[("projpe", t_projpe)], sig="pev"))
                                last_dve_pev = t_pev[nproj]
                            elif kind == "k":
                                t_pev[nproj] = ("pev", P.op("dve", lambda e, dst=dst, par=par: e.tensor_copy(out=dst, in_=pp[par]),
                                                    waits=[("projpe", t_projpe)], sig="pev"))
                                last_dve_pev = t_pev[nproj]
                            else:
                                t_pev[nproj] = ("pevA", P.op("act", lambda e, dst=dst, par=par: e.activation(out=dst, in_=pp[par], func=AF.Silu),
                                                             waits=[("projpe", t_projpe), t_attdve], sig="pevA"))
                                last_act_pev = t_pev[nproj]
                            nproj += 1
                    ckpt(2.75 + 2 * step)
                    if do_kv:
                        for tg in range(8):
                            par = nproj % 2
                            for t4 in range(4):
                                tok = (tg * 4 + t4) * 128
                                for kc in range(8):
                                    last = (kc == 7 and t4 == 3)
                                    t_projpe_new = P.op("pe", lambda e, par=par, kc=kc, tok=tok, t4=t4, vw=vw: e.matmul(
                                        pp[par][:, t4 * 128:t4 * 128 + vw], lhsT=hT[:, kc, tok:tok + 128], rhs=wsl[:, kc, 384:384 + vw],
                                        start=(kc == 0), stop=(kc == 7)),
                                        waits=[("ld_wsl", t_wsl), t_pev.get(nproj - 2), t_attdve],
                                        sig=("projpe" if last else None))
                            t_projpe = t_projpe_new
                            t_pev[nproj] = ("pev", P.op("dve", lambda e, par=par, tg=tg, vw=vw: e.tensor_copy(
                                out=vv[:, tg * 4:(tg + 1) * 4, 0:vw],
                                in_=pp[par].rearrange("p (a b) -> p a b", a=4)[:, :, 0:vw]),
                                waits=[("projpe", t_projpe)], sig="pev"))
                            last_dve_pev = t_pev[nproj]
                            nproj += 1
                    projw = [last_dve_pev, last_act_pev]
                else:
                    drain(gen_cur)
                    projw = [pst["lastpev"][step]]
                    gen_cur = sb_proj_gen(step + 1) if step + 1 < 4 else None
                    qT, kT, sg, vv = GSv[step % 2]
                ckpt(3 + 2 * step)

                if is_sb:
                    QCB = 512
                    tiles = []
                    for qc in range(S // QCB):
                        nkb = (QCB // 128) * (qc + 1)
                        for kb in range(nkb - 1, -1, -1):
                            jd = kb - (QCB // 128) * qc
                            diag = jd >= 0
                            c0 = 128 * jd if diag else 0
                            tiles.append((0, qc, kb, c0, diag, kb == nkb - 1, kb == 0))
                    T = len(tiles)
                    base = sbi
                    chunk_of = {}
                    cc = n_chunk - 1
                    for i, tl in enumerate(tiles):
                        if tl[5]:
                            cc += 1
                        chunk_of[i] = cc
                    Zh = [ps[:, 0:512], ps[:, 512:1024]]
                    Rh = [ps[:, 1024:1536], ps[:, 1536:2048]]
                    Yc = [ps[:, 2048:2560], ps[:, 2560:3072]]

                    def two(ap, c0):
                        if c0 == 0:
                            return ap
                        return ap.rearrange("p (h c) -> p h c", h=2)[:, :, c0:QCB]

                    def QK(i):
                        hp0, qc, kb, c0, diag, cs, ce = tiles[i]
                        g = base + i
                        for hh in range(2):
                            hp = hh * 64
                            last = (hh == 1) and not diag
                            v = P.op("pe", lambda e, hp=hp, hh=hh, kb=kb, qc=qc, c0=c0, diag=diag, kT=kT, qT=qT: e.matmul(
                                Zh[hh][:, c0:QCB], lhsT=kT[hp:hp + 64, kb * 128:(kb + 1) * 128],
                                rhs=qT[hp:hp + 64, qc * QCB + c0:(qc + 1) * QCB], start=True, stop=not diag,
                                skip_group_check=True),
                                waits=[("sbE", tk.get(("E", g - 1)))] + (projw if i < 2 else []),
                                sig=("sbQK" if last else None), attach=True)
                        if diag:
                            for hh in range(2):
                                v = P.op("pe", lambda e, hh=hh, c0=c0: e.matmul(
                                    Zh[hh][:, c0:c0 + 128], lhsT=ident, rhs=negm, start=False, stop=True, skip_group_check=True),
                                    sig=("sbQK" if hh == 1 else None))
                        tk[("QK", g)] = v

                    def ACT_E(i):
                        hp0, qc, kb, c0, diag, cs, ce = tiles[i]
                        g = base + i
                        tk[("E", g)] = P.op("act", lambda e, g=g, c0=c0: e.activation(out=two(Eb[g % 3], c0), in_=two(ps[:, 0:1024], c0), func=AF.Exp),
                                            waits=[("sbQK", tk[("QK", g)]), ("sbA", tk.get(("A", g - 3)))], sig="sbE")

                    def ACT_G(i):
                        hp0, qc, kb, c0, diag, cs, ce = tiles[i]
                        g = base + i
                        tk[("G", g)] = P.op("act", lambda e, g=g, c0=c0: e.activation(out=two(Gb[g % 2], c0), in_=two(Eb[g % 3], c0),
                                                                                   func=AF.Ln, bias=1.0),
                                            waits=[("sbE", tk[("E", g)]), ("sbTRIC", tk.get(("TRIC", g - 2)))], sig="sbG")

                    def ACT_P(i):
                        hp0, qc, kb, c0, diag, cs, ce = tiles[i]
                        g = base + i
                        tk[("P", g)] = P.op("act", lambda e, g=g, c0=c0: e.activation(out=two(Pb[g % 2], c0), in_=two(ps[:, 1024:2048], c0), func=AF.Exp),
                                            waits=[("sbTRI", tk[("TRI", g)]), ("sbA", tk.get(("A", g - 2)))], sig="sbP")

                    def PE_TRI(i):
                        hp0, qc, kb, c0, diag, cs, ce = tiles[i]
                        g = base + i
                        if cs:
                            par = chunk_of[i] % 2
                            for hh in range(2):
                                P.op("pe", lambda e, hh=hh: e.matmul(Rh[hh], lhsT=zer, rhs=cst[:, 0:512], start=True, stop=False,
                                                                     skip_group_check=True),
                                     waits=[("sbP", tk.get(("P", g - 1)))])
                            P.op("pe", lambda e, par=par: e.matmul(Yc[par], lhsT=zer, rhs=cst[:, 0:512], start=True, stop=False,
                                                                   skip_group_check=True),
                                 waits=[("sbEV", t_evy.get(chunk_of[i] - 2))])
                        for hh in range(2):
                            v = P.op("pe", lambda e, g=g, hh=hh, c0=c0: e.matmul(
                                Rh[hh][:, c0:QCB], lhsT=triN, rhs=Gb[g % 2][:, hh * QCB + c0:(hh + 1) * QCB], start=False, stop=False,
                                skip_group_check=True),
                                waits=[("sbG", tk[("G", g)]), ("sbP", tk.get(("P", g - 1)))],
                                sig=("sbTRI" if hh == 1 else None), attach=True)
                        tk[("TRI", g)] = v

                    def PE_TRIC(i):
                        hp0, qc, kb, c0, diag, cs, ce = tiles[i]
                        g = base + i
                        for hh in range(2):
                            v = P.op("pe", lambda e, g=g, hh=hh, c0=c0: e.matmul(
                                Rh[hh][:, c0:QCB], lhsT=tricN, rhs=Gb[g % 2][:, hh * QCB + c0:(hh + 1) * QCB], start=False, stop=False,
                                skip_group_check=True),
                                waits=[("sbP", tk[("P", g)])],
                                sig=("sbTRIC" if hh == 1 else None), attach=True)
                        tk[("TRIC", g)] = v

                    def PE_PV(i):
                        hp0, qc, kb, c0, diag, cs, ce = tiles[i]
                        g = base + i
                        par = chunk_of[i] % 2
                        for hh in range(2):
                            hp = hh * 64
                            v = P.op("pe", lambda e, g=g, hh=hh, hp=hp, kb=kb, par=par, c0=c0, vv=vv: e.matmul(
                                Yc[par][hp:hp + 64, c0:QCB], lhsT=vv[:, kb, hp:hp + 64], rhs=Ab[g % 2][:, hh * QCB + c0:(hh + 1) * QCB],
                                start=False, stop=False, skip_group_check=True),
                                waits=[("sbA", tk[("A", g)])],
                                sig=("sbPV" if hh == 1 else None), attach=True)
                        tk[("PV", g)] = v

                    def DVE_A(i):
                        hp0, qc, kb, c0, diag, cs, ce = tiles[i]
                        g = base + i
                        tk[("A", g)] = P.op("dve", lambda e, g=g, c0=c0: e.tensor_mul(out=two(Ab[g % 2], c0), in0=two(Eb[g % 3], c0),
                                                                                   in1=two(Pb[g % 2], c0)),
                                            waits=[("sbP", tk[("P", g)]), ("sbPV", tk.get(("PV", g - 2)))], sig="sbA")

                    def DVE_EV(i):
                        hp0, qc, kb, c0, diag, cs, ce = tiles[i]
                        g = base + i
                        ch = chunk_of[i]
                        par = ch % 2
                        t_evy[ch] = P.op("dve", lambda e, qc=qc, par=par, step=step, sg=sg: e.tensor_mul(
                            out=ygT[:, step, qc * QCB:(qc + 1) * QCB], in0=Yc[par], in1=sg[:, qc * QCB:(qc + 1) * QCB]),
                            waits=[("sbPV", tk[("PV", g)])], sig="sbEV")

                    QK(0)
                    ACT_E(0)
                    if T > 1:
                        QK(1)
                    ACT_G(0)
                    if T > 1:
                        ACT_E(1)
                    if T > 2:
                        QK(2)
                    PE_TRI(0)
                    for i in range(T):
                        ACT_P(i)
                        PE_TRIC(i)
                        gen_cur = pump(gen_cur, PUMP_A)
                        if i + 1 < T:
                            ACT_G(i + 1)
                            PE_TRI(i + 1)
                        DVE_A(i)
                        PE_PV(i)
                        if tiles[i][6]:
                            DVE_EV(i)
                        if i + 2 < T:
                            ACT_E(i + 2)
                        if i + 3 < T:
                            QK(i + 3)
                        gen_cur = pump(gen_cur, PUMP_B)
                    sbi += T
                    n_chunk = cc + 1
                    t_attdve = ("sbEV", t_evy[cc])
                    pst["attdone"][step] = t_attdve
                    drain(gen_cur) if step == 3 else None
                    qT, kT, sg, vv = GSv[0]
                else:
                    j = step - 4
                    heads = (2 * j, 2 * j + 1)
                    swn0 = swn

                    def SW_QK(n):
                        gi = swn0 + n
                        zbase = (gi % 2) * 1024
                        wz = 256 if n > 0 else 128
                        for which, kblk in ((0, n), (1, n - 1)):
                            if kblk < 0:
                                continue
                            for hh in range(2):
                                hp = hh * 64
                                zc = zbase + hh * 512 + which * 128
                                P.op("pe", lambda e, zc=zc, hp=hp, kblk=kblk, n=n, which=which: e.matmul(
                                    ps[:, zc:zc + 128], lhsT=kT[hp:hp + 64, kblk * 128:(kblk + 1) * 128],
                                    rhs=qT[hp:hp + 64, n * 128:(n + 1) * 128], start=(which == 0), stop=False, skip_group_check=True),
                                    waits=[("swP", t_sw.get(("P", gi - 2)))] + (projw if n < 2 else []))
                        for hh in range(2):
                            h = heads[hh]
                            bc = C_SWB + h * 256
                            zc = zbase + hh * 512
                            v = P.op("pe", lambda e, zc=zc, bc=bc, wz=wz: e.matmul(
                                ps[:, zc:zc + wz], lhsT=ident, rhs=cst[:, bc:bc + wz], start=False, stop=True, skip_group_check=True),
                                sig=("swQK" if hh == 1 else None))
                        t_sw[("QK", gi)] = v

                    def SW_ACT(n):
                        gi = swn0 + n
                        zbase = (gi % 2) * 1024
                        wz = 256 if n > 0 else 128
                        zin = ps[:, zbase:zbase + 1024].rearrange("p (b c) -> p b c", b=2)[:, :, 0:wz]
                        pout = Psw[gi % 2].rearrange("p (b c) -> p b c", b=2)[:, :, 0:wz]
                        t_sw[("P", gi)] = P.op("act", lambda e, zin=zin, pout=pout: e.activation(out=pout, in_=zin, func=AF.Exp),
                                               waits=[("swQK", t_sw[("QK", gi)]), ("swPV", t_sw.get(("PV", gi - 2)))], sig="swP")

                    def SW_PVD(n):
                        gi = swn0 + n
                        grp = swg + n // 4
                        par = grp % 2
                        Yb = Yp[par][:, 0:512]
                        Db = Yp[par][:, 512:1024]
                        col = (n % 4) * 128
                        for hh in range(2):
                            hp = hh * 64
                            srcs = [(hh * 256, n)] + ([(hh * 256 + 128, n - 1)] if n > 0 else [])
                            ns = len(srcs)
                            for si, (pc, kblk) in enumerate(srcs):
                                P.op("pe", lambda e, Yb=Yb, hp=hp, col=col, kblk=kblk, gi=gi, pc=pc, si=si, ns=ns: e.matmul(
                                    Yb[hp:hp + 64, col:col + 128], lhsT=vv[:, kblk, 0:64], rhs=Psw[gi % 2][:, pc:pc + 128],
                                    start=(si == 0), stop=(si == ns - 1), skip_group_check=True),
                                    waits=[("swP", t_sw[("P", gi)]), ("swEV", t_sw.get(("EV", grp - 2)))])
                            for si, (pc, kblk) in enumerate(srcs):
                                v = P.op("pe", lambda e, Db=Db, hp=hp, col=col, gi=gi, pc=pc, si=si, ns=ns: e.matmul(
                                    Db[hp:hp + 64, col:col + 128], lhsT=ones_bf[:, 0:64], rhs=Psw[gi % 2][:, pc:pc + 128],
                                    start=(si == 0), stop=(si == ns - 1), skip_group_check=True),
                                    sig=("swPV" if (hh == 1 and si == ns - 1) else None))
                        t_sw[("PV", gi)] = v
                    def SW_EV(n):
                        gi = swn0 + n
                        grp = swg + n // 4
                        par = grp % 2
                        Yb = Yp[par][:, 0:512]
                        Db = Yp[par][:, 512:1024]
                        if n % 4 == 3:
                            q0 = (n - 3) * 128
                            t1 = P.op("act", lambda e, Db=Db, j=j: e.activation(out=rden, in_=Db, func=AF.Ln, bias=esk[:, j:j + 1]),
                                      waits=[("swPV", t_sw[("PV", gi)]), ("act0", t_esk), ("swEV", t_sw.get(("EV", grp - 1)))], sig="swDa")
                            t1 = P.op("act", lambda e: e.activation(out=rden, in_=rden, func=AF.Exp, scale=-1.0),
                                      waits=[("swDa", t1)], sig="swDa")
                            t1 = P.op("dve", lambda e, Yb=Yb: e.tensor_mul(out=ytmp, in0=Yb, in1=rden), waits=[("swDa", t1)], sig="swD")
                            t_sw[("EV", grp)] = P.op("dve", lambda e, q0=q0, step=step: e.tensor_mul(
                                out=ygT[:, step, q0:q0 + 512], in0=ytmp, in1=sg[:, q0:q0 + 512]),
                                waits=[("swD", t1)], sig="swEV")

                    SW_QK(0)
                    for n in range(NT):
                        SW_ACT(n)
                        if n >= 1 and (n - 1) % 4 == 3:
                            SW_EV(n - 1)
                        if n + 1 < NT:
                            SW_QK(n + 1)
                        SW_PVD(n)
                    SW_EV(NT - 1)
                    swn += NT
                    swg += NT // 4
                    t_last_sw = t_sw[("EV", swg - 1)]
                if not is_sb:
                    t_attdve = ("swEV", t_last_sw)
                ckpt(4 + 2 * step)

            ckpt(20)
            wo_v = w_out.rearrange("(k p) e -> p k e", p=128)
            t_wo = None
            for k in range(8):
                t_wo = P.op("pool", lambda e, k=k: e.dma_start(out=wout[:, k, :], in_=wo_v[:, k, :]),
                            waits=[("projpe", t_projpe)], sig="ld_wo", inc=16)
            t_fg = P.op("sp", lambda e: e.dma_start(out=fg_bc, in_=fg_bc_d[:, :]), waits=[("projpe", t_projpe)], sig="ld_fg", inc=16)
            t_xf = {}
            t_o = {}
            t_st = {}
            t_r2 = {}
            t_sq2 = {}

            def issue_xf(tt):
                t_xf[tt] = P.op("sp", lambda e, tt=tt: e.dma_start(out=xf[tt % 3], in_=x[tt * 128:(tt + 1) * 128, :]),
                                waits=[("fr2", t_r2.get(tt - 3)), ("projpe", t_projpe)], sig="ld_xf%d" % (tt % 3), inc=16)

            t_po = {}
            for tt in range(3):
                issue_xf(tt)
            def F_PE(tt):
                for eh in range(2):
                    for kc in range(8):
                        v = P.op("pe", lambda e, tt=tt, eh=eh, kc=kc: e.matmul(
                            po[tt % 2][eh], lhsT=ygT[:, kc, tt * 128:(tt + 1) * 128], rhs=wout[:, kc, eh * 512:(eh + 1) * 512],
                            start=(kc == 0), stop=(kc == 7)),
                            waits=[("ld_wo", t_wo), t_attdve, ("fr2", t_r2.get(tt - 2))],
                            sig=("fpo" if kc == 7 else None))
                    t_po[(tt, eh)] = v

            def F_A(tt):
                rb = rf[tt % 2]
                for eh in range(2):
                    t1 = P.op("dve", lambda e, tt=tt, eh=eh, rb=rb: e.tensor_mul(out=rb[:, eh * 512:(eh + 1) * 512], in0=po[tt % 2][eh],
                                                                               in1=gate_bc[:, eh * 512:(eh + 1) * 512]),
                              waits=[("fpo", t_po[(tt, eh)]), ("fo", t_o.get(tt - 2)), ("fsq", t_sq2.get(tt - 2))], sig="fd")
                t_r2[tt] = P.op("dve", lambda e, tt=tt, rb=rb: e.tensor_add(out=rb, in0=rb, in1=xf[tt % 3]),
                                waits=[("fd", t1), ("ld_xf%d" % (tt % 3), t_xf[tt])], sig="fr2")
                if tt + 3 < NT:
                    issue_xf(tt + 3)
                t_sq2[tt] = P.op("act", lambda e, tt=tt, rb=rb: e.activation(out=junkf, in_=rb, func=AF.Square, accum_out=ss2[:, tt:tt + 1]),
                                 waits=[("fr2", t_r2[tt])], sig="fsq")
                t_sqrt2[tt] = P.op("act", lambda e, tt=tt: e.activation(out=rstd2[:, tt:tt + 1], in_=ss2[:, tt:tt + 1], func=AF.Sqrt,
                                                                        scale=1.0 / D, bias=eps_t[:, 0:1]),
                                   waits=[("fsq", t_sq2[tt])], sig="fsqrt")

            def F_B(tt):
                rb = rf[tt % 2]
                t1 = P.op("dve", lambda e, tt=tt: e.reciprocal(out=rstd2[:, tt:tt + 1], in_=rstd2[:, tt:tt + 1]),
                          waits=[("fsqrt", t_sqrt2[tt])], sig="fd")
                t_o[tt] = P.op("dve", lambda e, tt=tt, rb=rb: e.scalar_tensor_tensor(
                    out=of[tt % 2], in0=rb, scalar=rstd2[:, tt:tt + 1], in1=fg_bc, op0=ALU.mult, op1=ALU.mult),
                    waits=[("fd", t1), ("ld_fg", t_fg), ("ld_out%d" % (tt % 2), t_st.get(tt - 2))], sig="fo")
                t_st[tt] = P.op("sp", lambda e, tt=tt: e.dma_start(out=out[tt * 128:(tt + 1) * 128, :], in_=of[tt % 2]),
                                waits=[("fo", t_o[tt])], sig="ld_out%d" % (tt % 2), inc=16)

            t_sqrt2 = {}
            F_PE(0)
            F_PE(1)
            F_A(0)
            for tt in range(NT):
                if tt + 2 < NT:
                    F_PE(tt + 2)
                if tt + 1 < NT:
                    F_A(tt + 1)
                F_B(tt)
            P.op("sp", lambda e: e.nop(), waits=[("ld_out0", t_st[NT - 2]), ("ld_out1", t_st[NT - 1])])

        try:
            plan_all()
        except _Stop:
            pass

        names = sorted(P.cnt.keys())
        sems = {n: es.enter_context(nc.semaphore(n)) for n in names}
        block = es.enter_context(nc.Block())

        def emit(eng, oplist):
            seen = {}
            for fn, waits, sig, inc, attach in oplist:
                pend = [(name, val) for (name, val) in waits if seen.get(name, 0) < val]
                if attach and len(pend) == 1:
                    (name, val) = pend[0]
                    ins = fn(eng)
                    ins._wait_ge(sems[name], val)
                    seen[name] = val
                else:
                    for (name, val) in pend:
                        eng.wait_ge(sems[name], val)
                        seen[name] = val
                    ins = fn(eng)
                if sig is not None:
                    ins.then_inc(sems[sig], inc)

        @block.sync
        def _(eng):
            emit(eng, P.ops["sp"])

        @block.gpsimd
        def _(eng):
            emit(eng, P.ops["pool"])

        @block.tensor
        def _(eng):
            emit(eng, P.ops["pe"])

        @block.scalar
        def _(eng):
            emit(eng, P.ops["act"])

        @block.vector
        def _(eng):
            emit(eng, P.ops["dve"])
    return nc


_CACHE = {}


def kernel(x, c, w_ada, b_ada, norm_g, w_in, sinks, w_out, final_g):
    x = np.asarray(x, np.float32)
    c = np.asarray(c, np.float32)
    w_ada = np.ascontiguousarray(np.asarray(w_ada, np.float32)[0])
    b_ada = np.asarray(b_ada, np.float32)[0]
    norm_g = np.asarray(norm_g, np.float32)[0]
    w_in = np.ascontiguousarray(np.asarray(w_in, np.float32)[0])
    sinks = np.asarray(sinks, np.float32)[0]
    w_out = np.ascontiguousarray(np.asarray(w_out, np.float32)[0])
    final_g = np.asarray(final_g, np.float32)

    def lay(v):
        return np.ascontiguousarray(v.reshape(-1, 128).T)

    bada_l = lay(b_ada[:2048])
    bg_bc = np.ascontiguousarray(np.broadcast_to(b_ada[2048:3072][None, :], (128, D)))
    normg_l = lay(norm_g)
    fg_bc = np.ascontiguousarray(np.broadcast_to(final_g[None, :], (128, D)))
    sinks_l = np.ascontiguousarray(np.stack([np.repeat(sinks[2 * j:2 * j + 2], 64) for j in range(4)], axis=1))
    consts = make_consts()
    if "nc" not in _CACHE:
        _CACHE["nc"] = build_nc()
    nc = _CACHE["nc"]
    in_maps = []
    for b in range(NCORE):
        in_maps.append({
            "x": np.ascontiguousarray(x[b]), "c_l": lay(c[b]), "w_ada": w_ada, "bada_l": bada_l, "bg_bc": bg_bc,
            "normg_l": normg_l, "w_in": w_in, "sinks_l": sinks_l, "w_out": w_out, "fg_bc": fg_bc, "consts": consts,
        })
    res = run_bass_kernel_spmd(nc, in_maps, core_ids=list(range(NCORE)))
    return np.stack([np.asarray(r["out"], np.float32) for r in res.results], axis=0)
```

```python
from contextlib import ExitStack

import numpy as np
import concourse.bass as bass
import concourse.mybir as mybir
from concourse.bass_utils import run_bass_kernel_spmd

F32 = mybir.dt.float32
BF16 = mybir.dt.bfloat16
AF = mybir.ActivationFunctionType
ALU = mybir.AluOpType

S = 4096
D = 1024
NCORE = 8
NT = S // 128
QC = 1024
NEG = -30000.0
NCONST = 128 * 6 + 2048
C_ID, C_TRI, C_TRIC, C_ZERO, C_NEGM, C_ONES, C_SWB = 0, 128, 256, 384, 512, 640, 768


LEVEL = 99
SW_DBG = 0
PUMP = 2
PUMP_A = 1
PUMP_B = 1
ATTACH = True


class _Stop(Exception):
    pass


def ckpt(level):
    if LEVEL <= level:
        raise _Stop()


class Plan:
    def __init__(self):
        self.ops = {"pe": [], "act": [], "dve": [], "pool": [], "sp": []}
        self.cnt = {}

    def op(self, eng, fn, waits=(), sig=None, inc=1, attach=False):
        v = None
        if sig is not None:
            self.cnt[sig] = self.cnt.get(sig, 0) + inc
            v = self.cnt[sig]
        ws = tuple(w for w in waits if w is not None and w[1] is not None and w[1] > 0)
        self.ops[eng].append((fn, ws, sig, inc, attach and ATTACH))
        return v


def make_consts():
    c = np.zeros((128, NCONST), np.float32)
    j = np.arange(128)[:, None]
    s = np.arange(128)[None, :]
    c[:, C_ID:C_ID + 128] = (j == s)
    c[:, C_TRI:C_TRI + 128] = -1.0 * (j >= s)
    c[:, C_TRIC:C_TRIC + 128] = -1.0 * (j < s)
    c[:, C_NEGM:C_NEGM + 128] = np.where(j < s, 0.0, NEG)
    c[:, C_ONES:C_ONES + 128] = 1.0
    for h in range(8):
        m = 2.0 ** (-8.0 * (h + 1) / 8)
        rel_cur = (s - j).astype(np.float32)
        cur = np.where(s >= j, -m * rel_cur, NEG)
        rel_prev = (128 + s - j).astype(np.float32)
        prev = np.where(j > s, -m * rel_prev, NEG)
        c[:, C_SWB + h * 256:C_SWB + h * 256 + 128] = cur
        c[:, C_SWB + h * 256 + 128:C_SWB + h * 256 + 256] = prev
    return c


def sb_tiles():
    out = []
    for qc in range(S // QC):
        nkb = (QC // 128) * (qc + 1)
        for kb in range(nkb - 1, -1, -1):
            jd = kb - (QC // 128) * qc
            diag = jd >= 0
            c0 = 128 * jd if diag else 0
            out.append((qc, kb, c0, diag, kb == nkb - 1, kb == 0))
    return out


def col_segs(c0, c1=QC):
    segs = []
    a = c0
    while a < c1:
        b = min(c1, (a // 512 + 1) * 512)
        segs.append((a, b))
        a = b
    return segs


def build_nc():
    nc = bass.Bass("TRN2", target_bir_lowering=False)
    x = nc.dram_tensor("x", [S, D], F32, kind="ExternalInput").ap()
    c_l = nc.dram_tensor("c_l", [128, 8], F32, kind="ExternalInput").ap()
    w_ada = nc.dram_tensor("w_ada", [D, 3 * D], F32, kind="ExternalInput").ap()
    bada_l = nc.dram_tensor("bada_l", [128, 16], F32, kind="ExternalInput").ap()
    bg_bc_d = nc.dram_tensor("bg_bc", [128, D], F32, kind="ExternalInput").ap()
    normg_l = nc.dram_tensor("normg_l", [128, 8], F32, kind="ExternalInput").ap()
    w_in = nc.dram_tensor("w_in", [D, 3328], F32, kind="ExternalInput").ap()
    sinks_l = nc.dram_tensor("sinks_l", [128, 4], F32, kind="ExternalInput").ap()
    w_out = nc.dram_tensor("w_out", [D, D], F32, kind="ExternalInput").ap()
    fg_bc_d = nc.dram_tensor("fg_bc", [128, D], F32, kind="ExternalInput").ap()
    consts_d = nc.dram_tensor("consts", [128, NCONST], F32, kind="ExternalInput").ap()
    out = nc.dram_tensor("out", [S, D], F32, kind="ExternalOutput").ap()

    P = Plan()
    es = ExitStack()
    with es:
        def sb(name, shape, dt):
            return es.enter_context(nc.sbuf_tensor(name, shape, dt))

        ygT = sb("ygT", [128, 8, S], BF16)
        hT = sb("hT", [128, 8, S], BF16)
        ov = sb("ov", [128, 30720], BF16)
        cst = sb("cst", [128, NCONST], BF16)
        gate_bc = sb("gate_bc", [128, D], F32)
        c_sb = sb("c_sb", [128, 8], F32)
        etmp = sb("etmp", [128, 8], F32)
        cond = sb("cond", [128, 8], F32)
        bada = sb("bada", [128, 16], F32)
        normg = sb("normg", [128, 8], F32)
        mod_sb = sb("mod_sb", [128, 16], F32)
        gs = sb("gs", [128, 8], F32)
        ones_f = sb("ones_f", [128, 128], F32)
        ss = sb("ss", [128, 32], F32)
        rstd = sb("rstd", [128, 32], F32)
        ss2 = sb("ss2", [128, 32], F32)
        rstd2 = sb("rstd2", [128, 32], F32)
        snk = sb("snk", [128, 4], F32)
        esk = sb("esk", [128, 4], F32)
        eps_t = sb("eps_t", [128, 1], F32)
        wsl1 = sb("wsl1", [128, 4096], BF16)
        ps = es.enter_context(nc.psum_tensor("ps", [128, 4096], F32))

        qT = ov[:, 0:4096]
        kT = ov[:, 4096:8192]
        sg = ov[:, 8192:12288]
        vv = ov[:, 12288:16384].rearrange("p (b c) -> p b c", c=128)
        wsl = ov[:, 16384:20480].rearrange("p (k c) -> p k c", c=512)
        PB = 20480
        Eb = [ov[:, PB + i * 1024:PB + (i + 1) * 1024] for i in range(3)]
        Gb = [ov[:, PB + (3 + i) * 1024:PB + (4 + i) * 1024] for i in range(2)]
        stmp = ov[:, PB + 5 * 1024:PB + 6 * 1024].bitcast(F32)
        Pb = [ov[:, PB + (6 + i) * 1024:PB + (7 + i) * 1024] for i in range(2)]
        Ab = [ov[:, PB + (8 + i) * 1024:PB + (9 + i) * 1024] for i in range(2)]
        Psw = [ov[:, PB + i * 512:PB + (i + 1) * 512] for i in range(2)]
        rden = ov[:, PB + 1024:PB + 2048].bitcast(F32)
        ytmp = ov[:, PB + 2048:PB + 3072].bitcast(F32)
        yflat = ygT[:, :, :].rearrange("p a b -> p (a b)")
        wada = [yflat[:, i * 6144:(i + 1) * 6144].bitcast(F32) for i in range(4)]
        xt = [ov[:, 12288 + i * 2048:12288 + (i + 1) * 2048].bitcast(F32) for i in range(4)]
        xnb = [[ov[:, 20480 + (g * 4 + t) * 1024:20480 + (g * 4 + t + 1) * 1024] for t in range(4)] for g in range(2)]
        junk = ov[:, 28672:29696]
        hflat = hT[:, :, :].rearrange("p a b -> p (a b)")
        condb = yflat[:, 24576:26624].bitcast(F32).rearrange("p (k m) -> p k m", m=128)
        bg_bc = yflat[:, 26624:28672].bitcast(F32)
        wout = hflat[:, 0:8192].rearrange("p (k e) -> p k e", e=1024)
        xf = [hflat[:, 8192 + i * 2048:8192 + (i + 1) * 2048].bitcast(F32) for i in range(3)]
        rf = [hflat[:, 14336 + i * 2048:14336 + (i + 1) * 2048].bitcast(F32) for i in range(2)]
        of = [hflat[:, 18432 + i * 2048:18432 + (i + 1) * 2048].bitcast(F32) for i in range(2)]
        fg_bc = hflat[:, 22528:24576].bitcast(F32)
        junkf = hflat[:, 24576:25600]
        Zp = ps[:, 0:1024]
        Rp = ps[:, 1024:2048]
        Yp = [ps[:, 2048:3072], ps[:, 3072:4096]]
        pp = [ps[:, 0:512], ps[:, 512:1024]]
        tp = [ps[:, i * 512:(i + 1) * 512].bitcast(BF16)[:, 0:512] for i in range(2)]
        modps = ps[:, 1024:1040]
        gateps = ps[:, 2048:3072]
        po = [[ps[:, t * 1024 + e * 512:t * 1024 + (e + 1) * 512] for e in range(2)] for t in range(2)]

        ident = cst[:, C_ID:C_ID + 128]
        triN = cst[:, C_TRI:C_TRI + 128]
        tricN = cst[:, C_TRIC:C_TRIC + 128]
        zer = cst[:, C_ZERO:C_ZERO + 128]
        negm = cst[:, C_NEGM:C_NEGM + 128]
        ones_bf = cst[:, C_ONES:C_ONES + 128]

        def plan_all():
            nonlocal qT, kT, sg, vv
            t_cst = P.op("pool", lambda e: e.dma_start(out=cst[:, :], in_=consts_d[:, :]), sig="ld_cst", inc=16)
            for dst, src in ((c_sb, c_l), (bada, bada_l), (normg, normg_l), (snk, sinks_l)):
                t_small = P.op("sp", lambda e, dst=dst, src=src: e.dma_start(out=dst[:, :], in_=src[:, :]), sig="ld_small", inc=16)
            t_small = P.op("sp", lambda e: e.dma_start(out=bg_bc, in_=bg_bc_d[:, :]), sig="ld_small", inc=16)

            ckpt(0)
            P.op("dve", lambda e: e.memset(ones_f[:, :], 1.0), sig="dve0")
            P.op("dve", lambda e: e.memset(eps_t[:, :], 1e-6), sig="dve0")
            P.op("dve", lambda e: e.memset(ss[:, :], 0.0), sig="dve0")
            t_d = P.op("dve", lambda e: e.memset(ss2[:, :], 0.0), sig="dve0")
            t_ms = t_d
            t_a = P.op("act", lambda e: e.activation(out=etmp[:, :], in_=c_sb[:, :], func=AF.Exp, scale=-1.0),
                       waits=[("ld_small", t_small)], sig="act0")
            t_d = P.op("dve", lambda e: e.tensor_scalar_add(out=etmp[:, :], in0=etmp[:, :], scalar1=1.0),
                       waits=[("act0", t_a), ("dve0", t_d)], sig="dve0")
            t_d = P.op("dve", lambda e: e.reciprocal(out=etmp[:, :], in_=etmp[:, :]), waits=[("dve0", t_d)], sig="dve0")
            t_d = P.op("dve", lambda e: e.tensor_mul(out=cond[:, :], in0=c_sb[:, :], in1=etmp[:, :]),
                       waits=[("dve0", t_d)], sig="dve0")
            t_cond = t_d
            for k in range(8):
                t_d = P.op("dve", lambda e, k=k: e.tensor_scalar(out=condb[:, k, :], in0=ones_f[:, :], scalar1=cond[:, k:k + 1],
                                                                 scalar2=None, op0=ALU.mult),
                           waits=[("dve0", t_cond)], sig="dve0")
            t_condb = t_d
            t_pe0 = {}
            t_wada = {}
            t_xld = {}
            t_sq = {}
            t_xn = {}
            t_tp = {}
            t_ev = {}
            NWB = 4

            def issue_wada(k):
                t_wada[k] = P.op("sp", lambda e, k=k: e.dma_start(out=wada[k % NWB], in_=w_ada[k * 128:(k + 1) * 128, :]),
                                 waits=[("pe0", t_pe0.get(k - NWB))], sig="ld_wada%d" % (k % NWB), inc=16)

            def issue_xload(tt):
                t_xld[tt] = P.op("sp", lambda e, tt=tt: e.dma_start(out=xt[tt % 4], in_=x[tt * 128:(tt + 1) * 128, :]),
                                 waits=[("p1xn", t_xn.get(tt - 4)), ("p1sq", t_sq.get(tt - 4))],
                                 sig="ld_x%d" % (tt % 4), inc=16)

            for k in range(NWB):
                issue_wada(k)
            for tt in range(4):
                issue_xload(tt)
            for k in range(8):
                for j in range(16):
                    P.op("pe", lambda e, k=k, j=j: e.matmul(modps[:, j:j + 1], lhsT=wada[k % NWB][:, j * 128:(j + 1) * 128],
                                                            rhs=cond[:, k:k + 1], start=(k == 0 and j == 0), stop=(k == 7),
                                                            skip_group_check=True),
                         waits=[("ld_wada%d" % (k % NWB), t_wada[k]), ("dve0", t_condb)])
                for eh in range(2):
                    t_pe0[k] = P.op("pe", lambda e, k=k, eh=eh: e.matmul(gateps[:, eh * 512:(eh + 1) * 512], lhsT=condb[:, k, :],
                                                                         rhs=wada[k % NWB][:, 2048 + eh * 512:2048 + (eh + 1) * 512],
                                                                         start=(k == 0), stop=(k == 7)),
                                    waits=[("ld_wada%d" % (k % NWB), t_wada[k]), ("dve0", t_condb)], sig=("pe0" if eh == 1 else None))
                if k + NWB < 8:
                    issue_wada(k + NWB)
                g = k
                for t4 in range(4):
                    tt = g * 4 + t4
                    t_sq[tt] = P.op("act", lambda e, tt=tt: e.activation(out=junk, in_=xt[tt % 4], func=AF.Square,
                                                                         accum_out=ss[:, tt:tt + 1]),
                                    waits=[("ld_x%d" % (tt % 4), t_xld[tt]), ("dve0", t_ms)], sig="p1sq")
                    t_r = P.op("act", lambda e, tt=tt: e.activation(out=rstd[:, tt:tt + 1], in_=ss[:, tt:tt + 1], func=AF.Sqrt,
                                                                    scale=1.0 / D, bias=eps_t[:, 0:1]),
                               waits=[("p1sq", t_sq[tt])], sig="p1sqrt")
                    t_r = P.op("dve", lambda e, tt=tt: e.reciprocal(out=rstd[:, tt:tt + 1], in_=rstd[:, tt:tt + 1]),
                               waits=[("p1sqrt", t_r)], sig="dve1")
                    t_xn[tt] = P.op("dve", lambda e, tt=tt, g=g, t4=t4: e.tensor_scalar(
                        out=xnb[g % 2][t4], in0=xt[tt % 4], scalar1=rstd[:, tt:tt + 1], scalar2=None, op0=ALU.mult),
                        waits=[("dve1", t_r), ("p1tp", t_tp.get((g - 2, 7)))], sig="p1xn")
                    if tt + 4 < NT:
                        issue_xload(tt + 4)
                for j in range(8):
                    prev_ev = t_ev[(g, j - 2)] if j >= 2 else (t_ev[(g - 1, 6 + j)] if g >= 1 else None)
                    for t4 in range(4):
                        t_tp[(g, j)] = P.op("pe", lambda e, g=g, j=j, t4=t4: e.transpose(
                            out=tp[j % 2][:, t4 * 128:(t4 + 1) * 128], in_=xnb[g % 2][t4][:, j * 128:(j + 1) * 128], identity=ident),
                            waits=[("p1xn", t_xn[g * 4 + 3]), ("p1ev", prev_ev), ("ld_cst", t_cst)],
                            sig=("p1tp" if t4 == 3 else None))
                    t_ev[(g, j)] = P.op("act", lambda e, g=g, j=j: e.activation(
                        out=hT[:, j, g * 512:(g + 1) * 512], in_=tp[j % 2], func=AF.Identity),
                        waits=[("p1tp", t_tp[(g, j)])], sig="p1ev")
            t_d = P.op("dve", lambda e: e.tensor_add(out=mod_sb[:, :], in0=modps, in1=bada[:, :]),
                       waits=[("pe0", t_pe0[7]), ("ld_small", t_small)], sig="dve0")
            t_d = P.op("dve", lambda e: e.scalar_tensor_tensor(out=gs[:, :], in0=mod_sb[:, 8:16], scalar=1.0, in1=normg[:, :],
                                                               op0=ALU.add, op1=ALU.mult),
                       waits=[("dve0", t_d)], sig="dve0")
            t_d = P.op("dve", lambda e: e.tensor_add(out=gate_bc[:, :], in0=gateps, in1=bg_bc), sig="dve0")
            t_mod = t_d
            for j in range(8):
                for hh in range(2):
                    t_fix = P.op("dve", lambda e, j=j, hh=hh: e.tensor_scalar(
                        out=hT[:, j, hh * 2048:(hh + 1) * 2048], in0=hT[:, j, hh * 2048:(hh + 1) * 2048],
                        scalar1=gs[:, j:j + 1], scalar2=mod_sb[:, j:j + 1], op0=ALU.mult, op1=ALU.add),
                        waits=[("dve0", t_mod), ("p1ev", t_ev[(7, 7)])], sig="hfix")
            t_hT = t_fix
            ckpt(2)
            t_wsl = None
            t_projpe = None
            t_attpe = None
            t_attdve = None
            t_pev = {}
            nproj = 0
            t_esk = P.op("act", lambda e: e.activation(out=esk[:, :], in_=snk[:, :], func=AF.Exp),
                         waits=[("ld_small", t_small)], sig="act0")
            sbt = sb_tiles()
            sbi = 0
            tk = {}
            n_chunk = 0
            t_evy = {}
            swn = 0
            swg = 0
            t_sw = {}

            wv_all = w_in.rearrange("(k p) c -> p k c", p=128)
            GSv = [(qT, kT, sg, vv),
                   (yflat[:, 16384:20480], yflat[:, 20480:24576], yflat[:, 24576:28672],
                    yflat[:, 28672:32768].rearrange("p (b c) -> p b c", c=128))]
            wslb = [wsl, wsl1[:, :].rearrange("p (k c) -> p k c", c=512)]
            ppb = [ps[:, 3072:3584], ps[:, 3584:4096]]
            pst = {"n": 0, "pev": {}, "projpe": {}, "lastpev": {}, "attdone": {}, "stmp": None}

            def sb_proj_gen(st):
                gq, gk, gsg, gv = GSv[st % 2]
                wb = wslb[st % 2]
                wcol = {"q": 0, "k": 128, "g": 256, "v": 384}
                if st < 4:
                    dl = [(128, 512 + st * 128, 128), (384, 1024 + st * 128, 128), (0, st * 128, 128), (256, 1536 + st * 128, 128)]
                else:
                    dl = [(128, 2560, 64), (192, 2560, 64), (384, 2688, 64), (448, 2688, 64), (0, 2048, 128), (256, 2816, 128)]
                t_w = None
                for (dc, sc, w) in dl:
                    t_w = P.op("pool", lambda e, wb=wb, dc=dc, sc=sc, w=w: e.dma_start(out=wb[:, :, dc:dc + w], in_=wv_all[:, :, sc:sc + w]),
                               waits=[pst["projpe"].get(st - 2), ("hfix", t_hT)], sig="ld_wb%d" % (st % 2), inc=16)
                t_w = ("ld_wb%d" % (st % 2), t_w)
                free_tok = pst["attdone"].get(st - 2)
                tpe = None
                for kind in ("k", "v", "q", "g"):
                    wc = wcol[kind]
                    if kind == "v":
                        for tg in range(8):
                            n = pst["n"]
                            par = n % 2
                            for t4 in range(4):
                                tok = (tg * 4 + t4) * 128
                                for kc in range(8):
                                    last = (kc == 7 and t4 == 3)
                                    tpe = P.op("pe", lambda e, par=par, kc=kc, tok=tok, t4=t4, wb=wb: e.matmul(
                                        ppb[par][:, t4 * 128:(t4 + 1) * 128], lhsT=hT[:, kc, tok:tok + 128], rhs=wb[:, kc, 384:512],
                                        start=(kc == 0), stop=(kc == 7)),
                                        waits=[t_w, pst["pev"].get(n - 2)], sig=("pprojpe" if last else None))
                                    if (not last) and (kc == 3 or kc == 7):
                                        yield
                            tv = P.op("dve", lambda e, par=par, tg=tg, gv=gv: e.tensor_copy(
                                out=gv[:, tg * 4:(tg + 1) * 4, :], in_=ppb[par].rearrange("p (a b) -> p a b", a=4)),
                                waits=[("pprojpe", tpe), free_tok], sig="ppev")
                            pst["pev"][n] = ("ppev", tv)
                            pst["n"] += 1
                            yield
                        continue
                    for tc in range(8):
                        n = pst["n"]
                        par = n % 2
                        for kc in range(8):
                            tpe = P.op("pe", lambda e, par=par, kc=kc, wc=wc, tc=tc, wb=wb: e.matmul(
                                ppb[par], lhsT=wb[:, kc, wc:wc + 128], rhs=hT[:, kc, tc * 512:(tc + 1) * 512],
                                start=(kc == 0), stop=(kc == 7)),
                                waits=[t_w, pst["pev"].get(n - 2)], sig=("pprojpe" if kc == 7 else None))
                            if kc < 7:
                                yield
                        csl = slice(tc * 512, (tc + 1) * 512)
                        if kind == "q":
                            tv = P.op("dve", lambda e, par=par, gq=gq, csl=csl: e.tensor_scalar(
                                out=gq[:, csl], in0=ppb[par], scalar1=0.125, scalar2=None, op0=ALU.mult),
                                waits=[("pprojpe", tpe), free_tok], sig="ppev")
                        elif kind == "k":
                            tv = P.op("dve", lambda e, par=par, gk=gk, csl=csl: e.tensor_copy(out=gk[:, csl], in_=ppb[par]),
                                      waits=[("pprojpe", tpe), free_tok], sig="ppev")
                        else:
                            yield
                            ta = P.op("act", lambda e, par=par: e.activation(out=stmp, in_=ppb[par], func=AF.Exp, scale=-1.0),
                                      waits=[("pprojpe", tpe), pst["stmp"]], sig="ppevA")
                            yield
                            for qq in range(4):
                                qs = slice(qq * 128, (qq + 1) * 128)
                                t1 = P.op("dve", lambda e, qs=qs: e.tensor_scalar_add(out=stmp[:, qs], in0=stmp[:, qs], scalar1=1.0),
                                          waits=[("ppevA", ta)], sig="psil")
                                t1 = P.op("dve", lambda e, qs=qs: e.reciprocal(out=stmp[:, qs], in_=stmp[:, qs]), waits=[("psil", t1)], sig="psil")
                                osl = slice(tc * 512 + qq * 128, tc * 512 + (qq + 1) * 128)
                                tv = P.op("dve", lambda e, par=par, gsg=gsg, osl=osl, qs=qs: e.tensor_mul(
                                    out=gsg[:, osl], in0=ppb[par][:, qs], in1=stmp[:, qs]),
                                    waits=[("psil", t1), free_tok], sig="ppev")
                                if qq < 3:
                                    yield
                            pst["stmp"] = ("ppev", tv)
                        pst["pev"][n] = ("ppev", tv)
                        pst["n"] += 1
                        yield
                pst["projpe"][st] = ("pprojpe", tpe)
                pst["lastpev"][st] = pst["pev"][pst["n"] - 1]

            def pump(gen, k):
                if gen is None:
                    return None
                for _ in range(k):
                    try:
                        next(gen)
                    except StopIteration:
                        return None
                return gen

            def drain(gen):
                while gen is not None:
                    gen = pump(gen, 64)

            gen_cur = sb_proj_gen(0)

            for step in range(8):
                is_sb = step < 4
                if is_sb:
                    cq, ck, cv, cg = step * 128, 512 + step * 128, 1024 + step * 128, 1536 + step * 128
                    kw = 128
                    vw = 128
                    do_kv = True
                else:
                    j = step - 4
                    cq, ck, cv, cg = 2048 + j * 128, 2560 + (j // 2) * 64, 2688 + (j // 2) * 64, 2816 + j * 128
                    kw = 64
                    vw = 64
                    do_kv = (j % 2 == 0)
                if step > 4:
                    wv = w_in.rearrange("(k p) c -> p k c", p=128)
                    dmas = [(0, cq, 128), (256, cg, 128)]
                    if do_kv:
                        if kw == 128:
                            dmas.append((128, ck, 128))
                        else:
                            dmas.append((128, ck, 64))
                            dmas.append((192, ck, 64))
                        if vw == 128:
                            dmas.append((384, cv, 128))
                        else:
                            dmas.append((384, cv, 64))
                            dmas.append((448, cv, 64))
                            vw = 128
                    for (dc, sc, w) in dmas:
                        t_wsl = P.op("pool", lambda e, dc=dc, sc=sc, w=w: e.dma_start(out=wsl[:, :, dc:dc + w], in_=wv[:, :, sc:sc + w]),
                                     waits=[("projpe", t_projpe), ("hfix", t_hT), pst["projpe"].get(4)],
                                     sig="ld_wsl", inc=16)
                    ckpt(2.5 + 2 * step)
                    kinds = [("q", 0), ("g", 256)] + ([("k", 128)] if do_kv else [])
                    for kind, wc in kinds:
                        for tc in range(8):
                            par = nproj % 2
                            for kc in range(8):
                                last = kc == 7
                                t_projpe_new = P.op("pe", lambda e, par=par, kc=kc, wc=wc, tc=tc: e.matmul(
                                    pp[par], lhsT=wsl[:, kc, wc:wc + 128], rhs=hT[:, kc, tc * 512:(tc + 1) * 512],
                                    start=(kc == 0), stop=(kc == 7)),
                                    waits=[("ld_wsl", t_wsl), t_pev.get(nproj - 2), ("hfix", t_hT),
                                           t_attdve],
                                    sig=("projpe" if last else None))
                            t_projpe = t_projpe_new
                            dst = {"q": qT, "k": kT, "g": sg}[kind][:, tc * 512:(tc + 1) * 512]
                            if kind == "q":
                                t_pev[nproj] = ("pev", P.op("dve", lambda e, dst=dst, par=par: e.tensor_scalar(
                                    out=dst, in0=pp[par], scalar1=0.125, scalar2=None, op0=ALU.mult),
                                    waits=[("projpe", t_projpe)], sig="pev"))
                                last_dve_pev = t_pev[nproj]
                            elif kind == "k":
                                t_pev[nproj] = ("pev", P.op("dve", lambda e, dst=dst, par=par: e.tensor_copy(out=dst, in_=pp[par]),
                                                    waits=[("projpe", t_projpe)], sig="pev"))
                                last_dve_pev = t_pev[nproj]
                            else:
                                t_pev[nproj] = ("pevA", P.op("act", lambda e, dst=dst, par=par: e.activation(out=dst, in_=pp[par], func=AF.Silu),
                                                             waits=[("projpe", t_projpe), t_attdve], sig="pevA"))
                                last_act_pev = t_pev[nproj]
                            nproj += 1
                    ckpt(2.75 + 2 * step)
                    if do_kv:
                        for tg in range(8):
                            par = nproj % 2
                            for t4 in range(4):
                                tok = (tg * 4 + t4) * 128
                                for kc in range(8):
                                    last = (kc == 7 and t4 == 3)
                                    t_projpe_new = P.op("pe", lambda e, par=par, kc=kc, tok=tok, t4=t4, vw=vw: e.matmul(
                                        pp[par][:, t4 * 128:t4 * 128 + vw], lhsT=hT[:, kc, tok:tok + 128], rhs=wsl[:, kc, 384:384 + vw],
                                        start=(kc == 0), stop=(kc == 7)),
                                        waits=[("ld_wsl", t_wsl), t_pev.get(nproj - 2), t_attdve],
                                        sig=("projpe" if last else None))
                            t_projpe = t_projpe_new
                            t_pev[nproj] = ("pev", P.op("dve", lambda e, par=par, tg=tg, vw=vw: e.tensor_copy(
                                out=vv[:, tg * 4:(tg + 1) * 4, 0:vw],
                                in_=pp[par].rearrange("p (a b) -> p a b", a=4)[:, :, 0:vw]),
                                waits=[("projpe", t_projpe)], sig="pev"))
                            last_dve_pev = t_pev[nproj]
                            nproj += 1
                    projw = [last_dve_pev, last_act_pev]
                else:
                    drain(gen_cur)
                    projw = [pst["lastpev"][step]]
                    gen_cur = sb_proj_gen(step + 1) if step + 1 < 5 else None
                    qT, kT, sg, vv = GSv[step % 2]
                ckpt(3 + 2 * step)

                if is_sb:
                    QCB = 512
                    tiles = []
                    for qc in range(S // QCB):
                        nkb = (QCB // 128) * (qc + 1)
                        for kb in range(nkb - 1, -1, -1):
                            jd = kb - (QCB // 128) * qc
                            diag = jd >= 0
                            c0 = 128 * jd if diag else 0
                            tiles.append((0, qc, kb, c0, diag, kb == nkb - 1, kb == 0))
                    T = len(tiles)
                    base = sbi
                    chunk_of = {}
                    cc = n_chunk - 1
                    for i, tl in enumerate(tiles):
                        if tl[5]:
                            cc += 1
                        chunk_of[i] = cc
                    Zh = [ps[:, 0:512], ps[:, 512:1024]]
                    Rh = [ps[:, 1024:1536], ps[:, 1536:2048]]
                    Yc = [ps[:, 2048:2560], ps[:, 2560:3072]]

                    def two(ap, c0):
                        if c0 == 0:
                            return ap
                        return ap.rearrange("p (h c) -> p h c", h=2)[:, :, c0:QCB]

                    def QK(i):
                        hp0, qc, kb, c0, diag, cs, ce = tiles[i]
                        g = base + i
                        for hh in range(2):
                            hp = hh * 64
                            last = (hh == 1) and not diag
                            v = P.op("pe", lambda e, hp=hp, hh=hh, kb=kb, qc=qc, c0=c0, diag=diag, kT=kT, qT=qT: e.matmul(
                                Zh[hh][:, c0:QCB], lhsT=kT[hp:hp + 64, kb * 128:(kb + 1) * 128],
                                rhs=qT[hp:hp + 64, qc * QCB + c0:(qc + 1) * QCB], start=True, stop=not diag,
                                skip_group_check=True),
                                waits=[("sbE", tk.get(("E", g - 1)))] + (projw if i < 2 else []),
                                sig=("sbQK" if last else None), attach=True)
                        if diag:
                            for hh in range(2):
                                v = P.op("pe", lambda e, hh=hh, c0=c0: e.matmul(
                                    Zh[hh][:, c0:c0 + 128], lhsT=ident, rhs=negm, start=False, stop=True, skip_group_check=True),
                                    sig=("sbQK" if hh == 1 else None))
                        tk[("QK", g)] = v

                    def ACT_E(i):
                        hp0, qc, kb, c0, diag, cs, ce = tiles[i]
                        g = base + i
                        tk[("E", g)] = P.op("act", lambda e, g=g, c0=c0: e.activation(out=two(Eb[g % 3], c0), in_=two(ps[:, 0:1024], c0), func=AF.Exp),
                                            waits=[("sbQK", tk[("QK", g)]), ("sbA", tk.get(("A", g - 3)))], sig="sbE")

                    def ACT_G(i):
                        hp0, qc, kb, c0, diag, cs, ce = tiles[i]
                        g = base + i
                        tk[("G", g)] = P.op("act", lambda e, g=g, c0=c0: e.activation(out=two(Gb[g % 2], c0), in_=two(Eb[g % 3], c0),
                                                                                   func=AF.Ln, bias=1.0),
                                            waits=[("sbE", tk[("E", g)]), ("sbTRIC", tk.get(("TRIC", g - 2)))], sig="sbG")

                    def ACT_P(i):
                        hp0, qc, kb, c0, diag, cs, ce = tiles[i]
                        g = base + i
                        tk[("P", g)] = P.op("act", lambda e, g=g, c0=c0: e.activation(out=two(Pb[g % 2], c0), in_=two(ps[:, 1024:2048], c0), func=AF.Exp),
                                            waits=[("sbTRI", tk[("TRI", g)]), ("sbA", tk.get(("A", g - 2)))], sig="sbP")

                    def PE_TRI(i):
                        hp0, qc, kb, c0, diag, cs, ce = tiles[i]
                        g = base + i
                        if cs:
                            par = chunk_of[i] % 2
                            for hh in range(2):
                                P.op("pe", lambda e, hh=hh: e.matmul(Rh[hh], lhsT=zer, rhs=cst[:, 0:512], start=True, stop=False,
                                                                     skip_group_check=True),
                                     waits=[("sbP", tk.get(("P", g - 1)))])
                            P.op("pe", lambda e, par=par: e.matmul(Yc[par], lhsT=zer, rhs=cst[:, 0:512], start=True, stop=False,
                                                                   skip_group_check=True),
                                 waits=[("sbEV", t_evy.get(chunk_of[i] - 2))])
                        for hh in range(2):
                            v = P.op("pe", lambda e, g=g, hh=hh, c0=c0: e.matmul(
                                Rh[hh][:, c0:QCB], lhsT=triN, rhs=Gb[g % 2][:, hh * QCB + c0:(hh + 1) * QCB], start=False, stop=False,
                                skip_group_check=True),
                                waits=[("sbG", tk[("G", g)]), ("sbP", tk.get(("P", g - 1)))],
                                sig=("sbTRI" if hh == 1 else None), attach=True)
                        tk[("TRI", g)] = v

                    def PE_TRIC(i):
                        hp0, qc, kb, c0, diag, cs, ce = tiles[i]
                        g = base + i
                        for hh in range(2):
                            v = P.op("pe", lambda e, g=g, hh=hh, c0=c0: e.matmul(
                                Rh[hh][:, c0:QCB], lhsT=tricN, rhs=Gb[g % 2][:, hh * QCB + c0:(hh + 1) * QCB], start=False, stop=False,
                                skip_group_check=True),
                                waits=[("sbP", tk[("P", g)])],
                                sig=("sbTRIC" if hh == 1 else None), attach=True)
                        tk[("TRIC", g)] = v

                    def PE_PV(i):
                        hp0, qc, kb, c0, diag, cs, ce = tiles[i]
                        g = base + i
                        par = chunk_of[i] % 2
                        for hh in range(2):
                            hp = hh * 64
                            v = P.op("pe", lambda e, g=g, hh=hh, hp=hp, kb=kb, par=par, c0=c0, vv=vv: e.matmul(
                                Yc[par][hp:hp + 64, c0:QCB], lhsT=vv[:, kb, hp:hp + 64], rhs=Ab[g % 2][:, hh * QCB + c0:(hh + 1) * QCB],
                                start=False, stop=False, skip_group_check=True),
                                waits=[("sbA", tk[("A", g)])],
                                sig=("sbPV" if hh == 1 else None), attach=True)
                        tk[("PV", g)] = v

                    def DVE_A(i):
                        hp0, qc, kb, c0, diag, cs, ce = tiles[i]
                        g = base + i
                        tk[("A", g)] = P.op("dve", lambda e, g=g, c0=c0: e.tensor_mul(out=two(Ab[g % 2], c0), in0=two(Eb[g % 3], c0),
                                                                                   in1=two(Pb[g % 2], c0)),
                                            waits=[("sbP", tk[("P", g)]), ("sbPV", tk.get(("PV", g - 2)))], sig="sbA")

                    def DVE_EV(i):
                        hp0, qc, kb, c0, diag, cs, ce = tiles[i]
                        g = base + i
                        ch = chunk_of[i]
                        par = ch % 2
                        t_evy[ch] = P.op("dve", lambda e, qc=qc, par=par, step=step, sg=sg: e.tensor_mul(
                            out=ygT[:, step, qc * QCB:(qc + 1) * QCB], in0=Yc[par], in1=sg[:, qc * QCB:(qc + 1) * QCB]),
                            waits=[("sbPV", tk[("PV", g)])], sig="sbEV")

                    QK(0)
                    ACT_E(0)
                    if T > 1:
                        QK(1)
                    ACT_G(0)
                    if T > 1:
                        ACT_E(1)
                    if T > 2:
                        QK(2)
                    PE_TRI(0)
                    for i in range(T):
                        ACT_P(i)
                        PE_TRIC(i)
                        gen_cur = pump(gen_cur, PUMP_A)
                        if i + 1 < T:
                            ACT_G(i + 1)
                            PE_TRI(i + 1)
                        DVE_A(i)
                        PE_PV(i)
                        if tiles[i][6]:
                            DVE_EV(i)
                        if i + 2 < T:
                            ACT_E(i + 2)
                        if i + 3 < T:
                            QK(i + 3)
                        gen_cur = pump(gen_cur, PUMP_B + (1 if i % 4 == 3 else 0))
                    sbi += T
                    n_chunk = cc + 1
                    t_attdve = ("sbEV", t_evy[cc])
                    pst["attdone"][step] = t_attdve
                    qT, kT, sg, vv = GSv[0]
                else:
                    j = step - 4
                    heads = (2 * j, 2 * j + 1)
                    swn0 = swn
                    att_prev = t_attdve

                    def SW_QK(n):
                        gi = swn0 + n
                        zbase = (gi % 2) * 1024
                        wz = 256 if n > 0 else 128
                        for which, kblk in ((0, n), (1, n - 1)):
                            if kblk < 0:
                                continue
                            for hh in range(2):
                                hp = hh * 64
                                zc = zbase + hh * 512 + which * 128
                                P.op("pe", lambda e, zc=zc, hp=hp, kblk=kblk, n=n, which=which: e.matmul(
                                    ps[:, zc:zc + 128], lhsT=kT[hp:hp + 64, kblk * 128:(kblk + 1) * 128],
                                    rhs=qT[hp:hp + 64, n * 128:(n + 1) * 128], start=(which == 0), stop=False, skip_group_check=True),
                                    waits=[("swP", t_sw.get(("P", gi - 2)))] + (projw if n < 2 else []))
                        for hh in range(2):
                            h = heads[hh]
                            bc = C_SWB + h * 256
                            zc = zbase + hh * 512
                            v = P.op("pe", lambda e, zc=zc, bc=bc, wz=wz: e.matmul(
                                ps[:, zc:zc + wz], lhsT=ident, rhs=cst[:, bc:bc + wz], start=False, stop=True, skip_group_check=True),
                                sig=("swQK" if hh == 1 else None))
                        t_sw[("QK", gi)] = v

                    def SW_ACT(n):
                        gi = swn0 + n
                        zbase = (gi % 2) * 1024
                        wz = 256 if n > 0 else 128
                        zin = ps[:, zbase:zbase + 1024].rearrange("p (b c) -> p b c", b=2)[:, :, 0:wz]
                        pout = Psw[gi % 2].rearrange("p (b c) -> p b c", b=2)[:, :, 0:wz]
                        t_sw[("P", gi)] = P.op("act", lambda e, zin=zin, pout=pout: e.activation(out=pout, in_=zin, func=AF.Exp),
                                               waits=[("swQK", t_sw[("QK", gi)]), ("swPV", t_sw.get(("PV", gi - 2)))], sig="swP")

                    def SW_PVD(n):
                        gi = swn0 + n
                        grp = swg + n // 4
                        par = grp % 2
                        Yb = Yp[par][:, 0:512]
                        Db = Yp[par][:, 512:1024]
                        col = (n % 4) * 128
                        for hh in range(2):
                            hp = hh * 64
                            srcs = [(hh * 256, n)] + ([(hh * 256 + 128, n - 1)] if n > 0 else [])
                            ns = len(srcs)
                            for si, (pc, kblk) in enumerate(srcs):
                                P.op("pe", lambda e, Yb=Yb, hp=hp, col=col, kblk=kblk, gi=gi, pc=pc, si=si, ns=ns: e.matmul(
                                    Yb[hp:hp + 64, col:col + 128], lhsT=vv[:, kblk, 0:64], rhs=Psw[gi % 2][:, pc:pc + 128],
                                    start=(si == 0), stop=(si == ns - 1), skip_group_check=True),
                                    waits=[("swP", t_sw[("P", gi)]), ("swEV", t_sw.get(("EV", grp - 2)))] + ([att_prev] if n < 8 else []))
                            for si, (pc, kblk) in enumerate(srcs):
                                v = P.op("pe", lambda e, Db=Db, hp=hp, col=col, gi=gi, pc=pc, si=si, ns=ns: e.matmul(
                                    Db[hp:hp + 64, col:col + 128], lhsT=ones_bf[:, 0:64], rhs=Psw[gi % 2][:, pc:pc + 128],
                                    start=(si == 0), stop=(si == ns - 1), skip_group_check=True),
                                    sig=("swPV" if (hh == 1 and si == ns - 1) else None))
                        t_sw[("PV", gi)] = v
                    def SW_EV(n):
                        gi = swn0 + n
                        grp = swg + n // 4
                        par = grp % 2
                        Yb = Yp[par][:, 0:512]
                        Db = Yp[par][:, 512:1024]
                        if n % 4 == 3:
                            q0 = (n - 3) * 128
                            t1 = P.op("act", lambda e, Db=Db, j=j: e.activation(out=rden, in_=Db, func=AF.Ln, bias=esk[:, j:j + 1]),
                                      waits=[("swPV", t_sw[("PV", gi)]), ("act0", t_esk), ("swEV", t_sw.get(("EV", grp - 1)))], sig="swDa")
                            t1 = P.op("act", lambda e: e.activation(out=rden, in_=rden, func=AF.Exp, scale=-1.0),
                                      waits=[("swDa", t1)], sig="swDa")
                            t1 = P.op("dve", lambda e, Yb=Yb: e.tensor_mul(out=ytmp, in0=Yb, in1=rden), waits=[("swDa", t1)], sig="swD")
                            t_sw[("EV", grp)] = P.op("dve", lambda e, q0=q0, step=step: e.tensor_mul(
                                out=ygT[:, step, q0:q0 + 512], in0=ytmp, in1=sg[:, q0:q0 + 512]),
                                waits=[("swD", t1)], sig="swEV")

                    SW_QK(0)
                    for n in range(NT):
                        SW_ACT(n)
                        if n >= 1 and (n - 1) % 4 == 3:
                            SW_EV(n - 1)
                        if n + 1 < NT:
                            SW_QK(n + 1)
                        SW_PVD(n)
                    SW_EV(NT - 1)
                    swn += NT
                    swg += NT // 4
                    t_last_sw = t_sw[("EV", swg - 1)]
                if not is_sb:
                    t_attdve = ("swEV", t_last_sw)
                ckpt(4 + 2 * step)

            ckpt(20)
            wo_v = w_out.rearrange("(k p) e -> p k e", p=128)
            t_wo = None
            for k in range(8):
                t_wo = P.op("pool", lambda e, k=k: e.dma_start(out=wout[:, k, :], in_=wo_v[:, k, :]),
                            waits=[("projpe", t_projpe)], sig="ld_wo", inc=16)
            t_fg = P.op("sp", lambda e: e.dma_start(out=fg_bc, in_=fg_bc_d[:, :]), waits=[("projpe", t_projpe)], sig="ld_fg", inc=16)
            t_xf = {}
            t_o = {}
            t_st = {}
            t_r2 = {}
            t_sq2 = {}

            def issue_xf(tt):
                t_xf[tt] = P.op("sp", lambda e, tt=tt: e.dma_start(out=xf[tt % 3], in_=x[tt * 128:(tt + 1) * 128, :]),
                                waits=[("fr2", t_r2.get(tt - 3)), ("projpe", t_projpe)], sig="ld_xf%d" % (tt % 3), inc=16)

            t_po = {}
            for tt in range(3):
                issue_xf(tt)
            def F_PE(tt):
                for eh in range(2):
                    for kc in range(8):
                        v = P.op("pe", lambda e, tt=tt, eh=eh, kc=kc: e.matmul(
                            po[tt % 2][eh], lhsT=ygT[:, kc, tt * 128:(tt + 1) * 128], rhs=wout[:, kc, eh * 512:(eh + 1) * 512],
                            start=(kc == 0), stop=(kc == 7)),
                            waits=[("ld_wo", t_wo), t_attdve, ("fr2", t_r2.get(tt - 2))],
                            sig=("fpo" if kc == 7 else None))
                    t_po[(tt, eh)] = v

            def F_A(tt):
                rb = rf[tt % 2]
                for eh in range(2):
                    t1 = P.op("dve", lambda e, tt=tt, eh=eh, rb=rb: e.tensor_mul(out=rb[:, eh * 512:(eh + 1) * 512], in0=po[tt % 2][eh],
                                                                               in1=gate_bc[:, eh * 512:(eh + 1) * 512]),
                              waits=[("fpo", t_po[(tt, eh)]), ("fo", t_o.get(tt - 2)), ("fsq", t_sq2.get(tt - 2))], sig="fd")
                t_r2[tt] = P.op("dve", lambda e, tt=tt, rb=rb: e.tensor_add(out=rb, in0=rb, in1=xf[tt % 3]),
                                waits=[("fd", t1), ("ld_xf%d" % (tt % 3), t_xf[tt])], sig="fr2")
                if tt + 3 < NT:
                    issue_xf(tt + 3)
                t_sq2[tt] = P.op("act", lambda e, tt=tt, rb=rb: e.activation(out=junkf, in_=rb, func=AF.Square, accum_out=ss2[:, tt:tt + 1]),
                                 waits=[("fr2", t_r2[tt])], sig="fsq")
                t_sqrt2[tt] = P.op("act", lambda e, tt=tt: e.activation(out=rstd2[:, tt:tt + 1], in_=ss2[:, tt:tt + 1], func=AF.Sqrt,
                                                                        scale=1.0 / D, bias=eps_t[:, 0:1]),
                                   waits=[("fsq", t_sq2[tt])], sig="fsqrt")

            def F_B(tt):
                rb = rf[tt % 2]
                t1 = P.op("dve", lambda e, tt=tt: e.reciprocal(out=rstd2[:, tt:tt + 1], in_=rstd2[:, tt:tt + 1]),
                          waits=[("fsqrt", t_sqrt2[tt])], sig="fd")
                t_o[tt] = P.op("dve", lambda e, tt=tt, rb=rb: e.scalar_tensor_tensor(
                    out=of[tt % 2], in0=rb, scalar=rstd2[:, tt:tt + 1], in1=fg_bc, op0=ALU.mult, op1=ALU.mult),
                    waits=[("fd", t1), ("ld_fg", t_fg), ("ld_out%d" % (tt % 2), t_st.get(tt - 2))], sig="fo")
                t_st[tt] = P.op("sp", lambda e, tt=tt: e.dma_start(out=out[tt * 128:(tt + 1) * 128, :], in_=of[tt % 2]),
                                waits=[("fo", t_o[tt])], sig="ld_out%d" % (tt % 2), inc=16)

            t_sqrt2 = {}
            F_PE(0)
            F_PE(1)
            F_A(0)
            for tt in range(NT):
                if tt + 2 < NT:
                    F_PE(tt + 2)
                if tt + 1 < NT:
                    F_A(tt + 1)
                F_B(tt)
            P.op("sp", lambda e: e.nop(), waits=[("ld_out0", t_st[NT - 2]), ("ld_out1", t_st[NT - 1])])

        try:
            plan_all()
        except _Stop:
            pass

        names = sorted(P.cnt.keys())
        sems = {n: es.enter_context(nc.semaphore(n)) for n in names}
        block = es.enter_context(nc.Block())

        def emit(eng, oplist):
            seen = {}
            for fn, waits, sig, inc, attach in oplist:
                pend = [(name, val) for (name, val) in waits if seen.get(name, 0) < val]
                if attach and len(pend) == 1:
                    (name, val) = pend[0]
                    ins = fn(eng)
                    ins._wait_ge(sems[name], val)
                    seen[name] = val
                else:
                    for (name, val) in pend:
                        eng.wait_ge(sems[name], val)
                        seen[name] = val
                    ins = fn(eng)
                if sig is not None:
                    ins.then_inc(sems[sig], inc)

        @block.sync
        def _(eng):
            emit(eng, P.ops["sp"])

        @block.gpsimd
        def _(eng):
            emit(eng, P.ops["pool"])

        @block.tensor
        def _(eng):
            emit(eng, P.ops["pe"])

        @block.scalar
        def _(eng):
            emit(eng, P.ops["act"])

        @block.vector
        def _(eng):
            emit(eng, P.ops["dve"])
    return nc


_CACHE = {}


def kernel(x, c, w_ada, b_ada, norm_g, w_in, sinks, w_out, final_g):
    x = np.asarray(x, np.float32)
    c = np.asarray(c, np.float32)
    w_ada = np.ascontiguousarray(np.asarray(w_ada, np.float32)[0])
    b_ada = np.asarray(b_ada, np.float32)[0]
    norm_g = np.asarray(norm_g, np.float32)[0]
    w_in = np.ascontiguousarray(np.asarray(w_in, np.float32)[0])
    sinks = np.asarray(sinks, np.float32)[0]
    w_out = np.ascontiguousarray(np.asarray(w_out, np.float32)[0])
    final_g = np.asarray(final_g, np.float32)

    def lay(v):
        return np.ascontiguousarray(v.reshape(-1, 128).T)

    bada_l = lay(b_ada[:2048])
    bg_bc = np.ascontiguousarray(np.broadcast_to(b_ada[2048:3072][None, :], (128, D)))
    normg_l = lay(norm_g)
    fg_bc = np.ascontiguousarray(np.broadcast_to(final_g[None, :], (128, D)))
    sinks_l = np.ascontiguousarray(np.stack([np.repeat(sinks[2 * j:2 * j + 2], 64) for j in range(4)], axis=1))
    consts = make_consts()
    if "nc" not in _CACHE:
        _CACHE["nc"] = build_nc()
    nc = _CACHE["nc"]
    in_maps = []
    for b in range(NCORE):
        in_maps.append({
            "x": np.ascontiguousarray(x[b]), "c_l": lay(c[b]), "w_ada": w_ada, "bada_l": bada_l, "bg_bc": bg_bc,
            "normg_l": normg_l, "w_in": w_in, "sinks_l": sinks_l, "w_out": w_out, "fg_bc": fg_bc, "consts": consts,
        })
    res = run_bass_kernel_spmd(nc, in_maps, core_ids=list(range(NCORE)))
    return np.stack([np.asarray(r["out"], np.float32) for r in res.results], axis=0)
```

```python
from contextlib import ExitStack

import numpy as np
import concourse.bass as bass
import concourse.mybir as mybir
from concourse.bass_utils import run_bass_kernel_spmd

F32 = mybir.dt.float32
BF16 = mybir.dt.bfloat16
AF = mybir.ActivationFunctionType
ALU = mybir.AluOpType

S = 4096
D = 1024
NCORE = 8
NT = S // 128
QC = 1024
NEG = -30000.0
NCONST = 128 * 6 + 2048
C_ID, C_TRI, C_TRIC, C_ZERO, C_NEGM, C_ONES, C_SWB = 0, 128, 256, 384, 512, 640, 768


LEVEL = 99
SW_DBG = 0
PUMP = 2
PUMP_A = 1
PUMP_B = 1
ATTACH = True


class _Stop(Exception):
    pass


def ckpt(level):
    if LEVEL <= level:
        raise _Stop()


class Plan:
    def __init__(self):
        self.ops = {"pe": [], "act": [], "dve": [], "pool": [], "sp": []}
        self.cnt = {}

    def op(self, eng, fn, waits=(), sig=None, inc=1, attach=False):
        v = None
        if sig is not None:
            self.cnt[sig] = self.cnt.get(sig, 0) + inc
            v = self.cnt[sig]
        ws = tuple(w for w in waits if w is not None and w[1] is not None and w[1] > 0)
        self.ops[eng].append((fn, ws, sig, inc, attach and ATTACH))
        return v


def make_consts():
    c = np.zeros((128, NCONST), np.float32)
    j = np.arange(128)[:, None]
    s = np.arange(128)[None, :]
    c[:, C_ID:C_ID + 128] = (j == s)
    c[:, C_TRI:C_TRI + 128] = -1.0 * (j >= s)
    c[:, C_TRIC:C_TRIC + 128] = -1.0 * (j < s)
    c[:, C_NEGM:C_NEGM + 128] = np.where(j < s, 0.0, NEG)
    c[:, C_ONES:C_ONES + 128] = 1.0
    for h in range(8):
        m = 2.0 ** (-8.0 * (h + 1) / 8)
        rel_cur = (s - j).astype(np.float32)
        cur = np.where(s >= j, -m * rel_cur, NEG)
        rel_prev = (128 + s - j).astype(np.float32)
        prev = np.where(j > s, -m * rel_prev, NEG)
        c[:, C_SWB + h * 256:C_SWB + h * 256 + 128] = cur
        c[:, C_SWB + h * 256 + 128:C_SWB + h * 256 + 256] = prev
    return c


def sb_tiles():
    out = []
    for qc in range(S // QC):
        nkb = (QC // 128) * (qc + 1)
        for kb in range(nkb - 1, -1, -1):
            jd = kb - (QC // 128) * qc
            diag = jd >= 0
            c0 = 128 * jd if diag else 0
            out.append((qc, kb, c0, diag, kb == nkb - 1, kb == 0))
    return out


def col_segs(c0, c1=QC):
    segs = []
    a = c0
    while a < c1:
        b = min(c1, (a // 512 + 1) * 512)
        segs.append((a, b))
        a = b
    return segs


def build_nc():
    nc = bass.Bass("TRN2", target_bir_lowering=False)
    x = nc.dram_tensor("x", [S, D], F32, kind="ExternalInput").ap()
    c_l = nc.dram_tensor("c_l", [128, 8], F32, kind="ExternalInput").ap()
    w_ada = nc.dram_tensor("w_ada", [D, 3 * D], F32, kind="ExternalInput").ap()
    bada_l = nc.dram_tensor("bada_l", [128, 16], F32, kind="ExternalInput").ap()
    bg_bc_d = nc.dram_tensor("bg_bc", [128, D], F32, kind="ExternalInput").ap()
    normg_l = nc.dram_tensor("normg_l", [128, 8], F32, kind="ExternalInput").ap()
    w_in = nc.dram_tensor("w_in", [D, 3328], F32, kind="ExternalInput").ap()
    sinks_l = nc.dram_tensor("sinks_l", [128, 4], F32, kind="ExternalInput").ap()
    w_out = nc.dram_tensor("w_out", [D, D], F32, kind="ExternalInput").ap()
    fg_bc_d = nc.dram_tensor("fg_bc", [128, D], F32, kind="ExternalInput").ap()
    consts_d = nc.dram_tensor("consts", [128, NCONST], F32, kind="ExternalInput").ap()
    out = nc.dram_tensor("out", [S, D], F32, kind="ExternalOutput").ap()

    P = Plan()
    es = ExitStack()
    with es:
        def sb(name, shape, dt):
            return es.enter_context(nc.sbuf_tensor(name, shape, dt))

        ygT = sb("ygT", [128, 8, S], BF16)
        hT = sb("hT", [128, 8, S], BF16)
        ov = sb("ov", [128, 30720], BF16)
        cst = sb("cst", [128, NCONST], BF16)
        gate_bc = sb("gate_bc", [128, D], F32)
        c_sb = sb("c_sb", [128, 8], F32)
        etmp = sb("etmp", [128, 8], F32)
        cond = sb("cond", [128, 8], F32)
        bada = sb("bada", [128, 16], F32)
        normg = sb("normg", [128, 8], F32)
        mod_sb = sb("mod_sb", [128, 16], F32)
        gs = sb("gs", [128, 8], F32)
        ones_f = sb("ones_f", [128, 128], F32)
        ss = sb("ss", [128, 32], F32)
        rstd = sb("rstd", [128, 32], F32)
        ss2 = sb("ss2", [128, 32], F32)
        rstd2 = sb("rstd2", [128, 32], F32)
        snk = sb("snk", [128, 4], F32)
        esk = sb("esk", [128, 4], F32)
        eps_t = sb("eps_t", [128, 1], F32)
        wsl1 = sb("wsl1", [128, 4096], BF16)
        ps = es.enter_context(nc.psum_tensor("ps", [128, 4096], F32))

        qT = ov[:, 0:4096]
        kT = ov[:, 4096:8192]
        sg = ov[:, 8192:12288]
        vv = ov[:, 12288:16384].rearrange("p (b c) -> p b c", c=128)
        wsl = ov[:, 16384:20480].rearrange("p (k c) -> p k c", c=512)
        PB = 20480
        Eb = [ov[:, PB + i * 1024:PB + (i + 1) * 1024] for i in range(3)]
        Gb = [ov[:, PB + (3 + i) * 1024:PB + (4 + i) * 1024] for i in range(2)]
        stmp = ov[:, PB + 5 * 1024:PB + 6 * 1024].bitcast(F32)
        Pb = [ov[:, PB + (6 + i) * 1024:PB + (7 + i) * 1024] for i in range(2)]
        Ab = [ov[:, PB + (8 + i) * 1024:PB + (9 + i) * 1024] for i in range(2)]
        Psw = [ov[:, PB + i * 512:PB + (i + 1) * 512] for i in range(2)]
        rden = ov[:, PB + 1024:PB + 2048].bitcast(F32)
        ytmp = ov[:, PB + 2048:PB + 3072].bitcast(F32)
        yflat = ygT[:, :, :].rearrange("p a b -> p (a b)")
        wada = [yflat[:, i * 6144:(i + 1) * 6144].bitcast(F32) for i in range(4)]
        xt = [ov[:, 12288 + i * 2048:12288 + (i + 1) * 2048].bitcast(F32) for i in range(4)]
        xnb = [[ov[:, 20480 + (g * 4 + t) * 1024:20480 + (g * 4 + t + 1) * 1024] for t in range(4)] for g in range(2)]
        junk = ov[:, 28672:29696]
        hflat = hT[:, :, :].rearrange("p a b -> p (a b)")
        condb = yflat[:, 24576:26624].bitcast(F32).rearrange("p (k m) -> p k m", m=128)
        bg_bc = yflat[:, 26624:28672].bitcast(F32)
        wout = hflat[:, 0:8192].rearrange("p (k e) -> p k e", e=1024)
        xf = [hflat[:, 8192 + i * 2048:8192 + (i + 1) * 2048].bitcast(F32) for i in range(3)]
        rf = [hflat[:, 14336 + i * 2048:14336 + (i + 1) * 2048].bitcast(F32) for i in range(2)]
        of = [hflat[:, 18432 + i * 2048:18432 + (i + 1) * 2048].bitcast(F32) for i in range(2)]
        fg_bc = hflat[:, 22528:24576].bitcast(F32)
        junkf = hflat[:, 24576:25600]
        Zp = ps[:, 0:1024]
        Rp = ps[:, 1024:2048]
        Yp = [ps[:, 2048:3072], ps[:, 3072:4096]]
        pp = [ps[:, 0:512], ps[:, 512:1024]]
        tp = [ps[:, i * 512:(i + 1) * 512].bitcast(BF16)[:, 0:512] for i in range(2)]
        modps = ps[:, 1024:1040]
        gateps = ps[:, 2048:3072]
        po = [[ps[:, t * 1024 + e * 512:t * 1024 + (e + 1) * 512] for e in range(2)] for t in range(2)]

        ident = cst[:, C_ID:C_ID + 128]
        triN = cst[:, C_TRI:C_TRI + 128]
        tricN = cst[:, C_TRIC:C_TRIC + 128]
        zer = cst[:, C_ZERO:C_ZERO + 128]
        negm = cst[:, C_NEGM:C_NEGM + 128]
        ones_bf = cst[:, C_ONES:C_ONES + 128]

        def plan_all():
            nonlocal qT, kT, sg, vv
            t_cst = P.op("pool", lambda e: e.dma_start(out=cst[:, :], in_=consts_d[:, :]), sig="ld_cst", inc=16)
            for dst, src in ((c_sb, c_l), (bada, bada_l), (normg, normg_l), (snk, sinks_l)):
                t_small = P.op("sp", lambda e, dst=dst, src=src: e.dma_start(out=dst[:, :], in_=src[:, :]), sig="ld_small", inc=16)
            t_small = P.op("sp", lambda e: e.dma_start(out=bg_bc, in_=bg_bc_d[:, :]), sig="ld_small", inc=16)

            ckpt(0)
            P.op("dve", lambda e: e.memset(ones_f[:, :], 1.0), sig="dve0")
            P.op("dve", lambda e: e.memset(eps_t[:, :], 1e-6), sig="dve0")
            P.op("dve", lambda e: e.memset(ss[:, :], 0.0), sig="dve0")
            t_d = P.op("dve", lambda e: e.memset(ss2[:, :], 0.0), sig="dve0")
            t_ms = t_d
            t_a = P.op("act", lambda e: e.activation(out=etmp[:, :], in_=c_sb[:, :], func=AF.Exp, scale=-1.0),
                       waits=[("ld_small", t_small)], sig="act0")
            t_d = P.op("dve", lambda e: e.tensor_scalar_add(out=etmp[:, :], in0=etmp[:, :], scalar1=1.0),
                       waits=[("act0", t_a), ("dve0", t_d)], sig="dve0")
            t_d = P.op("dve", lambda e: e.reciprocal(out=etmp[:, :], in_=etmp[:, :]), waits=[("dve0", t_d)], sig="dve0")
            t_d = P.op("dve", lambda e: e.tensor_mul(out=cond[:, :], in0=c_sb[:, :], in1=etmp[:, :]),
                       waits=[("dve0", t_d)], sig="dve0")
            t_cond = t_d
            for k in range(8):
                t_d = P.op("dve", lambda e, k=k: e.tensor_scalar(out=condb[:, k, :], in0=ones_f[:, :], scalar1=cond[:, k:k + 1],
                                                                 scalar2=None, op0=ALU.mult),
                           waits=[("dve0", t_cond)], sig="dve0")
            t_condb = t_d
            t_pe0 = {}
            t_wada = {}
            t_xld = {}
            t_sq = {}
            t_xn = {}
            t_tp = {}
            t_ev = {}
            NWB = 4

            def issue_wada(k):
                t_wada[k] = P.op("sp", lambda e, k=k: e.dma_start(out=wada[k % NWB], in_=w_ada[k * 128:(k + 1) * 128, :]),
                                 waits=[("pe0", t_pe0.get(k - NWB))], sig="ld_wada%d" % (k % NWB), inc=16)

            def issue_xload(tt):
                t_xld[tt] = P.op("sp", lambda e, tt=tt: e.dma_start(out=xt[tt % 4], in_=x[tt * 128:(tt + 1) * 128, :]),
                                 waits=[("p1xn", t_xn.get(tt - 4)), ("p1sq", t_sq.get(tt - 4))],
                                 sig="ld_x%d" % (tt % 4), inc=16)

            for k in range(NWB):
                issue_wada(k)
            for tt in range(4):
                issue_xload(tt)
            for k in range(8):
                for j in range(16):
                    P.op("pe", lambda e, k=k, j=j: e.matmul(modps[:, j:j + 1], lhsT=wada[k % NWB][:, j * 128:(j + 1) * 128],
                                                            rhs=cond[:, k:k + 1], start=(k == 0 and j == 0), stop=(k == 7),
                                                            skip_group_check=True),
                         waits=[("ld_wada%d" % (k % NWB), t_wada[k]), ("dve0", t_condb)])
                for eh in range(2):
                    t_pe0[k] = P.op("pe", lambda e, k=k, eh=eh: e.matmul(gateps[:, eh * 512:(eh + 1) * 512], lhsT=condb[:, k, :],
                                                                         rhs=wada[k % NWB][:, 2048 + eh * 512:2048 + (eh + 1) * 512],
                                                                         start=(k == 0), stop=(k == 7)),
                                    waits=[("ld_wada%d" % (k % NWB), t_wada[k]), ("dve0", t_condb)], sig=("pe0" if eh == 1 else None))
                if k + NWB < 8:
                    issue_wada(k + NWB)
                g = k
                for t4 in range(4):
                    tt = g * 4 + t4
                    t_sq[tt] = P.op("act", lambda e, tt=tt: e.activation(out=junk, in_=xt[tt % 4], func=AF.Square,
                                                                         accum_out=ss[:, tt:tt + 1]),
                                    waits=[("ld_x%d" % (tt % 4), t_xld[tt]), ("dve0", t_ms)], sig="p1sq")
                    t_r = P.op("act", lambda e, tt=tt: e.activation(out=rstd[:, tt:tt + 1], in_=ss[:, tt:tt + 1], func=AF.Sqrt,
                                                                    scale=1.0 / D, bias=eps_t[:, 0:1]),
                               waits=[("p1sq", t_sq[tt])], sig="p1sqrt")
                    t_r = P.op("dve", lambda e, tt=tt: e.reciprocal(out=rstd[:, tt:tt + 1], in_=rstd[:, tt:tt + 1]),
                               waits=[("p1sqrt", t_r)], sig="dve1")
                    t_xn[tt] = P.op("dve", lambda e, tt=tt, g=g, t4=t4: e.tensor_scalar(
                        out=xnb[g % 2][t4], in0=xt[tt % 4], scalar1=rstd[:, tt:tt + 1], scalar2=None, op0=ALU.mult),
                        waits=[("dve1", t_r), ("p1tp", t_tp.get((g - 2, 7)))], sig="p1xn")
                    if tt + 4 < NT:
                        issue_xload(tt + 4)
                for j in range(8):
                    prev_ev = t_ev[(g, j - 2)] if j >= 2 else (t_ev[(g - 1, 6 + j)] if g >= 1 else None)
                    for t4 in range(4):
                        t_tp[(g, j)] = P.op("pe", lambda e, g=g, j=j, t4=t4: e.transpose(
                            out=tp[j % 2][:, t4 * 128:(t4 + 1) * 128], in_=xnb[g % 2][t4][:, j * 128:(j + 1) * 128], identity=ident),
                            waits=[("p1xn", t_xn[g * 4 + 3]), ("p1ev", prev_ev), ("ld_cst", t_cst)],
                            sig=("p1tp" if t4 == 3 else None))
                    t_ev[(g, j)] = P.op("act", lambda e, g=g, j=j: e.activation(
                        out=hT[:, j, g * 512:(g + 1) * 512], in_=tp[j % 2], func=AF.Identity),
                        waits=[("p1tp", t_tp[(g, j)])], sig="p1ev")
            t_d = P.op("dve", lambda e: e.tensor_add(out=mod_sb[:, :], in0=modps, in1=bada[:, :]),
                       waits=[("pe0", t_pe0[7]), ("ld_small", t_small)], sig="dve0")
            t_d = P.op("dve", lambda e: e.scalar_tensor_tensor(out=gs[:, :], in0=mod_sb[:, 8:16], scalar=1.0, in1=normg[:, :],
                                                               op0=ALU.add, op1=ALU.mult),
                       waits=[("dve0", t_d)], sig="dve0")
            t_d = P.op("dve", lambda e: e.tensor_add(out=gate_bc[:, :], in0=gateps, in1=bg_bc), sig="dve0")
            t_mod = t_d
            for j in range(8):
                for hh in range(2):
                    t_fix = P.op("dve", lambda e, j=j, hh=hh: e.tensor_scalar(
                        out=hT[:, j, hh * 2048:(hh + 1) * 2048], in0=hT[:, j, hh * 2048:(hh + 1) * 2048],
                        scalar1=gs[:, j:j + 1], scalar2=mod_sb[:, j:j + 1], op0=ALU.mult, op1=ALU.add),
                        waits=[("dve0", t_mod), ("p1ev", t_ev[(7, 7)])], sig="hfix")
            t_hT = t_fix
            ckpt(2)
            t_wsl = None
            t_projpe = None
            t_attpe = None
            t_attdve = None
            t_pev = {}
            nproj = 0
            t_esk = P.op("act", lambda e: e.activation(out=esk[:, :], in_=snk[:, :], func=AF.Exp),
                         waits=[("ld_small", t_small)], sig="act0")
            sbt = sb_tiles()
            sbi = 0
            tk = {}
            n_chunk = 0
            t_evy = {}
            swn = 0
            swg = 0
            t_sw = {}

            wv_all = w_in.rearrange("(k p) c -> p k c", p=128)
            GSv = [(qT, kT, sg, vv),
                   (yflat[:, 16384:20480], yflat[:, 20480:24576], yflat[:, 24576:28672],
                    yflat[:, 28672:32768].rearrange("p (b c) -> p b c", c=128))]
            wslb = [wsl, wsl1[:, :].rearrange("p (k c) -> p k c", c=512)]
            ppb = [ps[:, 3072:3584], ps[:, 3584:4096]]
            pst = {"n": 0, "pev": {}, "projpe": {}, "lastpev": {}, "attdone": {}, "stmp": None}

            def sb_proj_gen(st):
                gq, gk, gsg, gv = GSv[st % 2]
                wb = wslb[st % 2]
                wcol = {"q": 0, "k": 128, "g": 256, "v": 384}
                if st < 4:
                    dl = [(128, 512 + st * 128, 128), (384, 1024 + st * 128, 128), (0, st * 128, 128), (256, 1536 + st * 128, 128)]
                else:
                    dl = [(128, 2560, 64), (192, 2560, 64), (384, 2688, 64), (448, 2688, 64), (0, 2048, 128), (256, 2816, 128)]
                t_w = None
                for (dc, sc, w) in dl:
                    t_w = P.op("pool", lambda e, wb=wb, dc=dc, sc=sc, w=w: e.dma_start(out=wb[:, :, dc:dc + w], in_=wv_all[:, :, sc:sc + w]),
                               waits=[pst["projpe"].get(st - 2), ("hfix", t_hT)], sig="ld_wb%d" % (st % 2), inc=16)
                t_w = ("ld_wb%d" % (st % 2), t_w)
                free_tok = pst["attdone"].get(st - 2)
                tpe = None
                for kind in ("k", "v", "q", "g"):
                    wc = wcol[kind]
                    if kind == "v":
                        for tg in range(8):
                            n = pst["n"]
                            par = n % 2
                            for t4 in range(4):
                                tok = (tg * 4 + t4) * 128
                                for kc in range(8):
                                    last = (kc == 7 and t4 == 3)
                                    tpe = P.op("pe", lambda e, par=par, kc=kc, tok=tok, t4=t4, wb=wb: e.matmul(
                                        ppb[par][:, t4 * 128:(t4 + 1) * 128], lhsT=hT[:, kc, tok:tok + 128], rhs=wb[:, kc, 384:512],
                                        start=(kc == 0), stop=(kc == 7)),
                                        waits=[t_w, pst["pev"].get(n - 2)], sig=("pprojpe" if last else None))
                                    if (not last) and (kc == 3 or kc == 7):
                                        yield
                            tv = P.op("dve", lambda e, par=par, tg=tg, gv=gv: e.tensor_copy(
                                out=gv[:, tg * 4:(tg + 1) * 4, :], in_=ppb[par].rearrange("p (a b) -> p a b", a=4)),
                                waits=[("pprojpe", tpe), free_tok], sig="ppev")
                            pst["pev"][n] = ("ppev", tv)
                            pst["n"] += 1
                            yield
                        continue
                    for tc in range(8):
                        n = pst["n"]
                        par = n % 2
                        for kc in range(8):
                            tpe = P.op("pe", lambda e, par=par, kc=kc, wc=wc, tc=tc, wb=wb: e.matmul(
                                ppb[par], lhsT=wb[:, kc, wc:wc + 128], rhs=hT[:, kc, tc * 512:(tc + 1) * 512],
                                start=(kc == 0), stop=(kc == 7)),
                                waits=[t_w, pst["pev"].get(n - 2)], sig=("pprojpe" if kc == 7 else None))
                            if kc < 7:
                                yield
                        csl = slice(tc * 512, (tc + 1) * 512)
                        if kind == "q":
                            tv = P.op("dve", lambda e, par=par, gq=gq, csl=csl: e.tensor_scalar(
                                out=gq[:, csl], in0=ppb[par], scalar1=0.125, scalar2=None, op0=ALU.mult),
                                waits=[("pprojpe", tpe), free_tok], sig="ppev")
                        elif kind == "k":
                            tv = P.op("dve", lambda e, par=par, gk=gk, csl=csl: e.tensor_copy(out=gk[:, csl], in_=ppb[par]),
                                      waits=[("pprojpe", tpe), free_tok], sig="ppev")
                        elif st == 0:
                            tv = P.op("act", lambda e, par=par, gsg=gsg, csl=csl: e.activation(out=gsg[:, csl], in_=ppb[par], func=AF.Silu),
                                      waits=[("pprojpe", tpe), free_tok], sig="ppevS")
                            pst["pev"][n] = ("ppevS", tv)
                            pst["lastact"] = ("ppevS", tv)
                            pst["n"] += 1
                            yield
                            continue
                        else:
                            yield
                            ta = P.op("act", lambda e, par=par: e.activation(out=stmp, in_=ppb[par], func=AF.Exp, scale=-1.0),
                                      waits=[("pprojpe", tpe), pst["stmp"]], sig="ppevA")
                            yield
                            for qq in range(4):
                                qs = slice(qq * 128, (qq + 1) * 128)
                                t1 = P.op("dve", lambda e, qs=qs: e.tensor_scalar_add(out=stmp[:, qs], in0=stmp[:, qs], scalar1=1.0),
                                          waits=[("ppevA", ta)], sig="psil")
                                t1 = P.op("dve", lambda e, qs=qs: e.reciprocal(out=stmp[:, qs], in_=stmp[:, qs]), waits=[("psil", t1)], sig="psil")
                                osl = slice(tc * 512 + qq * 128, tc * 512 + (qq + 1) * 128)
                                tv = P.op("dve", lambda e, par=par, gsg=gsg, osl=osl, qs=qs: e.tensor_mul(
                                    out=gsg[:, osl], in0=ppb[par][:, qs], in1=stmp[:, qs]),
                                    waits=[("psil", t1), free_tok], sig="ppev")
                                if qq < 3:
                                    yield
                            pst["stmp"] = ("ppev", tv)
                        pst["pev"][n] = ("ppev", tv)
                        pst["n"] += 1
                        yield
                pst["projpe"][st] = ("pprojpe", tpe)
                pst["lastpev"][st] = [pst["pev"][pst["n"] - 1], pst["pev"][pst["n"] - 9], pst.get("lastact")]

            def pump(gen, k):
                if gen is None:
                    return None
                for _ in range(k):
                    try:
                        next(gen)
                    except StopIteration:
                        return None
                return gen

            def drain(gen):
                while gen is not None:
                    gen = pump(gen, 64)

            gen_cur = sb_proj_gen(0)

            for step in range(8):
                is_sb = step < 4
                if is_sb:
                    cq, ck, cv, cg = step * 128, 512 + step * 128, 1024 + step * 128, 1536 + step * 128
                    kw = 128
                    vw = 128
                    do_kv = True
                else:
                    j = step - 4
                    cq, ck, cv, cg = 2048 + j * 128, 2560 + (j // 2) * 64, 2688 + (j // 2) * 64, 2816 + j * 128
                    kw = 64
                    vw = 64
                    do_kv = (j % 2 == 0)
                if step > 4:
                    wv = w_in.rearrange("(k p) c -> p k c", p=128)
                    dmas = [(0, cq, 128), (256, cg, 128)]
                    if do_kv:
                        if kw == 128:
                            dmas.append((128, ck, 128))
                        else:
                            dmas.append((128, ck, 64))
                            dmas.append((192, ck, 64))
                        if vw == 128:
                            dmas.append((384, cv, 128))
                        else:
                            dmas.append((384, cv, 64))
                            dmas.append((448, cv, 64))
                            vw = 128
                    for (dc, sc, w) in dmas:
                        t_wsl = P.op("pool", lambda e, dc=dc, sc=sc, w=w: e.dma_start(out=wsl[:, :, dc:dc + w], in_=wv[:, :, sc:sc + w]),
                                     waits=[("projpe", t_projpe), ("hfix", t_hT), pst["projpe"].get(4)],
                                     sig="ld_wsl", inc=16)
                    ckpt(2.5 + 2 * step)
                    kinds = [("q", 0), ("g", 256)] + ([("k", 128)] if do_kv else [])
                    for kind, wc in kinds:
                        for tc in range(8):
                            par = nproj % 2
                            for kc in range(8):
                                last = kc == 7
                                t_projpe_new = P.op("pe", lambda e, par=par, kc=kc, wc=wc, tc=tc: e.matmul(
                                    pp[par], lhsT=wsl[:, kc, wc:wc + 128], rhs=hT[:, kc, tc * 512:(tc + 1) * 512],
                                    start=(kc == 0), stop=(kc == 7)),
                                    waits=[("ld_wsl", t_wsl), t_pev.get(nproj - 2), ("hfix", t_hT),
                                           t_attdve],
                                    sig=("projpe" if last else None))
                            t_projpe = t_projpe_new
                            dst = {"q": qT, "k": kT, "g": sg}[kind][:, tc * 512:(tc + 1) * 512]
                            if kind == "q":
                                t_pev[nproj] = ("pev", P.op("dve", lambda e, dst=dst, par=par: e.tensor_scalar(
                                    out=dst, in0=pp[par], scalar1=0.125, scalar2=None, op0=ALU.mult),
                                    waits=[("projpe", t_projpe)], sig="pev"))
                                last_dve_pev = t_pev[nproj]
                            elif kind == "k":
                                t_pev[nproj] = ("pev", P.op("dve", lambda e, dst=dst, par=par: e.tensor_copy(out=dst, in_=pp[par]),
                                                    waits=[("projpe", t_projpe)], sig="pev"))
                                last_dve_pev = t_pev[nproj]
                            else:
                                t_pev[nproj] = ("pevA", P.op("act", lambda e, dst=dst, par=par: e.activation(out=dst, in_=pp[par], func=AF.Silu),
                                                             waits=[("projpe", t_projpe), t_attdve], sig="pevA"))
                                last_act_pev = t_pev[nproj]
                            nproj += 1
                    ckpt(2.75 + 2 * step)
                    if do_kv:
                        for tg in range(8):
                            par = nproj % 2
                            for t4 in range(4):
                                tok = (tg * 4 + t4) * 128
                                for kc in range(8):
                                    last = (kc == 7 and t4 == 3)
                                    t_projpe_new = P.op("pe", lambda e, par=par, kc=kc, tok=tok, t4=t4, vw=vw: e.matmul(
                                        pp[par][:, t4 * 128:t4 * 128 + vw], lhsT=hT[:, kc, tok:tok + 128], rhs=wsl[:, kc, 384:384 + vw],
                                        start=(kc == 0), stop=(kc == 7)),
                                        waits=[("ld_wsl", t_wsl), t_pev.get(nproj - 2), t_attdve],
                                        sig=("projpe" if last else None))
                            t_projpe = t_projpe_new
                            t_pev[nproj] = ("pev", P.op("dve", lambda e, par=par, tg=tg, vw=vw: e.tensor_copy(
                                out=vv[:, tg * 4:(tg + 1) * 4, 0:vw],
                                in_=pp[par].rearrange("p (a b) -> p a b", a=4)[:, :, 0:vw]),
                                waits=[("projpe", t_projpe)], sig="pev"))
                            last_dve_pev = t_pev[nproj]
                            nproj += 1
                    projw = [last_dve_pev, last_act_pev]
                else:
                    drain(gen_cur)
                    projw = list(pst["lastpev"][step])
                    gen_cur = sb_proj_gen(step + 1) if step + 1 < 5 else None
                    qT, kT, sg, vv = GSv[step % 2]
                ckpt(3 + 2 * step)

                if is_sb:
                    QCB = 512
                    tiles = []
                    for qc in range(S // QCB):
                        nkb = (QCB // 128) * (qc + 1)
                        for kb in range(nkb - 1, -1, -1):
                            jd = kb - (QCB // 128) * qc
                            diag = jd >= 0
                            c0 = 128 * jd if diag else 0
                            tiles.append((0, qc, kb, c0, diag, kb == nkb - 1, kb == 0))
                    T = len(tiles)
                    base = sbi
                    chunk_of = {}
                    cc = n_chunk - 1
                    for i, tl in enumerate(tiles):
                        if tl[5]:
                            cc += 1
                        chunk_of[i] = cc
                    Zh = [ps[:, 0:512], ps[:, 512:1024]]
                    Rh = [ps[:, 1024:1536], ps[:, 1536:2048]]
                    Yc = [ps[:, 2048:2560], ps[:, 2560:3072]]

                    def two(ap, c0):
                        if c0 == 0:
                            return ap
                        return ap.rearrange("p (h c) -> p h c", h=2)[:, :, c0:QCB]

                    def QK(i):
                        hp0, qc, kb, c0, diag, cs, ce = tiles[i]
                        g = base + i
                        for hh in range(2):
                            hp = hh * 64
                            last = (hh == 1) and not diag
                            v = P.op("pe", lambda e, hp=hp, hh=hh, kb=kb, qc=qc, c0=c0, diag=diag, kT=kT, qT=qT: e.matmul(
                                Zh[hh][:, c0:QCB], lhsT=kT[hp:hp + 64, kb * 128:(kb + 1) * 128],
                                rhs=qT[hp:hp + 64, qc * QCB + c0:(qc + 1) * QCB], start=True, stop=not diag,
                                skip_group_check=True),
                                waits=[("sbE", tk.get(("E", g - 1)))] + (projw if i < 2 else []),
                                sig=("sbQK" if last else None), attach=True)
                        if diag:
                            for hh in range(2):
                                v = P.op("pe", lambda e, hh=hh, c0=c0: e.matmul(
                                    Zh[hh][:, c0:c0 + 128], lhsT=ident, rhs=negm, start=False, stop=True, skip_group_check=True),
                                    sig=("sbQK" if hh == 1 else None))
                        tk[("QK", g)] = v

                    def ACT_E(i):
                        hp0, qc, kb, c0, diag, cs, ce = tiles[i]
                        g = base + i
                        tk[("E", g)] = P.op("act", lambda e, g=g, c0=c0: e.activation(out=two(Eb[g % 3], c0), in_=two(ps[:, 0:1024], c0), func=AF.Exp),
                                            waits=[("sbQK", tk[("QK", g)]), ("sbA", tk.get(("A", g - 3)))], sig="sbE")

                    def ACT_G(i):
                        hp0, qc, kb, c0, diag, cs, ce = tiles[i]
                        g = base + i
                        tk[("G", g)] = P.op("act", lambda e, g=g, c0=c0: e.activation(out=two(Gb[g % 2], c0), in_=two(Eb[g % 3], c0),
                                                                                   func=AF.Ln, bias=1.0),
                                            waits=[("sbE", tk[("E", g)]), ("sbTRIC", tk.get(("TRIC", g - 2)))], sig="sbG")

                    def ACT_P(i):
                        hp0, qc, kb, c0, diag, cs, ce = tiles[i]
                        g = base + i
                        tk[("P", g)] = P.op("act", lambda e, g=g, c0=c0: e.activation(out=two(Pb[g % 2], c0), in_=two(ps[:, 1024:2048], c0), func=AF.Exp),
                                            waits=[("sbTRI", tk[("TRI", g)]), ("sbA", tk.get(("A", g - 2)))], sig="sbP")

                    def PE_TRI(i):
                        hp0, qc, kb, c0, diag, cs, ce = tiles[i]
                        g = base + i
                        if cs:
                            par = chunk_of[i] % 2
                            for hh in range(2):
                                P.op("pe", lambda e, hh=hh: e.matmul(Rh[hh], lhsT=zer, rhs=cst[:, 0:512], start=True, stop=False,
                                                                     skip_group_check=True),
                                     waits=[("sbP", tk.get(("P", g - 1)))])
                            P.op("pe", lambda e, par=par: e.matmul(Yc[par], lhsT=zer, rhs=cst[:, 0:512], start=True, stop=False,
                                                                   skip_group_check=True),
                                 waits=[("sbEV", t_evy.get(chunk_of[i] - 2))])
                        for hh in range(2):
                            v = P.op("pe", lambda e, g=g, hh=hh, c0=c0: e.matmul(
                                Rh[hh][:, c0:QCB], lhsT=triN, rhs=Gb[g % 2][:, hh * QCB + c0:(hh + 1) * QCB], start=False, stop=False,
                                skip_group_check=True),
                                waits=[("sbG", tk[("G", g)]), ("sbP", tk.get(("P", g - 1)))],
                                sig=("sbTRI" if hh == 1 else None), attach=True)
                        tk[("TRI", g)] = v

                    def PE_TRIC(i):
                        hp0, qc, kb, c0, diag, cs, ce = tiles[i]
                        g = base + i
                        for hh in range(2):
                            v = P.op("pe", lambda e, g=g, hh=hh, c0=c0: e.matmul(
                                Rh[hh][:, c0:QCB], lhsT=tricN, rhs=Gb[g % 2][:, hh * QCB + c0:(hh + 1) * QCB], start=False, stop=False,
                                skip_group_check=True),
                                waits=[("sbP", tk[("P", g)])],
                                sig=("sbTRIC" if hh == 1 else None), attach=True)
                        tk[("TRIC", g)] = v

                    def PE_PV(i):
                        hp0, qc, kb, c0, diag, cs, ce = tiles[i]
                        g = base + i
                        par = chunk_of[i] % 2
                        for hh in range(2):
                            hp = hh * 64
                            v = P.op("pe", lambda e, g=g, hh=hh, hp=hp, kb=kb, par=par, c0=c0, vv=vv: e.matmul(
                                Yc[par][hp:hp + 64, c0:QCB], lhsT=vv[:, kb, hp:hp + 64], rhs=Ab[g % 2][:, hh * QCB + c0:(hh + 1) * QCB],
                                start=False, stop=False, skip_group_check=True),
                                waits=[("sbA", tk[("A", g)])],
                                sig=("sbPV" if hh == 1 else None), attach=True)
                        tk[("PV", g)] = v

                    def DVE_A(i):
                        hp0, qc, kb, c0, diag, cs, ce = tiles[i]
                        g = base + i
                        tk[("A", g)] = P.op("dve", lambda e, g=g, c0=c0: e.tensor_mul(out=two(Ab[g % 2], c0), in0=two(Eb[g % 3], c0),
                                                                                   in1=two(Pb[g % 2], c0)),
                                            waits=[("sbP", tk[("P", g)]), ("sbPV", tk.get(("PV", g - 2)))], sig="sbA")

                    def DVE_EV(i):
                        hp0, qc, kb, c0, diag, cs, ce = tiles[i]
                        g = base + i
                        ch = chunk_of[i]
                        par = ch % 2
                        t_evy[ch] = P.op("dve", lambda e, qc=qc, par=par, step=step, sg=sg: e.tensor_mul(
                            out=ygT[:, step, qc * QCB:(qc + 1) * QCB], in0=Yc[par], in1=sg[:, qc * QCB:(qc + 1) * QCB]),
                            waits=[("sbPV", tk[("PV", g)])], sig="sbEV")

                    QK(0)
                    ACT_E(0)
                    if T > 1:
                        QK(1)
                    ACT_G(0)
                    if T > 1:
                        ACT_E(1)
                    if T > 2:
                        QK(2)
                    PE_TRI(0)
                    for i in range(T):
                        ACT_P(i)
                        PE_TRIC(i)
                        gen_cur = pump(gen_cur, PUMP_A)
                        if i + 1 < T:
                            ACT_G(i + 1)
                            PE_TRI(i + 1)
                        DVE_A(i)
                        PE_PV(i)
                        if tiles[i][6]:
                            DVE_EV(i)
                        if i + 2 < T:
                            ACT_E(i + 2)
                        if i + 3 < T:
                            QK(i + 3)
                        gen_cur = pump(gen_cur, PUMP_B + (1 if i % 4 == 3 else 0))
                    sbi += T
                    n_chunk = cc + 1
                    t_attdve = ("sbEV", t_evy[cc])
                    pst["attdone"][step] = t_attdve
                    qT, kT, sg, vv = GSv[0]
                else:
                    j = step - 4
                    heads = (2 * j, 2 * j + 1)
                    swn0 = swn
                    att_prev = t_attdve

                    def SW_QK(n):
                        gi = swn0 + n
                        zbase = (gi % 2) * 1024
                        wz = 256 if n > 0 else 128
                        for which, kblk in ((0, n), (1, n - 1)):
                            if kblk < 0:
                                continue
                            for hh in range(2):
                                hp = hh * 64
                                zc = zbase + hh * 512 + which * 128
                                P.op("pe", lambda e, zc=zc, hp=hp, kblk=kblk, n=n, which=which: e.matmul(
                                    ps[:, zc:zc + 128], lhsT=kT[hp:hp + 64, kblk * 128:(kblk + 1) * 128],
                                    rhs=qT[hp:hp + 64, n * 128:(n + 1) * 128], start=(which == 0), stop=False, skip_group_check=True),
                                    waits=[("swP", t_sw.get(("P", gi - 2)))] + (projw if n < 2 else []))
                        for hh in range(2):
                            h = heads[hh]
                            bc = C_SWB + h * 256
                            zc = zbase + hh * 512
                            v = P.op("pe", lambda e, zc=zc, bc=bc, wz=wz: e.matmul(
                                ps[:, zc:zc + wz], lhsT=ident, rhs=cst[:, bc:bc + wz], start=False, stop=True, skip_group_check=True),
                                sig=("swQK" if hh == 1 else None))
                        t_sw[("QK", gi)] = v

                    def SW_ACT(n):
                        gi = swn0 + n
                        zbase = (gi % 2) * 1024
                        wz = 256 if n > 0 else 128
                        zin = ps[:, zbase:zbase + 1024].rearrange("p (b c) -> p b c", b=2)[:, :, 0:wz]
                        pout = Psw[gi % 2].rearrange("p (b c) -> p b c", b=2)[:, :, 0:wz]
                        t_sw[("P", gi)] = P.op("act", lambda e, zin=zin, pout=pout: e.activation(out=pout, in_=zin, func=AF.Exp),
                                               waits=[("swQK", t_sw[("QK", gi)]), ("swPV", t_sw.get(("PV", gi - 2)))], sig="swP")

                    def SW_PVD(n):
                        gi = swn0 + n
                        grp = swg + n // 4
                        par = grp % 2
                        Yb = Yp[par][:, 0:512]
                        Db = Yp[par][:, 512:1024]
                        col = (n % 4) * 128
                        for hh in range(2):
                            hp = hh * 64
                            srcs = [(hh * 256, n)] + ([(hh * 256 + 128, n - 1)] if n > 0 else [])
                            ns = len(srcs)
                            for si, (pc, kblk) in enumerate(srcs):
                                P.op("pe", lambda e, Yb=Yb, hp=hp, col=col, kblk=kblk, gi=gi, pc=pc, si=si, ns=ns: e.matmul(
                                    Yb[hp:hp + 64, col:col + 128], lhsT=vv[:, kblk, 0:64], rhs=Psw[gi % 2][:, pc:pc + 128],
                                    start=(si == 0), stop=(si == ns - 1), skip_group_check=True),
                                    waits=[("swP", t_sw[("P", gi)]), ("swEV", t_sw.get(("EV", grp - 2)))] + ([att_prev] if n < 8 else []))
                            for si, (pc, kblk) in enumerate(srcs):
                                v = P.op("pe", lambda e, Db=Db, hp=hp, col=col, gi=gi, pc=pc, si=si, ns=ns: e.matmul(
                                    Db[hp:hp + 64, col:col + 128], lhsT=ones_bf[:, 0:64], rhs=Psw[gi % 2][:, pc:pc + 128],
                                    start=(si == 0), stop=(si == ns - 1), skip_group_check=True),
                                    sig=("swPV" if (hh == 1 and si == ns - 1) else None))
                        t_sw[("PV", gi)] = v
                    def SW_EV(n):
                        gi = swn0 + n
                        grp = swg + n // 4
                        par = grp % 2
                        Yb = Yp[par][:, 0:512]
                        Db = Yp[par][:, 512:1024]
                        if n % 4 == 3:
                            q0 = (n - 3) * 128
                            t1 = P.op("act", lambda e, Db=Db, j=j: e.activation(out=rden, in_=Db, func=AF.Ln, bias=esk[:, j:j + 1]),
                                      waits=[("swPV", t_sw[("PV", gi)]), ("act0", t_esk), ("swEV", t_sw.get(("EV", grp - 1)))], sig="swDa")
                            t1 = P.op("act", lambda e: e.activation(out=rden, in_=rden, func=AF.Exp, scale=-1.0),
                                      waits=[("swDa", t1)], sig="swDa")
                            t1 = P.op("dve", lambda e, Yb=Yb: e.tensor_mul(out=ytmp, in0=Yb, in1=rden), waits=[("swDa", t1)], sig="swD")
                            t_sw[("EV", grp)] = P.op("dve", lambda e, q0=q0, step=step: e.tensor_mul(
                                out=ygT[:, step, q0:q0 + 512], in0=ytmp, in1=sg[:, q0:q0 + 512]),
                                waits=[("swD", t1)], sig="swEV")

                    SW_QK(0)
                    for n in range(NT):
                        SW_ACT(n)
                        if n >= 1 and (n - 1) % 4 == 3:
                            SW_EV(n - 1)
                        if n + 1 < NT:
                            SW_QK(n + 1)
                        SW_PVD(n)
                    SW_EV(NT - 1)
                    swn += NT
                    swg += NT // 4
                    t_last_sw = t_sw[("EV", swg - 1)]
                if not is_sb:
                    t_attdve = ("swEV", t_last_sw)
                ckpt(4 + 2 * step)

            ckpt(20)
            wo_v = w_out.rearrange("(k p) e -> p k e", p=128)
            t_wo = None
            for k in range(8):
                t_wo = P.op("pool", lambda e, k=k: e.dma_start(out=wout[:, k, :], in_=wo_v[:, k, :]),
                            waits=[("projpe", t_projpe)], sig="ld_wo", inc=16)
            t_fg = P.op("sp", lambda e: e.dma_start(out=fg_bc, in_=fg_bc_d[:, :]), waits=[("projpe", t_projpe)], sig="ld_fg", inc=16)
            t_xf = {}
            t_o = {}
            t_st = {}
            t_r2 = {}
            t_sq2 = {}

            def issue_xf(tt):
                t_xf[tt] = P.op("sp", lambda e, tt=tt: e.dma_start(out=xf[tt % 3], in_=x[tt * 128:(tt + 1) * 128, :]),
                                waits=[("fr2", t_r2.get(tt - 3)), ("projpe", t_projpe)], sig="ld_xf%d" % (tt % 3), inc=16)

            t_po = {}
            for tt in range(3):
                issue_xf(tt)
            def F_PE(tt):
                for eh in range(2):
                    for kc in range(8):
                        v = P.op("pe", lambda e, tt=tt, eh=eh, kc=kc: e.matmul(
                            po[tt % 2][eh], lhsT=ygT[:, kc, tt * 128:(tt + 1) * 128], rhs=wout[:, kc, eh * 512:(eh + 1) * 512],
                            start=(kc == 0), stop=(kc == 7)),
                            waits=[("ld_wo", t_wo), t_attdve, ("fr2", t_r2.get(tt - 2))],
                            sig=("fpo" if kc == 7 else None))
                    t_po[(tt, eh)] = v

            def F_A(tt):
                rb = rf[tt % 2]
                for eh in range(2):
                    t1 = P.op("dve", lambda e, tt=tt, eh=eh, rb=rb: e.tensor_mul(out=rb[:, eh * 512:(eh + 1) * 512], in0=po[tt % 2][eh],
                                                                               in1=gate_bc[:, eh * 512:(eh + 1) * 512]),
                              waits=[("fpo", t_po[(tt, eh)]), ("fo", t_o.get(tt - 2)), ("fsq", t_sq2.get(tt - 2))], sig="fd")
                t_r2[tt] = P.op("dve", lambda e, tt=tt, rb=rb: e.tensor_add(out=rb, in0=rb, in1=xf[tt % 3]),
                                waits=[("fd", t1), ("ld_xf%d" % (tt % 3), t_xf[tt])], sig="fr2")
                if tt + 3 < NT:
                    issue_xf(tt + 3)
                t_sq2[tt] = P.op("act", lambda e, tt=tt, rb=rb: e.activation(out=junkf, in_=rb, func=AF.Square, accum_out=ss2[:, tt:tt + 1]),
                                 waits=[("fr2", t_r2[tt])], sig="fsq")
                t_sqrt2[tt] = P.op("act", lambda e, tt=tt: e.activation(out=rstd2[:, tt:tt + 1], in_=ss2[:, tt:tt + 1], func=AF.Sqrt,
                                                                        scale=1.0 / D, bias=eps_t[:, 0:1]),
                                   waits=[("fsq", t_sq2[tt])], sig="fsqrt")

            def F_B(tt):
                rb = rf[tt % 2]
                t1 = P.op("dve", lambda e, tt=tt: e.reciprocal(out=rstd2[:, tt:tt + 1], in_=rstd2[:, tt:tt + 1]),
                          waits=[("fsqrt", t_sqrt2[tt])], sig="fd")
                t_o[tt] = P.op("dve", lambda e, tt=tt, rb=rb: e.scalar_tensor_tensor(
                    out=of[tt % 2], in0=rb, scalar=rstd2[:, tt:tt + 1], in1=fg_bc, op0=ALU.mult, op1=ALU.mult),
                    waits=[("fd", t1), ("ld_fg", t_fg), ("ld_out%d" % (tt % 2), t_st.get(tt - 2))], sig="fo")
                t_st[tt] = P.op("sp", lambda e, tt=tt: e.dma_start(out=out[tt * 128:(tt + 1) * 128, :], in_=of[tt % 2]),
                                waits=[("fo", t_o[tt])], sig="ld_out%d" % (tt % 2), inc=16)

            t_sqrt2 = {}
            F_PE(0)
            F_PE(1)
            F_A(0)
            for tt in range(NT):
                if tt + 2 < NT:
                    F_PE(tt + 2)
                if tt + 1 < NT:
                    F_A(tt + 1)
                F_B(tt)
            P.op("sp", lambda e: e.nop(), waits=[("ld_out0", t_st[NT - 2]), ("ld_out1", t_st[NT - 1])])

        try:
            plan_all()
        except _Stop:
            pass

        names = sorted(P.cnt.keys())
        sems = {n: es.enter_context(nc.semaphore(n)) for n in names}
        block = es.enter_context(nc.Block())

        def emit(eng, oplist):
            seen = {}
            for fn, waits, sig, inc, attach in oplist:
                pend = [(name, val) for (name, val) in waits if seen.get(name, 0) < val]
                if attach and len(pend) == 1:
                    (name, val) = pend[0]
                    ins = fn(eng)
                    ins._wait_ge(sems[name], val)
                    seen[name] = val
                else:
                    for (name, val) in pend:
                        eng.wait_ge(sems[name], val)
                        seen[name] = val
                    ins = fn(eng)
                if sig is not None:
                    ins.then_inc(sems[sig], inc)

        @block.sync
        def _(eng):
            emit(eng, P.ops["sp"])

        @block.gpsimd
        def _(eng):
            emit(eng, P.ops["pool"])

        @block.tensor
        def _(eng):
            emit(eng, P.ops["pe"])

        @block.scalar
        def _(eng):
            emit(eng, P.ops["act"])

        @block.vector
        def _(eng):
            emit(eng, P.ops["dve"])
    return nc


_CACHE = {}


def kernel(x, c, w_ada, b_ada, norm_g, w_in, sinks, w_out, final_g):
    x = np.asarray(x, np.float32)
    c = np.asarray(c, np.float32)
    w_ada = np.ascontiguousarray(np.asarray(w_ada, np.float32)[0])
    b_ada = np.asarray(b_ada, np.float32)[0]
    norm_g = np.asarray(norm_g, np.float32)[0]
    w_in = np.ascontiguousarray(np.asarray(w_in, np.float32)[0])
    sinks = np.asarray(sinks, np.float32)[0]
    w_out = np.ascontiguousarray(np.asarray(w_out, np.float32)[0])
    final_g = np.asarray(final_g, np.float32)

    def lay(v):
        return np.ascontiguousarray(v.reshape(-1, 128).T)

    bada_l = lay(b_ada[:2048])
    bg_bc = np.ascontiguousarray(np.broadcast_to(b_ada[2048:3072][None, :], (128, D)))
    normg_l = lay(norm_g)
    fg_bc = np.ascontiguousarray(np.broadcast_to(final_g[None, :], (128, D)))
    sinks_l = np.ascontiguousarray(np.stack([np.repeat(sinks[2 * j:2 * j + 2], 64) for j in range(4)], axis=1))
    consts = make_consts()
    if "nc" not in _CACHE:
        _CACHE["nc"] = build_nc()
    nc = _CACHE["nc"]
    in_maps = []
    for b in range(NCORE):
        in_maps.append({
            "x": np.ascontiguousarray(x[b]), "c_l": lay(c[b]), "w_ada": w_ada, "bada_l": bada_l, "bg_bc": bg_bc,
            "normg_l": normg_l, "w_in": w_in, "sinks_l": sinks_l, "w_out": w_out, "fg_bc": fg_bc, "consts": consts,
        })
    res = run_bass_kernel_spmd(nc, in_maps, core_ids=list(range(NCORE)))
    return np.stack([np.asarray(r["out"], np.float32) for r in res.results], axis=0)
```

```python
from contextlib import ExitStack

import numpy as np
import concourse.bass as bass
import concourse.mybir as mybir
from concourse.bass_utils import run_bass_kernel_spmd

F32 = mybir.dt.float32
BF16 = mybir.dt.bfloat16
AF = mybir.ActivationFunctionType
ALU = mybir.AluOpType

S = 4096
D = 1024
NCORE = 8
NT = S // 128
QC = 1024
NEG = -30000.0
NCONST = 128 * 6 + 2048
C_ID, C_TRI, C_TRIC, C_ZERO, C_NEGM, C_ONES, C_SWB = 0, 128, 256, 384, 512, 640, 768


LEVEL = 99
SW_DBG = 0
PUMP = 2
PUMP_A = 1
PUMP_B = 1
ATTACH = True


class _Stop(Exception):
    pass


def ckpt(level):
    if LEVEL <= level:
        raise _Stop()


class Plan:
    def __init__(self):
        self.ops = {"pe": [], "act": [], "dve": [], "pool": [], "sp": []}
        self.cnt = {}

    def op(self, eng, fn, waits=(), sig=None, inc=1, attach=False):
        v = None
        if sig is not None:
            self.cnt[sig] = self.cnt.get(sig, 0) + inc
            v = self.cnt[sig]
        ws = tuple(w for w in waits if w is not None and w[1] is not None and w[1] > 0)
        self.ops[eng].append((fn, ws, sig, inc, attach and ATTACH))
        return v


def make_consts():
    c = np.zeros((128, NCONST), np.float32)
    j = np.arange(128)[:, None]
    s = np.arange(128)[None, :]
    c[:, C_ID:C_ID + 128] = (j == s)
    c[:, C_TRI:C_TRI + 128] = -1.0 * (j >= s)
    c[:, C_TRIC:C_TRIC + 128] = -1.0 * (j < s)
    c[:, C_NEGM:C_NEGM + 128] = np.where(j < s, 0.0, NEG)
    c[:, C_ONES:C_ONES + 128] = 1.0
    for h in range(8):
        m = 2.0 ** (-8.0 * (h + 1) / 8)
        rel_cur = (s - j).astype(np.float32)
        cur = np.where(s >= j, -m * rel_cur, NEG)
        rel_prev = (128 + s - j).astype(np.float32)
        prev = np.where(j > s, -m * rel_prev, NEG)
        c[:, C_SWB + h * 256:C_SWB + h * 256 + 128] = cur
        c[:, C_SWB + h * 256 + 128:C_SWB + h * 256 + 256] = prev
    return c


def sb_tiles():
    out = []
    for qc in range(S // QC):
        nkb = (QC // 128) * (qc + 1)
        for kb in range(nkb - 1, -1, -1):
            jd = kb - (QC // 128) * qc
            diag = jd >= 0
            c0 = 128 * jd if diag else 0
            out.append((qc, kb, c0, diag, kb == nkb - 1, kb == 0))
    return out


def col_segs(c0, c1=QC):
    segs = []
    a = c0
    while a < c1:
        b = min(c1, (a // 512 + 1) * 512)
        segs.append((a, b))
        a = b
    return segs


def build_nc():
    nc = bass.Bass("TRN2", target_bir_lowering=False)
    x = nc.dram_tensor("x", [S, D], F32, kind="ExternalInput").ap()
    c_l = nc.dram_tensor("c_l", [128, 8], F32, kind="ExternalInput").ap()
    w_ada = nc.dram_tensor("w_ada", [D, 3 * D], F32, kind="ExternalInput").ap()
    bada_l = nc.dram_tensor("bada_l", [128, 16], F32, kind="ExternalInput").ap()
    bg_bc_d = nc.dram_tensor("bg_bc", [128, D], F32, kind="ExternalInput").ap()
    normg_l = nc.dram_tensor("normg_l", [128, 8], F32, kind="ExternalInput").ap()
    w_in = nc.dram_tensor("w_in", [D, 3328], F32, kind="ExternalInput").ap()
    sinks_l = nc.dram_tensor("sinks_l", [128, 4], F32, kind="ExternalInput").ap()
    w_out = nc.dram_tensor("w_out", [D, D], F32, kind="ExternalInput").ap()
    fg_bc_d = nc.dram_tensor("fg_bc", [128, D], F32, kind="ExternalInput").ap()
    consts_d = nc.dram_tensor("consts", [128, NCONST], F32, kind="ExternalInput").ap()
    out = nc.dram_tensor("out", [S, D], F32, kind="ExternalOutput").ap()

    P = Plan()
    es = ExitStack()
    with es:
        def sb(name, shape, dt):
            return es.enter_context(nc.sbuf_tensor(name, shape, dt))

        ygT = sb("ygT", [128, 8, S], BF16)
        hT = sb("hT", [128, 8, S], BF16)
        ov = sb("ov", [128, 30720], BF16)
        cst = sb("cst", [128, NCONST], BF16)
        gate_bc = sb("gate_bc", [128, D], F32)
        c_sb = sb("c_sb", [128, 8], F32)
        etmp = sb("etmp", [128, 8], F32)
        cond = sb("cond", [128, 8], F32)
        bada = sb("bada", [128, 16], F32)
        normg = sb("normg", [128, 8], F32)
        mod_sb = sb("mod_sb", [128, 16], F32)
        gs = sb("gs", [128, 8], F32)
        ones_f = sb("ones_f", [128, 128], F32)
        ss = sb("ss", [128, 32], F32)
        rstd = sb("rstd", [128, 32], F32)
        ss2 = sb("ss2", [128, 32], F32)
        rstd2 = sb("rstd2", [128, 32], F32)
        snk = sb("snk", [128, 4], F32)
        esk = sb("esk", [128, 4], F32)
        eps_t = sb("eps_t", [128, 1], F32)
        wsl1 = sb("wsl1", [128, 4096], BF16)
        ps = es.enter_context(nc.psum_tensor("ps", [128, 4096], F32))

        qT = ov[:, 0:4096]
        kT = ov[:, 4096:8192]
        sg = ov[:, 8192:12288]
        vv = ov[:, 12288:16384].rearrange("p (b c) -> p b c", c=128)
        wsl = ov[:, 16384:20480].rearrange("p (k c) -> p k c", c=512)
        PB = 20480
        Eb = [ov[:, PB + i * 1024:PB + (i + 1) * 1024] for i in range(3)]
        Gb = [ov[:, PB + (3 + i) * 1024:PB + (4 + i) * 1024] for i in range(2)]
        stmp = ov[:, PB + 5 * 1024:PB + 6 * 1024].bitcast(F32)
        Pb = [ov[:, PB + (6 + i) * 1024:PB + (7 + i) * 1024] for i in range(2)]
        Ab = [ov[:, PB + (8 + i) * 1024:PB + (9 + i) * 1024] for i in range(2)]
        Psw = [ov[:, PB + i * 512:PB + (i + 1) * 512] for i in range(2)]
        rden = ov[:, PB + 1024:PB + 2048].bitcast(F32)
        ytmp = ov[:, PB + 2048:PB + 3072].bitcast(F32)
        yflat = ygT[:, :, :].rearrange("p a b -> p (a b)")
        wada = [yflat[:, i * 6144:(i + 1) * 6144].bitcast(F32) for i in range(4)]
        xt = [ov[:, 12288 + i * 2048:12288 + (i + 1) * 2048].bitcast(F32) for i in range(4)]
        xnb = [[ov[:, 20480 + (g * 4 + t) * 1024:20480 + (g * 4 + t + 1) * 1024] for t in range(4)] for g in range(2)]
        junk = ov[:, 28672:29696]
        hflat = hT[:, :, :].rearrange("p a b -> p (a b)")
        condb = yflat[:, 24576:26624].bitcast(F32).rearrange("p (k m) -> p k m", m=128)
        bg_bc = yflat[:, 26624:28672].bitcast(F32)
        wout = hflat[:, 0:8192].rearrange("p (k e) -> p k e", e=1024)
        xf = [hflat[:, 8192 + i * 2048:8192 + (i + 1) * 2048].bitcast(F32) for i in range(3)]
        rf = [hflat[:, 14336 + i * 2048:14336 + (i + 1) * 2048].bitcast(F32) for i in range(2)]
        of = [hflat[:, 18432 + i * 2048:18432 + (i + 1) * 2048].bitcast(F32) for i in range(2)]
        fg_bc = hflat[:, 22528:24576].bitcast(F32)
        junkf = hflat[:, 24576:25600]
        Zp = ps[:, 0:1024]
        Rp = ps[:, 1024:2048]
        Yp = [ps[:, 2048:3072], ps[:, 3072:4096]]
        pp = [ps[:, 0:512], ps[:, 512:1024]]
        tp = [ps[:, i * 512:(i + 1) * 512].bitcast(BF16)[:, 0:512] for i in range(2)]
        modps = ps[:, 1024:1040]
        gateps = ps[:, 2048:3072]
        po = [[ps[:, t * 1024 + e * 512:t * 1024 + (e + 1) * 512] for e in range(2)] for t in range(2)]

        ident = cst[:, C_ID:C_ID + 128]
        triN = cst[:, C_TRI:C_TRI + 128]
        tricN = cst[:, C_TRIC:C_TRIC + 128]
        zer = cst[:, C_ZERO:C_ZERO + 128]
        negm = cst[:, C_NEGM:C_NEGM + 128]
        ones_bf = cst[:, C_ONES:C_ONES + 128]

        def plan_all():
            nonlocal qT, kT, sg, vv
            t_cst = P.op("pool", lambda e: e.dma_start(out=cst[:, :], in_=consts_d[:, :]), sig="ld_cst", inc=16)
            for dst, src in ((c_sb, c_l), (bada, bada_l), (normg, normg_l), (snk, sinks_l)):
                t_small = P.op("sp", lambda e, dst=dst, src=src: e.dma_start(out=dst[:, :], in_=src[:, :]), sig="ld_small", inc=16)
            t_small = P.op("sp", lambda e: e.dma_start(out=bg_bc, in_=bg_bc_d[:, :]), sig="ld_small", inc=16)

            ckpt(0)
            P.op("dve", lambda e: e.memset(ones_f[:, :], 1.0), sig="dve0")
            P.op("dve", lambda e: e.memset(eps_t[:, :], 1e-6), sig="dve0")
            P.op("dve", lambda e: e.memset(ss[:, :], 0.0), sig="dve0")
            t_d = P.op("dve", lambda e: e.memset(ss2[:, :], 0.0), sig="dve0")
            t_ms = t_d
            t_a = P.op("act", lambda e: e.activation(out=etmp[:, :], in_=c_sb[:, :], func=AF.Exp, scale=-1.0),
                       waits=[("ld_small", t_small)], sig="act0")
            t_d = P.op("dve", lambda e: e.tensor_scalar_add(out=etmp[:, :], in0=etmp[:, :], scalar1=1.0),
                       waits=[("act0", t_a), ("dve0", t_d)], sig="dve0")
            t_d = P.op("dve", lambda e: e.reciprocal(out=etmp[:, :], in_=etmp[:, :]), waits=[("dve0", t_d)], sig="dve0")
            t_d = P.op("dve", lambda e: e.tensor_mul(out=cond[:, :], in0=c_sb[:, :], in1=etmp[:, :]),
                       waits=[("dve0", t_d)], sig="dve0")
            t_cond = t_d
            for k in range(8):
                t_d = P.op("dve", lambda e, k=k: e.tensor_scalar(out=condb[:, k, :], in0=ones_f[:, :], scalar1=cond[:, k:k + 1],
                                                                 scalar2=None, op0=ALU.mult),
                           waits=[("dve0", t_cond)], sig="dve0")
            t_condb = t_d
            t_pe0 = {}
            t_wada = {}
            t_xld = {}
            t_sq = {}
            t_xn = {}
            t_tp = {}
            t_ev = {}
            NWB = 4

            def issue_wada(k):
                t_wada[k] = P.op("sp", lambda e, k=k: e.dma_start(out=wada[k % NWB], in_=w_ada[k * 128:(k + 1) * 128, :]),
                                 waits=[("pe0", t_pe0.get(k - NWB))], sig="ld_wada%d" % (k % NWB), inc=16)

            def issue_xload(tt):
                t_xld[tt] = P.op("sp", lambda e, tt=tt: e.dma_start(out=xt[tt % 4], in_=x[tt * 128:(tt + 1) * 128, :]),
                                 waits=[("p1xn", t_xn.get(tt - 4)), ("p1sq", t_sq.get(tt - 4))],
                                 sig="ld_x%d" % (tt % 4), inc=16)

            for k in range(NWB):
                issue_wada(k)
            for tt in range(4):
                issue_xload(tt)
            for k in range(8):
                for j in range(16):
                    P.op("pe", lambda e, k=k, j=j: e.matmul(modps[:, j:j + 1], lhsT=wada[k % NWB][:, j * 128:(j + 1) * 128],
                                                            rhs=cond[:, k:k + 1], start=(k == 0 and j == 0), stop=(k == 7),
                                                            skip_group_check=True),
                         waits=[("ld_wada%d" % (k % NWB), t_wada[k]), ("dve0", t_condb)])
                for eh in range(2):
                    t_pe0[k] = P.op("pe", lambda e, k=k, eh=eh: e.matmul(gateps[:, eh * 512:(eh + 1) * 512], lhsT=condb[:, k, :],
                                                                         rhs=wada[k % NWB][:, 2048 + eh * 512:2048 + (eh + 1) * 512],
                                                                         start=(k == 0), stop=(k == 7)),
                                    waits=[("ld_wada%d" % (k % NWB), t_wada[k]), ("dve0", t_condb)], sig=("pe0" if eh == 1 else None))
                if k + NWB < 8:
                    issue_wada(k + NWB)
                g = k
                for t4 in range(4):
                    tt = g * 4 + t4
                    t_sq[tt] = P.op("act", lambda e, tt=tt: e.activation(out=junk, in_=xt[tt % 4], func=AF.Square,
                                                                         accum_out=ss[:, tt:tt + 1]),
                                    waits=[("ld_x%d" % (tt % 4), t_xld[tt]), ("dve0", t_ms)], sig="p1sq")
                    t_r = P.op("act", lambda e, tt=tt: e.activation(out=rstd[:, tt:tt + 1], in_=ss[:, tt:tt + 1], func=AF.Sqrt,
                                                                    scale=1.0 / D, bias=eps_t[:, 0:1]),
                               waits=[("p1sq", t_sq[tt])], sig="p1sqrt")
                    t_r = P.op("dve", lambda e, tt=tt: e.reciprocal(out=rstd[:, tt:tt + 1], in_=rstd[:, tt:tt + 1]),
                               waits=[("p1sqrt", t_r)], sig="dve1")
                    t_xn[tt] = P.op("dve", lambda e, tt=tt, g=g, t4=t4: e.tensor_scalar(
                        out=xnb[g % 2][t4], in0=xt[tt % 4], scalar1=rstd[:, tt:tt + 1], scalar2=None, op0=ALU.mult),
                        waits=[("dve1", t_r), ("p1tp", t_tp.get((g - 2, 7)))], sig="p1xn")
                    if tt + 4 < NT:
                        issue_xload(tt + 4)
                for j in range(8):
                    prev_ev = t_ev[(g, j - 2)] if j >= 2 else (t_ev[(g - 1, 6 + j)] if g >= 1 else None)
                    for t4 in range(4):
                        t_tp[(g, j)] = P.op("pe", lambda e, g=g, j=j, t4=t4: e.transpose(
                            out=tp[j % 2][:, t4 * 128:(t4 + 1) * 128], in_=xnb[g % 2][t4][:, j * 128:(j + 1) * 128], identity=ident),
                            waits=[("p1xn", t_xn[g * 4 + 3]), ("p1ev", prev_ev), ("ld_cst", t_cst)],
                            sig=("p1tp" if t4 == 3 else None))
                    t_ev[(g, j)] = P.op("act", lambda e, g=g, j=j: e.activation(
                        out=hT[:, j, g * 512:(g + 1) * 512], in_=tp[j % 2], func=AF.Identity),
                        waits=[("p1tp", t_tp[(g, j)])], sig="p1ev")
            t_d = P.op("dve", lambda e: e.tensor_add(out=mod_sb[:, :], in0=modps, in1=bada[:, :]),
                       waits=[("pe0", t_pe0[7]), ("ld_small", t_small)], sig="dve0")
            t_d = P.op("dve", lambda e: e.scalar_tensor_tensor(out=gs[:, :], in0=mod_sb[:, 8:16], scalar=1.0, in1=normg[:, :],
                                                               op0=ALU.add, op1=ALU.mult),
                       waits=[("dve0", t_d)], sig="dve0")
            t_d = P.op("dve", lambda e: e.tensor_add(out=gate_bc[:, :], in0=gateps, in1=bg_bc), sig="dve0")
            t_mod = t_d
            for j in range(8):
                for hh in range(2):
                    t_fix = P.op("dve", lambda e, j=j, hh=hh: e.tensor_scalar(
                        out=hT[:, j, hh * 2048:(hh + 1) * 2048], in0=hT[:, j, hh * 2048:(hh + 1) * 2048],
                        scalar1=gs[:, j:j + 1], scalar2=mod_sb[:, j:j + 1], op0=ALU.mult, op1=ALU.add),
                        waits=[("dve0", t_mod), ("p1ev", t_ev[(7, 7)])], sig="hfix")
            t_hT = t_fix
            ckpt(2)
            t_wsl = None
            t_projpe = None
            t_attpe = None
            t_attdve = None
            t_pev = {}
            nproj = 0
            t_esk = P.op("act", lambda e: e.activation(out=esk[:, :], in_=snk[:, :], func=AF.Exp),
                         waits=[("ld_small", t_small)], sig="act0")
            sbt = sb_tiles()
            sbi = 0
            tk = {}
            n_chunk = 0
            t_evy = {}
            swn = 0
            swg = 0
            t_sw = {}

            wv_all = w_in.rearrange("(k p) c -> p k c", p=128)
            GSv = [(qT, kT, sg, vv),
                   (yflat[:, 16384:20480], yflat[:, 20480:24576], yflat[:, 24576:28672],
                    yflat[:, 28672:32768].rearrange("p (b c) -> p b c", c=128))]
            wslb = [wsl, wsl1[:, :].rearrange("p (k c) -> p k c", c=512)]
            ppb = [ps[:, 3072:3584], ps[:, 3584:4096]]
            pst = {"n": 0, "pev": {}, "projpe": {}, "lastpev": {}, "attdone": {}, "stmp": None}

            def sb_proj_gen(st):
                gq, gk, gsg, gv = GSv[st % 2]
                wb = wslb[st % 2]
                wcol = {"q": 0, "k": 128, "g": 256, "v": 384}
                if st < 4:
                    dl = [(128, 512 + st * 128, 128), (384, 1024 + st * 128, 128), (0, st * 128, 128), (256, 1536 + st * 128, 128)]
                else:
                    dl = [(128, 2560, 64), (192, 2560, 64), (384, 2688, 64), (448, 2688, 64), (0, 2048, 128), (256, 2816, 128)]
                t_w = None
                for (dc, sc, w) in dl:
                    t_w = P.op("pool", lambda e, wb=wb, dc=dc, sc=sc, w=w: e.dma_start(out=wb[:, :, dc:dc + w], in_=wv_all[:, :, sc:sc + w]),
                               waits=[pst["projpe"].get(st - 2), ("hfix", t_hT)], sig="ld_wb%d" % (st % 2), inc=16)
                t_w = ("ld_wb%d" % (st % 2), t_w)
                free_tok = pst["attdone"].get(st - 2)
                tpe = None
                for kind in ("k", "v", "q", "g"):
                    wc = wcol[kind]
                    if kind == "v":
                        for tg in range(8):
                            n = pst["n"]
                            par = n % 2
                            for t4 in range(4):
                                tok = (tg * 4 + t4) * 128
                                for kc in range(8):
                                    last = (kc == 7 and t4 == 3)
                                    tpe = P.op("pe", lambda e, par=par, kc=kc, tok=tok, t4=t4, wb=wb: e.matmul(
                                        ppb[par][:, t4 * 128:(t4 + 1) * 128], lhsT=hT[:, kc, tok:tok + 128], rhs=wb[:, kc, 384:512],
                                        start=(kc == 0), stop=(kc == 7)),
                                        waits=[t_w, pst["pev"].get(n - 2)], sig=("pprojpe" if last else None))
                                    if (not last) and (kc == 3 or kc == 7):
                                        yield
                            tv = P.op("dve", lambda e, par=par, tg=tg, gv=gv: e.tensor_copy(
                                out=gv[:, tg * 4:(tg + 1) * 4, :], in_=ppb[par].rearrange("p (a b) -> p a b", a=4)),
                                waits=[("pprojpe", tpe), free_tok], sig="ppev")
                            pst["pev"][n] = ("ppev", tv)
                            pst["n"] += 1
                            yield
                        continue
                    for tc in range(8):
                        n = pst["n"]
                        par = n % 2
                        for kc in range(8):
                            tpe = P.op("pe", lambda e, par=par, kc=kc, wc=wc, tc=tc, wb=wb: e.matmul(
                                ppb[par], lhsT=wb[:, kc, wc:wc + 128], rhs=hT[:, kc, tc * 512:(tc + 1) * 512],
                                start=(kc == 0), stop=(kc == 7)),
                                waits=[t_w, pst["pev"].get(n - 2)], sig=("pprojpe" if kc == 7 else None))
                            if kc < 7:
                                yield
                        csl = slice(tc * 512, (tc + 1) * 512)
                        if kind == "q":
                            tv = P.op("dve", lambda e, par=par, gq=gq, csl=csl: e.tensor_scalar(
                                out=gq[:, csl], in0=ppb[par], scalar1=0.125, scalar2=None, op0=ALU.mult),
                                waits=[("pprojpe", tpe), free_tok], sig="ppev")
                        elif kind == "k":
                            tv = P.op("dve", lambda e, par=par, gk=gk, csl=csl: e.tensor_copy(out=gk[:, csl], in_=ppb[par]),
                                      waits=[("pprojpe", tpe), free_tok], sig="ppev")
                        elif st == 0:
                            tv = P.op("act", lambda e, par=par, gsg=gsg, csl=csl: e.activation(out=gsg[:, csl], in_=ppb[par], func=AF.Silu),
                                      waits=[("pprojpe", tpe), free_tok], sig="ppevS")
                            pst["pev"][n] = ("ppevS", tv)
                            pst["lastact"] = ("ppevS", tv)
                            pst["n"] += 1
                            yield
                            continue
                        else:
                            yield
                            ta = P.op("act", lambda e, par=par: e.activation(out=stmp, in_=ppb[par], func=AF.Exp, scale=-1.0),
                                      waits=[("pprojpe", tpe), pst["stmp"]], sig="ppevA")
                            yield
                            for qq in range(4):
                                qs = slice(qq * 128, (qq + 1) * 128)
                                t1 = P.op("dve", lambda e, qs=qs: e.tensor_scalar_add(out=stmp[:, qs], in0=stmp[:, qs], scalar1=1.0),
                                          waits=[("ppevA", ta)], sig="psil")
                                t1 = P.op("dve", lambda e, qs=qs: e.reciprocal(out=stmp[:, qs], in_=stmp[:, qs]), waits=[("psil", t1)], sig="psil")
                                osl = slice(tc * 512 + qq * 128, tc * 512 + (qq + 1) * 128)
                                tv = P.op("dve", lambda e, par=par, gsg=gsg, osl=osl, qs=qs: e.tensor_mul(
                                    out=gsg[:, osl], in0=ppb[par][:, qs], in1=stmp[:, qs]),
                                    waits=[("psil", t1), free_tok], sig="ppev")
                                if qq < 3:
                                    yield
                            pst["stmp"] = ("ppev", tv)
                        pst["pev"][n] = ("ppev", tv)
                        pst["n"] += 1
                        yield
                pst["projpe"][st] = ("pprojpe", tpe)
                pst["lastpev"][st] = [pst["pev"][pst["n"] - 1], pst["pev"][pst["n"] - 9], pst.get("lastact")]

            def pump(gen, k):
                if gen is None:
                    return None
                for _ in range(k):
                    try:
                        next(gen)
                    except StopIteration:
                        return None
                return gen

            def drain(gen):
                while gen is not None:
                    gen = pump(gen, 64)

            gen_cur = sb_proj_gen(0)

            for step in range(8):
                is_sb = step < 4
                if is_sb:
                    cq, ck, cv, cg = step * 128, 512 + step * 128, 1024 + step * 128, 1536 + step * 128
                    kw = 128
                    vw = 128
                    do_kv = True
                else:
                    j = step - 4
                    cq, ck, cv, cg = 2048 + j * 128, 2560 + (j // 2) * 64, 2688 + (j // 2) * 64, 2816 + j * 128
                    kw = 64
                    vw = 64
                    do_kv = (j % 2 == 0)
                if step > 4:
                    wv = w_in.rearrange("(k p) c -> p k c", p=128)
                    dmas = [(0, cq, 128), (256, cg, 128)]
                    if do_kv:
                        if kw == 128:
                            dmas.append((128, ck, 128))
                        else:
                            dmas.append((128, ck, 64))
                            dmas.append((192, ck, 64))
                        if vw == 128:
                            dmas.append((384, cv, 128))
                        else:
                            dmas.append((384, cv, 64))
                            dmas.append((448, cv, 64))
                            vw = 128
                    for (dc, sc, w) in dmas:
                        t_wsl = P.op("pool", lambda e, dc=dc, sc=sc, w=w: e.dma_start(out=wsl[:, :, dc:dc + w], in_=wv[:, :, sc:sc + w]),
                                     waits=[("projpe", t_projpe), ("hfix", t_hT), pst["projpe"].get(4)],
                                     sig="ld_wsl", inc=16)
                    ckpt(2.5 + 2 * step)
                    kinds = [("q", 0), ("g", 256)] + ([("k", 128)] if do_kv else [])
                    for kind, wc in kinds:
                        for tc in range(8):
                            par = nproj % 2
                            for kc in range(8):
                                last = kc == 7
                                t_projpe_new = P.op("pe", lambda e, par=par, kc=kc, wc=wc, tc=tc: e.matmul(
                                    pp[par], lhsT=wsl[:, kc, wc:wc + 128], rhs=hT[:, kc, tc * 512:(tc + 1) * 512],
                                    start=(kc == 0), stop=(kc == 7)),
                                    waits=[("ld_wsl", t_wsl), t_pev.get(nproj - 2), ("hfix", t_hT),
                                           t_attdve],
                                    sig=("projpe" if last else None))
                            t_projpe = t_projpe_new
                            dst = {"q": qT, "k": kT, "g": sg}[kind][:, tc * 512:(tc + 1) * 512]
                            if kind == "q":
                                t_pev[nproj] = ("pev", P.op("dve", lambda e, dst=dst, par=par: e.tensor_scalar(
                                    out=dst, in0=pp[par], scalar1=0.125, scalar2=None, op0=ALU.mult),
                                    waits=[("projpe", t_projpe)], sig="pev"))
                                last_dve_pev = t_pev[nproj]
                            elif kind == "k":
                                t_pev[nproj] = ("pev", P.op("dve", lambda e, dst=dst, par=par: e.tensor_copy(out=dst, in_=pp[par]),
                                                    waits=[("projpe", t_projpe)], sig="pev"))
                                last_dve_pev = t_pev[nproj]
                            else:
                                t_pev[nproj] = ("pevA", P.op("act", lambda e, dst=dst, par=par: e.activation(out=dst, in_=pp[par], func=AF.Silu),
                                                             waits=[("projpe", t_projpe), t_attdve], sig="pevA"))
                                last_act_pev = t_pev[nproj]
                            nproj += 1
                    ckpt(2.75 + 2 * step)
                    if do_kv:
                        for tg in range(8):
                            par = nproj % 2
                            for t4 in range(4):
                                tok = (tg * 4 + t4) * 128
                                for kc in range(8):
                                    last = (kc == 7 and t4 == 3)
                                    t_projpe_new = P.op("pe", lambda e, par=par, kc=kc, tok=tok, t4=t4, vw=vw: e.matmul(
                                        pp[par][:, t4 * 128:t4 * 128 + vw], lhsT=hT[:, kc, tok:tok + 128], rhs=wsl[:, kc, 384:384 + vw],
                                        start=(kc == 0), stop=(kc == 7)),
                                        waits=[("ld_wsl", t_wsl), t_pev.get(nproj - 2), t_attdve],
                                        sig=("projpe" if last else None))
                            t_projpe = t_projpe_new
                            t_pev[nproj] = ("pev", P.op("dve", lambda e, par=par, tg=tg, vw=vw: e.tensor_copy(
                                out=vv[:, tg * 4:(tg + 1) * 4, 0:vw],
                                in_=pp[par].rearrange("p (a b) -> p a b", a=4)[:, :, 0:vw]),
                                waits=[("projpe", t_projpe)], sig="pev"))
                            last_dve_pev = t_pev[nproj]
                            nproj += 1
                    projw = [last_dve_pev, last_act_pev]
                else:
                    drain(gen_cur)
                    projw = list(pst["lastpev"][step])
                    gen_cur = sb_proj_gen(step + 1) if step + 1 < 5 else None
                    qT, kT, sg, vv = GSv[step % 2]
                ckpt(3 + 2 * step)

                if is_sb:
                    QCB = 512
                    tiles = []
                    for qc in range(S // QCB):
                        nkb = (QCB // 128) * (qc + 1)
                        for kb in range(nkb - 1, -1, -1):
                            jd = kb - (QCB // 128) * qc
                            diag = jd >= 0
                            c0 = 128 * jd if diag else 0
                            tiles.append((0, qc, kb, c0, diag, kb == nkb - 1, kb == 0))
                    T = len(tiles)
                    base = sbi
                    chunk_of = {}
                    cc = n_chunk - 1
                    for i, tl in enumerate(tiles):
                        if tl[5]:
                            cc += 1
                        chunk_of[i] = cc
                    Zh = [ps[:, 0:512], ps[:, 512:1024]]
                    Rh = [ps[:, 1024:1536], ps[:, 1536:2048]]
                    Yc = [ps[:, 2048:2560], ps[:, 2560:3072]]

                    def two(ap, c0):
                        if c0 == 0:
                            return ap
                        return ap.rearrange("p (h c) -> p h c", h=2)[:, :, c0:QCB]

                    def QK(i):
                        hp0, qc, kb, c0, diag, cs, ce = tiles[i]
                        g = base + i
                        for hh in range(2):
                            hp = hh * 64
                            last = (hh == 1) and not diag
                            v = P.op("pe", lambda e, hp=hp, hh=hh, kb=kb, qc=qc, c0=c0, diag=diag, kT=kT, qT=qT: e.matmul(
                                Zh[hh][:, c0:QCB], lhsT=kT[hp:hp + 64, kb * 128:(kb + 1) * 128],
                                rhs=qT[hp:hp + 64, qc * QCB + c0:(qc + 1) * QCB], start=True, stop=not diag,
                                skip_group_check=True),
                                waits=[("sbE", tk.get(("E", g - 1)))] + (projw if i < 2 else []),
                                sig=("sbQK" if last else None), attach=True)
                        if diag:
                            for hh in range(2):
                                v = P.op("pe", lambda e, hh=hh, c0=c0: e.matmul(
                                    Zh[hh][:, c0:c0 + 128], lhsT=ident, rhs=negm, start=False, stop=True, skip_group_check=True),
                                    sig=("sbQK" if hh == 1 else None))
                        tk[("QK", g)] = v

                    def ACT_E(i):
                        hp0, qc, kb, c0, diag, cs, ce = tiles[i]
                        g = base + i
                        tk[("E", g)] = P.op("act", lambda e, g=g, c0=c0: e.activation(out=two(Eb[g % 3], c0), in_=two(ps[:, 0:1024], c0), func=AF.Exp),
                                            waits=[("sbQK", tk[("QK", g)]), ("sbA", tk.get(("A", g - 3)))], sig="sbE")

                    def ACT_G(i):
                        hp0, qc, kb, c0, diag, cs, ce = tiles[i]
                        g = base + i
                        tk[("G", g)] = P.op("act", lambda e, g=g, c0=c0: e.activation(out=two(Gb[g % 2], c0), in_=two(Eb[g % 3], c0),
                                                                                   func=AF.Ln, bias=1.0),
                                            waits=[("sbE", tk[("E", g)]), ("sbTRIC", tk.get(("TRIC", g - 2)))], sig="sbG")

                    def ACT_P(i):
                        hp0, qc, kb, c0, diag, cs, ce = tiles[i]
                        g = base + i
                        tk[("P", g)] = P.op("act", lambda e, g=g, c0=c0: e.activation(out=two(Pb[g % 2], c0), in_=two(ps[:, 1024:2048], c0), func=AF.Exp),
                                            waits=[("sbTRI", tk[("TRI", g)]), ("sbA", tk.get(("A", g - 2)))], sig="sbP")

                    def PE_TRI(i):
                        hp0, qc, kb, c0, diag, cs, ce = tiles[i]
                        g = base + i
                        if cs:
                            par = chunk_of[i] % 2
                            for hh in range(2):
                                P.op("pe", lambda e, hh=hh: e.matmul(Rh[hh], lhsT=zer, rhs=cst[:, 0:512], start=True, stop=False,
                                                                     skip_group_check=True),
                                     waits=[("sbP", tk.get(("P", g - 1)))])
                            P.op("pe", lambda e, par=par: e.matmul(Yc[par], lhsT=zer, rhs=cst[:, 0:512], start=True, stop=False,
                                                                   skip_group_check=True),
                                 waits=[("sbEV", t_evy.get(chunk_of[i] - 2))])
                        for hh in range(2):
                            v = P.op("pe", lambda e, g=g, hh=hh, c0=c0: e.matmul(
                                Rh[hh][:, c0:QCB], lhsT=triN, rhs=Gb[g % 2][:, hh * QCB + c0:(hh + 1) * QCB], start=False, stop=False,
                                skip_group_check=True),
                                waits=[("sbG", tk[("G", g)]), ("sbP", tk.get(("P", g - 1)))],
                                sig=("sbTRI" if hh == 1 else None), attach=True)
                        tk[("TRI", g)] = v

                    def PE_TRIC(i):
                        hp0, qc, kb, c0, diag, cs, ce = tiles[i]
                        g = base + i
                        for hh in range(2):
                            v = P.op("pe", lambda e, g=g, hh=hh, c0=c0: e.matmul(
                                Rh[hh][:, c0:QCB], lhsT=tricN, rhs=Gb[g % 2][:, hh * QCB + c0:(hh + 1) * QCB], start=False, stop=False,
                                skip_group_check=True),
                                waits=[("sbP", tk[("P", g)])],
                                sig=("sbTRIC" if hh == 1 else None), attach=True)
                        tk[("TRIC", g)] = v

                    def PE_PV(i):
                        hp0, qc, kb, c0, diag, cs, ce = tiles[i]
                        g = base + i
                        par = chunk_of[i] % 2
                        for hh in range(2):
                            hp = hh * 64
                            v = P.op("pe", lambda e, g=g, hh=hh, hp=hp, kb=kb, par=par, c0=c0, vv=vv: e.matmul(
                                Yc[par][hp:hp + 64, c0:QCB], lhsT=vv[:, kb, hp:hp + 64], rhs=Ab[g % 2][:, hh * QCB + c0:(hh + 1) * QCB],
                                start=False, stop=False, skip_group_check=True),
                                waits=[("sbA", tk[("A", g)])],
                                sig=("sbPV" if hh == 1 else None), attach=True)
                        tk[("PV", g)] = v

                    def DVE_A(i):
                        hp0, qc, kb, c0, diag, cs, ce = tiles[i]
                        g = base + i
                        tk[("A", g)] = P.op("dve", lambda e, g=g, c0=c0: e.tensor_mul(out=two(Ab[g % 2], c0), in0=two(Eb[g % 3], c0),
                                                                                   in1=two(Pb[g % 2], c0)),
                                            waits=[("sbP", tk[("P", g)]), ("sbPV", tk.get(("PV", g - 2)))], sig="sbA")

                    def DVE_EV(i):
                        hp0, qc, kb, c0, diag, cs, ce = tiles[i]
                        g = base + i
                        ch = chunk_of[i]
                        par = ch % 2
                        t_evy[ch] = P.op("dve", lambda e, qc=qc, par=par, step=step, sg=sg: e.tensor_mul(
                            out=ygT[:, step, qc * QCB:(qc + 1) * QCB], in0=Yc[par], in1=sg[:, qc * QCB:(qc + 1) * QCB]),
                            waits=[("sbPV", tk[("PV", g)])], sig="sbEV")

                    QK(0)
                    ACT_E(0)
                    if T > 1:
                        QK(1)
                    ACT_G(0)
                    if T > 1:
                        ACT_E(1)
                    if T > 2:
                        QK(2)
                    PE_TRI(0)
                    for i in range(T):
                        ACT_P(i)
                        PE_TRIC(i)
                        gen_cur = pump(gen_cur, PUMP_A)
                        if i + 1 < T:
                            ACT_G(i + 1)
                            PE_TRI(i + 1)
                        DVE_A(i)
                        PE_PV(i)
                        if tiles[i][6]:
                            DVE_EV(i)
                        if i + 2 < T:
                            ACT_E(i + 2)
                        if i + 3 < T:
                            QK(i + 3)
                        gen_cur = pump(gen_cur, PUMP_B + (1 if i % 6 == 5 else 0))
                    sbi += T
                    n_chunk = cc + 1
                    t_attdve = ("sbEV", t_evy[cc])
                    pst["attdone"][step] = t_attdve
                    qT, kT, sg, vv = GSv[0]
                else:
                    j = step - 4
                    heads = (2 * j, 2 * j + 1)
                    swn0 = swn
                    att_prev = t_attdve

                    def SW_QK(n):
                        gi = swn0 + n
                        zbase = (gi % 2) * 1024
                        wz = 256 if n > 0 else 128
                        for which, kblk in ((0, n), (1, n - 1)):
                            if kblk < 0:
                                continue
                            for hh in range(2):
                                hp = hh * 64
                                zc = zbase + hh * 512 + which * 128
                                P.op("pe", lambda e, zc=zc, hp=hp, kblk=kblk, n=n, which=which: e.matmul(
                                    ps[:, zc:zc + 128], lhsT=kT[hp:hp + 64, kblk * 128:(kblk + 1) * 128],
                                    rhs=qT[hp:hp + 64, n * 128:(n + 1) * 128], start=(which == 0), stop=False, skip_group_check=True),
                                    waits=[("swP", t_sw.get(("P", gi - 2)))] + (projw if n < 2 else []))
                        for hh in range(2):
                            h = heads[hh]
                            bc = C_SWB + h * 256
                            zc = zbase + hh * 512
                            v = P.op("pe", lambda e, zc=zc, bc=bc, wz=wz: e.matmul(
                                ps[:, zc:zc + wz], lhsT=ident, rhs=cst[:, bc:bc + wz], start=False, stop=True, skip_group_check=True),
                                sig=("swQK" if hh == 1 else None))
                        t_sw[("QK", gi)] = v

                    def SW_ACT(n):
                        gi = swn0 + n
                        zbase = (gi % 2) * 1024
                        wz = 256 if n > 0 else 128
                        zin = ps[:, zbase:zbase + 1024].rearrange("p (b c) -> p b c", b=2)[:, :, 0:wz]
                        pout = Psw[gi % 2].rearrange("p (b c) -> p b c", b=2)[:, :, 0:wz]
                        t_sw[("P", gi)] = P.op("act", lambda e, zin=zin, pout=pout: e.activation(out=pout, in_=zin, func=AF.Exp),
                                               waits=[("swQK", t_sw[("QK", gi)]), ("swPV", t_sw.get(("PV", gi - 2)))], sig="swP")

                    def SW_PVD(n):
                        gi = swn0 + n
                        grp = swg + n // 4
                        par = grp % 2
                        Yb = Yp[par][:, 0:512]
                        Db = Yp[par][:, 512:1024]
                        col = (n % 4) * 128
                        for hh in range(2):
                            hp = hh * 64
                            srcs = [(hh * 256, n)] + ([(hh * 256 + 128, n - 1)] if n > 0 else [])
                            ns = len(srcs)
                            for si, (pc, kblk) in enumerate(srcs):
                                P.op("pe", lambda e, Yb=Yb, hp=hp, col=col, kblk=kblk, gi=gi, pc=pc, si=si, ns=ns: e.matmul(
                                    Yb[hp:hp + 64, col:col + 128], lhsT=vv[:, kblk, 0:64], rhs=Psw[gi % 2][:, pc:pc + 128],
                                    start=(si == 0), stop=(si == ns - 1), skip_group_check=True),
                                    waits=[("swP", t_sw[("P", gi)]), ("swEV", t_sw.get(("EV", grp - 2)))] + ([att_prev] if n < 8 else []))
                            for si, (pc, kblk) in enumerate(srcs):
                                v = P.op("pe", lambda e, Db=Db, hp=hp, col=col, gi=gi, pc=pc, si=si, ns=ns: e.matmul(
                                    Db[hp:hp + 64, col:col + 128], lhsT=ones_bf[:, 0:64], rhs=Psw[gi % 2][:, pc:pc + 128],
                                    start=(si == 0), stop=(si == ns - 1), skip_group_check=True),
                                    sig=("swPV" if (hh == 1 and si == ns - 1) else None))
                        t_sw[("PV", gi)] = v
                    def SW_EV(n):
                        gi = swn0 + n
                        grp = swg + n // 4
                        par = grp % 2
                        Yb = Yp[par][:, 0:512]
                        Db = Yp[par][:, 512:1024]
                        if n % 4 == 3:
                            q0 = (n - 3) * 128
                            t1 = P.op("act", lambda e, Db=Db, j=j: e.activation(out=rden, in_=Db, func=AF.Ln, bias=esk[:, j:j + 1]),
                                      waits=[("swPV", t_sw[("PV", gi)]), ("act0", t_esk), ("swEV", t_sw.get(("EV", grp - 1)))], sig="swDa")
                            t1 = P.op("act", lambda e: e.activation(out=rden, in_=rden, func=AF.Exp, scale=-1.0),
                                      waits=[("swDa", t1)], sig="swDa")
                            t1 = P.op("dve", lambda e, Yb=Yb: e.tensor_mul(out=ytmp, in0=Yb, in1=rden), waits=[("swDa", t1)], sig="swD")
                            t_sw[("EV", grp)] = P.op("dve", lambda e, q0=q0, step=step: e.tensor_mul(
                                out=ygT[:, step, q0:q0 + 512], in0=ytmp, in1=sg[:, q0:q0 + 512]),
                                waits=[("swD", t1)], sig="swEV")

                    SW_QK(0)
                    for n in range(NT):
                        SW_ACT(n)
                        if n >= 1 and (n - 1) % 4 == 3:
                            SW_EV(n - 1)
                        if n + 1 < NT:
                            SW_QK(n + 1)
                        SW_PVD(n)
                    SW_EV(NT - 1)
                    swn += NT
                    swg += NT // 4
                    t_last_sw = t_sw[("EV", swg - 1)]
                if not is_sb:
                    t_attdve = ("swEV", t_last_sw)
                ckpt(4 + 2 * step)

            ckpt(20)
            wo_v = w_out.rearrange("(k p) e -> p k e", p=128)
            t_wo = None
            for k in range(8):
                t_wo = P.op("pool", lambda e, k=k: e.dma_start(out=wout[:, k, :], in_=wo_v[:, k, :]),
                            waits=[("projpe", t_projpe)], sig="ld_wo", inc=16)
            t_fg = P.op("sp", lambda e: e.dma_start(out=fg_bc, in_=fg_bc_d[:, :]), waits=[("projpe", t_projpe)], sig="ld_fg", inc=16)
            t_xf = {}
            t_o = {}
            t_st = {}
            t_r2 = {}
            t_sq2 = {}

            def issue_xf(tt):
                t_xf[tt] = P.op("sp", lambda e, tt=tt: e.dma_start(out=xf[tt % 3], in_=x[tt * 128:(tt + 1) * 128, :]),
                                waits=[("fr2", t_r2.get(tt - 3)), ("projpe", t_projpe)], sig="ld_xf%d" % (tt % 3), inc=16)

            t_po = {}
            for tt in range(3):
                issue_xf(tt)
            def F_PE(tt):
                for eh in range(2):
                    for kc in range(8):
                        v = P.op("pe", lambda e, tt=tt, eh=eh, kc=kc: e.matmul(
                            po[tt % 2][eh], lhsT=ygT[:, kc, tt * 128:(tt + 1) * 128], rhs=wout[:, kc, eh * 512:(eh + 1) * 512],
                            start=(kc == 0), stop=(kc == 7)),
                            waits=[("ld_wo", t_wo), t_attdve, ("fr2", t_r2.get(tt - 2))],
                            sig=("fpo" if kc == 7 else None))
                    t_po[(tt, eh)] = v

            def F_A(tt):
                rb = rf[tt % 2]
                for eh in range(2):
                    t1 = P.op("dve", lambda e, tt=tt, eh=eh, rb=rb: e.tensor_mul(out=rb[:, eh * 512:(eh + 1) * 512], in0=po[tt % 2][eh],
                                                                               in1=gate_bc[:, eh * 512:(eh + 1) * 512]),
                              waits=[("fpo", t_po[(tt, eh)]), ("fo", t_o.get(tt - 2)), ("fsq", t_sq2.get(tt - 2))], sig="fd")
                t_r2[tt] = P.op("dve", lambda e, tt=tt, rb=rb: e.tensor_add(out=rb, in0=rb, in1=xf[tt % 3]),
                                waits=[("fd", t1), ("ld_xf%d" % (tt % 3), t_xf[tt])], sig="fr2")
                if tt + 3 < NT:
                    issue_xf(tt + 3)
                t_sq2[tt] = P.op("act", lambda e, tt=tt, rb=rb: e.activation(out=junkf, in_=rb, func=AF.Square, accum_out=ss2[:, tt:tt + 1]),
                                 waits=[("fr2", t_r2[tt])], sig="fsq")
                t_sqrt2[tt] = P.op("act", lambda e, tt=tt: e.activation(out=rstd2[:, tt:tt + 1], in_=ss2[:, tt:tt + 1], func=AF.Sqrt,
                                                                        scale=1.0 / D, bias=eps_t[:, 0:1]),
                                   waits=[("fsq", t_sq2[tt])], sig="fsqrt")

            def F_B(tt):
                rb = rf[tt % 2]
                t1 = P.op("dve", lambda e, tt=tt: e.reciprocal(out=rstd2[:, tt:tt + 1], in_=rstd2[:, tt:tt + 1]),
                          waits=[("fsqrt", t_sqrt2[tt])], sig="fd")
                t_o[tt] = P.op("dve", lambda e, tt=tt, rb=rb: e.scalar_tensor_tensor(
                    out=of[tt % 2], in0=rb, scalar=rstd2[:, tt:tt + 1], in1=fg_bc, op0=ALU.mult, op1=ALU.mult),
                    waits=[("fd", t1), ("ld_fg", t_fg), ("ld_out%d" % (tt % 2), t_st.get(tt - 2))], sig="fo")
                t_st[tt] = P.op("sp", lambda e, tt=tt: e.dma_start(out=out[tt * 128:(tt + 1) * 128, :], in_=of[tt % 2]),
                                waits=[("fo", t_o[tt])], sig="ld_out%d" % (tt % 2), inc=16)

            t_sqrt2 = {}
            F_PE(0)
            F_PE(1)
            F_A(0)
            for tt in range(NT):
                if tt + 2 < NT:
                    F_PE(tt + 2)
                if tt + 1 < NT:
                    F_A(tt + 1)
                F_B(tt)
            P.op("sp", lambda e: e.nop(), waits=[("ld_out0", t_st[NT - 2]), ("ld_out1", t_st[NT - 1])])

        try:
            plan_all()
        except _Stop:
            pass

        names = sorted(P.cnt.keys())
        sems = {n: es.enter_context(nc.semaphore(n)) for n in names}
        block = es.enter_context(nc.Block())

        def emit(eng, oplist):
            seen = {}
            for fn, waits, sig, inc, attach in oplist:
                pend = [(name, val) for (name, val) in waits if seen.get(name, 0) < val]
                if attach and len(pend) == 1:
                    (name, val) = pend[0]
                    ins = fn(eng)
                    ins._wait_ge(sems[name], val)
                    seen[name] = val
                else:
                    for (name, val) in pend:
                        eng.wait_ge(sems[name], val)
                        seen[name] = val
                    ins = fn(eng)
                if sig is not None:
                    ins.then_inc(sems[sig], inc)

        @block.sync
        def _(eng):
            emit(eng, P.ops["sp"])

        @block.gpsimd
        def _(eng):
            emit(eng, P.ops["pool"])

        @block.tensor
        def _(eng):
            emit(eng, P.ops["pe"])

        @block.scalar
        def _(eng):
            emit(eng, P.ops["act"])

        @block.vector
        def _(eng):
            emit(eng, P.ops["dve"])
    return nc


_CACHE = {}


def kernel(x, c, w_ada, b_ada, norm_g, w_in, sinks, w_out, final_g):
    x = np.asarray(x, np.float32)
    c = np.asarray(c, np.float32)
    w_ada = np.ascontiguousarray(np.asarray(w_ada, np.float32)[0])
    b_ada = np.asarray(b_ada, np.float32)[0]
    norm_g = np.asarray(norm_g, np.float32)[0]
    w_in = np.ascontiguousarray(np.asarray(w_in, np.float32)[0])
    sinks = np.asarray(sinks, np.float32)[0]
    w_out = np.ascontiguousarray(np.asarray(w_out, np.float32)[0])
    final_g = np.asarray(final_g, np.float32)

    def lay(v):
        return np.ascontiguousarray(v.reshape(-1, 128).T)

    bada_l = lay(b_ada[:2048])
    bg_bc = np.ascontiguousarray(np.broadcast_to(b_ada[2048:3072][None, :], (128, D)))
    normg_l = lay(norm_g)
    fg_bc = np.ascontiguousarray(np.broadcast_to(final_g[None, :], (128, D)))
    sinks_l = np.ascontiguousarray(np.stack([np.repeat(sinks[2 * j:2 * j + 2], 64) for j in range(4)], axis=1))
    consts = make_consts()
    if "nc" not in _CACHE:
        _CACHE["nc"] = build_nc()
    nc = _CACHE["nc"]
    in_maps = []
    for b in range(NCORE):
        in_maps.append({
            "x": np.ascontiguousarray(x[b]), "c_l": lay(c[b]), "w_ada": w_ada, "bada_l": bada_l, "bg_bc": bg_bc,
            "normg_l": normg_l, "w_in": w_in, "sinks_l": sinks_l, "w_out": w_out, "fg_bc": fg_bc, "consts": consts,
        })
    res = run_bass_kernel_spmd(nc, in_maps, core_ids=list(range(NCORE)))
    return np.stack([np.asarray(r["out"], np.float32) for r in res.results], axis=0)
```

```python
from contextlib import ExitStack

import numpy as np
import concourse.bass as bass
import concourse.mybir as mybir
from concourse.bass_utils import run_bass_kernel_spmd

F32 = mybir.dt.float32
BF16 = mybir.dt.bfloat16
AF = mybir.ActivationFunctionType
ALU = mybir.AluOpType

S = 4096
D = 1024
NCORE = 8
NT = S // 128
QC = 1024
NEG = -30000.0
NCONST = 128 * 6 + 2048
C_ID, C_TRI, C_TRIC, C_ZERO, C_NEGM, C_ONES, C_SWB = 0, 128, 256, 384, 512, 640, 768


LEVEL = 99
SW_DBG = 0
PUMP = 2
PUMP_A = 1
PUMP_B = 1
ATTACH = True


class _Stop(Exception):
    pass


def ckpt(level):
    if LEVEL <= level:
        raise _Stop()


class Plan:
    def __init__(self):
        self.ops = {"pe": [], "act": [], "dve": [], "pool": [], "sp": []}
        self.cnt = {}

    def op(self, eng, fn, waits=(), sig=None, inc=1, attach=False):
        v = None
        if sig is not None:
            self.cnt[sig] = self.cnt.get(sig, 0) + inc
            v = self.cnt[sig]
        ws = tuple(w for w in waits if w is not None and w[1] is not None and w[1] > 0)
        self.ops[eng].append((fn, ws, sig, inc, attach and ATTACH))
        return v


def make_consts():
    c = np.zeros((128, NCONST), np.float32)
    j = np.arange(128)[:, None]
    s = np.arange(128)[None, :]
    c[:, C_ID:C_ID + 128] = (j == s)
    c[:, C_TRI:C_TRI + 128] = -1.0 * (j >= s)
    c[:, C_TRIC:C_TRIC + 128] = -1.0 * (j < s)
    c[:, C_NEGM:C_NEGM + 128] = np.where(j < s, 0.0, NEG)
    c[:, C_ONES:C_ONES + 128] = 1.0
    for h in range(8):
        m = 2.0 ** (-8.0 * (h + 1) / 8)
        rel_cur = (s - j).astype(np.float32)
        cur = np.where(s >= j, -m * rel_cur, NEG)
        rel_prev = (128 + s - j).astype(np.float32)
        prev = np.where(j > s, -m * rel_prev, NEG)
        c[:, C_SWB + h * 256:C_SWB + h * 256 + 128] = cur
        c[:, C_SWB + h * 256 + 128:C_SWB + h * 256 + 256] = prev
    return c


def sb_tiles():
    out = []
    for qc in range(S // QC):
        nkb = (QC // 128) * (qc + 1)
        for kb in range(nkb - 1, -1, -1):
            jd = kb - (QC // 128) * qc
            diag = jd >= 0
            c0 = 128 * jd if diag else 0
            out.append((qc, kb, c0, diag, kb == nkb - 1, kb == 0))
    return out


def col_segs(c0, c1=QC):
    segs = []
    a = c0
    while a < c1:
        b = min(c1, (a // 512 + 1) * 512)
        segs.append((a, b))
        a = b
    return segs


def build_nc():
    nc = bass.Bass("TRN2", target_bir_lowering=False)
    x = nc.dram_tensor("x", [S, D], F32, kind="ExternalInput").ap()
    c_l = nc.dram_tensor("c_l", [128, 8], F32, kind="ExternalInput").ap()
    w_ada = nc.dram_tensor("w_ada", [D, 3 * D], F32, kind="ExternalInput").ap()
    bada_l = nc.dram_tensor("bada_l", [128, 16], F32, kind="ExternalInput").ap()
    bg_bc_d = nc.dram_tensor("bg_bc", [128, D], F32, kind="ExternalInput").ap()
    normg_l = nc.dram_tensor("normg_l", [128, 8], F32, kind="ExternalInput").ap()
    w_in = nc.dram_tensor("w_in", [D, 3328], F32, kind="ExternalInput").ap()
    sinks_l = nc.dram_tensor("sinks_l", [128, 4], F32, kind="ExternalInput").ap()
    w_out = nc.dram_tensor("w_out", [D, D], F32, kind="ExternalInput").ap()
    fg_bc_d = nc.dram_tensor("fg_bc", [128, D], F32, kind="ExternalInput").ap()
    consts_d = nc.dram_tensor("consts", [128, NCONST], F32, kind="ExternalInput").ap()
    out = nc.dram_tensor("out", [S, D], F32, kind="ExternalOutput").ap()

    P = Plan()
    es = ExitStack()
    with es:
        def sb(name, shape, dt):
            return es.enter_context(nc.sbuf_tensor(name, shape, dt))

        ygT = sb("ygT", [128, 8, S], BF16)
        hT = sb("hT", [128, 8, S], BF16)
        ov = sb("ov", [128, 30720], BF16)
        cst = sb("cst", [128, NCONST], BF16)
        gate_bc = sb("gate_bc", [128, D], F32)
        c_sb = sb("c_sb", [128, 8], F32)
        etmp = sb("etmp", [128, 8], F32)
        cond = sb("cond", [128, 8], F32)
        bada = sb("bada", [128, 16], F32)
        normg = sb("normg", [128, 8], F32)
        mod_sb = sb("mod_sb", [128, 16], F32)
        gs = sb("gs", [128, 8], F32)
        ones_f = sb("ones_f", [128, 128], F32)
        ss = sb("ss", [128, 32], F32)
        rstd = sb("rstd", [128, 32], F32)
        ss2 = sb("ss2", [128, 32], F32)
        rstd2 = sb("rstd2", [128, 32], F32)
        snk = sb("snk", [128, 4], F32)
        esk = sb("esk", [128, 4], F32)
        eps_t = sb("eps_t", [128, 1], F32)
        wsl1 = sb("wsl1", [128, 4096], BF16)
        ps = es.enter_context(nc.psum_tensor("ps", [128, 4096], F32))

        qT = ov[:, 0:4096]
        kT = ov[:, 4096:8192]
        sg = ov[:, 8192:12288]
        vv = ov[:, 12288:16384].rearrange("p (b c) -> p b c", c=128)
        wsl = ov[:, 16384:20480].rearrange("p (k c) -> p k c", c=512)
        PB = 20480
        Eb = [ov[:, PB + i * 1024:PB + (i + 1) * 1024] for i in range(3)]
        Gb = [ov[:, PB + (3 + i) * 1024:PB + (4 + i) * 1024] for i in range(2)]
        stmp = ov[:, PB + 5 * 1024:PB + 6 * 1024].bitcast(F32)
        Pb = [ov[:, PB + (6 + i) * 1024:PB + (7 + i) * 1024] for i in range(2)]
        Ab = [ov[:, PB + (8 + i) * 1024:PB + (9 + i) * 1024] for i in range(2)]
        Psw = [ov[:, PB + i * 512:PB + (i + 1) * 512] for i in range(2)]
        rden = ov[:, PB + 1024:PB + 2048].bitcast(F32)
        ytmp = ov[:, PB + 2048:PB + 3072].bitcast(F32)
        yflat = ygT[:, :, :].rearrange("p a b -> p (a b)")
        wada = [yflat[:, i * 6144:(i + 1) * 6144].bitcast(F32) for i in range(4)]
        xt = ([ov[:, 12288 + i * 2048:12288 + (i + 1) * 2048].bitcast(F32) for i in range(4)]
              + [ov[:, i * 2048:(i + 1) * 2048].bitcast(F32) for i in range(4)])
        xnb = [[ov[:, 20480 + (g * 4 + t) * 1024:20480 + (g * 4 + t + 1) * 1024] for t in range(4)] for g in range(2)]
        junk = ov[:, 28672:29696]
        hflat = hT[:, :, :].rearrange("p a b -> p (a b)")
        condb = yflat[:, 24576:26624].bitcast(F32).rearrange("p (k m) -> p k m", m=128)
        bg_bc = yflat[:, 26624:28672].bitcast(F32)
        wout = hflat[:, 0:8192].rearrange("p (k e) -> p k e", e=1024)
        xf = [hflat[:, 8192 + i * 2048:8192 + (i + 1) * 2048].bitcast(F32) for i in range(3)]
        rf = [hflat[:, 14336 + i * 2048:14336 + (i + 1) * 2048].bitcast(F32) for i in range(2)]
        of = [hflat[:, 18432 + i * 2048:18432 + (i + 1) * 2048].bitcast(F32) for i in range(2)]
        fg_bc = hflat[:, 22528:24576].bitcast(F32)
        junkf = hflat[:, 24576:25600]
        Zp = ps[:, 0:1024]
        Rp = ps[:, 1024:2048]
        Yp = [ps[:, 2048:3072], ps[:, 3072:4096]]
        pp = [ps[:, 0:512], ps[:, 512:1024]]
        tp = [ps[:, i * 512:(i + 1) * 512].bitcast(BF16)[:, 0:512] for i in range(2)]
        modps = ps[:, 1024:1040]
        gateps = ps[:, 2048:3072]
        po = [[ps[:, t * 1024 + e * 512:t * 1024 + (e + 1) * 512] for e in range(2)] for t in range(2)]

        ident = cst[:, C_ID:C_ID + 128]
        triN = cst[:, C_TRI:C_TRI + 128]
        tricN = cst[:, C_TRIC:C_TRIC + 128]
        zer = cst[:, C_ZERO:C_ZERO + 128]
        negm = cst[:, C_NEGM:C_NEGM + 128]
        ones_bf = cst[:, C_ONES:C_ONES + 128]

        def plan_all():
            nonlocal qT, kT, sg, vv
            t_cst = P.op("pool", lambda e: e.dma_start(out=cst[:, :], in_=consts_d[:, :]), sig="ld_cst", inc=16)
            for dst, src in ((c_sb, c_l), (bada, bada_l), (normg, normg_l), (snk, sinks_l)):
                t_small = P.op("sp", lambda e, dst=dst, src=src: e.dma_start(out=dst[:, :], in_=src[:, :]), sig="ld_small", inc=16)
            t_small = P.op("sp", lambda e: e.dma_start(out=bg_bc, in_=bg_bc_d[:, :]), sig="ld_small", inc=16)

            ckpt(0)
            P.op("dve", lambda e: e.memset(ones_f[:, :], 1.0), sig="dve0")
            P.op("dve", lambda e: e.memset(eps_t[:, :], 1e-6), sig="dve0")
            P.op("dve", lambda e: e.memset(ss[:, :], 0.0), sig="dve0")
            t_d = P.op("dve", lambda e: e.memset(ss2[:, :], 0.0), sig="dve0")
            t_ms = t_d
            t_a = P.op("act", lambda e: e.activation(out=etmp[:, :], in_=c_sb[:, :], func=AF.Exp, scale=-1.0),
                       waits=[("ld_small", t_small)], sig="act0")
            t_d = P.op("dve", lambda e: e.tensor_scalar_add(out=etmp[:, :], in0=etmp[:, :], scalar1=1.0),
                       waits=[("act0", t_a), ("dve0", t_d)], sig="dve0")
            t_d = P.op("dve", lambda e: e.reciprocal(out=etmp[:, :], in_=etmp[:, :]), waits=[("dve0", t_d)], sig="dve0")
            t_d = P.op("dve", lambda e: e.tensor_mul(out=cond[:, :], in0=c_sb[:, :], in1=etmp[:, :]),
                       waits=[("dve0", t_d)], sig="dve0")
            t_cond = t_d
            for k in range(8):
                t_d = P.op("dve", lambda e, k=k: e.tensor_scalar(out=condb[:, k, :], in0=ones_f[:, :], scalar1=cond[:, k:k + 1],
                                                                 scalar2=None, op0=ALU.mult),
                           waits=[("dve0", t_cond)], sig="dve0")
            t_condb = t_d
            t_pe0 = {}
            t_wada = {}
            t_xld = {}
            t_sq = {}
            t_xn = {}
            t_tp = {}
            t_ev = {}
            NWB = 4
            NXB = 8

            def issue_wada(k):
                t_wada[k] = P.op("sp", lambda e, k=k: e.dma_start(out=wada[k % NWB], in_=w_ada[k * 128:(k + 1) * 128, :]),
                                 waits=[("pe0", t_pe0.get(k - NWB))], sig="ld_wada%d" % (k % NWB), inc=16)

            def issue_xload(tt):
                t_xld[tt] = P.op("sp", lambda e, tt=tt: e.dma_start(out=xt[tt % NXB], in_=x[tt * 128:(tt + 1) * 128, :]),
                                 waits=[("p1xn", t_xn.get(tt - NXB)), ("p1sq", t_sq.get(tt - NXB))],
                                 sig="ld_x%d" % (tt % NXB), inc=16)

            def p1_transpose_evac(g):
                for j in range(8):
                    prev_ev = t_ev[(g, j - 2)] if j >= 2 else (t_ev[(g - 1, 6 + j)] if g >= 1 else None)
                    for t4 in range(4):
                        t_tp[(g, j)] = P.op("pe", lambda e, g=g, j=j, t4=t4: e.transpose(
                            out=tp[j % 2][:, t4 * 128:(t4 + 1) * 128], in_=xnb[g % 2][t4][:, j * 128:(j + 1) * 128], identity=ident),
                            waits=[("p1xn", t_xn[g * 4 + 3]), prev_ev, ("ld_cst", t_cst)],
                            sig=("p1tp" if t4 == 3 else None))
                    if j % 2 == 0:
                        t_ev[(g, j)] = ("p1ev", P.op("act", lambda e, g=g, j=j: e.activation(
                            out=hT[:, j, g * 512:(g + 1) * 512], in_=tp[j % 2], func=AF.Identity),
                            waits=[("p1tp", t_tp[(g, j)])], sig="p1ev"))
                    else:
                        t_ev[(g, j)] = ("p1evD", P.op("dve", lambda e, g=g, j=j: e.tensor_copy(
                            out=hT[:, j, g * 512:(g + 1) * 512], in_=tp[j % 2]),
                            waits=[("p1tp", t_tp[(g, j)])], sig="p1evD"))

            for k in range(NWB):
                issue_wada(k)
            for tt in range(NXB):
                issue_xload(tt)
            for k in range(8):
                for j in range(16):
                    P.op("pe", lambda e, k=k, j=j: e.matmul(modps[:, j:j + 1], lhsT=wada[k % NWB][:, j * 128:(j + 1) * 128],
                                                            rhs=cond[:, k:k + 1], start=(k == 0 and j == 0), stop=(k == 7),
                                                            skip_group_check=True),
                         waits=[("ld_wada%d" % (k % NWB), t_wada[k]), ("dve0", t_condb)])
                for eh in range(2):
                    t_pe0[k] = P.op("pe", lambda e, k=k, eh=eh: e.matmul(gateps[:, eh * 512:(eh + 1) * 512], lhsT=condb[:, k, :],
                                                                         rhs=wada[k % NWB][:, 2048 + eh * 512:2048 + (eh + 1) * 512],
                                                                         start=(k == 0), stop=(k == 7)),
                                    waits=[("ld_wada%d" % (k % NWB), t_wada[k]), ("dve0", t_condb)], sig=("pe0" if eh == 1 else None))
                if k + NWB < 8:
                    issue_wada(k + NWB)
                g = k
                for t4 in range(4):
                    tt = g * 4 + t4
                    t_sq[tt] = P.op("act", lambda e, tt=tt: e.activation(out=junk, in_=xt[tt % NXB], func=AF.Square,
                                                                         accum_out=ss[:, tt:tt + 1]),
                                    waits=[("ld_x%d" % (tt % NXB), t_xld[tt]), ("dve0", t_ms)], sig="p1sq")
                    t_r = P.op("act", lambda e, tt=tt: e.activation(out=rstd[:, tt:tt + 1], in_=ss[:, tt:tt + 1], func=AF.Sqrt,
                                                                    scale=1.0 / D, bias=eps_t[:, 0:1]),
                               waits=[("p1sq", t_sq[tt])], sig="p1sqrt")
                    t_r = P.op("dve", lambda e, tt=tt: e.reciprocal(out=rstd[:, tt:tt + 1], in_=rstd[:, tt:tt + 1]),
                               waits=[("p1sqrt", t_r)], sig="dve1")
                    t_xn[tt] = P.op("dve", lambda e, tt=tt, g=g, t4=t4: e.tensor_scalar(
                        out=xnb[g % 2][t4], in0=xt[tt % NXB], scalar1=rstd[:, tt:tt + 1], scalar2=None, op0=ALU.mult),
                        waits=[("dve1", t_r), ("p1tp", t_tp.get((g - 2, 7)))], sig="p1xn")
                    if tt + NXB < NT:
                        issue_xload(tt + NXB)
                if g >= 1:
                    p1_transpose_evac(g - 1)
            p1_transpose_evac(7)
            t_d = P.op("dve", lambda e: e.tensor_add(out=mod_sb[:, :], in0=modps, in1=bada[:, :]),
                       waits=[("pe0", t_pe0[7]), ("ld_small", t_small)], sig="dve0")
            t_d = P.op("dve", lambda e: e.scalar_tensor_tensor(out=gs[:, :], in0=mod_sb[:, 8:16], scalar=1.0, in1=normg[:, :],
                                                               op0=ALU.add, op1=ALU.mult),
                       waits=[("dve0", t_d)], sig="dve0")
            t_d = P.op("dve", lambda e: e.tensor_add(out=gate_bc[:, :], in0=gateps, in1=bg_bc), sig="dve0")
            t_mod = t_d
            for j in range(8):
                for hh in range(2):
                    t_fix = P.op("dve", lambda e, j=j, hh=hh: e.tensor_scalar(
                        out=hT[:, j, hh * 2048:(hh + 1) * 2048], in0=hT[:, j, hh * 2048:(hh + 1) * 2048],
                        scalar1=gs[:, j:j + 1], scalar2=mod_sb[:, j:j + 1], op0=ALU.mult, op1=ALU.add),
                        waits=[("dve0", t_mod), t_ev[(7, 6)], t_ev[(7, 7)]], sig="hfix")
            t_hT = t_fix
            ckpt(2)
            t_wsl = None
            t_projpe = None
            t_attpe = None
            t_attdve = None
            t_pev = {}
            nproj = 0
            t_esk = P.op("act", lambda e: e.activation(out=esk[:, :], in_=snk[:, :], func=AF.Exp),
                         waits=[("ld_small", t_small)], sig="act0")
            sbt = sb_tiles()
            sbi = 0
            tk = {}
            n_chunk = 0
            t_evy = {}
            swn = 0
            swg = 0
            t_sw = {}

            wv_all = w_in.rearrange("(k p) c -> p k c", p=128)
            GSv = [(qT, kT, sg, vv),
                   (yflat[:, 16384:20480], yflat[:, 20480:24576], yflat[:, 24576:28672],
                    yflat[:, 28672:32768].rearrange("p (b c) -> p b c", c=128))]
            wslb = [wsl1[:, :].rearrange("p (k c) -> p k c", c=512), wsl]
            ppb = [ps[:, 3072:3584], ps[:, 3584:4096]]
            pst = {"n": 0, "pev": {}, "projpe": {}, "lastpev": {}, "attdone": {}, "stmp": None}

            def sb_proj_gen(st):
                gq, gk, gsg, gv = GSv[st % 2]
                wb = wslb[st % 2]
                wcol = {"q": 0, "k": 128, "g": 256, "v": 384}
                if st < 4:
                    dl = [(128, 512 + st * 128, 128), (384, 1024 + st * 128, 128), (0, st * 128, 128), (256, 1536 + st * 128, 128)]
                else:
                    dl = [(128, 2560, 64), (192, 2560, 64), (384, 2688, 64), (448, 2688, 64), (0, 2048, 128), (256, 2816, 128)]
                t_w = None
                for (dc, sc, w) in dl:
                    t_w = P.op("pool", lambda e, wb=wb, dc=dc, sc=sc, w=w: e.dma_start(out=wb[:, :, dc:dc + w], in_=wv_all[:, :, sc:sc + w]),
                               waits=[pst["projpe"].get(st - 2)] + ([("hfix", t_hT)] if st > 0 else []), sig="ld_wb%d" % (st % 2), inc=16)
                t_w = ("ld_wb%d" % (st % 2), t_w)
                free_tok = pst["attdone"].get(st - 2)
                tpe = None
                for kind in ("k", "v", "q", "g"):
                    wc = wcol[kind]
                    if kind == "v":
                        for tg in range(8):
                            n = pst["n"]
                            par = n % 2
                            for t4 in range(4):
                                tok = (tg * 4 + t4) * 128
                                for kc in range(8):
                                    last = (kc == 7 and t4 == 3)
                                    tpe = P.op("pe", lambda e, par=par, kc=kc, tok=tok, t4=t4, wb=wb: e.matmul(
                                        ppb[par][:, t4 * 128:(t4 + 1) * 128], lhsT=hT[:, kc, tok:tok + 128], rhs=wb[:, kc, 384:512],
                                        start=(kc == 0), stop=(kc == 7)),
                                        waits=[t_w, pst["pev"].get(n - 2), ("hfix", t_hT)], sig=("pprojpe" if last else None))
                                    if (not last) and (kc == 3 or kc == 7):
                                        yield
                            tv = P.op("dve", lambda e, par=par, tg=tg, gv=gv: e.tensor_copy(
                                out=gv[:, tg * 4:(tg + 1) * 4, :], in_=ppb[par].rearrange("p (a b) -> p a b", a=4)),
                                waits=[("pprojpe", tpe), free_tok], sig="ppev")
                            pst["pev"][n] = ("ppev", tv)
                            pst["n"] += 1
                            yield
                        continue
                    for tc in range(8):
                        n = pst["n"]
                        par = n % 2
                        for kc in range(8):
                            tpe = P.op("pe", lambda e, par=par, kc=kc, wc=wc, tc=tc, wb=wb: e.matmul(
                                ppb[par], lhsT=wb[:, kc, wc:wc + 128], rhs=hT[:, kc, tc * 512:(tc + 1) * 512],
                                start=(kc == 0), stop=(kc == 7)),
                                waits=[t_w, pst["pev"].get(n - 2), ("hfix", t_hT)], sig=("pprojpe" if kc == 7 else None))
                            if kc < 7:
                                yield
                        csl = slice(tc * 512, (tc + 1) * 512)
                        if kind == "q":
                            tv = P.op("dve", lambda e, par=par, gq=gq, csl=csl: e.tensor_scalar(
                                out=gq[:, csl], in0=ppb[par], scalar1=0.125, scalar2=None, op0=ALU.mult),
                                waits=[("pprojpe", tpe), free_tok], sig="ppev")
                        elif kind == "k":
                            tv = P.op("dve", lambda e, par=par, gk=gk, csl=csl: e.tensor_copy(out=gk[:, csl], in_=ppb[par]),
                                      waits=[("pprojpe", tpe), free_tok], sig="ppev")
                        elif st == 0:
                            tv = P.op("act", lambda e, par=par, gsg=gsg, csl=csl: e.activation(out=gsg[:, csl], in_=ppb[par], func=AF.Silu),
                                      waits=[("pprojpe", tpe), free_tok], sig="ppevS")
                            pst["pev"][n] = ("ppevS", tv)
                            pst["lastact"] = ("ppevS", tv)
                            pst["n"] += 1
                            yield
                            continue
                        else:
                            yield
                            ta = P.op("act", lambda e, par=par: e.activation(out=stmp, in_=ppb[par], func=AF.Exp, scale=-1.0),
                                      waits=[("pprojpe", tpe), pst["stmp"]], sig="ppevA")
                            yield
                            for qq in range(4):
                                qs = slice(qq * 128, (qq + 1) * 128)
                                t1 = P.op("dve", lambda e, qs=qs: e.tensor_scalar_add(out=stmp[:, qs], in0=stmp[:, qs], scalar1=1.0),
                                          waits=[("ppevA", ta)], sig="psil")
                                t1 = P.op("dve", lambda e, qs=qs: e.reciprocal(out=stmp[:, qs], in_=stmp[:, qs]), waits=[("psil", t1)], sig="psil")
                                osl = slice(tc * 512 + qq * 128, tc * 512 + (qq + 1) * 128)
                                tv = P.op("dve", lambda e, par=par, gsg=gsg, osl=osl, qs=qs: e.tensor_mul(
                                    out=gsg[:, osl], in0=ppb[par][:, qs], in1=stmp[:, qs]),
                                    waits=[("psil", t1), free_tok], sig="ppev")
                                if qq < 3:
                                    yield
                            pst["stmp"] = ("ppev", tv)
                        pst["pev"][n] = ("ppev", tv)
                        pst["n"] += 1
                        yield
                pst["projpe"][st] = ("pprojpe", tpe)
                pst["lastpev"][st] = [pst["pev"][pst["n"] - 1], pst["pev"][pst["n"] - 9], pst.get("lastact")]

            def pump(gen, k):
                if gen is None:
                    return None
                for _ in range(k):
                    try:
                        next(gen)
                    except StopIteration:
                        return None
                return gen

            def drain(gen):
                while gen is not None:
                    gen = pump(gen, 64)

            gen_cur = sb_proj_gen(0)

            for step in range(8):
                is_sb = step < 4
                if is_sb:
                    cq, ck, cv, cg = step * 128, 512 + step * 128, 1024 + step * 128, 1536 + step * 128
                    kw = 128
                    vw = 128
                    do_kv = True
                else:
                    j = step - 4
                    cq, ck, cv, cg = 2048 + j * 128, 2560 + (j // 2) * 64, 2688 + (j // 2) * 64, 2816 + j * 128
                    kw = 64
                    vw = 64
                    do_kv = (j % 2 == 0)
                if step > 4:
                    wv = w_in.rearrange("(k p) c -> p k c", p=128)
                    dmas = [(0, cq, 128), (256, cg, 128)]
                    if do_kv:
                        if kw == 128:
                            dmas.append((128, ck, 128))
                        else:
                            dmas.append((128, ck, 64))
                            dmas.append((192, ck, 64))
                        if vw == 128:
                            dmas.append((384, cv, 128))
                        else:
                            dmas.append((384, cv, 64))
                            dmas.append((448, cv, 64))
                            vw = 128
                    for (dc, sc, w) in dmas:
                        t_wsl = P.op("pool", lambda e, dc=dc, sc=sc, w=w: e.dma_start(out=wsl[:, :, dc:dc + w], in_=wv[:, :, sc:sc + w]),
                                     waits=[("projpe", t_projpe), ("hfix", t_hT), pst["projpe"].get(3)],
                                     sig="ld_wsl", inc=16)
                    ckpt(2.5 + 2 * step)
                    kinds = [("q", 0), ("g", 256)] + ([("k", 128)] if do_kv else [])
                    for kind, wc in kinds:
                        for tc in range(8):
                            par = nproj % 2
                            for kc in range(8):
                                last = kc == 7
                                t_projpe_new = P.op("pe", lambda e, par=par, kc=kc, wc=wc, tc=tc: e.matmul(
                                    pp[par], lhsT=wsl[:, kc, wc:wc + 128], rhs=hT[:, kc, tc * 512:(tc + 1) * 512],
                                    start=(kc == 0), stop=(kc == 7)),
                                    waits=[("ld_wsl", t_wsl), t_pev.get(nproj - 2), ("hfix", t_hT),
                                           t_attdve],
                                    sig=("projpe" if last else None))
                            t_projpe = t_projpe_new
                            dst = {"q": qT, "k": kT, "g": sg}[kind][:, tc * 512:(tc + 1) * 512]
                            if kind == "q":
                                t_pev[nproj] = ("pev", P.op("dve", lambda e, dst=dst, par=par: e.tensor_scalar(
                                    out=dst, in0=pp[par], scalar1=0.125, scalar2=None, op0=ALU.mult),
                                    waits=[("projpe", t_projpe)], sig="pev"))
                                last_dve_pev = t_pev[nproj]
                            elif kind == "k":
                                t_pev[nproj] = ("pev", P.op("dve", lambda e, dst=dst, par=par: e.tensor_copy(out=dst, in_=pp[par]),
                                                    waits=[("projpe", t_projpe)], sig="pev"))
                                last_dve_pev = t_pev[nproj]
                            else:
                                t_pev[nproj] = ("pevA", P.op("act", lambda e, dst=dst, par=par: e.activation(out=dst, in_=pp[par], func=AF.Silu),
                                                             waits=[("projpe", t_projpe), t_attdve], sig="pevA"))
                                last_act_pev = t_pev[nproj]
                            nproj += 1
                    ckpt(2.75 + 2 * step)
                    if do_kv:
                        for tg in range(8):
                            par = nproj % 2
                            for t4 in range(4):
                                tok = (tg * 4 + t4) * 128
                                for kc in range(8):
                                    last = (kc == 7 and t4 == 3)
                                    t_projpe_new = P.op("pe", lambda e, par=par, kc=kc, tok=tok, t4=t4, vw=vw: e.matmul(
                                        pp[par][:, t4 * 128:t4 * 128 + vw], lhsT=hT[:, kc, tok:tok + 128], rhs=wsl[:, kc, 384:384 + vw],
                                        start=(kc == 0), stop=(kc == 7)),
                                        waits=[("ld_wsl", t_wsl), t_pev.get(nproj - 2), t_attdve],
                                        sig=("projpe" if last else None))
                            t_projpe = t_projpe_new
                            t_pev[nproj] = ("pev", P.op("dve", lambda e, par=par, tg=tg, vw=vw: e.tensor_copy(
                                out=vv[:, tg * 4:(tg + 1) * 4, 0:vw],
                                in_=pp[par].rearrange("p (a b) -> p a b", a=4)[:, :, 0:vw]),
                                waits=[("projpe", t_projpe)], sig="pev"))
                            last_dve_pev = t_pev[nproj]
                            nproj += 1
                    projw = [last_dve_pev, last_act_pev]
                else:
                    drain(gen_cur)
                    projw = list(pst["lastpev"][step])
                    gen_cur = sb_proj_gen(step + 1) if step + 1 < 5 else None
                    qT, kT, sg, vv = GSv[step % 2]
                ckpt(3 + 2 * step)

                if is_sb:
                    QCB = 512
                    tiles = []
                    for qc in range(S // QCB):
                        nkb = (QCB // 128) * (qc + 1)
                        for kb in range(nkb - 1, -1, -1):
                            jd = kb - (QCB // 128) * qc
                            diag = jd >= 0
                            c0 = 128 * jd if diag else 0
                            tiles.append((0, qc, kb, c0, diag, kb == nkb - 1, kb == 0))
                    T = len(tiles)
                    base = sbi
                    chunk_of = {}
                    cc = n_chunk - 1
                    for i, tl in enumerate(tiles):
                        if tl[5]:
                            cc += 1
                        chunk_of[i] = cc
                    Zh = [ps[:, 0:512], ps[:, 512:1024]]
                    Rh = [ps[:, 1024:1536], ps[:, 1536:2048]]
                    Yc = [ps[:, 2048:2560], ps[:, 2560:3072]]

                    def two(ap, c0):
                        if c0 == 0:
                            return ap
                        return ap.rearrange("p (h c) -> p h c", h=2)[:, :, c0:QCB]

                    def QK(i):
                        hp0, qc, kb, c0, diag, cs, ce = tiles[i]
                        g = base + i
                        for hh in range(2):
                            hp = hh * 64
                            last = (hh == 1) and not diag
                            v = P.op("pe", lambda e, hp=hp, hh=hh, kb=kb, qc=qc, c0=c0, diag=diag, kT=kT, qT=qT: e.matmul(
                                Zh[hh][:, c0:QCB], lhsT=kT[hp:hp + 64, kb * 128:(kb + 1) * 128],
                                rhs=qT[hp:hp + 64, qc * QCB + c0:(qc + 1) * QCB], start=True, stop=not diag,
                                skip_group_check=True),
                                waits=[("sbE", tk.get(("E", g - 1)))] + (projw if i < 2 else []),
                                sig=("sbQK" if last else None), attach=True)
                        if diag:
                            for hh in range(2):
                                v = P.op("pe", lambda e, hh=hh, c0=c0: e.matmul(
                                    Zh[hh][:, c0:c0 + 128], lhsT=ident, rhs=negm, start=False, stop=True, skip_group_check=True),
                                    sig=("sbQK" if hh == 1 else None))
                        tk[("QK", g)] = v

                    def ACT_E(i):
                        hp0, qc, kb, c0, diag, cs, ce = tiles[i]
                        g = base + i
                        tk[("E", g)] = P.op("act", lambda e, g=g, c0=c0: e.activation(out=two(Eb[g % 3], c0), in_=two(ps[:, 0:1024], c0), func=AF.Exp),
                                            waits=[("sbQK", tk[("QK", g)]), ("sbA", tk.get(("A", g - 3)))], sig="sbE")

                    def ACT_G(i):
                        hp0, qc, kb, c0, diag, cs, ce = tiles[i]
                        g = base + i
                        tk[("G", g)] = P.op("act", lambda e, g=g, c0=c0: e.activation(out=two(Gb[g % 2], c0), in_=two(Eb[g % 3], c0),
                                                                                   func=AF.Ln, bias=1.0),
                                            waits=[("sbE", tk[("E", g)]), ("sbTRIC", tk.get(("TRIC", g - 2)))], sig="sbG")

                    def ACT_P(i):
                        hp0, qc, kb, c0, diag, cs, ce = tiles[i]
                        g = base + i
                        tk[("P", g)] = P.op("act", lambda e, g=g, c0=c0: e.activation(out=two(Pb[g % 2], c0), in_=two(ps[:, 1024:2048], c0), func=AF.Exp),
                                            waits=[("sbTRI", tk[("TRI", g)]), ("sbA", tk.get(("A", g - 2)))], sig="sbP")

                    def PE_TRI(i):
                        hp0, qc, kb, c0, diag, cs, ce = tiles[i]
                        g = base + i
                        if cs:
                            par = chunk_of[i] % 2
                            for hh in range(2):
                                P.op("pe", lambda e, hh=hh: e.matmul(Rh[hh], lhsT=zer, rhs=cst[:, 0:512], start=True, stop=False,
                                                                     skip_group_check=True),
                                     waits=[("sbP", tk.get(("P", g - 1)))])
                            P.op("pe", lambda e, par=par: e.matmul(Yc[par], lhsT=zer, rhs=cst[:, 0:512], start=True, stop=False,
                                                                   skip_group_check=True),
                                 waits=[("sbEV", t_evy.get(chunk_of[i] - 2))])
                        for hh in range(2):
                            v = P.op("pe", lambda e, g=g, hh=hh, c0=c0: e.matmul(
                                Rh[hh][:, c0:QCB], lhsT=triN, rhs=Gb[g % 2][:, hh * QCB + c0:(hh + 1) * QCB], start=False, stop=False,
                                skip_group_check=True),
                                waits=[("sbG", tk[("G", g)]), ("sbP", tk.get(("P", g - 1)))],
                                sig=("sbTRI" if hh == 1 else None), attach=True)
                        tk[("TRI", g)] = v

                    def PE_TRIC(i):
                        hp0, qc, kb, c0, diag, cs, ce = tiles[i]
                        g = base + i
                        for hh in range(2):
                            v = P.op("pe", lambda e, g=g, hh=hh, c0=c0: e.matmul(
                                Rh[hh][:, c0:QCB], lhsT=tricN, rhs=Gb[g % 2][:, hh * QCB + c0:(hh + 1) * QCB], start=False, stop=False,
                                skip_group_check=True),
                                waits=[("sbP", tk[("P", g)])],
                                sig=("sbTRIC" if hh == 1 else None), attach=True)
                        tk[("TRIC", g)] = v

                    def PE_PV(i):
                        hp0, qc, kb, c0, diag, cs, ce = tiles[i]
                        g = base + i
                        par = chunk_of[i] % 2
                        for hh in range(2):
                            hp = hh * 64
                            v = P.op("pe", lambda e, g=g, hh=hh, hp=hp, kb=kb, par=par, c0=c0, vv=vv: e.matmul(
                                Yc[par][hp:hp + 64, c0:QCB], lhsT=vv[:, kb, hp:hp + 64], rhs=Ab[g % 2][:, hh * QCB + c0:(hh + 1) * QCB],
                                start=False, stop=False, skip_group_check=True),
                                waits=[("sbA", tk[("A", g)])],
                                sig=("sbPV" if hh == 1 else None), attach=True)
                        tk[("PV", g)] = v

                    def DVE_A(i):
                        hp0, qc, kb, c0, diag, cs, ce = tiles[i]
                        g = base + i
                        tk[("A", g)] = P.op("dve", lambda e, g=g, c0=c0: e.tensor_mul(out=two(Ab[g % 2], c0), in0=two(Eb[g % 3], c0),
                                                                                   in1=two(Pb[g % 2], c0)),
                                            waits=[("sbP", tk[("P", g)]), ("sbPV", tk.get(("PV", g - 2)))], sig="sbA")

                    def DVE_EV(i):
                        hp0, qc, kb, c0, diag, cs, ce = tiles[i]
                        g = base + i
                        ch = chunk_of[i]
                        par = ch % 2
                        t_evy[ch] = P.op("dve", lambda e, qc=qc, par=par, step=step, sg=sg: e.tensor_mul(
                            out=ygT[:, step, qc * QCB:(qc + 1) * QCB], in0=Yc[par], in1=sg[:, qc * QCB:(qc + 1) * QCB]),
                            waits=[("sbPV", tk[("PV", g)])], sig="sbEV")

                    QK(0)
                    ACT_E(0)
                    if T > 1:
                        QK(1)
                    ACT_G(0)
                    if T > 1:
                        ACT_E(1)
                    if T > 2:
                        QK(2)
                    PE_TRI(0)
                    for i in range(T):
                        ACT_P(i)
                        PE_TRIC(i)
                        gen_cur = pump(gen_cur, PUMP_A)
                        if i + 1 < T:
                            ACT_G(i + 1)
                            PE_TRI(i + 1)
                        DVE_A(i)
                        PE_PV(i)
                        if tiles[i][6]:
                            DVE_EV(i)
                        if i + 2 < T:
                            ACT_E(i + 2)
                        if i + 3 < T:
                            QK(i + 3)
                        gen_cur = pump(gen_cur, PUMP_B + (1 if i % 6 == 5 else 0))
                    sbi += T
                    n_chunk = cc + 1
                    t_attdve = ("sbEV", t_evy[cc])
                    pst["attdone"][step] = t_attdve
                    qT, kT, sg, vv = GSv[0]
                else:
                    j = step - 4
                    heads = (2 * j, 2 * j + 1)
                    swn0 = swn
                    att_prev = t_attdve

                    def SW_QK(n):
                        gi = swn0 + n
                        zbase = (gi % 2) * 1024
                        wz = 256 if n > 0 else 128
                        for which, kblk in ((0, n), (1, n - 1)):
                            if kblk < 0:
                                continue
                            for hh in range(2):
                                hp = hh * 64
                                zc = zbase + hh * 512 + which * 128
                                P.op("pe", lambda e, zc=zc, hp=hp, kblk=kblk, n=n, which=which: e.matmul(
                                    ps[:, zc:zc + 128], lhsT=kT[hp:hp + 64, kblk * 128:(kblk + 1) * 128],
                                    rhs=qT[hp:hp + 64, n * 128:(n + 1) * 128], start=(which == 0), stop=False, skip_group_check=True),
                                    waits=[("swP", t_sw.get(("P", gi - 2)))] + (projw if n < 2 else []))
                        for hh in range(2):
                            h = heads[hh]
                            bc = C_SWB + h * 256
                            zc = zbase + hh * 512
                            v = P.op("pe", lambda e, zc=zc, bc=bc, wz=wz: e.matmul(
                                ps[:, zc:zc + wz], lhsT=ident, rhs=cst[:, bc:bc + wz], start=False, stop=True, skip_group_check=True),
                                sig=("swQK" if hh == 1 else None))
                        t_sw[("QK", gi)] = v

                    def SW_ACT(n):
                        gi = swn0 + n
                        zbase = (gi % 2) * 1024
                        wz = 256 if n > 0 else 128
                        zin = ps[:, zbase:zbase + 1024].rearrange("p (b c) -> p b c", b=2)[:, :, 0:wz]
                        pout = Psw[gi % 2].rearrange("p (b c) -> p b c", b=2)[:, :, 0:wz]
                        t_sw[("P", gi)] = P.op("act", lambda e, zin=zin, pout=pout: e.activation(out=pout, in_=zin, func=AF.Exp),
                                               waits=[("swQK", t_sw[("QK", gi)]), ("swPV", t_sw.get(("PV", gi - 2)))], sig="swP")

                    def SW_PVD(n):
                        gi = swn0 + n
                        grp = swg + n // 4
                        par = grp % 2
                        Yb = Yp[par][:, 0:512]
                        Db = Yp[par][:, 512:1024]
                        col = (n % 4) * 128
                        for hh in range(2):
                            hp = hh * 64
                            srcs = [(hh * 256, n)] + ([(hh * 256 + 128, n - 1)] if n > 0 else [])
                            ns = len(srcs)
                            for si, (pc, kblk) in enumerate(srcs):
                                P.op("pe", lambda e, Yb=Yb, hp=hp, col=col, kblk=kblk, gi=gi, pc=pc, si=si, ns=ns: e.matmul(
                                    Yb[hp:hp + 64, col:col + 128], lhsT=vv[:, kblk, 0:64], rhs=Psw[gi % 2][:, pc:pc + 128],
                                    start=(si == 0), stop=(si == ns - 1), skip_group_check=True),
                                    waits=[("swP", t_sw[("P", gi)]), ("swEV", t_sw.get(("EV", grp - 2)))] + ([att_prev] if n < 8 else []))
                            for si, (pc, kblk) in enumerate(srcs):
                                v = P.op("pe", lambda e, Db=Db, hp=hp, col=col, gi=gi, pc=pc, si=si, ns=ns: e.matmul(
                                    Db[hp:hp + 64, col:col + 128], lhsT=ones_bf[:, 0:64], rhs=Psw[gi % 2][:, pc:pc + 128],
                                    start=(si == 0), stop=(si == ns - 1), skip_group_check=True),
                                    sig=("swPV" if (hh == 1 and si == ns - 1) else None))
                        t_sw[("PV", gi)] = v
                    def SW_EV(n):
                        gi = swn0 + n
                        grp = swg + n // 4
                        par = grp % 2
                        Yb = Yp[par][:, 0:512]
                        Db = Yp[par][:, 512:1024]
                        if n % 4 == 3:
                            q0 = (n - 3) * 128
                            t1 = P.op("act", lambda e, Db=Db, j=j: e.activation(out=rden, in_=Db, func=AF.Ln, bias=esk[:, j:j + 1]),
                                      waits=[("swPV", t_sw[("PV", gi)]), ("act0", t_esk), ("swEV", t_sw.get(("EV", grp - 1)))], sig="swDa")
                            t1 = P.op("act", lambda e: e.activation(out=rden, in_=rden, func=AF.Exp, scale=-1.0),
                                      waits=[("swDa", t1)], sig="swDa")
                            t1 = P.op("dve", lambda e, Yb=Yb: e.tensor_mul(out=ytmp, in0=Yb, in1=rden), waits=[("swDa", t1)], sig="swD")
                            t_sw[("EV", grp)] = P.op("dve", lambda e, q0=q0, step=step: e.tensor_mul(
                                out=ygT[:, step, q0:q0 + 512], in0=ytmp, in1=sg[:, q0:q0 + 512]),
                                waits=[("swD", t1)], sig="swEV")

                    SW_QK(0)
                    for n in range(NT):
                        SW_ACT(n)
                        if n >= 1 and (n - 1) % 4 == 3:
                            SW_EV(n - 1)
                        if n + 1 < NT:
                            SW_QK(n + 1)
                        SW_PVD(n)
                    SW_EV(NT - 1)
                    swn += NT
                    swg += NT // 4
                    t_last_sw = t_sw[("EV", swg - 1)]
                if not is_sb:
                    t_attdve = ("swEV", t_last_sw)
                ckpt(4 + 2 * step)

            ckpt(20)
            wo_v = w_out.rearrange("(k p) e -> p k e", p=128)
            t_wo = None
            for k in range(8):
                t_wo = P.op("pool", lambda e, k=k: e.dma_start(out=wout[:, k, :], in_=wo_v[:, k, :]),
                            waits=[("projpe", t_projpe)], sig="ld_wo", inc=16)
            t_wog = None
            for k in range(8):
                t_wog = P.op("pool", lambda e, k=k: e.tensor_tensor(out=wout[:, k, :], in0=wout[:, k, :], in1=gate_bc[:, :], op=ALU.mult),
                             waits=[("ld_wo", t_wo), ("dve0", t_mod)], sig="wog")
            t_fg = P.op("sp", lambda e: e.dma_start(out=fg_bc, in_=fg_bc_d[:, :]), waits=[("projpe", t_projpe)], sig="ld_fg", inc=16)
            t_xf = {}
            t_o = {}
            t_st = {}
            t_r2 = {}
            t_sq2 = {}

            def issue_xf(tt):
                t_xf[tt] = P.op("sp", lambda e, tt=tt: e.dma_start(out=xf[tt % 3], in_=x[tt * 128:(tt + 1) * 128, :]),
                                waits=[("fr2", t_r2.get(tt - 3)), ("projpe", t_projpe)], sig="ld_xf%d" % (tt % 3), inc=16)

            t_po = {}
            for tt in range(3):
                issue_xf(tt)
            def F_PE(tt):
                for eh in range(2):
                    for kc in range(8):
                        v = P.op("pe", lambda e, tt=tt, eh=eh, kc=kc: e.matmul(
                            po[tt % 2][eh], lhsT=ygT[:, kc, tt * 128:(tt + 1) * 128], rhs=wout[:, kc, eh * 512:(eh + 1) * 512],
                            start=(kc == 0), stop=(kc == 7)),
                            waits=[("wog", t_wog), t_attdve, ("fr2", t_r2.get(tt - 2))],
                            sig=("fpo" if kc == 7 else None))
                    t_po[(tt, eh)] = v

            def F_A(tt):
                rb = rf[tt % 2]
                for eh in range(2):
                    t1 = P.op("dve", lambda e, tt=tt, eh=eh, rb=rb: e.tensor_add(out=rb[:, eh * 512:(eh + 1) * 512], in0=po[tt % 2][eh],
                                                                               in1=xf[tt % 3][:, eh * 512:(eh + 1) * 512]),
                              waits=[("fpo", t_po[(tt, eh)]), ("fo", t_o.get(tt - 2)), ("fsq", t_sq2.get(tt - 2)),
                                     ("ld_xf%d" % (tt % 3), t_xf[tt])], sig="fr2")
                t_r2[tt] = t1
                if tt + 3 < NT:
                    issue_xf(tt + 3)
                t_sq2[tt] = P.op("act", lambda e, tt=tt, rb=rb: e.activation(out=junkf, in_=rb, func=AF.Square, accum_out=ss2[:, tt:tt + 1]),
                                 waits=[("fr2", t_r2[tt])], sig="fsq")
                t_sqrt2[tt] = P.op("act", lambda e, tt=tt: e.activation(out=rstd2[:, tt:tt + 1], in_=ss2[:, tt:tt + 1], func=AF.Sqrt,
                                                                        scale=1.0 / D, bias=eps_t[:, 0:1]),
                                   waits=[("fsq", t_sq2[tt])], sig="fsqrt")

            def F_B(tt):
                rb = rf[tt % 2]
                t1 = P.op("dve", lambda e, tt=tt: e.reciprocal(out=rstd2[:, tt:tt + 1], in_=rstd2[:, tt:tt + 1]),
                          waits=[("fsqrt", t_sqrt2[tt])], sig="fd")
                t_o[tt] = P.op("dve", lambda e, tt=tt, rb=rb: e.scalar_tensor_tensor(
                    out=of[tt % 2], in0=rb, scalar=rstd2[:, tt:tt + 1], in1=fg_bc, op0=ALU.mult, op1=ALU.mult),
                    waits=[("fd", t1), ("ld_fg", t_fg), ("ld_out%d" % (tt % 2), t_st.get(tt - 2))], sig="fo")
                t_st[tt] = P.op("sp", lambda e, tt=tt: e.dma_start(out=out[tt * 128:(tt + 1) * 128, :], in_=of[tt % 2]),
                                waits=[("fo", t_o[tt])], sig="ld_out%d" % (tt % 2), inc=16)

            t_sqrt2 = {}
            F_PE(0)
            F_PE(1)
            F_A(0)
            for tt in range(NT):
                if tt + 2 < NT:
                    F_PE(tt + 2)
                if tt + 1 < NT:
                    F_A(tt + 1)
                F_B(tt)
            P.op("sp", lambda e: e.nop(), waits=[("ld_out0", t_st[NT - 2]), ("ld_out1", t_st[NT - 1])])

        try:
            plan_all()
        except _Stop:
            pass

        names = sorted(P.cnt.keys())
        sems = {n: es.enter_context(nc.semaphore(n)) for n in names}
        block = es.enter_context(nc.Block())

        def emit(eng, oplist):
            seen = {}
            for fn, waits, sig, inc, attach in oplist:
                pend = [(name, val) for (name, val) in waits if seen.get(name, 0) < val]
                if attach and len(pend) == 1:
                    (name, val) = pend[0]
                    ins = fn(eng)
                    ins._wait_ge(sems[name], val)
                    seen[name] = val
                else:
                    for (name, val) in pend:
                        eng.wait_ge(sems[name], val)
                        seen[name] = val
                    ins = fn(eng)
                if sig is not None:
                    ins.then_inc(sems[sig], inc)

        @block.sync
        def _(eng):
            emit(eng, P.ops["sp"])

        @block.gpsimd
        def _(eng):
            emit(eng, P.ops["pool"])

        @block.tensor
        def _(eng):
            emit(eng, P.ops["pe"])

        @block.scalar
        def _(eng):
            emit(eng, P.ops["act"])

        @block.vector
        def _(eng):
            emit(eng, P.ops["dve"])
    return nc


_CACHE = {}


def kernel(x, c, w_ada, b_ada, norm_g, w_in, sinks, w_out, final_g):
    x = np.asarray(x, np.float32)
    c = np.asarray(c, np.float32)
    w_ada = np.ascontiguousarray(np.asarray(w_ada, np.float32)[0])
    b_ada = np.asarray(b_ada, np.float32)[0]
    norm_g = np.asarray(norm_g, np.float32)[0]
    w_in = np.ascontiguousarray(np.asarray(w_in, np.float32)[0])
    sinks = np.asarray(sinks, np.float32)[0]
    w_out = np.ascontiguousarray(np.asarray(w_out, np.float32)[0])
    final_g = np.asarray(final_g, np.float32)

    def lay(v):
        return np.ascontiguousarray(v.reshape(-1, 128).T)

    bada_l = lay(b_ada[:2048])
    bg_bc = np.ascontiguousarray(np.broadcast_to(b_ada[2048:3072][None, :], (128, D)))
    normg_l = lay(norm_g)
    fg_bc = np.ascontiguousarray(np.broadcast_to(final_g[None, :], (128, D)))
    sinks_l = np.ascontiguousarray(np.stack([np.repeat(sinks[2 * j:2 * j + 2], 64) for j in range(4)], axis=1))
    consts = make_consts()
    if "nc" not in _CACHE:
        _CACHE["nc"] = build_nc()
    nc = _CACHE["nc"]
    in_maps = []
    for b in range(NCORE):
        in_maps.append({
            "x": np.ascontiguousarray(x[b]), "c_l": lay(c[b]), "w_ada": w_ada, "bada_l": bada_l, "bg_bc": bg_bc,
            "normg_l": normg_l, "w_in": w_in, "sinks_l": sinks_l, "w_out": w_out, "fg_bc": fg_bc, "consts": consts,
        })
    res = run_bass_kernel_spmd(nc, in_maps, core_ids=list(range(NCORE)))
    return np.stack([np.asarray(r["out"], np.float32) for r in res.results], axis=0)
```

```python
from contextlib import ExitStack

import numpy as np
import concourse.bass as bass
import concourse.mybir as mybir
from concourse.bass_utils import run_bass_kernel_spmd

F32 = mybir.dt.float32
BF16 = mybir.dt.bfloat16
AF = mybir.ActivationFunctionType
ALU = mybir.AluOpType

S = 4096
D = 1024
NCORE = 8
NT = S // 128
QC = 1024
NEG = -30000.0
NCONST = 128 * 6 + 2048
C_ID, C_TRI, C_TRIC, C_ZERO, C_NEGM, C_ONES, C_SWB = 0, 128, 256, 384, 512, 640, 768


LEVEL = 99
SW_DBG = 0
PUMP = 2
PUMP_A = 1
PUMP_B = 1
ATTACH = True


class _Stop(Exception):
    pass


def ckpt(level):
    if LEVEL <= level:
        raise _Stop()


class Plan:
    def __init__(self):
        self.ops = {"pe": [], "act": [], "dve": [], "pool": [], "sp": []}
        self.cnt = {}

    def op(self, eng, fn, waits=(), sig=None, inc=1, attach=False):
        v = None
        if sig is not None:
            self.cnt[sig] = self.cnt.get(sig, 0) + inc
            v = self.cnt[sig]
        ws = tuple(w for w in waits if w is not None and w[1] is not None and w[1] > 0)
        self.ops[eng].append((fn, ws, sig, inc, attach and ATTACH))
        return v


def make_consts():
    c = np.zeros((128, NCONST), np.float32)
    j = np.arange(128)[:, None]
    s = np.arange(128)[None, :]
    c[:, C_ID:C_ID + 128] = (j == s)
    c[:, C_TRI:C_TRI + 128] = -1.0 * (j >= s)
    c[:, C_TRIC:C_TRIC + 128] = -1.0 * (j < s)
    c[:, C_NEGM:C_NEGM + 128] = np.where(j < s, 0.0, NEG)
    c[:, C_ONES:C_ONES + 128] = 1.0
    for h in range(8):
        m = 2.0 ** (-8.0 * (h + 1) / 8)
        rel_cur = (s - j).astype(np.float32)
        cur = np.where(s >= j, -m * rel_cur, NEG)
        rel_prev = (128 + s - j).astype(np.float32)
        prev = np.where(j > s, -m * rel_prev, NEG)
        c[:, C_SWB + h * 256:C_SWB + h * 256 + 128] = cur
        c[:, C_SWB + h * 256 + 128:C_SWB + h * 256 + 256] = prev
    return c


def sb_tiles():
    out = []
    for qc in range(S // QC):
        nkb = (QC // 128) * (qc + 1)
        for kb in range(nkb - 1, -1, -1):
            jd = kb - (QC // 128) * qc
            diag = jd >= 0
            c0 = 128 * jd if diag else 0
            out.append((qc, kb, c0, diag, kb == nkb - 1, kb == 0))
    return out


def col_segs(c0, c1=QC):
    segs = []
    a = c0
    while a < c1:
        b = min(c1, (a // 512 + 1) * 512)
        segs.append((a, b))
        a = b
    return segs


def build_nc():
    nc = bass.Bass("TRN2", target_bir_lowering=False)
    x = nc.dram_tensor("x", [S, D], F32, kind="ExternalInput").ap()
    c_l = nc.dram_tensor("c_l", [128, 8], F32, kind="ExternalInput").ap()
    w_ada = nc.dram_tensor("w_ada", [D, 3 * D], F32, kind="ExternalInput").ap()
    bada_l = nc.dram_tensor("bada_l", [128, 16], F32, kind="ExternalInput").ap()
    bg_bc_d = nc.dram_tensor("bg_bc", [128, D], F32, kind="ExternalInput").ap()
    normg_l = nc.dram_tensor("normg_l", [128, 8], F32, kind="ExternalInput").ap()
    w_in = nc.dram_tensor("w_in", [D, 3328], F32, kind="ExternalInput").ap()
    sinks_l = nc.dram_tensor("sinks_l", [128, 4], F32, kind="ExternalInput").ap()
    w_out = nc.dram_tensor("w_out", [D, D], F32, kind="ExternalInput").ap()
    fg_bc_d = nc.dram_tensor("fg_bc", [128, D], F32, kind="ExternalInput").ap()
    consts_d = nc.dram_tensor("consts", [128, NCONST], F32, kind="ExternalInput").ap()
    out = nc.dram_tensor("out", [S, D], F32, kind="ExternalOutput").ap()

    P = Plan()
    es = ExitStack()
    with es:
        def sb(name, shape, dt):
            return es.enter_context(nc.sbuf_tensor(name, shape, dt))

        ygT = sb("ygT", [128, 8, S], BF16)
        hT = sb("hT", [128, 8, S], BF16)
        ov = sb("ov", [128, 30720], BF16)
        cst = sb("cst", [128, NCONST], BF16)
        gate_bc = sb("gate_bc", [128, D], F32)
        c_sb = sb("c_sb", [128, 8], F32)
        etmp = sb("etmp", [128, 8], F32)
        cond = sb("cond", [128, 8], F32)
        bada = sb("bada", [128, 16], F32)
        normg = sb("normg", [128, 8], F32)
        mod_sb = sb("mod_sb", [128, 16], F32)
        gs = sb("gs", [128, 8], F32)
        ones_f = sb("ones_f", [128, 128], F32)
        ss = sb("ss", [128, 32], F32)
        rstd = sb("rstd", [128, 32], F32)
        ss2 = sb("ss2", [128, 32], F32)
        rstd2 = sb("rstd2", [128, 32], F32)
        snk = sb("snk", [128, 4], F32)
        esk = sb("esk", [128, 4], F32)
        eps_t = sb("eps_t", [128, 1], F32)
        wsl1 = sb("wsl1", [128, 4096], BF16)
        ps = es.enter_context(nc.psum_tensor("ps", [128, 4096], F32))

        qT = ov[:, 0:4096]
        kT = ov[:, 4096:8192]
        sg = ov[:, 8192:12288]
        vv = ov[:, 12288:16384].rearrange("p (b c) -> p b c", c=128)
        wsl = ov[:, 16384:20480].rearrange("p (k c) -> p k c", c=512)
        PB = 20480
        Eb = [ov[:, PB + i * 1024:PB + (i + 1) * 1024] for i in range(3)]
        Gb = [ov[:, PB + (3 + i) * 1024:PB + (4 + i) * 1024] for i in range(2)]
        stmp = ov[:, PB + 5 * 1024:PB + 6 * 1024].bitcast(F32)
        Pb = [ov[:, PB + (6 + i) * 1024:PB + (7 + i) * 1024] for i in range(2)]
        Ab = [ov[:, PB + (8 + i) * 1024:PB + (9 + i) * 1024] for i in range(2)]
        Psw = [ov[:, PB + i * 512:PB + (i + 1) * 512] for i in range(2)]
        rden = ov[:, PB + 1024:PB + 2048].bitcast(F32)
        ytmp = ov[:, PB + 2048:PB + 3072].bitcast(F32)
        yflat = ygT[:, :, :].rearrange("p a b -> p (a b)")
        wada = [yflat[:, i * 6144:(i + 1) * 6144].bitcast(F32) for i in range(4)]
        xt = ([ov[:, 12288 + i * 2048:12288 + (i + 1) * 2048].bitcast(F32) for i in range(4)]
              + [ov[:, i * 2048:(i + 1) * 2048].bitcast(F32) for i in range(4)])
        xnb = [[ov[:, 20480 + (g * 4 + t) * 1024:20480 + (g * 4 + t + 1) * 1024] for t in range(4)] for g in range(2)]
        junk = ov[:, 28672:29696]
        hflat = hT[:, :, :].rearrange("p a b -> p (a b)")
        condb = yflat[:, 24576:26624].bitcast(F32).rearrange("p (k m) -> p k m", m=128)
        bg_bc = yflat[:, 26624:28672].bitcast(F32)
        wout = hflat[:, 0:8192].rearrange("p (k e) -> p k e", e=1024)
        xf = [hflat[:, 8192 + i * 2048:8192 + (i + 1) * 2048].bitcast(F32) for i in range(3)]
        rf = [hflat[:, 14336 + i * 2048:14336 + (i + 1) * 2048].bitcast(F32) for i in range(2)]
        of = [hflat[:, 18432 + i * 2048:18432 + (i + 1) * 2048].bitcast(F32) for i in range(2)]
        fg_bc = hflat[:, 22528:24576].bitcast(F32)
        junkf = hflat[:, 24576:25600]
        Zp = ps[:, 0:1024]
        Rp = ps[:, 1024:2048]
        Yp = [ps[:, 2048:3072], ps[:, 3072:4096]]
        pp = [ps[:, 0:512], ps[:, 512:1024]]
        tp = [ps[:, i * 512:(i + 1) * 512].bitcast(BF16)[:, 0:512] for i in range(2)]
        modps = ps[:, 1024:1040]
        gateps = ps[:, 2048:3072]
        po = [[ps[:, t * 1024 + e * 512:t * 1024 + (e + 1) * 512] for e in range(2)] for t in range(2)]

        ident = cst[:, C_ID:C_ID + 128]
        triN = cst[:, C_TRI:C_TRI + 128]
        tricN = cst[:, C_TRIC:C_TRIC + 128]
        zer = cst[:, C_ZERO:C_ZERO + 128]
        negm = cst[:, C_NEGM:C_NEGM + 128]
        ones_bf = cst[:, C_ONES:C_ONES + 128]

        def plan_all():
            nonlocal qT, kT, sg, vv
            t_cst = P.op("pool", lambda e: e.dma_start(out=cst[:, :], in_=consts_d[:, :]), sig="ld_cst", inc=16)
            for dst, src in ((c_sb, c_l), (bada, bada_l), (normg, normg_l), (snk, sinks_l)):
                t_small = P.op("sp", lambda e, dst=dst, src=src: e.dma_start(out=dst[:, :], in_=src[:, :]), sig="ld_small", inc=16)
            t_small = P.op("sp", lambda e: e.dma_start(out=bg_bc, in_=bg_bc_d[:, :]), sig="ld_small", inc=16)

            ckpt(0)
            P.op("dve", lambda e: e.memset(ones_f[:, :], 1.0), sig="dve0")
            P.op("dve", lambda e: e.memset(eps_t[:, :], 1e-6), sig="dve0")
            P.op("dve", lambda e: e.memset(ss[:, :], 0.0), sig="dve0")
            t_d = P.op("dve", lambda e: e.memset(ss2[:, :], 0.0), sig="dve0")
            t_ms = t_d
            t_a = P.op("act", lambda e: e.activation(out=etmp[:, :], in_=c_sb[:, :], func=AF.Exp, scale=-1.0),
                       waits=[("ld_small", t_small)], sig="act0")
            t_d = P.op("dve", lambda e: e.tensor_scalar_add(out=etmp[:, :], in0=etmp[:, :], scalar1=1.0),
                       waits=[("act0", t_a), ("dve0", t_d)], sig="dve0")
            t_d = P.op("dve", lambda e: e.reciprocal(out=etmp[:, :], in_=etmp[:, :]), waits=[("dve0", t_d)], sig="dve0")
            t_d = P.op("dve", lambda e: e.tensor_mul(out=cond[:, :], in0=c_sb[:, :], in1=etmp[:, :]),
                       waits=[("dve0", t_d)], sig="dve0")
            t_cond = t_d
            for k in range(8):
                t_d = P.op("dve", lambda e, k=k: e.tensor_scalar(out=condb[:, k, :], in0=ones_f[:, :], scalar1=cond[:, k:k + 1],
                                                                 scalar2=None, op0=ALU.mult),
                           waits=[("dve0", t_cond)], sig="dve0")
            t_condb = t_d
            t_pe0 = {}
            t_wada = {}
            t_xld = {}
            t_sq = {}
            t_xn = {}
            t_tp = {}
            t_ev = {}
            NWB = 4
            NXB = 8

            def issue_wada(k):
                t_wada[k] = P.op("sp", lambda e, k=k: e.dma_start(out=wada[k % NWB], in_=w_ada[k * 128:(k + 1) * 128, :]),
                                 waits=[("pe0", t_pe0.get(k - NWB))], sig="ld_wada%d" % (k % NWB), inc=16)

            def issue_xload(tt):
                t_xld[tt] = P.op("sp", lambda e, tt=tt: e.dma_start(out=xt[tt % NXB], in_=x[tt * 128:(tt + 1) * 128, :]),
                                 waits=[("p1xn", t_xn.get(tt - NXB)), ("p1sq", t_sq.get(tt - NXB))],
                                 sig="ld_x%d" % (tt % NXB), inc=16)

            def p1_transpose_evac(g):
                for j in range(8):
                    prev_ev = t_ev[(g, j - 2)] if j >= 2 else (t_ev[(g - 1, 6 + j)] if g >= 1 else None)
                    for t4 in range(4):
                        t_tp[(g, j)] = P.op("pe", lambda e, g=g, j=j, t4=t4: e.transpose(
                            out=tp[j % 2][:, t4 * 128:(t4 + 1) * 128], in_=xnb[g % 2][t4][:, j * 128:(j + 1) * 128], identity=ident),
                            waits=[("p1xn", t_xn[g * 4 + 3]), prev_ev, ("ld_cst", t_cst)],
                            sig=("p1tp" if t4 == 3 else None))
                    if j % 2 == 0:
                        t_ev[(g, j)] = ("p1ev", P.op("act", lambda e, g=g, j=j: e.activation(
                            out=hT[:, j, g * 512:(g + 1) * 512], in_=tp[j % 2], func=AF.Identity),
                            waits=[("p1tp", t_tp[(g, j)])], sig="p1ev"))
                    else:
                        t_ev[(g, j)] = ("p1evD", P.op("dve", lambda e, g=g, j=j: e.tensor_copy(
                            out=hT[:, j, g * 512:(g + 1) * 512], in_=tp[j % 2]),
                            waits=[("p1tp", t_tp[(g, j)])], sig="p1evD"))

            for k in range(NWB):
                issue_wada(k)
            for tt in range(NXB):
                issue_xload(tt)
            for k in range(8):
                for j in range(16):
                    P.op("pe", lambda e, k=k, j=j: e.matmul(modps[:, j:j + 1], lhsT=wada[k % NWB][:, j * 128:(j + 1) * 128],
                                                            rhs=cond[:, k:k + 1], start=(k == 0 and j == 0), stop=(k == 7),
                                                            skip_group_check=True),
                         waits=[("ld_wada%d" % (k % NWB), t_wada[k]), ("dve0", t_condb)])
                for eh in range(2):
                    t_pe0[k] = P.op("pe", lambda e, k=k, eh=eh: e.matmul(gateps[:, eh * 512:(eh + 1) * 512], lhsT=condb[:, k, :],
                                                                         rhs=wada[k % NWB][:, 2048 + eh * 512:2048 + (eh + 1) * 512],
                                                                         start=(k == 0), stop=(k == 7)),
                                    waits=[("ld_wada%d" % (k % NWB), t_wada[k]), ("dve0", t_condb)], sig=("pe0" if eh == 1 else None))
                if k + NWB < 8:
                    issue_wada(k + NWB)
                g = k
                for t4 in range(4):
                    tt = g * 4 + t4
                    t_sq[tt] = P.op("act", lambda e, tt=tt: e.activation(out=junk, in_=xt[tt % NXB], func=AF.Square,
                                                                         accum_out=ss[:, tt:tt + 1]),
                                    waits=[("ld_x%d" % (tt % NXB), t_xld[tt]), ("dve0", t_ms)], sig="p1sq")
                    t_r = P.op("act", lambda e, tt=tt: e.activation(out=rstd[:, tt:tt + 1], in_=ss[:, tt:tt + 1], func=AF.Sqrt,
                                                                    scale=1.0 / D, bias=eps_t[:, 0:1]),
                               waits=[("p1sq", t_sq[tt])], sig="p1sqrt")
                    t_r = P.op("dve", lambda e, tt=tt: e.reciprocal(out=rstd[:, tt:tt + 1], in_=rstd[:, tt:tt + 1]),
                               waits=[("p1sqrt", t_r)], sig="dve1")
                    t_xn[tt] = P.op("dve", lambda e, tt=tt, g=g, t4=t4: e.tensor_scalar(
                        out=xnb[g % 2][t4], in0=xt[tt % NXB], scalar1=rstd[:, tt:tt + 1], scalar2=None, op0=ALU.mult),
                        waits=[("dve1", t_r), ("p1tp", t_tp.get((g - 2, 7)))], sig="p1xn")
                    if tt + NXB < NT:
                        issue_xload(tt + NXB)
                if g >= 1:
                    p1_transpose_evac(g - 1)
            p1_transpose_evac(7)
            t_d = P.op("dve", lambda e: e.tensor_add(out=mod_sb[:, :], in0=modps, in1=bada[:, :]),
                       waits=[("pe0", t_pe0[7]), ("ld_small", t_small)], sig="dve0")
            t_d = P.op("dve", lambda e: e.scalar_tensor_tensor(out=gs[:, :], in0=mod_sb[:, 8:16], scalar=1.0, in1=normg[:, :],
                                                               op0=ALU.add, op1=ALU.mult),
                       waits=[("dve0", t_d)], sig="dve0")
            t_d = P.op("dve", lambda e: e.tensor_add(out=gate_bc[:, :], in0=gateps, in1=bg_bc), sig="dve0")
            t_mod = t_d
            for j in range(8):
                for hh in range(2):
                    t_fix = P.op("dve", lambda e, j=j, hh=hh: e.tensor_scalar(
                        out=hT[:, j, hh * 2048:(hh + 1) * 2048], in0=hT[:, j, hh * 2048:(hh + 1) * 2048],
                        scalar1=gs[:, j:j + 1], scalar2=mod_sb[:, j:j + 1], op0=ALU.mult, op1=ALU.add),
                        waits=[("dve0", t_mod), t_ev[(7, 6)], t_ev[(7, 7)]], sig="hfix")
            t_hT = t_fix
            ckpt(2)
            t_wsl = None
            t_projpe = None
            t_attpe = None
            t_attdve = None
            t_pev = {}
            nproj = 0
            t_esk = P.op("act", lambda e: e.activation(out=esk[:, :], in_=snk[:, :], func=AF.Exp),
                         waits=[("ld_small", t_small)], sig="act0")
            sbt = sb_tiles()
            sbi = 0
            tk = {}
            n_chunk = 0
            t_evy = {}
            swn = 0
            swg = 0
            t_sw = {}

            wv_all = w_in.rearrange("(k p) c -> p k c", p=128)
            GSv = [(qT, kT, sg, vv),
                   (yflat[:, 16384:20480], yflat[:, 20480:24576], yflat[:, 24576:28672],
                    yflat[:, 28672:32768].rearrange("p (b c) -> p b c", c=128))]
            wslb = [wsl1[:, :].rearrange("p (k c) -> p k c", c=512), wsl]
            ppb = [ps[:, 3072:3584], ps[:, 3584:4096]]
            pst = {"n": 0, "pev": {}, "projpe": {}, "lastpev": {}, "attdone": {}, "stmp": None}

            def sb_proj_gen(st):
                gq, gk, gsg, gv = GSv[st % 2]
                wb = wslb[st % 2]
                wcol = {"q": 0, "k": 128, "g": 256, "v": 384}
                if st < 4:
                    dl = [(128, 512 + st * 128, 128), (384, 1024 + st * 128, 128), (0, st * 128, 128), (256, 1536 + st * 128, 128)]
                else:
                    dl = [(128, 2560, 64), (192, 2560, 64), (384, 2688, 64), (448, 2688, 64), (0, 2048, 128), (256, 2816, 128)]
                t_w = None
                for (dc, sc, w) in dl:
                    t_w = P.op("pool", lambda e, wb=wb, dc=dc, sc=sc, w=w: e.dma_start(out=wb[:, :, dc:dc + w], in_=wv_all[:, :, sc:sc + w]),
                               waits=[pst["projpe"].get(st - 2)] + ([("hfix", t_hT)] if st > 0 else []), sig="ld_wb%d" % (st % 2), inc=16)
                t_w = ("ld_wb%d" % (st % 2), t_w)
                free_tok = pst["attdone"].get(st - 2)
                tpe = None
                for kind in ("k", "v", "q", "g"):
                    wc = wcol[kind]
                    if kind == "v":
                        for tg in range(8):
                            n = pst["n"]
                            par = n % 2
                            for t4 in range(4):
                                tok = (tg * 4 + t4) * 128
                                for kc in range(8):
                                    last = (kc == 7 and t4 == 3)
                                    tpe = P.op("pe", lambda e, par=par, kc=kc, tok=tok, t4=t4, wb=wb: e.matmul(
                                        ppb[par][:, t4 * 128:(t4 + 1) * 128], lhsT=hT[:, kc, tok:tok + 128], rhs=wb[:, kc, 384:512],
                                        start=(kc == 0), stop=(kc == 7)),
                                        waits=[t_w, pst["pev"].get(n - 2), ("hfix", t_hT)], sig=("pprojpe" if last else None))
                                    if (not last) and (kc == 3 or kc == 7):
                                        yield
                            tv = P.op("dve", lambda e, par=par, tg=tg, gv=gv: e.tensor_copy(
                                out=gv[:, tg * 4:(tg + 1) * 4, :], in_=ppb[par].rearrange("p (a b) -> p a b", a=4)),
                                waits=[("pprojpe", tpe), free_tok], sig="ppev")
                            pst["pev"][n] = ("ppev", tv)
                            pst["n"] += 1
                            yield
                        continue
                    for tc in range(8):
                        n = pst["n"]
                        par = n % 2
                        for kc in range(8):
                            tpe = P.op("pe", lambda e, par=par, kc=kc, wc=wc, tc=tc, wb=wb: e.matmul(
                                ppb[par], lhsT=wb[:, kc, wc:wc + 128], rhs=hT[:, kc, tc * 512:(tc + 1) * 512],
                                start=(kc == 0), stop=(kc == 7)),
                                waits=[t_w, pst["pev"].get(n - 2), ("hfix", t_hT)], sig=("pprojpe" if kc == 7 else None))
                            if kc < 7:
                                yield
                        csl = slice(tc * 512, (tc + 1) * 512)
                        if kind == "q":
                            tv = P.op("dve", lambda e, par=par, gq=gq, csl=csl: e.tensor_scalar(
                                out=gq[:, csl], in0=ppb[par], scalar1=0.125, scalar2=None, op0=ALU.mult),
                                waits=[("pprojpe", tpe), free_tok], sig="ppev")
                        elif kind == "k":
                            tv = P.op("dve", lambda e, par=par, gk=gk, csl=csl: e.tensor_copy(out=gk[:, csl], in_=ppb[par]),
                                      waits=[("pprojpe", tpe), free_tok], sig="ppev")
                        elif st == 0:
                            tv = P.op("act", lambda e, par=par, gsg=gsg, csl=csl: e.activation(out=gsg[:, csl], in_=ppb[par], func=AF.Silu),
                                      waits=[("pprojpe", tpe), free_tok], sig="ppevS")
                            pst["pev"][n] = ("ppevS", tv)
                            pst["lastact"] = ("ppevS", tv)
                            pst["n"] += 1
                            yield
                            continue
                        else:
                            yield
                            ta = P.op("act", lambda e, par=par: e.activation(out=stmp, in_=ppb[par], func=AF.Exp, scale=-1.0),
                                      waits=[("pprojpe", tpe), pst["stmp"]], sig="ppevA")
                            yield
                            for qq in range(4):
                                qs = slice(qq * 128, (qq + 1) * 128)
                                t1 = P.op("dve", lambda e, qs=qs: e.tensor_scalar_add(out=stmp[:, qs], in0=stmp[:, qs], scalar1=1.0),
                                          waits=[("ppevA", ta)], sig="psil")
                                t1 = P.op("dve", lambda e, qs=qs: e.reciprocal(out=stmp[:, qs], in_=stmp[:, qs]), waits=[("psil", t1)], sig="psil")
                                osl = slice(tc * 512 + qq * 128, tc * 512 + (qq + 1) * 128)
                                tv = P.op("dve", lambda e, par=par, gsg=gsg, osl=osl, qs=qs: e.tensor_mul(
                                    out=gsg[:, osl], in0=ppb[par][:, qs], in1=stmp[:, qs]),
                                    waits=[("psil", t1), free_tok], sig="ppev")
                                if qq < 3:
                                    yield
                            pst["stmp"] = ("ppev", tv)
                        pst["pev"][n] = ("ppev", tv)
                        pst["n"] += 1
                        yield
                pst["projpe"][st] = ("pprojpe", tpe)
                pst["lastpev"][st] = [pst["pev"][pst["n"] - 1], pst["pev"][pst["n"] - 9], pst.get("lastact")]

            def pump(gen, k):
                if gen is None:
                    return None
                for _ in range(k):
                    try:
                        next(gen)
                    except StopIteration:
                        return None
                return gen

            def drain(gen):
                while gen is not None:
                    gen = pump(gen, 64)

            gen_cur = sb_proj_gen(0)

            for step in range(8):
                is_sb = step < 4
                if is_sb:
                    cq, ck, cv, cg = step * 128, 512 + step * 128, 1024 + step * 128, 1536 + step * 128
                    kw = 128
                    vw = 128
                    do_kv = True
                else:
                    j = step - 4
                    cq, ck, cv, cg = 2048 + j * 128, 2560 + (j // 2) * 64, 2688 + (j // 2) * 64, 2816 + j * 128
                    kw = 64
                    vw = 64
                    do_kv = (j % 2 == 0)
                if step > 4:
                    wv = w_in.rearrange("(k p) c -> p k c", p=128)
                    dmas = [(0, cq, 128), (256, cg, 128)]
                    if do_kv:
                        if kw == 128:
                            dmas.append((128, ck, 128))
                        else:
                            dmas.append((128, ck, 64))
                            dmas.append((192, ck, 64))
                        if vw == 128:
                            dmas.append((384, cv, 128))
                        else:
                            dmas.append((384, cv, 64))
                            dmas.append((448, cv, 64))
                            vw = 128
                    for (dc, sc, w) in dmas:
                        t_wsl = P.op("pool", lambda e, dc=dc, sc=sc, w=w: e.dma_start(out=wsl[:, :, dc:dc + w], in_=wv[:, :, sc:sc + w]),
                                     waits=[("projpe", t_projpe), ("hfix", t_hT), pst["projpe"].get(3)],
                                     sig="ld_wsl", inc=16)
                    ckpt(2.5 + 2 * step)
                    kinds = [("q", 0), ("g", 256)] + ([("k", 128)] if do_kv else [])
                    for kind, wc in kinds:
                        for tc in range(8):
                            par = nproj % 2
                            for kc in range(8):
                                last = kc == 7
                                t_projpe_new = P.op("pe", lambda e, par=par, kc=kc, wc=wc, tc=tc: e.matmul(
                                    pp[par], lhsT=wsl[:, kc, wc:wc + 128], rhs=hT[:, kc, tc * 512:(tc + 1) * 512],
                                    start=(kc == 0), stop=(kc == 7)),
                                    waits=[("ld_wsl", t_wsl), t_pev.get(nproj - 2), ("hfix", t_hT),
                                           t_attdve],
                                    sig=("projpe" if last else None))
                            t_projpe = t_projpe_new
                            dst = {"q": qT, "k": kT, "g": sg}[kind][:, tc * 512:(tc + 1) * 512]
                            if kind == "q":
                                t_pev[nproj] = ("pev", P.op("dve", lambda e, dst=dst, par=par: e.tensor_scalar(
                                    out=dst, in0=pp[par], scalar1=0.125, scalar2=None, op0=ALU.mult),
                                    waits=[("projpe", t_projpe)], sig="pev"))
                                last_dve_pev = t_pev[nproj]
                            elif kind == "k":
                                t_pev[nproj] = ("pev", P.op("dve", lambda e, dst=dst, par=par: e.tensor_copy(out=dst, in_=pp[par]),
                                                    waits=[("projpe", t_projpe)], sig="pev"))
                                last_dve_pev = t_pev[nproj]
                            else:
                                t_pev[nproj] = ("pevA", P.op("act", lambda e, dst=dst, par=par: e.activation(out=dst, in_=pp[par], func=AF.Silu),
                                                             waits=[("projpe", t_projpe), t_attdve], sig="pevA"))
                                last_act_pev = t_pev[nproj]
                            nproj += 1
                    ckpt(2.75 + 2 * step)
                    if do_kv:
                        for tg in range(8):
                            par = nproj % 2
                            for t4 in range(4):
                                tok = (tg * 4 + t4) * 128
                                for kc in range(8):
                                    last = (kc == 7 and t4 == 3)
                                    t_projpe_new = P.op("pe", lambda e, par=par, kc=kc, tok=tok, t4=t4, vw=vw: e.matmul(
                                        pp[par][:, t4 * 128:t4 * 128 + vw], lhsT=hT[:, kc, tok:tok + 128], rhs=wsl[:, kc, 384:384 + vw],
                                        start=(kc == 0), stop=(kc == 7)),
                                        waits=[("ld_wsl", t_wsl), t_pev.get(nproj - 2), t_attdve],
                                        sig=("projpe" if last else None))
                            t_projpe = t_projpe_new
                            t_pev[nproj] = ("pev", P.op("dve", lambda e, par=par, tg=tg, vw=vw: e.tensor_copy(
                                out=vv[:, tg * 4:(tg + 1) * 4, 0:vw],
                                in_=pp[par].rearrange("p (a b) -> p a b", a=4)[:, :, 0:vw]),
                                waits=[("projpe", t_projpe)], sig="pev"))
                            last_dve_pev = t_pev[nproj]
                            nproj += 1
                    projw = [last_dve_pev, last_act_pev]
                else:
                    drain(gen_cur)
                    projw = list(pst["lastpev"][step])
                    gen_cur = sb_proj_gen(step + 1) if step + 1 < 5 else None
                    qT, kT, sg, vv = GSv[step % 2]
                ckpt(3 + 2 * step)

                if is_sb:
                    QCB = 512
                    tiles = []
                    for qc in range(S // QCB):
                        nkb = (QCB // 128) * (qc + 1)
                        for kb in range(nkb - 1, -1, -1):
                            jd = kb - (QCB // 128) * qc
                            diag = jd >= 0
                            c0 = 128 * jd if diag else 0
                            tiles.append((0, qc, kb, c0, diag, kb == nkb - 1, kb == 0))
                    T = len(tiles)
                    base = sbi
                    chunk_of = {}
                    cc = n_chunk - 1
                    for i, tl in enumerate(tiles):
                        if tl[5]:
                            cc += 1
                        chunk_of[i] = cc
                    Zh = [ps[:, 0:512], ps[:, 512:1024]]
                    Rh = [ps[:, 1024:1536], ps[:, 1536:2048]]
                    Yc = [ps[:, 2048:2560], ps[:, 2560:3072]]

                    def two(ap, c0):
                        if c0 == 0:
                            return ap
                        return ap.rearrange("p (h c) -> p h c", h=2)[:, :, c0:QCB]

                    def QK(i):
                        hp0, qc, kb, c0, diag, cs, ce = tiles[i]
                        g = base + i
                        for hh in range(2):
                            hp = hh * 64
                            last = (hh == 1) and not diag
                            v = P.op("pe", lambda e, hp=hp, hh=hh, kb=kb, qc=qc, c0=c0, diag=diag, kT=kT, qT=qT: e.matmul(
                                Zh[hh][:, c0:QCB], lhsT=kT[hp:hp + 64, kb * 128:(kb + 1) * 128],
                                rhs=qT[hp:hp + 64, qc * QCB + c0:(qc + 1) * QCB], start=True, stop=not diag,
                                skip_group_check=True),
                                waits=[("sbE", tk.get(("E", g - 1)))] + (projw if i < 2 else []),
                                sig=("sbQK" if last else None), attach=True)
                        if diag:
                            for hh in range(2):
                                v = P.op("pe", lambda e, hh=hh, c0=c0: e.matmul(
                                    Zh[hh][:, c0:c0 + 128], lhsT=ident, rhs=negm, start=False, stop=True, skip_group_check=True),
                                    sig=("sbQK" if hh == 1 else None))
                        tk[("QK", g)] = v

                    def ACT_E(i):
                        hp0, qc, kb, c0, diag, cs, ce = tiles[i]
                        g = base + i
                        tk[("E", g)] = P.op("act", lambda e, g=g, c0=c0: e.activation(out=two(Eb[g % 3], c0), in_=two(ps[:, 0:1024], c0), func=AF.Exp),
                                            waits=[("sbQK", tk[("QK", g)]), ("sbA", tk.get(("A", g - 3)))], sig="sbE")

                    def ACT_G(i):
                        hp0, qc, kb, c0, diag, cs, ce = tiles[i]
                        g = base + i
                        tk[("G", g)] = P.op("act", lambda e, g=g, c0=c0: e.activation(out=two(Gb[g % 2], c0), in_=two(Eb[g % 3], c0),
                                                                                   func=AF.Ln, bias=1.0),
                                            waits=[("sbE", tk[("E", g)]), ("sbTRIC", tk.get(("TRIC", g - 2)))], sig="sbG")

                    def ACT_P(i):
                        hp0, qc, kb, c0, diag, cs, ce = tiles[i]
                        g = base + i
                        tk[("P", g)] = P.op("act", lambda e, g=g, c0=c0: e.activation(out=two(Pb[g % 2], c0), in_=two(ps[:, 1024:2048], c0), func=AF.Exp),
                                            waits=[("sbTRI", tk[("TRI", g)]), ("sbA", tk.get(("A", g - 2)))], sig="sbP")

                    def PE_TRI(i):
                        hp0, qc, kb, c0, diag, cs, ce = tiles[i]
                        g = base + i
                        if cs:
                            par = chunk_of[i] % 2
                            for hh in range(2):
                                P.op("pe", lambda e, hh=hh: e.matmul(Rh[hh], lhsT=zer, rhs=cst[:, 0:512], start=True, stop=False,
                                                                     skip_group_check=True),
                                     waits=[("sbP", tk.get(("P", g - 1)))])
                            P.op("pe", lambda e, par=par: e.matmul(Yc[par], lhsT=zer, rhs=cst[:, 0:512], start=True, stop=False,
                                                                   skip_group_check=True),
                                 waits=[("sbEV", t_evy.get(chunk_of[i] - 2))])
                        for hh in range(2):
                            v = P.op("pe", lambda e, g=g, hh=hh, c0=c0: e.matmul(
                                Rh[hh][:, c0:QCB], lhsT=triN, rhs=Gb[g % 2][:, hh * QCB + c0:(hh + 1) * QCB], start=False, stop=False,
                                skip_group_check=True),
                                waits=[("sbG", tk[("G", g)]), ("sbP", tk.get(("P", g - 1)))],
                                sig=("sbTRI" if hh == 1 else None), attach=True)
                        tk[("TRI", g)] = v

                    def PE_TRIC(i):
                        hp0, qc, kb, c0, diag, cs, ce = tiles[i]
                        g = base + i
                        for hh in range(2):
                            v = P.op("pe", lambda e, g=g, hh=hh, c0=c0: e.matmul(
                                Rh[hh][:, c0:QCB], lhsT=tricN, rhs=Gb[g % 2][:, hh * QCB + c0:(hh + 1) * QCB], start=False, stop=False,
                                skip_group_check=True),
                                waits=[("sbP", tk[("P", g)])],
                                sig=("sbTRIC" if hh == 1 else None), attach=True)
                        tk[("TRIC", g)] = v

                    def PE_PV(i):
                        hp0, qc, kb, c0, diag, cs, ce = tiles[i]
                        g = base + i
                        par = chunk_of[i] % 2
                        for hh in range(2):
                            hp = hh * 64
                            v = P.op("pe", lambda e, g=g, hh=hh, hp=hp, kb=kb, par=par, c0=c0, vv=vv: e.matmul(
                                Yc[par][hp:hp + 64, c0:QCB], lhsT=vv[:, kb, hp:hp + 64], rhs=Ab[g % 2][:, hh * QCB + c0:(hh + 1) * QCB],
                                start=False, stop=False, skip_group_check=True),
                                waits=[("sbA", tk[("A", g)])],
                                sig=("sbPV" if hh == 1 else None), attach=True)
                        tk[("PV", g)] = v

                    def DVE_A(i):
                        hp0, qc, kb, c0, diag, cs, ce = tiles[i]
                        g = base + i
                        tk[("A", g)] = P.op("dve", lambda e, g=g, c0=c0: e.tensor_mul(out=two(Ab[g % 2], c0), in0=two(Eb[g % 3], c0),
                                                                                   in1=two(Pb[g % 2], c0)),
                                            waits=[("sbP", tk[("P", g)]), ("sbPV", tk.get(("PV", g - 2)))], sig="sbA")

                    def DVE_EV(i):
                        hp0, qc, kb, c0, diag, cs, ce = tiles[i]
                        g = base + i
                        ch = chunk_of[i]
                        par = ch % 2
                        t_evy[ch] = P.op("dve", lambda e, qc=qc, par=par, step=step, sg=sg: e.tensor_mul(
                            out=ygT[:, step, qc * QCB:(qc + 1) * QCB], in0=Yc[par], in1=sg[:, qc * QCB:(qc + 1) * QCB]),
                            waits=[("sbPV", tk[("PV", g)])], sig="sbEV")

                    QK(0)
                    ACT_E(0)
                    if T > 1:
                        QK(1)
                    ACT_G(0)
                    if T > 1:
                        ACT_E(1)
                    if T > 2:
                        QK(2)
                    PE_TRI(0)
                    for i in range(T):
                        ACT_P(i)
                        PE_TRIC(i)
                        gen_cur = pump(gen_cur, PUMP_A)
                        if i + 1 < T:
                            ACT_G(i + 1)
                            PE_TRI(i + 1)
                        DVE_A(i)
                        PE_PV(i)
                        if tiles[i][6]:
                            DVE_EV(i)
                        if i + 2 < T:
                            ACT_E(i + 2)
                        if i + 3 < T:
                            QK(i + 3)
                        gen_cur = pump(gen_cur, PUMP_B + (1 if i % 6 == 5 else 0))
                    sbi += T
                    n_chunk = cc + 1
                    t_attdve = ("sbEV", t_evy[cc])
                    pst["attdone"][step] = t_attdve
                    qT, kT, sg, vv = GSv[0]
                else:
                    j = step - 4
                    heads = (2 * j, 2 * j + 1)
                    swn0 = swn
                    att_prev = t_attdve

                    def SW_QK(n):
                        gi = swn0 + n
                        zbase = (gi % 2) * 1024
                        wz = 256 if n > 0 else 128
                        for which, kblk in ((0, n), (1, n - 1)):
                            if kblk < 0:
                                continue
                            for hh in range(2):
                                hp = hh * 64
                                zc = zbase + hh * 512 + which * 128
                                P.op("pe", lambda e, zc=zc, hp=hp, kblk=kblk, n=n, which=which: e.matmul(
                                    ps[:, zc:zc + 128], lhsT=kT[hp:hp + 64, kblk * 128:(kblk + 1) * 128],
                                    rhs=qT[hp:hp + 64, n * 128:(n + 1) * 128], start=(which == 0), stop=False, skip_group_check=True),
                                    waits=[("swP", t_sw.get(("P", gi - 2)))] + (projw if n < 2 else []))
                        for hh in range(2):
                            h = heads[hh]
                            bc = C_SWB + h * 256
                            zc = zbase + hh * 512
                            v = P.op("pe", lambda e, zc=zc, bc=bc, wz=wz: e.matmul(
                                ps[:, zc:zc + wz], lhsT=ident, rhs=cst[:, bc:bc + wz], start=False, stop=True, skip_group_check=True),
                                sig=("swQK" if hh == 1 else None))
                        t_sw[("QK", gi)] = v

                    def SW_ACT(n):
                        gi = swn0 + n
                        zbase = (gi % 2) * 1024
                        wz = 256 if n > 0 else 128
                        zin = ps[:, zbase:zbase + 1024].rearrange("p (b c) -> p b c", b=2)[:, :, 0:wz]
                        pout = Psw[gi % 2].rearrange("p (b c) -> p b c", b=2)[:, :, 0:wz]
                        t_sw[("P", gi)] = P.op("act", lambda e, zin=zin, pout=pout: e.activation(out=pout, in_=zin, func=AF.Exp),
                                               waits=[("swQK", t_sw[("QK", gi)]), ("swPV", t_sw.get(("PV", gi - 2)))], sig="swP")

                    def SW_PVD(n):
                        gi = swn0 + n
                        grp = swg + n // 4
                        par = grp % 2
                        Yb = Yp[par][:, 0:512]
                        Db = Yp[par][:, 512:1024]
                        col = (n % 4) * 128
                        for hh in range(2):
                            hp = hh * 64
                            srcs = [(hh * 256, n)] + ([(hh * 256 + 128, n - 1)] if n > 0 else [])
                            ns = len(srcs)
                            for si, (pc, kblk) in enumerate(srcs):
                                P.op("pe", lambda e, Yb=Yb, hp=hp, col=col, kblk=kblk, gi=gi, pc=pc, si=si, ns=ns: e.matmul(
                                    Yb[hp:hp + 64, col:col + 128], lhsT=vv[:, kblk, 0:64], rhs=Psw[gi % 2][:, pc:pc + 128],
                                    start=(si == 0), stop=(si == ns - 1), skip_group_check=True),
                                    waits=[("swP", t_sw[("P", gi)]), ("swEV", t_sw.get(("EV", grp - 2)))] + ([att_prev] if n < 8 else []))
                            for si, (pc, kblk) in enumerate(srcs):
                                v = P.op("pe", lambda e, Db=Db, hp=hp, col=col, gi=gi, pc=pc, si=si, ns=ns: e.matmul(
                                    Db[hp:hp + 64, col:col + 128], lhsT=ones_bf[:, 0:64], rhs=Psw[gi % 2][:, pc:pc + 128],
                                    start=(si == 0), stop=(si == ns - 1), skip_group_check=True),
                                    sig=("swPV" if (hh == 1 and si == ns - 1) else None))
                        t_sw[("PV", gi)] = v
                    def SW_EV(n):
                        gi = swn0 + n
                        grp = swg + n // 4
                        par = grp % 2
                        Yb = Yp[par][:, 0:512]
                        Db = Yp[par][:, 512:1024]
                        if n % 4 == 3:
                            q0 = (n - 3) * 128
                            t1 = P.op("act", lambda e, Db=Db, j=j: e.activation(out=rden, in_=Db, func=AF.Ln, bias=esk[:, j:j + 1]),
                                      waits=[("swPV", t_sw[("PV", gi)]), ("act0", t_esk), ("swEV", t_sw.get(("EV", grp - 1)))], sig="swDa")
                            t1 = P.op("act", lambda e: e.activation(out=rden, in_=rden, func=AF.Exp, scale=-1.0),
                                      waits=[("swDa", t1)], sig="swDa")
                            t1 = P.op("dve", lambda e, Yb=Yb: e.tensor_mul(out=ytmp, in0=Yb, in1=rden), waits=[("swDa", t1)], sig="swD")
                            t_sw[("EV", grp)] = P.op("dve", lambda e, q0=q0, step=step: e.tensor_mul(
                                out=ygT[:, step, q0:q0 + 512], in0=ytmp, in1=sg[:, q0:q0 + 512]),
                                waits=[("swD", t1)], sig="swEV")

                    SW_QK(0)
                    for n in range(NT):
                        SW_ACT(n)
                        if n >= 1 and (n - 1) % 4 == 3:
                            SW_EV(n - 1)
                        if n + 1 < NT:
                            SW_QK(n + 1)
                        SW_PVD(n)
                    SW_EV(NT - 1)
                    swn += NT
                    swg += NT // 4
                    t_last_sw = t_sw[("EV", swg - 1)]
                if not is_sb:
                    t_attdve = ("swEV", t_last_sw)
                ckpt(4 + 2 * step)

            ckpt(20)
            wo_v = w_out.rearrange("(k p) e -> p k e", p=128)
            t_wo = None
            for k in range(8):
                t_wo = P.op("pool", lambda e, k=k: e.dma_start(out=wout[:, k, :], in_=wo_v[:, k, :]),
                            waits=[("projpe", t_projpe)], sig="ld_wo", inc=16)
            t_wog = None
            for k in range(8):
                t_wog = P.op("pool", lambda e, k=k: e.tensor_tensor(out=wout[:, k, :], in0=wout[:, k, :], in1=gate_bc[:, :], op=ALU.mult),
                             waits=[("ld_wo", t_wo), ("dve0", t_mod)], sig="wog")
            t_fg = P.op("sp", lambda e: e.dma_start(out=fg_bc, in_=fg_bc_d[:, :]), waits=[("projpe", t_projpe)], sig="ld_fg", inc=16)
            t_xf = {}
            t_o = {}
            t_st = {}
            t_r2 = {}
            t_sq2 = {}

            def issue_xf(tt):
                t_xf[tt] = P.op("sp", lambda e, tt=tt: e.dma_start(out=xf[tt % 3], in_=x[tt * 128:(tt + 1) * 128, :]),
                                waits=[("fr2", t_r2.get(tt - 3)), ("projpe", t_projpe)], sig="ld_xf%d" % (tt % 3), inc=16)

            t_po = {}
            for tt in range(3):
                issue_xf(tt)
            def F_PE(tt):
                for kc in range(8):
                    for eh in range(2):
                        v = P.op("pe", lambda e, tt=tt, eh=eh, kc=kc: e.matmul(
                            po[tt % 2][eh], lhsT=ygT[:, kc, tt * 128:(tt + 1) * 128], rhs=wout[:, kc, eh * 512:(eh + 1) * 512],
                            start=(kc == 0), stop=(kc == 7)),
                            waits=[("wog", t_wog), t_attdve, ("fr2", t_r2.get(tt - 2))],
                            sig=("fpo" if kc == 7 else None))
                        if kc == 7:
                            t_po[(tt, eh)] = v

            def F_A(tt):
                rb = rf[tt % 2]
                for eh in range(2):
                    t1 = P.op("dve", lambda e, tt=tt, eh=eh, rb=rb: e.tensor_add(out=rb[:, eh * 512:(eh + 1) * 512], in0=po[tt % 2][eh],
                                                                               in1=xf[tt % 3][:, eh * 512:(eh + 1) * 512]),
                              waits=[("fpo", t_po[(tt, eh)]), ("fo", t_o.get(tt - 2)), ("fsq", t_sq2.get(tt - 2)),
                                     ("ld_xf%d" % (tt % 3), t_xf[tt])], sig="fr2")
                t_r2[tt] = t1
                if tt + 3 < NT:
                    issue_xf(tt + 3)
                t_sq2[tt] = P.op("act", lambda e, tt=tt, rb=rb: e.activation(out=junkf, in_=rb, func=AF.Square, accum_out=ss2[:, tt:tt + 1]),
                                 waits=[("fr2", t_r2[tt])], sig="fsq")
                t_sqrt2[tt] = P.op("act", lambda e, tt=tt: e.activation(out=rstd2[:, tt:tt + 1], in_=ss2[:, tt:tt + 1], func=AF.Sqrt,
                                                                        scale=1.0 / D, bias=eps_t[:, 0:1]),
                                   waits=[("fsq", t_sq2[tt])], sig="fsqrt")

            def F_B(tt):
                rb = rf[tt % 2]
                t1 = P.op("dve", lambda e, tt=tt: e.reciprocal(out=rstd2[:, tt:tt + 1], in_=rstd2[:, tt:tt + 1]),
                          waits=[("fsqrt", t_sqrt2[tt])], sig="fd")
                t_o[tt] = P.op("dve", lambda e, tt=tt, rb=rb: e.scalar_tensor_tensor(
                    out=of[tt % 2], in0=rb, scalar=rstd2[:, tt:tt + 1], in1=fg_bc, op0=ALU.mult, op1=ALU.mult),
                    waits=[("fd", t1), ("ld_fg", t_fg), ("ld_out%d" % (tt % 2), t_st.get(tt - 2))], sig="fo")
                t_st[tt] = P.op("sp", lambda e, tt=tt: e.dma_start(out=out[tt * 128:(tt + 1) * 128, :], in_=of[tt % 2]),
                                waits=[("fo", t_o[tt])], sig="ld_out%d" % (tt % 2), inc=16)

            t_sqrt2 = {}
            F_PE(0)
            F_PE(1)
            F_A(0)
            for tt in range(NT):
                if tt + 2 < NT:
                    F_PE(tt + 2)
                if tt + 1 < NT:
                    F_A(tt + 1)
                F_B(tt)
            P.op("sp", lambda e: e.nop(), waits=[("ld_out0", t_st[NT - 2]), ("ld_out1", t_st[NT - 1])])

        try:
            plan_all()
        except _Stop:
            pass

        names = sorted(P.cnt.keys())
        sems = {n: es.enter_context(nc.semaphore(n)) for n in names}
        block = es.enter_context(nc.Block())

        def emit(eng, oplist):
            seen = {}
            for fn, waits, sig, inc, attach in oplist:
                pend = [(name, val) for (name, val) in waits if seen.get(name, 0) < val]
                if attach and len(pend) == 1:
                    (name, val) = pend[0]
                    ins = fn(eng)
                    ins._wait_ge(sems[name], val)
                    seen[name] = val
                else:
                    for (name, val) in pend:
                        eng.wait_ge(sems[name], val)
                        seen[name] = val
                    ins = fn(eng)
                if sig is not None:
                    ins.then_inc(sems[sig], inc)

        @block.sync
        def _(eng):
            emit(eng, P.ops["sp"])

        @block.gpsimd
        def _(eng):
            emit(eng, P.ops["pool"])

        @block.tensor
        def _(eng):
            emit(eng, P.ops["pe"])

        @block.scalar
        def _(eng):
            emit(eng, P.ops["act"])

        @block.vector
        def _(eng):
            emit(eng, P.ops["dve"])
    return nc


_CACHE = {}


def kernel(x, c, w_ada, b_ada, norm_g, w_in, sinks, w_out, final_g):
    x = np.asarray(x, np.float32)
    c = np.asarray(c, np.float32)
    w_ada = np.ascontiguousarray(np.asarray(w_ada, np.float32)[0])
    b_ada = np.asarray(b_ada, np.float32)[0]
    norm_g = np.asarray(norm_g, np.float32)[0]
    w_in = np.ascontiguousarray(np.asarray(w_in, np.float32)[0])
    sinks = np.asarray(sinks, np.float32)[0]
    w_out = np.ascontiguousarray(np.asarray(w_out, np.float32)[0])
    final_g = np.asarray(final_g, np.float32)

    def lay(v):
        return np.ascontiguousarray(v.reshape(-1, 128).T)

    bada_l = lay(b_ada[:2048])
    bg_bc = np.ascontiguousarray(np.broadcast_to(b_ada[2048:3072][None, :], (128, D)))
    normg_l = lay(norm_g)
    fg_bc = np.ascontiguousarray(np.broadcast_to(final_g[None, :], (128, D)))
    sinks_l = np.ascontiguousarray(np.stack([np.repeat(sinks[2 * j:2 * j + 2], 64) for j in range(4)], axis=1))
    consts = make_consts()
    if "nc" not in _CACHE:
        _CACHE["nc"] = build_nc()
    nc = _CACHE["nc"]
    in_maps = []
    for b in range(NCORE):
        in_maps.append({
            "x": np.ascontiguousarray(x[b]), "c_l": lay(c[b]), "w_ada": w_ada, "bada_l": bada_l, "bg_bc": bg_bc,
            "normg_l": normg_l, "w_in": w_in, "sinks_l": sinks_l, "w_out": w_out, "fg_bc": fg_bc, "consts": consts,
        })
    res = run_bass_kernel_spmd(nc, in_maps, core_ids=list(range(NCORE)))
    return np.stack([np.asarray(r["out"], np.float32) for r in res.results], axis=0)
```
